# Optimizing a Trainium2 kernel written in Bass

```python
import jax, jax.numpy as jnp
from jax import lax
import numpy as np

D_MODEL = 1024
BATCH = 4
SEQ = 4096
DEPTH = 2

PLE_DIM = 256
POOL_WINDOWS = (2, 4, 8, 16)
N_POOL_GROUPS = 4
POOL_WIDTH = D_MODEL
POOL_GROUP_DIM = POOL_WIDTH // N_POOL_GROUPS
SGU_WIDTH = D_MODEL
SGU_CHUNK = 128
SGU_GROUPS = 8
SGU_GROUP_DIM = SGU_WIDTH // SGU_GROUPS
N_BRANCH = 2
IN_WIDTH = POOL_WIDTH + 2 * SGU_WIDTH + N_BRANCH * D_MODEL
D_FF = 4 * D_MODEL
EPS = 1e-6

kernel_name = "hybrid_pool_sgu_gated_block"


def rmsnorm(x, g):
    xf = x.astype(jnp.float32)
    y = xf * lax.rsqrt(jnp.mean(xf * xf, axis=-1, keepdims=True) + EPS)
    return (y * g.astype(jnp.float32)).astype(x.dtype)


def layernorm(x, g, b):
    xf = x.astype(jnp.float32)
    mu = jnp.mean(xf, axis=-1, keepdims=True)
    xc = xf - mu
    y = xc * lax.rsqrt(jnp.mean(xc * xc, axis=-1, keepdims=True) + EPS)
    return (y * g.astype(jnp.float32) + b.astype(jnp.float32)).astype(x.dtype)


def pool_mixer(z, pool_w, pool_scale):
    B, S, _ = z.shape
    zf = z.astype(jnp.float32).reshape(B, S, N_POOL_GROUPS, POOL_GROUP_DIM)
    c = jnp.cumsum(zf, axis=1)
    t = jnp.arange(S)
    outs = []
    for gi, w in enumerate(POOL_WINDOWS):
        cg = c[:, :, gi]
        lagged = jnp.pad(cg, ((0, 0), (w, 0), (0, 0)))[:, :S]
        cnt = jnp.minimum(t + 1, w).astype(jnp.float32)[None, :, None]
        outs.append((cg - lagged) / cnt)
    pooled = (jnp.stack(outs, axis=2) - zf).astype(z.dtype)
    mixed = jnp.einsum('bsgc,gcd->bsgd', pooled, pool_w)
    return mixed.reshape(B, S, POOL_WIDTH) * pool_scale


def sgu_mixer(u, v, ln_g, ln_b, w_s, b_s):
    B, S, _ = v.shape
    n_chunks = S // SGU_CHUNK
    vn = layernorm(v, ln_g, ln_b).reshape(B, n_chunks, SGU_CHUNK, SGU_GROUPS, SGU_GROUP_DIM)
    causal = jnp.tril(jnp.ones((SGU_CHUNK, SGU_CHUNK), dtype=bool))
    ws = jnp.where(causal[None], w_s, jnp.zeros_like(w_s))
    mixed = jnp.einsum('hts,bcshd->bcthd', ws, vn) + b_s.T[None, None, :, :, None]
    return u * mixed.reshape(B, S, SGU_WIDTH)


def setup_inputs(seed: int = 0) -> dict:
    key = jax.random.key(seed)
    ks = jax.random.split(key, 24)

    def nrm(k, shape, scale):
        return jax.random.normal(k, shape, jnp.float32) * scale

    def gain(k, shape):
        return 1.0 + 0.05 * jax.random.normal(k, shape, jnp.float32)

    L, D = DEPTH, D_MODEL
    return {
        "x": nrm(ks[0], (BATCH, SEQ, D), 1.0),
        "p": nrm(ks[1], (DEPTH, BATCH, SEQ, PLE_DIM), 1.0),
        "pre_mix_g": gain(ks[2], (L, D)),
        "w_in": nrm(ks[3], (L, D, IN_WIDTH), D ** -0.5),
        "b_in": nrm(ks[4], (L, IN_WIDTH), 0.02),
        "pool_w": nrm(ks[5], (L, N_POOL_GROUPS, POOL_GROUP_DIM, POOL_GROUP_DIM), POOL_GROUP_DIM ** -0.5),
        "pool_scale": gain(ks[6], (L, POOL_WIDTH)),
        "sgu_ln_g": gain(ks[7], (L, SGU_WIDTH)),
        "sgu_ln_b": nrm(ks[8], (L, SGU_WIDTH), 0.02),
        "sgu_w_s": nrm(ks[9], (L, SGU_GROUPS, SGU_CHUNK, SGU_CHUNK), SGU_CHUNK ** -0.5),
        "sgu_b_s": 1.0 + 0.1 * jax.random.normal(ks[10], (L, SGU_GROUPS, SGU_CHUNK), jnp.float32),
        "w_pa": nrm(ks[11], (L, POOL_WIDTH, D), POOL_WIDTH ** -0.5),
        "w_pb": nrm(ks[12], (L, SGU_WIDTH, D), SGU_WIDTH ** -0.5),
        "w_o": nrm(ks[13], (L, D, D), D ** -0.5),
        "post_mix_g": gain(ks[14], (L, D)),
        "pre_ffn_g": gain(ks[15], (L, D)),
        "w_ff1": nrm(ks[16], (L, D, D_FF), D ** -0.5),
        "w_ff2": nrm(ks[17], (L, D_FF, D), D_FF ** -0.5),
        "post_ffn_g": gain(ks[18], (L, D)),
        "w_ple_gate": nrm(ks[19], (L, D, D), D ** -0.5),
        "w_ple_proj": nrm(ks[20], (L, PLE_DIM, D), PLE_DIM ** -0.5),
        "post_ple_g": gain(ks[21], (L, D)),
    }


def reference(x, p, pre_mix_g, w_in, b_in, pool_w, pool_scale, sgu_ln_g, sgu_ln_b, sgu_w_s,
              sgu_b_s, w_pa, w_pb, w_o, post_mix_g, pre_ffn_g, w_ff1, w_ff2, post_ffn_g,
              w_ple_gate, w_ple_proj, post_ple_g):
    B, S, D = x.shape
    for i in range(DEPTH):
        h = rmsnorm(x, pre_mix_g[i])
        proj = h @ w_in[i] + b_in[i]
        z = proj[..., :POOL_WIDTH]
        uv = jax.nn.gelu(proj[..., POOL_WIDTH:POOL_WIDTH + 2 * SGU_WIDTH])
        gates = jax.nn.sigmoid(proj[..., POOL_WIDTH + 2 * SGU_WIDTH:]).reshape(B, S, N_BRANCH, D)
        u, v = uv[..., :SGU_WIDTH], uv[..., SGU_WIDTH:]
        ya = pool_mixer(z, pool_w[i], pool_scale[i]) @ w_pa[i]
        yb = sgu_mixer(u, v, sgu_ln_g[i], sgu_ln_b[i], sgu_w_s[i], sgu_b_s[i]) @ w_pb[i]
        merged = gates[:, :, 0] * ya + gates[:, :, 1] * yb
        x = x + rmsnorm(merged @ w_o[i], post_mix_g[i])
        h = rmsnorm(x, pre_ffn_g[i])
        f = jnp.square(jax.nn.relu(h @ w_ff1[i])) @ w_ff2[i]
        x = x + rmsnorm(f, post_ffn_g[i])
        gate = jax.nn.sigmoid(x @ w_ple_gate[i])
        e = p[i] @ w_ple_proj[i]
        x = x + rmsnorm(gate * e, post_ple_g[i])
    return x
```

```python
import numpy as np
import concourse.bass as bass
import concourse.mybir as mybir
from concourse.bass_utils import run_bass_kernel_spmd

F32 = mybir.dt.float32
BF16 = mybir.dt.bfloat16
AF = mybir.ActivationFunctionType
ALU = mybir.AluOpType

D = 1024
L = 2
NCORES = 8
OWN = 2048
HALO = 128
NTOK = OWN + HALO
TILES = [(0, 640), (640, 512), (1152, 512), (1664, 512)]
NTMAX = 640
ZW = 16 + NTMAX
EPS = 1e-6
NBLK = 36
NWBUF = 4
BLK = 4096
V_PREMIX, V_POSTMIX, V_PREFFN, V_POSTFFN, V_POSTPLE, V_PSCALE, V_LNG, V_LNB = range(8)
POOL_WINDOWS = (2, 4, 8, 16)


class Res:
    __slots__ = ("w", "rs", "const")

    def __init__(self, const=False):
        self.w = None
        self.rs = {}
        self.const = const


class Prog:
    ENG = ("pe", "act", "dve", "pool", "sp")

    def __init__(self):
        self.q = {k: [] for k in self.ENG}
        self.semh = {}
        self.cnt = {}
        self.seen = {k: {} for k in self.ENG}

    def add_sem(self, key, handle):
        self.semh[key] = handle
        self.cnt[key] = 0

    def _wait(self, eng, toks):
        need = {}
        for t in toks:
            if t is None:
                continue
            key, val = t
            if key == "pe" and eng == "pe":
                continue
            if self.seen[eng].get(key, 0) >= val:
                continue
            if need.get(key, 0) < val:
                need[key] = val
        for key, val in need.items():
            self.seen[eng][key] = val
            s = self.semh[key]
            self.q[eng].append(lambda e, s=s, val=val: e.wait_ge(s, val))

    @staticmethod
    def _deps(reads, writes):
        deps = []
        for r in reads:
            deps.append(r.w)
        for w in writes:
            deps.append(w.w)
            deps.extend(w.rs.items())
        return deps

    @staticmethod
    def _commit(tok, reads, writes):
        for r in reads:
            if not r.const:
                if r.rs.get(tok[0], 0) < tok[1]:
                    r.rs[tok[0]] = tok[1]
        for w in writes:
            w.w = tok
            w.rs = {}

    def op(self, eng, fn, reads=(), writes=()):
        self._wait(eng, self._deps(reads, writes))
        self.cnt[eng] += 1
        tok = (eng, self.cnt[eng])
        s = self.semh[eng]
        self.q[eng].append(lambda e, fn=fn, s=s: fn(e).then_inc(s, 1))
        self._commit(tok, reads, writes)
        return tok

    def group(self, eng, fns, reads=(), writes=()):
        self._wait(eng, self._deps(reads, writes))
        self.cnt[eng] += 1
        tok = (eng, self.cnt[eng])
        s = self.semh[eng]
        for f in fns[:-1]:
            self.q[eng].append(f)
        last = fns[-1]
        self.q[eng].append(lambda e, fn=last, s=s: fn(e).then_inc(s, 1))
        self._commit(tok, reads, writes)
        return tok

    def dma(self, eng, semkey, fn, reads=(), writes=()):
        self._wait(eng, self._deps(reads, writes))
        self.cnt[semkey] += 16
        tok = (semkey, self.cnt[semkey])
        s = self.semh[semkey]
        self.q[eng].append(lambda e, fn=fn, s=s: fn(e).then_inc(s, 16))
        self._commit(tok, reads, writes)
        return tok

    def final_wait(self, eng, toks):
        self._wait(eng, toks)


def R(n, const=False):
    return [Res(const) for _ in range(n)]


def build_nc(TILES=TILES, L_RUN=L, NOUT=OWN, DUMPS=()):
    nc = bass.Bass("TRN2", target_bir_lowering=False)
    xT = nc.dram_tensor("xT", [128, 8, NTOK], F32, kind="ExternalInput").ap()
    pT = nc.dram_tensor("pT", [L, 128, 2, NTOK], F32, kind="ExternalInput").ap()
    wst = nc.dram_tensor("wst", [L, NBLK, 128, BLK], F32, kind="ExternalInput").ap()
    cvec_d = nc.dram_tensor("cvec", [128, L * 64], F32, kind="ExternalInput").ap()
    binfm_d = nc.dram_tensor("binfm", [128, L * 32], F32, kind="ExternalInput").ap()
    bvrow_d = nc.dram_tensor("bvrow", [L, 1024], F32, kind="ExternalInput").ap()
    wsT_d = nc.dram_tensor("wsT", [128, L * 1024], F32, kind="ExternalInput").ap()
    bsbc_d = nc.dram_tensor("bsbc", [128, L * 1024], F32, kind="ExternalInput").ap()
    cmask_d = nc.dram_tensor("cmask", [128, 128], F32, kind="ExternalInput").ap()
    pcore_d = nc.dram_tensor("pcore", [128, 80], F32, kind="ExternalInput").ap()
    outT = nc.dram_tensor("outT", [128, 8, NOUT], F32, kind="ExternalOutput").ap()

    dump_d = {}
    for (nm, shp, dt) in DUMPS:
        dump_d[nm] = nc.dram_tensor("dbg_" + nm, shp, dt, kind="ExternalOutput").ap()
    P = Prog()
    from contextlib import ExitStack
    with ExitStack() as es:
        def sb(name, shape, dt):
            return es.enter_context(nc.sbuf_tensor(name, shape, dt))

        X = sb("X", [128, 8, NTMAX], F32)
        H = sb("H", [128, 8, NTMAX], BF16)
        O = sb("O", [128, 8, NTMAX], F32)
        Z = sb("Z", [128, 8, ZW], F32)
        S2 = sb("S2", [128, 2, ZW], F32)
        BB = sb("BB", [128, 6, 8 * NTMAX], BF16)
        WB = sb("WB", [128, NWBUF, BLK], BF16)
        PB = sb("PB", [128, 2, NTMAX], BF16)
        RB = sb("RB", [128, NTMAX], F32)
        TMP = sb("TMP", [128, 2, ZW], F32)
        S1 = TMP
        ZH = sb("ZH", [128, L, 8, 16], F32)
        T16 = sb("T16", [128, 2, 16], F32)
        ST = sb("ST", [128, 2, 6], F32)
        MVA = sb("MVA", [128, 8, 2], F32)
        RSTD = sb("RSTD", [128, 8], F32)
        EPSC = sb("EPSC", [128, 1], F32)
        CV = sb("CV", [128, L * 64], F32)
        BIN = sb("BIN", [128, L * 32], F32)
        VBH = sb("VBH", [33, 1024], BF16)
        VBL = sb("VBL", [33, 1024], BF16)
        WSB = sb("WSB", [128, L * 1024], BF16)
        BF = sb("BF", [128, L * 1024], F32)
        CM = sb("CM", [128, 128], F32)
        PC = sb("PC", [128, 80], F32)
        ONEB = sb("ONEB", [128, 128], BF16)
        ONEF = sb("ONEF", [128, 128], F32)
        PS = es.enter_context(nc.psum_tensor("PS", [128, 8, 512], F32))
        WSF = Z[:, :, :].rearrange("p c t -> p (c t)")[:, 0:L * 1024]
        BSB = X[:, :, :].rearrange("p c t -> p (c t)")[:, 0:L * 1024]

        for key in ("pe", "act", "dve", "pool", "sp", "cst", "xld", "ost", "pld", "dbg") + tuple(
                "w%d" % i for i in range(NWBUF)):
            P.add_sem(key, es.enter_context(nc.semaphore("s_" + key)))

        rX, rH, rO, rZ = R(8), R(8), R(8), R(8)
        rB = [R(8) for _ in range(6)]
        rS2 = Res()
        rW = R(NWBUF)
        rPB, rRB = Res(), Res()
        rTMP = R(2)
        rS1 = rTMP
        rZH = R(L)
        rT16, rST, rMV, rRSTD = Res(), Res(), Res(), Res()
        rPS = R(8)
        rC = Res()
        rWSF = None

        def Bv(i):
            return BB[:, i, :].rearrange("p (c t) -> p c t", c=8)

        U, GA, GB, NB_, PL, MS = (Bv(i) for i in range(6))
        SQ = PL
        rU, rGA, rGB, rN, rPL, rMS = rB
        rSQ = rPL
        NTM = BB[:, 3, :]
        ACTB = BB[:, 0:4, :].rearrange("p a (c t) -> p (a c) t", c=8)
        rACT = rB[0] + rB[1] + rB[2] + rB[3]
        VTM = O[:, :, :].rearrange("p c t -> p (c t)")

        psn = [0]

        def banks(n):
            b = psn[0]
            if b + n > 8:
                b = 0
            psn[0] = (b + n) % 8
            return list(range(b, b + n))

        stream = []
        for (t0, nt) in TILES:
            for l in range(L_RUN):
                for j in range(NBLK):
                    stream.append((l, j))
        wstate = {"issued": 0, "cons": 0}

        def blk_len(j):
            return 2048 if j in (10, 35) else BLK

        def w_issue():
            i = wstate["issued"]
            if i >= len(stream):
                return
            l, j = stream[i]
            b = i % NWBUF
            n = blk_len(j)
            P.dma("pool", "w%d" % b,
                  lambda e, b=b, l=l, j=j, n=n: e.dma_start(out=WB[:, b, 0:n], in_=wst[l, j, :, 0:n]),
                  writes=[rW[b]])
            wstate["issued"] += 1

        def w_next():
            i = wstate["cons"]
            wstate["cons"] += 1
            return i % NWBUF

        def w_done():
            w_issue()

        cst = []
        for (dst, src) in ((CV[:], cvec_d), (BIN[:], binfm_d), (WSF, wsT_d), (BSB, bsbc_d),
                           (CM[:], cmask_d), (PC[:], pcore_d)):
            nd = len(dst.shape)
            P.dma("sp", "cst", lambda e, dst=dst, src=src: e.dma_start(out=dst, in_=src), writes=[rC])
        for _ in range(NWBUF):
            w_issue()

        BVR = VTM[0:33, 0:1024]
        BVT = VTM[0:33, 1024:2048]
        for l in range(L):
            P.dma("sp", "cst", lambda e, l=l: e.dma_start(out=VTM[32 * l:32 * l + 1, 0:1024], in_=bvrow_d[l:l + 1, :]),
                  writes=[rC])
        P.op("dve", lambda e: e.memset(ONEB[:], 1.0), writes=[rC])
        P.op("dve", lambda e: e.memset(ONEF[:], 1.0), writes=[rC])
        P.op("dve", lambda e: e.memset(EPSC[:], EPS), writes=[rC])
        P.op("dve", lambda e: e.tensor_copy(out=VBH[:], in_=BVR), reads=[rC], writes=[rC])
        P.op("dve", lambda e: e.tensor_copy(out=BVT, in_=VBH[:]), reads=[rC], writes=[rC])
        P.op("dve", lambda e: e.tensor_tensor(out=BVT, in0=BVR, in1=BVT, op=ALU.subtract),
             reads=[rC], writes=[rC] + rO)
        P.op("dve", lambda e: e.tensor_copy(out=VBL[:], in_=BVT), reads=[rC], writes=[rC] + rO)
        for l in range(L):
            for h in range(8):
                sl = slice(l * 1024 + h * 128, l * 1024 + (h + 1) * 128)
                P.op("dve", lambda e, sl=sl: e.tensor_tensor(out=WSF[:, sl], in0=WSF[:, sl], in1=CM[:], op=ALU.mult),
                     reads=[rC], writes=rZ)
            P.op("dve", lambda e, l=l: e.tensor_copy(out=WSB[:, l * 1024:(l + 1) * 1024],
                                                     in_=WSF[:, l * 1024:(l + 1) * 1024]),
                 reads=rZ, writes=[rC])
            for hh in range(2):
                bk = banks(1)[0]
                P.group("pe", [lambda e, bk=bk, l=l, hh=hh: e.matmul(
                    PS[:, bk, :], ONEF[:], WSF[:, l * 1024 + hh * 512: l * 1024 + (hh + 1) * 512],
                    start=True, stop=True)], reads=[rC] + rZ, writes=[rPS[bk]])
                for h4 in range(4):
                    h = hh * 4 + h4
                    sl = slice(l * 1024 + h * 128, l * 1024 + (h + 1) * 128)
                    col = (l * 8 + V_LNB) * 8 + h
                    P.op("dve", lambda e, bk=bk, h4=h4, sl=sl, col=col: e.scalar_tensor_tensor(
                        out=BF[:, sl], in0=PS[:, bk, h4 * 128:(h4 + 1) * 128], scalar=CV[:, col:col + 1],
                        in1=BSB[:, sl], op0=ALU.mult, op1=ALU.add),
                        reads=[rPS[bk], rC] + rX, writes=[rC])
        rC_done = rC.w
        rC.const = True

        def cv(l, vi, c):
            col = (l * 8 + vi) * 8 + c
            return CV[:, col:col + 1]

        dump_toks = []

        def dump(nm, ap, res):
            if nm in dump_d and nm not in [d[0] for d in dump_toks]:
                dump_toks.append((nm, P.dma("sp", "dbg", lambda e: e.dma_start(out=dump_d[nm], in_=ap), reads=res)))

        def colblocks(nt):
            cbs = []
            c = 0
            while c < nt:
                w = min(512, nt - c)
                cbs.append((c, w))
                c += w
            return cbs

        def norm_R(nt, sq_res):
            for (c0, w) in colblocks(nt):
                bk = banks(1)[0]
                fns = []
                for dc in range(8):
                    fns.append(lambda e, bk=bk, dc=dc, c0=c0, w=w: e.matmul(
                        PS[:, bk, 0:w], ONEB[:], SQ[:, dc, c0:c0 + w], start=(dc == 0), stop=(dc == 7)))
                P.group("pe", fns, reads=[rC] + sq_res, writes=[rPS[bk]])
                P.op("act", lambda e, bk=bk, c0=c0, w=w: e.activation(
                    out=RB[:, c0:c0 + w], in_=PS[:, bk, 0:w], func=AF.Sqrt, bias=EPSC[:, 0:1], scale=1.0 / D),
                    reads=[rPS[bk], rC], writes=[rRB])
                P.op("dve", lambda e, c0=c0, w=w: e.reciprocal(out=RB[:, c0:c0 + w], in_=RB[:, c0:c0 + w]),
                     reads=[rRB], writes=[rRB])

        def squares_of_X(nt):
            for dc in range(8):
                P.op("act", lambda e, dc=dc: e.activation(out=SQ[:, dc, 0:nt], in_=X[:, dc, 0:nt], func=AF.Square),
                     reads=[rX[dc]], writes=[rSQ[dc]])

        def make_H(l, vi, nt):
            for dc in range(8):
                P.op("dve", lambda e, dc=dc: e.scalar_tensor_tensor(
                    out=H[:, dc, 0:nt], in0=X[:, dc, 0:nt], scalar=cv(l, vi, dc), in1=RB[:, 0:nt],
                    op0=ALU.mult, op1=ALU.mult), reads=[rX[dc], rRB, rC], writes=[rH[dc]])

        def residual(l, vi, nt):
            for dc in range(8):
                P.op("dve", lambda e, dc=dc: e.tensor_tensor(
                    out=O[:, dc, 0:nt], in0=O[:, dc, 0:nt], in1=RB[:, 0:nt], op=ALU.mult),
                    reads=[rO[dc], rRB], writes=[rO[dc]])
                P.op("dve", lambda e, dc=dc: e.scalar_tensor_tensor(
                    out=X[:, dc, 0:nt], in0=O[:, dc, 0:nt], scalar=cv(l, vi, dc), in1=X[:, dc, 0:nt],
                    op0=ALU.mult, op1=ALU.add), reads=[rO[dc], rX[dc], rC], writes=[rX[dc]])

        def fm_matmul(b, nt, rhs, rhs_res, nk, evac, fis=range(4), kstride=512):
            cbs = colblocks(nt)
            for fi in fis:
                bks = banks(len(cbs))
                fns = []
                for k in range(nk):
                    for ci, (c0, w) in enumerate(cbs):
                        fns.append(lambda e, bk=bks[ci], k=k, fi=fi, c0=c0, w=w: e.matmul(
                            PS[:, bk, 0:w], WB[:, b, k * kstride + fi * 128: k * kstride + (fi + 1) * 128],
                            rhs[:, k, c0:c0 + w], start=(k == 0), stop=(k == nk - 1)))
                P.group("pe", fns, reads=[rW[b]] + rhs_res, writes=[rPS[bk] for bk in bks])
                for ci, (c0, w) in enumerate(cbs):
                    evac(fi, c0, w, bks[ci])

        def tile_layer(ti, t0, nt, l):
            cbs = colblocks(nt)
            nch = nt // 128
            P.dma("pool", "pld", lambda e: e.dma_start(out=PB[:, :, 0:nt], in_=pT[l, :, :, t0:t0 + nt]),
                  writes=[rPB])

            squares_of_X(nt)
            norm_R(nt, rSQ)
            make_H(l, V_PREMIX, nt)

            specs = [(Z, rZ, AF.Identity, 0, 16), (U, rU, AF.Gelu_apprx_tanh, 8, 0),
                     (GA, rGA, AF.Sigmoid, 16, 0), (GB, rGB, AF.Sigmoid, 24, 0)]
            for (dst, dres, func, bcol, off) in specs:
                for half in range(2):
                    b = w_next()

                    def evac(fi, c0, w, bk, dst=dst, dres=dres, func=func, bcol=bcol, off=off, half=half):
                        fc = half * 4 + fi
                        col = l * 32 + bcol + fc
                        P.op("act", lambda e: e.activation(
                            out=dst[:, fc, off + c0: off + c0 + w], in_=PS[:, bk, 0:w], func=func,
                            bias=BIN[:, col:col + 1]), reads=[rPS[bk], rC], writes=[dres[fc]])
                    fm_matmul(b, nt, H, rH, 8, evac)
                    w_done()

            W_ = 16 + nt
            if ti == 0:
                P.op("dve", lambda e: e.memset(Z[:, :, 0:16], 0.0), writes=rZ)
                P.op("dve", lambda e: e.tensor_scalar(
                    out=Z[:, :, 16:16 + HALO], in0=Z[:, :, 16:16 + HALO], scalar1=PC[:, 0:1], scalar2=None,
                    op0=ALU.mult), reads=rZ + [rC], writes=rZ)
            else:
                P.op("dve", lambda e: e.tensor_copy(out=Z[:, :, 0:16], in_=ZH[:, l, :, :]),
                     reads=[rZH[l]], writes=rZ)
            for g in range(4):
                zs = Z[:, 2 * g:2 * g + 2, :]
                zres = [rZ[2 * g], rZ[2 * g + 1]]
                P.op("dve", lambda e, zs=zs: e.tensor_tensor(
                    out=S1[:, :, 1:W_], in0=zs[:, :, 1:W_], in1=zs[:, :, 0:W_ - 1], op=ALU.add),
                    reads=zres, writes=rS1)
                cur, rcur = S1, rS1
                if g >= 1:
                    P.op("dve", lambda e: e.tensor_tensor(
                        out=S2[:, :, 3:W_], in0=S1[:, :, 3:W_], in1=S1[:, :, 1:W_ - 2], op=ALU.add),
                        reads=rS1, writes=[rS2])
                    cur, rcur = S2, [rS2]
                if g >= 2:
                    P.op("dve", lambda e: e.tensor_tensor(
                        out=S1[:, :, 7:W_], in0=S2[:, :, 7:W_], in1=S2[:, :, 3:W_ - 4], op=ALU.add),
                        reads=[rS2], writes=rS1)
                    cur, rcur = S1, rS1
                if g >= 3:
                    P.op("dve", lambda e: e.tensor_tensor(
                        out=S2[:, :, 15:W_], in0=S1[:, :, 15:W_], in1=S1[:, :, 7:W_ - 8], op=ALU.add),
                        reads=rS1, writes=[rS2])
                    cur, rcur = S2, [rS2]
                wdw = POOL_WINDOWS[g]
                P.op("dve", lambda e, cur=cur, g=g, wdw=wdw: e.scalar_tensor_tensor(
                    out=PL[:, 2 * g:2 * g + 2, 0:nt], in0=cur[:, :, 16:16 + nt], scalar=1.0 / wdw,
                    in1=Z[:, 2 * g:2 * g + 2, 16:16 + nt], op0=ALU.mult, op1=ALU.subtract),
                    reads=rcur + zres, writes=[rPL[2 * g], rPL[2 * g + 1]])
                if ti == 0:
                    for cc in range(2):
                        P.op("dve", lambda e, cur=cur, g=g, cc=cc: e.tensor_tensor(
                            out=T16[:, cc, :], in0=cur[:, cc, 16 + HALO:32 + HALO],
                            in1=PC[:, 1 + g * 16: 1 + (g + 1) * 16], op=ALU.mult),
                            reads=rcur + [rC], writes=[rT16])
                    P.op("dve", lambda e, g=g: e.tensor_tensor(
                        out=PL[:, 2 * g:2 * g + 2, HALO:HALO + 16], in0=T16[:, :, :],
                        in1=Z[:, 2 * g:2 * g + 2, 16 + HALO:32 + HALO], op=ALU.subtract),
                        reads=[rT16] + zres, writes=[rPL[2 * g], rPL[2 * g + 1]])
            dump("Z", Z[:, :, :], rZ)
            dump("PL", PL, rPL)
            dump("H", H[:, :, :], rH)
            P.op("dve", lambda e: e.tensor_copy(out=ZH[:, l, :, :], in_=Z[:, :, nt:nt + 16]),
                 reads=rZ, writes=[rZH[l]])

            vb = [w_next(), w_next()]
            for half in range(2):
                b = vb[half]
                for c in range(nch):
                    bk = banks(1)[0]
                    fns = []
                    for k in range(8):
                        fns.append(lambda e, bk=bk, k=k, c=c, b=b: e.matmul(
                            PS[:, bk, :], H[:, k, c * 128:(c + 1) * 128], WB[:, b, k * 512:(k + 1) * 512],
                            start=(k == 0), stop=False))
                    vsl = slice(half * 512, (half + 1) * 512)
                    pr = slice(32 * l, 32 * l + 1)
                    fns.append(lambda e, bk=bk, vsl=vsl, pr=pr: e.matmul(
                        PS[:, bk, :], ONEB[pr, :], VBH[pr, vsl], start=False, stop=False))
                    fns.append(lambda e, bk=bk, vsl=vsl, pr=pr: e.matmul(
                        PS[:, bk, :], ONEB[pr, :], VBL[pr, vsl], start=False, stop=True))
                    P.group("pe", fns, reads=[rW[b], rC] + rH, writes=[rPS[bk]])
                    o0 = c * 1024 + half * 512
                    P.op("act", lambda e, bk=bk, o0=o0: e.activation(
                        out=VTM[:, o0:o0 + 512], in_=PS[:, bk, :], func=AF.Gelu_apprx_tanh),
                        reads=[rPS[bk]], writes=rO)
                w_done()
            for c in range(nch):
                for half in range(2):
                    o0 = c * 1024 + half * 512
                    P.op("dve", lambda e, o0=o0, half=half: e.bn_stats(out=ST[:, half, :], in_=VTM[:, o0:o0 + 512]),
                         reads=rO, writes=[rST])
                P.op("dve", lambda e, c=c: e.bn_aggr(out=MVA[:, c, :], in_=ST[:, :, :].rearrange("p a b -> p (a b)")),
                     reads=[rST], writes=[rMV])
            P.op("act", lambda e: e.activation(out=RSTD[:, 0:nch], in_=MVA[:, 0:nch, 1], func=AF.Sqrt,
                                               bias=EPSC[:, 0:1], scale=1.0), reads=[rMV, rC], writes=[rRSTD])
            P.op("dve", lambda e: e.reciprocal(out=RSTD[:, 0:nch], in_=RSTD[:, 0:nch]), reads=[rRSTD], writes=[rRSTD])
            for c in range(nch):
                P.op("dve", lambda e, c=c: e.tensor_scalar(
                    out=NTM[:, c * 1024:(c + 1) * 1024], in0=VTM[:, c * 1024:(c + 1) * 1024],
                    scalar1=MVA[:, c, 0:1], scalar2=RSTD[:, c:c + 1], op0=ALU.subtract, op1=ALU.mult),
                    reads=rO + [rMV, rRSTD], writes=rN)

            dump("VTM", VTM, rO)
            dump("NTM", NTM, rN)
            b = w_next()
            for g in range(4):
                for dh in range(2):
                    bks = banks(len(cbs))
                    fns = []
                    for cc in range(2):
                        for ci, (c0, w) in enumerate(cbs):
                            o0 = (g * 2 + cc) * 256 + dh * 128
                            fns.append(lambda e, bk=bks[ci], o0=o0, g=g, cc=cc, c0=c0, w=w, b=b: e.matmul(
                                PS[:, bk, 0:w], WB[:, b, o0:o0 + 128], PL[:, 2 * g + cc, c0:c0 + w],
                                start=(cc == 0), stop=(cc == 1)))
                    P.group("pe", fns, reads=[rW[b], rPL[2 * g], rPL[2 * g + 1]], writes=[rPS[bk] for bk in bks])
                    oc = 2 * g + dh
                    for ci, (c0, w) in enumerate(cbs):
                        P.op("dve", lambda e, bk=bks[ci], oc=oc, c0=c0, w=w: e.tensor_scalar(
                            out=MS[:, oc, c0:c0 + w], in0=PS[:, bk, 0:w], scalar1=cv(l, V_PSCALE, oc), scalar2=None,
                            op0=ALU.mult), reads=[rPS[bks[ci]], rC], writes=[rMS[oc]])
            w_done()

            dump("MS", MS, rMS)
            for half in range(2):
                b = w_next()

                def evac(fi, c0, w, bk, half=half):
                    oc = half * 4 + fi
                    P.op("dve", lambda e: e.tensor_tensor(
                        out=GA[:, oc, c0:c0 + w], in0=PS[:, bk, 0:w], in1=GA[:, oc, c0:c0 + w], op=ALU.mult),
                        reads=[rPS[bk], rGA[oc]], writes=[rGA[oc]])
                fm_matmul(b, nt, MS, rMS, 8, evac)
                w_done()

            for h in range(8):
                for ci, (c0, w) in enumerate(cbs):
                    bk = banks(1)[0]
                    nck = w // 128
                    fns = []
                    for cc in range(nck):
                        c = c0 // 128 + cc
                        fns.append(lambda e, bk=bk, cc=cc, c=c, h=h: e.matmul(
                            PS[:, bk, cc * 128:(cc + 1) * 128], NTM[:, c * 1024 + h * 128: c * 1024 + (h + 1) * 128],
                            WSB[:, l * 1024 + h * 128: l * 1024 + (h + 1) * 128], start=True, stop=True))
                    P.group("pe", fns, reads=rN + [rC], writes=[rPS[bk]])
                    tb = (h * len(cbs) + ci) % 2
                    bf = BF[:, l * 1024 + h * 128: l * 1024 + (h + 1) * 128]
                    P.op("dve", lambda e, bk=bk, nck=nck, w=w, tb=tb, bf=bf, h=h: e.scalar_tensor_tensor(
                        out=TMP[:, tb, 0:w].rearrange("p (a t) -> p a t", a=nck),
                        in0=PS[:, bk, 0:w].rearrange("p (a t) -> p a t", a=nck),
                        scalar=cv(l, V_LNG, h),
                        in1=bf.unsqueeze(1).broadcast_to([128, nck, 128]),
                        op0=ALU.mult, op1=ALU.add), reads=[rPS[bk], rC], writes=[rTMP[tb]])
                    P.op("dve", lambda e, tb=tb, h=h, c0=c0, w=w: e.tensor_tensor(
                        out=U[:, h, c0:c0 + w], in0=TMP[:, tb, 0:w], in1=U[:, h, c0:c0 + w], op=ALU.mult),
                        reads=[rTMP[tb], rU[h]], writes=[rU[h]])

            dump("SG", U, rU)
            dump("M1", GA, rGA)
            for half in range(2):
                b = w_next()

                def evac(fi, c0, w, bk, half=half):
                    oc = half * 4 + fi
                    P.op("dve", lambda e: e.tensor_tensor(
                        out=GB[:, oc, c0:c0 + w], in0=PS[:, bk, 0:w], in1=GB[:, oc, c0:c0 + w], op=ALU.mult),
                        reads=[rPS[bk], rGB[oc]], writes=[rGB[oc]])
                    P.op("dve", lambda e: e.tensor_tensor(
                        out=GB[:, oc, c0:c0 + w], in0=GB[:, oc, c0:c0 + w], in1=GA[:, oc, c0:c0 + w], op=ALU.add),
                        reads=[rGB[oc], rGA[oc]], writes=[rGB[oc]])
                fm_matmul(b, nt, U, rU, 8, evac)
                w_done()

            def evac_O(oc, c0, w, bk):
                P.op("dve", lambda e: e.tensor_copy(out=O[:, oc, c0:c0 + w], in_=PS[:, bk, 0:w]),
                     reads=[rPS[bk]], writes=[rO[oc]])
                P.op("act", lambda e: e.activation(out=SQ[:, oc, c0:c0 + w], in_=O[:, oc, c0:c0 + w], func=AF.Square),
                     reads=[rO[oc]], writes=[rSQ[oc]])

            for half in range(2):
                b = w_next()
                fm_matmul(b, nt, GB, rGB, 8, lambda fi, c0, w, bk, half=half: evac_O(half * 4 + fi, c0, w, bk))
                w_done()

            dump("MG", GB, rGB)
            dump("O1", O[:, :, :], rO)
            norm_R(nt, rSQ)
            residual(l, V_POSTMIX, nt)
            squares_of_X(nt)
            norm_R(nt, rSQ)
            make_H(l, V_PREFFN, nt)

            dump("X1", X[:, :, :], rX)
            for j in range(8):
                b = w_next()

                def evac(fi, c0, w, bk, j=j):
                    fc = j * 4 + fi
                    tb = fi % 2
                    P.op("act", lambda e: e.activation(out=TMP[:, tb, c0:c0 + w], in_=PS[:, bk, 0:w], func=AF.Relu),
                         reads=[rPS[bk]], writes=[rTMP[tb]])
                    P.op("dve", lambda e: e.tensor_tensor(
                        out=ACTB[:, fc, c0:c0 + w], in0=PS[:, bk, 0:w], in1=TMP[:, tb, c0:c0 + w], op=ALU.mult),
                        reads=[rPS[bk], rTMP[tb]], writes=[rACT[fc]])
                fm_matmul(b, nt, H, rH, 8, evac)
                w_done()

            for j in range(8):
                b = w_next()
                fm_matmul(b, nt, ACTB, rACT, 32, lambda fi, c0, w, bk, j=j: evac_O(j, c0, w, bk),
                          fis=[0], kstride=128)
                w_done()

            norm_R(nt, rSQ)
            residual(l, V_POSTFFN, nt)
            for dc in range(8):
                P.op("act", lambda e, dc=dc: e.activation(out=H[:, dc, 0:nt], in_=X[:, dc, 0:nt], func=AF.Copy),
                     reads=[rX[dc]], writes=[rH[dc]])

            for half in range(2):
                b = w_next()

                def evac(fi, c0, w, bk, half=half):
                    oc = half * 4 + fi
                    P.op("act", lambda e: e.activation(
                        out=Z[:, oc, 16 + c0:16 + c0 + w], in_=PS[:, bk, 0:w], func=AF.Sigmoid),
                        reads=[rPS[bk]], writes=[rZ[oc]])
                fm_matmul(b, nt, H, rH, 8, evac)
                w_done()

            b = w_next()
            for oc in range(8):
                bks = banks(len(cbs))
                fns = []
                for kc in range(2):
                    for ci, (c0, w) in enumerate(cbs):
                        fns.append(lambda e, bk=bks[ci], kc=kc, oc=oc, c0=c0, w=w, b=b: e.matmul(
                            PS[:, bk, 0:w], WB[:, b, kc * 1024 + oc * 128: kc * 1024 + (oc + 1) * 128],
                            PB[:, kc, c0:c0 + w], start=(kc == 0), stop=(kc == 1)))
                P.group("pe", fns, reads=[rW[b], rPB], writes=[rPS[bk] for bk in bks])
                for ci, (c0, w) in enumerate(cbs):
                    bk = bks[ci]
                    P.op("dve", lambda e, bk=bk, oc=oc, c0=c0, w=w: e.tensor_tensor(
                        out=O[:, oc, c0:c0 + w], in0=PS[:, bk, 0:w], in1=Z[:, oc, 16 + c0:16 + c0 + w], op=ALU.mult),
                        reads=[rPS[bk], rZ[oc]], writes=[rO[oc]])
                    P.op("act", lambda e, oc=oc, c0=c0, w=w: e.activation(
                        out=SQ[:, oc, c0:c0 + w], in_=O[:, oc, c0:c0 + w], func=AF.Square),
                        reads=[rO[oc]], writes=[rSQ[oc]])
            w_done()

            norm_R(nt, rSQ)
            residual(l, V_POSTPLE, nt)

        last_store = None
        for ti, (t0, nt) in enumerate(TILES):
            P.dma("sp", "xld", lambda e, t0=t0, nt=nt: e.dma_start(out=X[:, :, 0:nt], in_=xT[:, :, t0:t0 + nt]),
                  writes=rX)
            for l in range(L_RUN):
                tile_layer(ti, t0, nt, l)
            s0 = HALO if ti == 0 else 0
            o0 = t0 + s0 - HALO
            n_out = nt - s0
            last_store = P.dma("sp", "ost", lambda e, s0=s0, o0=o0, n_out=n_out: e.dma_start(
                out=outT[:, :, o0:o0 + n_out], in_=X[:, :, s0:s0 + n_out]), reads=rX)
        P.final_wait("sp", [last_store] + [d[1] for d in dump_toks])

        with nc.Block() as block:
            @block.tensor
            def _(e):
                for f in P.q["pe"]:
                    f(e)

            @block.scalar
            def _(e):
                for f in P.q["act"]:
                    f(e)

            @block.vector
            def _(e):
                for f in P.q["dve"]:
                    f(e)

            @block.gpsimd
            def _(e):
                for f in P.q["pool"]:
                    f(e)

            @block.sync
            def _(e):
                for f in P.q["sp"]:
                    f(e)
    return nc


def _fm8(v):
    return np.ascontiguousarray(v.reshape(8, 128).T)


def _blk_k512(W, col0):
    K = W.shape[0]
    return W[:, col0:col0 + 512].reshape(K // 128, 128, 512).transpose(1, 0, 2).reshape(128, -1)


def _build_wstream(inp):
    ws = np.zeros((L, NBLK, 128, BLK), np.float32)
    for l in range(L):
        w_in = inp["w_in"][l]
        order = [0, 512, 1024, 1536, 3072, 3584, 4096, 4608, 2048, 2560]
        for j, c0 in enumerate(order):
            ws[l, j] = _blk_k512(w_in, c0)
        ws[l, 10, :, :2048] = inp["pool_w"][l].reshape(4, 2, 128, 256).transpose(2, 0, 1, 3).reshape(128, 2048)
        for i, name in enumerate(("w_pa", "w_pb", "w_o")):
            for half in range(2):
                ws[l, 11 + 2 * i + half] = _blk_k512(inp[name][l], half * 512)
        for j in range(8):
            ws[l, 17 + j] = _blk_k512(inp["w_ff1"][l], j * 512)
        w2 = inp["w_ff2"][l]
        for j in range(8):
            ws[l, 25 + j] = w2[:, j * 128:(j + 1) * 128].reshape(32, 128, 128).transpose(1, 0, 2).reshape(128, BLK)
        for half in range(2):
            ws[l, 33 + half] = _blk_k512(inp["w_ple_gate"][l], half * 512)
        ws[l, 35, :, :2048] = inp["w_ple_proj"][l].reshape(2, 128, 1024).transpose(1, 0, 2).reshape(128, 2048)
    return ws


_NC_CACHE = {}


def make_in_maps(inp):
    x, p = inp["x"], inp["p"]
    B, S, _ = x.shape
    wst = _build_wstream(inp)
    cvec = np.zeros((128, L * 64), np.float32)
    names = ["pre_mix_g", "post_mix_g", "pre_ffn_g", "post_ffn_g", "post_ple_g", "pool_scale", "sgu_ln_g", "sgu_ln_b"]
    for l in range(L):
        for vi, nm in enumerate(names):
            cvec[:, (l * 8 + vi) * 8:(l * 8 + vi + 1) * 8] = _fm8(inp[nm][l])
    binfm = np.zeros((128, L * 32), np.float32)
    bvrow = np.zeros((L, 1024), np.float32)
    for l in range(L):
        b = inp["b_in"][l]
        for gi, c0 in enumerate((0, 1024, 3072, 4096)):
            binfm[:, l * 32 + gi * 8: l * 32 + (gi + 1) * 8] = _fm8(b[c0:c0 + 1024])
        bvrow[l] = b[2048:3072]
    wsT = np.ascontiguousarray(inp["sgu_w_s"].transpose(3, 0, 1, 2)).reshape(128, L * 1024)
    bsbc = np.ascontiguousarray(np.broadcast_to(inp["sgu_b_s"].reshape(1, L * 1024), (128, L * 1024)))
    si = np.arange(128)
    cmask = (si[:, None] <= si[None, :]).astype(np.float32)

    in_maps = []
    for c in range(NCORES):
        b, half = c // 2, c % 2
        s0 = half * OWN
        xt = np.zeros((NTOK, D), np.float32)
        pt = np.zeros((L, NTOK, 256), np.float32)
        xt[HALO:] = x[b, s0:s0 + OWN]
        pt[:, HALO:] = p[:, b, s0:s0 + OWN]
        pcore = np.zeros((128, 80), np.float32)
        if half == 1:
            xt[:HALO] = x[b, s0 - HALO:s0]
            pt[:, :HALO] = p[:, b, s0 - HALO:s0]
            pcore[:, 0] = 1.0
        for g, w in enumerate(POOL_WINDOWS):
            j = np.arange(16)
            cnt = np.minimum(j + 1, w) if half == 0 else np.full(16, w)
            pcore[:, 1 + g * 16: 1 + (g + 1) * 16] = (1.0 / cnt.astype(np.float32))[None, :]
        xTc = np.ascontiguousarray(xt.T.reshape(8, 128, NTOK).transpose(1, 0, 2))
        pTc = np.ascontiguousarray(pt.transpose(0, 2, 1).reshape(L, 2, 128, NTOK).transpose(0, 2, 1, 3))
        in_maps.append({"xT": xTc, "pT": pTc, "wst": wst, "cvec": cvec, "binfm": binfm, "bvrow": bvrow,
                        "wsT": wsT, "bsbc": bsbc, "cmask": cmask, "pcore": pcore})
    return in_maps


def kernel(**inputs):
    inp = {k: np.asarray(v, dtype=np.float32) for k, v in inputs.items()}
    B, S, _ = inp["x"].shape
    in_maps = make_in_maps(inp)
    if "nc" not in _NC_CACHE:
        _NC_CACHE["nc"] = build_nc()
    nc = _NC_CACHE["nc"]
    res = run_bass_kernel_spmd(nc, in_maps, core_ids=list(range(NCORES)))
    out = np.empty((B, S, D), np.float32)
    for c in range(NCORES):
        b, half = c // 2, c % 2
        o = np.asarray(res.results[c]["outT"], dtype=np.float32)
        out[b, half * OWN:(half + 1) * OWN, :] = o.transpose(1, 0, 2).reshape(D, OWN).T
    return out
```

```python
import numpy as np
import concourse.bass as bass
import concourse.mybir as mybir
from concourse.bass_utils import run_bass_kernel_spmd

F32 = mybir.dt.float32
BF16 = mybir.dt.bfloat16
AF = mybir.ActivationFunctionType
ALU = mybir.AluOpType

D = 1024
L = 2
NCORES = 8
OWN = 2048
HALO = 128
NTOK = OWN + HALO
TILES = [(0, 640), (640, 512), (1152, 512), (1664, 512)]
NTMAX = 640
ZW = 16 + NTMAX
EPS = 1e-6
NBLK = 36
NWBUF = 4
BLK = 4096
V_PREMIX, V_POSTMIX, V_PREFFN, V_POSTFFN, V_POSTPLE, V_PSCALE, V_LNG, V_LNB = range(8)
POOL_WINDOWS = (2, 4, 8, 16)


class Res:
    __slots__ = ("w", "rs", "const")

    def __init__(self, const=False):
        self.w = None
        self.rs = {}
        self.const = const


class Prog:
    ENG = ("pe", "act", "dve", "pool", "sp")

    def __init__(self):
        self.q = {k: [] for k in self.ENG}
        self.semh = {}
        self.cnt = {}
        self.seen = {k: {} for k in self.ENG}

    def add_sem(self, key, handle):
        self.semh[key] = handle
        self.cnt[key] = 0

    def _wait(self, eng, toks):
        need = {}
        for t in toks:
            if t is None:
                continue
            key, val = t
            if key == "pe" and eng == "pe":
                continue
            if self.seen[eng].get(key, 0) >= val:
                continue
            if need.get(key, 0) < val:
                need[key] = val
        for key, val in need.items():
            self.seen[eng][key] = val
            s = self.semh[key]
            self.q[eng].append(lambda e, s=s, val=val: e.wait_ge(s, val))

    @staticmethod
    def _deps(reads, writes):
        deps = []
        for r in reads:
            deps.append(r.w)
        for w in writes:
            deps.append(w.w)
            deps.extend(w.rs.items())
        return deps

    @staticmethod
    def _commit(tok, reads, writes):
        for r in reads:
            if not r.const:
                if r.rs.get(tok[0], 0) < tok[1]:
                    r.rs[tok[0]] = tok[1]
        for w in writes:
            w.w = tok
            w.rs = {}

    def op(self, eng, fn, reads=(), writes=()):
        self._wait(eng, self._deps(reads, writes))
        self.cnt[eng] += 1
        tok = (eng, self.cnt[eng])
        s = self.semh[eng]
        self.q[eng].append(lambda e, fn=fn, s=s: fn(e).then_inc(s, 1))
        self._commit(tok, reads, writes)
        return tok

    def group(self, eng, fns, reads=(), writes=()):
        self._wait(eng, self._deps(reads, writes))
        self.cnt[eng] += 1
        tok = (eng, self.cnt[eng])
        s = self.semh[eng]
        for f in fns[:-1]:
            self.q[eng].append(f)
        last = fns[-1]
        self.q[eng].append(lambda e, fn=last, s=s: fn(e).then_inc(s, 1))
        self._commit(tok, reads, writes)
        return tok

    def dma(self, eng, semkey, fn, reads=(), writes=()):
        self._wait(eng, self._deps(reads, writes))
        self.cnt[semkey] += 16
        tok = (semkey, self.cnt[semkey])
        s = self.semh[semkey]
        self.q[eng].append(lambda e, fn=fn, s=s: fn(e).then_inc(s, 16))
        self._commit(tok, reads, writes)
        return tok

    def final_wait(self, eng, toks):
        self._wait(eng, toks)


def R(n, const=False):
    return [Res(const) for _ in range(n)]


def build_nc(TILES=TILES, L_RUN=L, NOUT=OWN, DUMPS=()):
    nc = bass.Bass("TRN2", target_bir_lowering=False)
    xT = nc.dram_tensor("xT", [128, 8, NTOK], F32, kind="ExternalInput").ap()
    pT = nc.dram_tensor("pT", [L, 128, 2, NTOK], F32, kind="ExternalInput").ap()
    wst = nc.dram_tensor("wst", [L, NBLK, 128, BLK], F32, kind="ExternalInput").ap()
    cvec_d = nc.dram_tensor("cvec", [128, L * 64], F32, kind="ExternalInput").ap()
    binfm_d = nc.dram_tensor("binfm", [128, L * 32], F32, kind="ExternalInput").ap()
    bvrow_d = nc.dram_tensor("bvrow", [L, 1024], F32, kind="ExternalInput").ap()
    wsT_d = nc.dram_tensor("wsT", [128, L * 1024], F32, kind="ExternalInput").ap()
    bsbc_d = nc.dram_tensor("bsbc", [128, L * 1024], F32, kind="ExternalInput").ap()
    cmask_d = nc.dram_tensor("cmask", [128, 128], F32, kind="ExternalInput").ap()
    pcore_d = nc.dram_tensor("pcore", [128, 80], F32, kind="ExternalInput").ap()
    outT = nc.dram_tensor("outT", [128, 8, NOUT], F32, kind="ExternalOutput").ap()

    dump_d = {}
    for (nm, shp, dt) in DUMPS:
        dump_d[nm] = nc.dram_tensor("dbg_" + nm, shp, dt, kind="ExternalOutput").ap()
    P = Prog()
    from contextlib import ExitStack
    with ExitStack() as es:
        def sb(name, shape, dt):
            return es.enter_context(nc.sbuf_tensor(name, shape, dt))

        X = sb("X", [128, 8, NTMAX], F32)
        H = sb("H", [128, 8, NTMAX], BF16)
        O = sb("O", [128, 8, NTMAX], F32)
        Z = sb("Z", [128, 8, ZW], F32)
        S2 = sb("S2", [128, 2, ZW], F32)
        BB = sb("BB", [128, 6, 8 * NTMAX], BF16)
        WB = sb("WB", [128, NWBUF, BLK], BF16)
        PB = sb("PB", [128, 2, NTMAX], BF16)
        RB = sb("RB", [128, NTMAX], F32)
        TMP = sb("TMP", [128, 2, ZW], F32)
        S1 = TMP
        ZH = sb("ZH", [128, L, 8, 16], F32)
        T16 = sb("T16", [128, 2, 16], F32)
        ST = sb("ST", [128, 2, 6], F32)
        MVA = sb("MVA", [128, 8, 2], F32)
        RSTD = sb("RSTD", [128, 8], F32)
        EPSC = sb("EPSC", [128, 1], F32)
        NMR = sb("NMR", [128, 8], F32)
        EPSP = sb("EPSP", [128, NTMAX], F32)
        CV = sb("CV", [128, L * 64], F32)
        BIN = sb("BIN", [128, L * 32], F32)
        VBH = sb("VBH", [33, 1024], BF16)
        VBL = sb("VBL", [33, 1024], BF16)
        WSB = sb("WSB", [128, L * 1024], BF16)
        BF = sb("BF", [128, L * 1024], F32)
        CM = sb("CM", [128, 128], F32)
        PC = sb("PC", [128, 80], F32)
        ONEB = sb("ONEB", [128, 128], BF16)
        ONEF = sb("ONEF", [128, 128], F32)
        PS = es.enter_context(nc.psum_tensor("PS", [128, 8, 512], F32))
        WSF = Z[:, :, :].rearrange("p c t -> p (c t)")[:, 0:L * 1024]
        BSB = X[:, :, :].rearrange("p c t -> p (c t)")[:, 0:L * 1024]

        for key in ("pe", "act", "dve", "pool", "sp", "cst", "xld", "ost", "pld", "dbg") + tuple(
                "w%d" % i for i in range(NWBUF)):
            P.add_sem(key, es.enter_context(nc.semaphore("s_" + key)))

        rX, rH, rO, rZ = R(8), R(8), R(8), R(8)
        rB = [R(8) for _ in range(6)]
        rS2 = Res()
        rW = R(NWBUF)
        rPB, rRB = Res(), Res()
        rTMP = R(2)
        rS1 = rTMP
        rZH = R(L)
        rT16, rST, rMV, rRSTD, rNMR, rEPSP = Res(), Res(), Res(), Res(), Res(), Res()
        rBT = R(8)
        rPS = R(8)
        rC = Res()
        rWSF = None

        def Bv(i):
            return BB[:, i, :].rearrange("p (c t) -> p c t", c=8)

        U, GA, GB, NB_, PL, MS = (Bv(i) for i in range(6))
        SQ = PL
        rU, rGA, rGB, rN, rPL, rMS = rB
        rSQ = rPL
        NTM = BB[:, 3, :]
        ACTB = BB[:, 0:4, :].rearrange("p a (c t) -> p (a c) t", c=8)
        rACT = rB[0] + rB[1] + rB[2] + rB[3]
        VTM = O[:, :, :].rearrange("p c t -> p (c t)")

        psn = [0]

        def banks(n):
            b = psn[0]
            if b + n > 8:
                b = 0
            psn[0] = (b + n) % 8
            return list(range(b, b + n))

        stream = []
        for (t0, nt) in TILES:
            for l in range(L_RUN):
                for j in range(NBLK):
                    stream.append((l, j))
        wstate = {"issued": 0, "cons": 0}

        def blk_len(j):
            return 2048 if j in (10, 35) else BLK

        def w_issue():
            i = wstate["issued"]
            if i >= len(stream):
                return
            l, j = stream[i]
            b = i % NWBUF
            n = blk_len(j)
            P.dma("pool", "w%d" % b,
                  lambda e, b=b, l=l, j=j, n=n: e.dma_start(out=WB[:, b, 0:n], in_=wst[l, j, :, 0:n]),
                  writes=[rW[b]])
            wstate["issued"] += 1

        def w_next():
            i = wstate["cons"]
            wstate["cons"] += 1
            return i % NWBUF

        def w_done():
            w_issue()

        cst = []
        for (dst, src) in ((CV[:], cvec_d), (BIN[:], binfm_d), (WSF, wsT_d), (BSB, bsbc_d),
                           (CM[:], cmask_d), (PC[:], pcore_d)):
            nd = len(dst.shape)
            P.dma("sp", "cst", lambda e, dst=dst, src=src: e.dma_start(out=dst, in_=src), writes=[rC])
        for _ in range(NWBUF):
            w_issue()

        BVR = VTM[0:33, 0:1024]
        BVT = VTM[0:33, 1024:2048]
        P.op("dve", lambda e: e.memset(VTM[0:33, 0:2048], 0.0), writes=rO)
        for l in range(L):
            P.dma("sp", "cst", lambda e, l=l: e.dma_start(out=VTM[32 * l:32 * l + 1, 0:1024], in_=bvrow_d[l:l + 1, :]),
                  writes=[rC] + rO)
        P.op("dve", lambda e: e.memset(ONEB[:], 1.0), writes=[rC])
        P.op("dve", lambda e: e.memset(ONEF[:], 1.0), writes=[rC])
        P.op("dve", lambda e: e.memset(EPSC[:], EPS), writes=[rC])
        P.op("dve", lambda e: e.tensor_copy(out=VBH[:], in_=BVR), reads=[rC], writes=[rC])
        P.op("dve", lambda e: e.tensor_copy(out=BVT, in_=VBH[:]), reads=[rC], writes=[rC])
        P.op("dve", lambda e: e.tensor_tensor(out=BVT, in0=BVR, in1=BVT, op=ALU.subtract),
             reads=[rC], writes=[rC] + rO)
        P.op("dve", lambda e: e.tensor_copy(out=VBL[:], in_=BVT), reads=[rC], writes=[rC] + rO)
        for l in range(L):
            for h in range(8):
                sl = slice(l * 1024 + h * 128, l * 1024 + (h + 1) * 128)
                P.op("dve", lambda e, sl=sl: e.tensor_tensor(out=WSF[:, sl], in0=WSF[:, sl], in1=CM[:], op=ALU.mult),
                     reads=[rC], writes=rZ)
            P.op("dve", lambda e, l=l: e.tensor_copy(out=WSB[:, l * 1024:(l + 1) * 1024],
                                                     in_=WSF[:, l * 1024:(l + 1) * 1024]),
                 reads=rZ, writes=[rC])
            for hh in range(2):
                bk = banks(1)[0]
                P.group("pe", [lambda e, bk=bk, l=l, hh=hh: e.matmul(
                    PS[:, bk, :], ONEF[:], WSF[:, l * 1024 + hh * 512: l * 1024 + (hh + 1) * 512],
                    start=True, stop=True)], reads=[rC] + rZ, writes=[rPS[bk]])
                for h4 in range(4):
                    h = hh * 4 + h4
                    sl = slice(l * 1024 + h * 128, l * 1024 + (h + 1) * 128)
                    col = (l * 8 + V_LNB) * 8 + h
                    P.op("dve", lambda e, bk=bk, h4=h4, sl=sl, col=col: e.scalar_tensor_tensor(
                        out=BF[:, sl], in0=PS[:, bk, h4 * 128:(h4 + 1) * 128], scalar=CV[:, col:col + 1],
                        in1=BSB[:, sl], op0=ALU.mult, op1=ALU.add),
                        reads=[rPS[bk], rC] + rX, writes=[rC])
        rC_done = rC.w
        rC.const = True

        def cv(l, vi, c):
            col = (l * 8 + vi) * 8 + c
            return CV[:, col:col + 1]

        dump_toks = []

        def dump(nm, ap, res):
            if nm in dump_d and nm not in [d[0] for d in dump_toks]:
                dump_toks.append((nm, P.dma("sp", "dbg", lambda e: e.dma_start(out=dump_d[nm], in_=ap), reads=res)))

        def colblocks(nt):
            cbs = []
            c = 0
            while c < nt:
                w = min(512, nt - c)
                cbs.append((c, w))
                c += w
            return cbs

        def norm_R(nt, sq_res, mode="rsqrt"):
            for (c0, w) in colblocks(nt):
                bk = banks(1)[0]
                for dc in range(8):
                    P.group("pe", [lambda e, bk=bk, dc=dc, c0=c0, w=w: e.matmul(
                        PS[:, bk, 0:w], ONEB[:], SQ[:, dc, c0:c0 + w], start=(dc == 0), stop=(dc == 7))],
                        reads=[rC, sq_res[dc]], writes=[rPS[bk]])
                if mode == "epsp":
                    P.op("dve", lambda e, bk=bk, c0=c0, w=w: e.tensor_scalar(
                        out=EPSP[:, c0:c0 + w], in0=PS[:, bk, 0:w], scalar1=1.0 / D, scalar2=EPS,
                        op0=ALU.mult, op1=ALU.add), reads=[rPS[bk]], writes=[rEPSP])
                    P.op("dve", lambda e, c0=c0, w=w: e.scalar_tensor_tensor(
                        out=EPSP[:, c0:c0 + w], in0=EPSP[:, c0:c0 + w], scalar=EPS, in1=EPSP[:, c0:c0 + w],
                        op0=ALU.mult, op1=ALU.mult), reads=[rEPSP], writes=[rEPSP])
                    continue
                if mode == "rsqrt_epsp":
                    P.op("dve", lambda e, bk=bk, c0=c0, w=w: e.scalar_tensor_tensor(
                        out=RB[:, c0:c0 + w], in0=PS[:, bk, 0:w], scalar=1.0 / D, in1=EPSP[:, c0:c0 + w],
                        op0=ALU.mult, op1=ALU.add), reads=[rPS[bk], rEPSP], writes=[rRB])
                    P.op("act", lambda e, c0=c0, w=w: e.activation(
                        out=RB[:, c0:c0 + w], in_=RB[:, c0:c0 + w], func=AF.Ln), reads=[rRB], writes=[rRB])
                else:
                    P.op("act", lambda e, bk=bk, c0=c0, w=w: e.activation(
                        out=RB[:, c0:c0 + w], in_=PS[:, bk, 0:w], func=AF.Ln, bias=EPSC[:, 0:1], scale=1.0 / D),
                        reads=[rPS[bk], rC], writes=[rRB])
                P.op("act", lambda e, c0=c0, w=w: e.activation(
                    out=RB[:, c0:c0 + w], in_=RB[:, c0:c0 + w], func=AF.Exp, scale=-0.5), reads=[rRB], writes=[rRB])

        def squares_of_X(nt):
            for dc in range(8):
                P.op("act", lambda e, dc=dc: e.activation(out=SQ[:, dc, 0:nt], in_=X[:, dc, 0:nt], func=AF.Square),
                     reads=[rX[dc]], writes=[rSQ[dc]])

        def make_H(l, vi, nt):
            for dc in range(8):
                P.op("dve", lambda e, dc=dc: e.scalar_tensor_tensor(
                    out=H[:, dc, 0:nt], in0=X[:, dc, 0:nt], scalar=cv(l, vi, dc), in1=RB[:, 0:nt],
                    op0=ALU.mult, op1=ALU.mult), reads=[rX[dc], rRB, rC], writes=[rH[dc]])

        def residual(l, vi, nt, hmode=None, hvi=None):
            for dc in range(8):
                P.op("dve", lambda e, dc=dc: e.tensor_tensor(
                    out=O[:, dc, 0:nt], in0=O[:, dc, 0:nt], in1=RB[:, 0:nt], op=ALU.mult),
                    reads=[rO[dc], rRB], writes=[rO[dc]])
                P.op("dve", lambda e, dc=dc: e.scalar_tensor_tensor(
                    out=X[:, dc, 0:nt], in0=O[:, dc, 0:nt], scalar=cv(l, vi, dc), in1=X[:, dc, 0:nt],
                    op0=ALU.mult, op1=ALU.add), reads=[rO[dc], rX[dc], rC], writes=[rX[dc]])
                if hmode == "gain":
                    P.op("act", lambda e, dc=dc: e.activation(
                        out=H[:, dc, 0:nt], in_=X[:, dc, 0:nt], func=AF.Identity, scale=cv(l, hvi, dc)),
                        reads=[rX[dc], rC], writes=[rH[dc]])
                elif hmode == "copy":
                    P.op("act", lambda e, dc=dc: e.activation(out=H[:, dc, 0:nt], in_=X[:, dc, 0:nt], func=AF.Copy),
                         reads=[rX[dc]], writes=[rH[dc]])

        def fm_matmul(b, nt, rhs, rhs_res, nk, evac, fis=range(4), kstride=512):
            cbs = colblocks(nt)
            for fi in fis:
                bks = banks(len(cbs))
                fns = []
                for k in range(nk):
                    for ci, (c0, w) in enumerate(cbs):
                        fns.append(lambda e, bk=bks[ci], k=k, fi=fi, c0=c0, w=w: e.matmul(
                            PS[:, bk, 0:w], WB[:, b, k * kstride + fi * 128: k * kstride + (fi + 1) * 128],
                            rhs[:, k, c0:c0 + w], start=(k == 0), stop=(k == nk - 1)))
                P.group("pe", fns, reads=[rW[b]] + rhs_res, writes=[rPS[bk] for bk in bks])
                for ci, (c0, w) in enumerate(cbs):
                    evac(fi, c0, w, bks[ci])

        def tile_layer(ti, t0, nt, l):
            cbs = colblocks(nt)
            nch = nt // 128
            P.dma("pool", "pld", lambda e: e.dma_start(out=PB[:, :, 0:nt], in_=pT[l, :, :, t0:t0 + nt]),
                  writes=[rPB])

            squares_of_X(nt)
            norm_R(nt, rSQ)
            make_H(l, V_PREMIX, nt)

            specs = [(Z, rZ, AF.Identity, 0, 16), (U, rU, AF.Gelu_apprx_tanh, 8, 0),
                     (GA, rGA, AF.Sigmoid, 16, 0), (GB, rGB, AF.Sigmoid, 24, 0)]
            for (dst, dres, func, bcol, off) in specs:
                for half in range(2):
                    b = w_next()

                    def evac(fi, c0, w, bk, dst=dst, dres=dres, func=func, bcol=bcol, off=off, half=half):
                        fc = half * 4 + fi
                        col = l * 32 + bcol + fc
                        P.op("act", lambda e: e.activation(
                            out=dst[:, fc, off + c0: off + c0 + w], in_=PS[:, bk, 0:w], func=func,
                            bias=BIN[:, col:col + 1]), reads=[rPS[bk], rC], writes=[dres[fc]])
                    fm_matmul(b, nt, H, rH, 8, evac)
                    w_done()

            W_ = 16 + nt
            if ti == 0:
                P.op("dve", lambda e: e.memset(Z[:, :, 0:16], 0.0), writes=rZ)
                P.op("dve", lambda e: e.tensor_scalar(
                    out=Z[:, :, 16:16 + HALO], in0=Z[:, :, 16:16 + HALO], scalar1=PC[:, 0:1], scalar2=None,
                    op0=ALU.mult), reads=rZ + [rC], writes=rZ)
            else:
                P.op("dve", lambda e: e.tensor_copy(out=Z[:, :, 0:16], in_=ZH[:, l, :, :]),
                     reads=[rZH[l]], writes=rZ)
            for g in range(4):
                zs = Z[:, 2 * g:2 * g + 2, :]
                zres = [rZ[2 * g], rZ[2 * g + 1]]
                P.op("dve", lambda e, zs=zs: e.tensor_tensor(
                    out=S1[:, :, 1:W_], in0=zs[:, :, 1:W_], in1=zs[:, :, 0:W_ - 1], op=ALU.add),
                    reads=zres, writes=rS1)
                cur, rcur = S1, rS1
                if g >= 1:
                    P.op("dve", lambda e: e.tensor_tensor(
                        out=S2[:, :, 3:W_], in0=S1[:, :, 3:W_], in1=S1[:, :, 1:W_ - 2], op=ALU.add),
                        reads=rS1, writes=[rS2])
                    cur, rcur = S2, [rS2]
                if g >= 2:
                    P.op("dve", lambda e: e.tensor_tensor(
                        out=S1[:, :, 7:W_], in0=S2[:, :, 7:W_], in1=S2[:, :, 3:W_ - 4], op=ALU.add),
                        reads=[rS2], writes=rS1)
                    cur, rcur = S1, rS1
                if g >= 3:
                    P.op("dve", lambda e: e.tensor_tensor(
                        out=S2[:, :, 15:W_], in0=S1[:, :, 15:W_], in1=S1[:, :, 7:W_ - 8], op=ALU.add),
                        reads=rS1, writes=[rS2])
                    cur, rcur = S2, [rS2]
                wdw = POOL_WINDOWS[g]
                P.op("dve", lambda e, cur=cur, g=g, wdw=wdw: e.scalar_tensor_tensor(
                    out=PL[:, 2 * g:2 * g + 2, 0:nt], in0=cur[:, :, 16:16 + nt], scalar=1.0 / wdw,
                    in1=Z[:, 2 * g:2 * g + 2, 16:16 + nt], op0=ALU.mult, op1=ALU.subtract),
                    reads=rcur + zres, writes=[rPL[2 * g], rPL[2 * g + 1]])
                if ti == 0:
                    for cc in range(2):
                        P.op("dve", lambda e, cur=cur, g=g, cc=cc: e.tensor_tensor(
                            out=T16[:, cc, :], in0=cur[:, cc, 16 + HALO:32 + HALO],
                            in1=PC[:, 1 + g * 16: 1 + (g + 1) * 16], op=ALU.mult),
                            reads=rcur + [rC], writes=[rT16])
                    P.op("dve", lambda e, g=g: e.tensor_tensor(
                        out=PL[:, 2 * g:2 * g + 2, HALO:HALO + 16], in0=T16[:, :, :],
                        in1=Z[:, 2 * g:2 * g + 2, 16 + HALO:32 + HALO], op=ALU.subtract),
                        reads=[rT16] + zres, writes=[rPL[2 * g], rPL[2 * g + 1]])
            dump("Z", Z[:, :, :], rZ)
            dump("PL", PL, rPL)
            dump("H", H[:, :, :], rH)
            P.op("dve", lambda e: e.tensor_copy(out=ZH[:, l, :, :], in_=Z[:, :, nt:nt + 16]),
                 reads=rZ, writes=[rZH[l]])

            vb = [w_next(), w_next()]
            for half in range(2):
                b = vb[half]
                for c in range(nch):
                    bk = banks(1)[0]
                    fns = []
                    for k in range(8):
                        fns.append(lambda e, bk=bk, k=k, c=c, b=b: e.matmul(
                            PS[:, bk, :], H[:, k, c * 128:(c + 1) * 128], WB[:, b, k * 512:(k + 1) * 512],
                            start=(k == 0), stop=False))
                    vsl = slice(half * 512, (half + 1) * 512)
                    pr = slice(32 * l, 32 * l + 1)
                    fns.append(lambda e, bk=bk, vsl=vsl, pr=pr: e.matmul(
                        PS[:, bk, :], ONEB[pr, :], VBH[pr, vsl], start=False, stop=False))
                    fns.append(lambda e, bk=bk, vsl=vsl, pr=pr: e.matmul(
                        PS[:, bk, :], ONEB[pr, :], VBL[pr, vsl], start=False, stop=True))
                    P.group("pe", fns, reads=[rW[b], rC] + rH, writes=[rPS[bk]])
                    o0 = c * 1024 + half * 512
                    P.op("act", lambda e, bk=bk, o0=o0: e.activation(
                        out=VTM[:, o0:o0 + 512], in_=PS[:, bk, :], func=AF.Gelu_apprx_tanh),
                        reads=[rPS[bk]], writes=rO)
                w_done()
            for c in range(nch):
                for half in range(2):
                    o0 = c * 1024 + half * 512
                    P.op("dve", lambda e, o0=o0, half=half: e.bn_stats(out=ST[:, half, :], in_=VTM[:, o0:o0 + 512]),
                         reads=rO, writes=[rST])
                P.op("dve", lambda e, c=c: e.bn_aggr(out=MVA[:, c, :], in_=ST[:, :, :].rearrange("p a b -> p (a b)")),
                     reads=[rST], writes=[rMV])
            P.op("act", lambda e: e.activation(out=RSTD[:, 0:nch], in_=MVA[:, 0:nch, 1], func=AF.Ln,
                                               bias=EPSC[:, 0:1], scale=1.0), reads=[rMV, rC], writes=[rRSTD])
            P.op("act", lambda e: e.activation(out=RSTD[:, 0:nch], in_=RSTD[:, 0:nch], func=AF.Exp, scale=-0.5),
                 reads=[rRSTD], writes=[rRSTD])
            P.op("dve", lambda e: e.scalar_tensor_tensor(
                out=NMR[:, 0:nch], in0=MVA[:, 0:nch, 0], scalar=-1.0, in1=RSTD[:, 0:nch], op0=ALU.mult, op1=ALU.mult),
                reads=[rMV, rRSTD], writes=[rNMR])
            for c in range(nch):
                P.op("act", lambda e, c=c: e.activation(
                    out=NTM[:, c * 1024:(c + 1) * 1024], in_=VTM[:, c * 1024:(c + 1) * 1024], func=AF.Identity,
                    scale=RSTD[:, c:c + 1], bias=NMR[:, c:c + 1]), reads=rO + [rRSTD, rNMR], writes=rN)

            dump("VTM", VTM, rO)
            dump("NTM", NTM, rN)
            b = w_next()
            for g in range(4):
                for dh in range(2):
                    bks = banks(len(cbs))
                    fns = []
                    for cc in range(2):
                        for ci, (c0, w) in enumerate(cbs):
                            o0 = (g * 2 + cc) * 256 + dh * 128
                            fns.append(lambda e, bk=bks[ci], o0=o0, g=g, cc=cc, c0=c0, w=w, b=b: e.matmul(
                                PS[:, bk, 0:w], WB[:, b, o0:o0 + 128], PL[:, 2 * g + cc, c0:c0 + w],
                                start=(cc == 0), stop=(cc == 1)))
                    P.group("pe", fns, reads=[rW[b], rPL[2 * g], rPL[2 * g + 1]], writes=[rPS[bk] for bk in bks])
                    oc = 2 * g + dh
                    for ci, (c0, w) in enumerate(cbs):
                        P.op("act", lambda e, bk=bks[ci], oc=oc, c0=c0, w=w: e.activation(
                            out=MS[:, oc, c0:c0 + w], in_=PS[:, bk, 0:w], func=AF.Identity,
                            scale=cv(l, V_PSCALE, oc)), reads=[rPS[bks[ci]], rC], writes=[rMS[oc]])
            w_done()

            dump("MS", MS, rMS)
            for half in range(2):
                b = w_next()

                def evac(fi, c0, w, bk, half=half):
                    oc = half * 4 + fi
                    P.op("dve", lambda e: e.tensor_tensor(
                        out=GA[:, oc, c0:c0 + w], in0=PS[:, bk, 0:w], in1=GA[:, oc, c0:c0 + w], op=ALU.mult),
                        reads=[rPS[bk], rGA[oc]], writes=[rGA[oc]])
                fm_matmul(b, nt, MS, rMS, 8, evac)
                w_done()

            for h in range(8):
                for ci, (c0, w) in enumerate(cbs):
                    bk = banks(1)[0]
                    nck = w // 128
                    fns = []
                    for cc in range(nck):
                        c = c0 // 128 + cc
                        fns.append(lambda e, bk=bk, cc=cc, c=c, h=h: e.matmul(
                            PS[:, bk, cc * 128:(cc + 1) * 128], NTM[:, c * 1024 + h * 128: c * 1024 + (h + 1) * 128],
                            WSB[:, l * 1024 + h * 128: l * 1024 + (h + 1) * 128], start=True, stop=True))
                    P.group("pe", fns, reads=rN + [rC], writes=[rPS[bk]])
                    tb = (h * len(cbs) + ci) % 2
                    bf = BF[:, l * 1024 + h * 128: l * 1024 + (h + 1) * 128]
                    P.op("dve", lambda e, bk=bk, nck=nck, w=w, tb=tb, bf=bf, h=h: e.scalar_tensor_tensor(
                        out=TMP[:, tb, 0:w].rearrange("p (a t) -> p a t", a=nck),
                        in0=PS[:, bk, 0:w].rearrange("p (a t) -> p a t", a=nck),
                        scalar=cv(l, V_LNG, h),
                        in1=bf.unsqueeze(1).broadcast_to([128, nck, 128]),
                        op0=ALU.mult, op1=ALU.add), reads=[rPS[bk], rC], writes=[rTMP[tb]])
                    P.op("dve", lambda e, tb=tb, h=h, c0=c0, w=w: e.tensor_tensor(
                        out=U[:, h, c0:c0 + w], in0=TMP[:, tb, 0:w], in1=U[:, h, c0:c0 + w], op=ALU.mult),
                        reads=[rTMP[tb], rU[h]], writes=[rU[h]])

            dump("SG", U, rU)
            dump("M1", GA, rGA)
            for half in range(2):
                b = w_next()

                def evac(fi, c0, w, bk, half=half):
                    oc = half * 4 + fi
                    P.op("dve", lambda e: e.tensor_tensor(
                        out=GB[:, oc, c0:c0 + w], in0=PS[:, bk, 0:w], in1=GB[:, oc, c0:c0 + w], op=ALU.mult),
                        reads=[rPS[bk], rGB[oc]], writes=[rGB[oc]])
                    P.op("dve", lambda e: e.tensor_tensor(
                        out=GB[:, oc, c0:c0 + w], in0=GB[:, oc, c0:c0 + w], in1=GA[:, oc, c0:c0 + w], op=ALU.add),
                        reads=[rGB[oc], rGA[oc]], writes=[rGB[oc]])
                fm_matmul(b, nt, U, rU, 8, evac)
                w_done()

            def evac_O(oc, c0, w, bk):
                P.op("act", lambda e: e.activation(out=SQ[:, oc, c0:c0 + w], in_=PS[:, bk, 0:w], func=AF.Square),
                     reads=[rPS[bk]], writes=[rSQ[oc], rBT[bk]])
                P.op("dve", lambda e: e.tensor_copy(out=O[:, oc, c0:c0 + w], in_=PS[:, bk, 0:w]),
                     reads=[rPS[bk], rBT[bk]], writes=[rO[oc]])

            for half in range(2):
                b = w_next()
                fm_matmul(b, nt, GB, rGB, 8, lambda fi, c0, w, bk, half=half: evac_O(half * 4 + fi, c0, w, bk))
                w_done()

            dump("MG", GB, rGB)
            dump("O1", O[:, :, :], rO)
            norm_R(nt, rSQ)
            residual(l, V_POSTMIX, nt, hmode="gain", hvi=V_PREFFN)
            squares_of_X(nt)
            norm_R(nt, rSQ, mode="epsp")

            dump("X1", X[:, :, :], rX)
            for j in range(8):
                b = w_next()

                def evac(fi, c0, w, bk, j=j):
                    fc = j * 4 + fi
                    tb = fi % 2
                    P.op("act", lambda e: e.activation(out=TMP[:, tb, c0:c0 + w], in_=PS[:, bk, 0:w], func=AF.Relu),
                         reads=[rPS[bk]], writes=[rTMP[tb]])
                    P.op("dve", lambda e: e.tensor_tensor(
                        out=ACTB[:, fc, c0:c0 + w], in0=PS[:, bk, 0:w], in1=TMP[:, tb, c0:c0 + w], op=ALU.mult),
                        reads=[rPS[bk], rTMP[tb]], writes=[rACT[fc]])
                fm_matmul(b, nt, H, rH, 8, evac)
                w_done()

            for j in range(8):
                b = w_next()
                fm_matmul(b, nt, ACTB, rACT, 32, lambda fi, c0, w, bk, j=j: evac_O(j, c0, w, bk),
                          fis=[0], kstride=128)
                w_done()

            norm_R(nt, rSQ, mode="rsqrt_epsp")
            residual(l, V_POSTFFN, nt, hmode="copy")

            for half in range(2):
                b = w_next()

                def evac(fi, c0, w, bk, half=half):
                    oc = half * 4 + fi
                    P.op("act", lambda e: e.activation(
                        out=Z[:, oc, 16 + c0:16 + c0 + w], in_=PS[:, bk, 0:w], func=AF.Sigmoid),
                        reads=[rPS[bk]], writes=[rZ[oc]])
                fm_matmul(b, nt, H, rH, 8, evac)
                w_done()

            b = w_next()
            for oc in range(8):
                bks = banks(len(cbs))
                fns = []
                for kc in range(2):
                    for ci, (c0, w) in enumerate(cbs):
                        fns.append(lambda e, bk=bks[ci], kc=kc, oc=oc, c0=c0, w=w, b=b: e.matmul(
                            PS[:, bk, 0:w], WB[:, b, kc * 1024 + oc * 128: kc * 1024 + (oc + 1) * 128],
                            PB[:, kc, c0:c0 + w], start=(kc == 0), stop=(kc == 1)))
                P.group("pe", fns, reads=[rW[b], rPB], writes=[rPS[bk] for bk in bks])
                for ci, (c0, w) in enumerate(cbs):
                    bk = bks[ci]
                    P.op("dve", lambda e, bk=bk, oc=oc, c0=c0, w=w: e.tensor_tensor(
                        out=O[:, oc, c0:c0 + w], in0=PS[:, bk, 0:w], in1=Z[:, oc, 16 + c0:16 + c0 + w], op=ALU.mult),
                        reads=[rPS[bk], rZ[oc]], writes=[rO[oc]])
                    P.op("act", lambda e, oc=oc, c0=c0, w=w: e.activation(
                        out=SQ[:, oc, c0:c0 + w], in_=O[:, oc, c0:c0 + w], func=AF.Square),
                        reads=[rO[oc]], writes=[rSQ[oc]])
            w_done()

            norm_R(nt, rSQ)
            residual(l, V_POSTPLE, nt)

        last_store = None
        for ti, (t0, nt) in enumerate(TILES):
            P.dma("sp", "xld", lambda e, t0=t0, nt=nt: e.dma_start(out=X[:, :, 0:nt], in_=xT[:, :, t0:t0 + nt]),
                  writes=rX)
            for l in range(L_RUN):
                tile_layer(ti, t0, nt, l)
            s0 = HALO if ti == 0 else 0
            o0 = t0 + s0 - HALO
            n_out = nt - s0
            last_store = P.dma("sp", "ost", lambda e, s0=s0, o0=o0, n_out=n_out: e.dma_start(
                out=outT[:, :, o0:o0 + n_out], in_=X[:, :, s0:s0 + n_out]), reads=rX)
        P.final_wait("sp", [last_store] + [d[1] for d in dump_toks])

        with nc.Block() as block:
            @block.tensor
            def _(e):
                for f in P.q["pe"]:
                    f(e)

            @block.scalar
            def _(e):
                for f in P.q["act"]:
                    f(e)

            @block.vector
            def _(e):
                for f in P.q["dve"]:
                    f(e)

            @block.gpsimd
            def _(e):
                for f in P.q["pool"]:
                    f(e)

            @block.sync
            def _(e):
                for f in P.q["sp"]:
                    f(e)
    return nc


def _fm8(v):
    return np.ascontiguousarray(v.reshape(8, 128).T)


def _blk_k512(W, col0):
    K = W.shape[0]
    return W[:, col0:col0 + 512].reshape(K // 128, 128, 512).transpose(1, 0, 2).reshape(128, -1)


def _build_wstream(inp):
    ws = np.zeros((L, NBLK, 128, BLK), np.float32)
    for l in range(L):
        w_in = inp["w_in"][l]
        order = [0, 512, 1024, 1536, 3072, 3584, 4096, 4608, 2048, 2560]
        for j, c0 in enumerate(order):
            ws[l, j] = _blk_k512(w_in, c0)
        ws[l, 10, :, :2048] = inp["pool_w"][l].reshape(4, 2, 128, 256).transpose(2, 0, 1, 3).reshape(128, 2048)
        for i, name in enumerate(("w_pa", "w_pb", "w_o")):
            for half in range(2):
                ws[l, 11 + 2 * i + half] = _blk_k512(inp[name][l], half * 512)
        for j in range(8):
            ws[l, 17 + j] = _blk_k512(inp["w_ff1"][l], j * 512)
        w2 = inp["w_ff2"][l]
        for j in range(8):
            ws[l, 25 + j] = w2[:, j * 128:(j + 1) * 128].reshape(32, 128, 128).transpose(1, 0, 2).reshape(128, BLK)
        for half in range(2):
            ws[l, 33 + half] = _blk_k512(inp["w_ple_gate"][l], half * 512)
        ws[l, 35, :, :2048] = inp["w_ple_proj"][l].reshape(2, 128, 1024).transpose(1, 0, 2).reshape(128, 2048)
    return ws


_NC_CACHE = {}


def make_in_maps(inp):
    x, p = inp["x"], inp["p"]
    B, S, _ = x.shape
    wst = _build_wstream(inp)
    cvec = np.zeros((128, L * 64), np.float32)
    names = ["pre_mix_g", "post_mix_g", "pre_ffn_g", "post_ffn_g", "post_ple_g", "pool_scale", "sgu_ln_g", "sgu_ln_b"]
    for l in range(L):
        for vi, nm in enumerate(names):
            cvec[:, (l * 8 + vi) * 8:(l * 8 + vi + 1) * 8] = _fm8(inp[nm][l])
    binfm = np.zeros((128, L * 32), np.float32)
    bvrow = np.zeros((L, 1024), np.float32)
    for l in range(L):
        b = inp["b_in"][l]
        for gi, c0 in enumerate((0, 1024, 3072, 4096)):
            binfm[:, l * 32 + gi * 8: l * 32 + (gi + 1) * 8] = _fm8(b[c0:c0 + 1024])
        bvrow[l] = b[2048:3072]
    wsT = np.ascontiguousarray(inp["sgu_w_s"].transpose(3, 0, 1, 2)).reshape(128, L * 1024)
    bsbc = np.ascontiguousarray(np.broadcast_to(inp["sgu_b_s"].reshape(1, L * 1024), (128, L * 1024)))
    si = np.arange(128)
    cmask = (si[:, None] <= si[None, :]).astype(np.float32)

    in_maps = []
    for c in range(NCORES):
        b, half = c // 2, c % 2
        s0 = half * OWN
        xt = np.zeros((NTOK, D), np.float32)
        pt = np.zeros((L, NTOK, 256), np.float32)
        xt[HALO:] = x[b, s0:s0 + OWN]
        pt[:, HALO:] = p[:, b, s0:s0 + OWN]
        pcore = np.zeros((128, 80), np.float32)
        if half == 1:
            xt[:HALO] = x[b, s0 - HALO:s0]
            pt[:, :HALO] = p[:, b, s0 - HALO:s0]
            pcore[:, 0] = 1.0
        for g, w in enumerate(POOL_WINDOWS):
            j = np.arange(16)
            cnt = np.minimum(j + 1, w) if half == 0 else np.full(16, w)
            pcore[:, 1 + g * 16: 1 + (g + 1) * 16] = (1.0 / cnt.astype(np.float32))[None, :]
        xTc = np.ascontiguousarray(xt.T.reshape(8, 128, NTOK).transpose(1, 0, 2))
        pTc = np.ascontiguousarray(pt.transpose(0, 2, 1).reshape(L, 2, 128, NTOK).transpose(0, 2, 1, 3))
        in_maps.append({"xT": xTc, "pT": pTc, "wst": wst, "cvec": cvec, "binfm": binfm, "bvrow": bvrow,
                        "wsT": wsT, "bsbc": bsbc, "cmask": cmask, "pcore": pcore})
    return in_maps


def kernel(**inputs):
    inp = {k: np.asarray(v, dtype=np.float32) for k, v in inputs.items()}
    B, S, _ = inp["x"].shape
    in_maps = make_in_maps(inp)
    if "nc" not in _NC_CACHE:
        _NC_CACHE["nc"] = build_nc()
    nc = _NC_CACHE["nc"]
    res = run_bass_kernel_spmd(nc, in_maps, core_ids=list(range(NCORES)))
    out = np.empty((B, S, D), np.float32)
    for c in range(NCORES):
        b, half = c // 2, c % 2
        o = np.asarray(res.results[c]["outT"], dtype=np.float32)
        out[b, half * OWN:(half + 1) * OWN, :] = o.transpose(1, 0, 2).reshape(D, OWN).T
    return out
```

```python
import numpy as np
import concourse.bass as bass
import concourse.mybir as mybir
from concourse.bass_utils import run_bass_kernel_spmd

F32 = mybir.dt.float32
BF16 = mybir.dt.bfloat16
AF = mybir.ActivationFunctionType
ALU = mybir.AluOpType

D = 1024
L = 2
NCORES = 8
OWN = 2048
HALO = 128
NTOK = OWN + HALO
TILES = [(0, 640), (640, 512), (1152, 512), (1664, 512)]
NTMAX = 640
ZW = 16 + NTMAX
EPS = 1e-6
NBLK = 36
NWBUF = 4
BLK = 4096
V_PREMIX, V_POSTMIX, V_PREFFN, V_POSTFFN, V_POSTPLE, V_PSCALE, V_LNG, V_LNB = range(8)
POOL_WINDOWS = (2, 4, 8, 16)
POOL_CHUNKS = ()
RES_ORDER = (0, 1, 2, 3, 4, 5, 6, 7)


class Res:
    __slots__ = ("w", "rs", "const")

    def __init__(self, const=False):
        self.w = None
        self.rs = {}
        self.const = const


class Prog:
    ENG = ("pe", "act", "dve", "pool", "sp")

    def __init__(self):
        self.q = {k: [] for k in self.ENG}
        self.semh = {}
        self.cnt = {}
        self.seen = {k: {} for k in self.ENG}

    def add_sem(self, key, handle):
        self.semh[key] = handle
        self.cnt[key] = 0

    def _wait(self, eng, toks):
        need = {}
        for t in toks:
            if t is None:
                continue
            key, val = t
            if key == "pe" and eng == "pe":
                continue
            if self.seen[eng].get(key, 0) >= val:
                continue
            if need.get(key, 0) < val:
                need[key] = val
        for key, val in need.items():
            self.seen[eng][key] = val
            s = self.semh[key]
            self.q[eng].append(lambda e, s=s, val=val: e.wait_ge(s, val))

    @staticmethod
    def _deps(reads, writes):
        deps = []
        for r in reads:
            deps.append(r.w)
        for w in writes:
            deps.append(w.w)
            deps.extend(w.rs.items())
        return deps

    @staticmethod
    def _commit(tok, reads, writes):
        for r in reads:
            if not r.const:
                if r.rs.get(tok[0], 0) < tok[1]:
                    r.rs[tok[0]] = tok[1]
        for w in writes:
            w.w = tok
            w.rs = {}

    def op(self, eng, fn, reads=(), writes=()):
        self._wait(eng, self._deps(reads, writes))
        self.cnt[eng] += 1
        tok = (eng, self.cnt[eng])
        s = self.semh[eng]
        self.q[eng].append(lambda e, fn=fn, s=s: fn(e).then_inc(s, 1))
        self._commit(tok, reads, writes)
        return tok

    def group(self, eng, fns, reads=(), writes=()):
        self._wait(eng, self._deps(reads, writes))
        self.cnt[eng] += 1
        tok = (eng, self.cnt[eng])
        s = self.semh[eng]
        for f in fns[:-1]:
            self.q[eng].append(f)
        last = fns[-1]
        self.q[eng].append(lambda e, fn=last, s=s: fn(e).then_inc(s, 1))
        self._commit(tok, reads, writes)
        return tok

    def dma(self, eng, semkey, fn, reads=(), writes=()):
        self._wait(eng, self._deps(reads, writes))
        self.cnt[semkey] += 16
        tok = (semkey, self.cnt[semkey])
        s = self.semh[semkey]
        self.q[eng].append(lambda e, fn=fn, s=s: fn(e).then_inc(s, 16))
        self._commit(tok, reads, writes)
        return tok

    def final_wait(self, eng, toks):
        self._wait(eng, toks)


def R(n, const=False):
    return [Res(const) for _ in range(n)]


def build_nc(TILES=TILES, L_RUN=L, NOUT=OWN, DUMPS=()):
    nc = bass.Bass("TRN2", target_bir_lowering=False)
    xT = nc.dram_tensor("xT", [128, 8, NTOK], F32, kind="ExternalInput").ap()
    pT = nc.dram_tensor("pT", [L, 128, 2, NTOK], F32, kind="ExternalInput").ap()
    wst = nc.dram_tensor("wst", [L, NBLK, 128, BLK], F32, kind="ExternalInput").ap()
    cvec_d = nc.dram_tensor("cvec", [128, L * 64], F32, kind="ExternalInput").ap()
    binfm_d = nc.dram_tensor("binfm", [128, L * 32], F32, kind="ExternalInput").ap()
    bvrow_d = nc.dram_tensor("bvrow", [L, 1024], F32, kind="ExternalInput").ap()
    wsT_d = nc.dram_tensor("wsT", [128, L * 1024], F32, kind="ExternalInput").ap()
    bsbc_d = nc.dram_tensor("bsbc", [128, L * 1024], F32, kind="ExternalInput").ap()
    cmask_d = nc.dram_tensor("cmask", [128, 128], F32, kind="ExternalInput").ap()
    pcore_d = nc.dram_tensor("pcore", [128, 80], F32, kind="ExternalInput").ap()
    outT = nc.dram_tensor("outT", [128, 8, NOUT], F32, kind="ExternalOutput").ap()

    dump_d = {}
    for (nm, shp, dt) in DUMPS:
        dump_d[nm] = nc.dram_tensor("dbg_" + nm, shp, dt, kind="ExternalOutput").ap()
    P = Prog()
    from contextlib import ExitStack
    with ExitStack() as es:
        def sb(name, shape, dt):
            return es.enter_context(nc.sbuf_tensor(name, shape, dt))

        X = sb("X", [128, 8, ZW], F32)
        H = sb("H", [128, 8, NTMAX], BF16)
        O = sb("O", [128, 8, NTMAX], F32)
        Z = sb("Z", [128, 8, ZW], F32)
        S2 = sb("S2", [128, 2, ZW], F32)
        BB = sb("BB", [128, 6, 8 * NTMAX], BF16)
        WB = sb("WB", [128, NWBUF, BLK], BF16)
        PB = sb("PB", [128, 2, NTMAX], BF16)
        RB = sb("RB", [128, NTMAX], F32)
        TMP = sb("TMP", [128, 2, ZW], F32)
        S1 = TMP
        ZH = sb("ZH", [128, L, 8, 16], F32)
        T16 = sb("T16", [128, 2, 16], F32)
        ST = sb("ST", [128, 2, 6], F32)
        MVA = sb("MVA", [128, 8, 2], F32)
        RSTD = sb("RSTD", [128, 8], F32)
        EPSC = sb("EPSC", [128, 1], F32)
        NMR = sb("NMR", [128, 8], F32)
        EPSP = sb("EPSP", [128, NTMAX], F32)
        CV = sb("CV", [128, L * 64], F32)
        BIN = sb("BIN", [128, L * 32], F32)
        VBH = sb("VBH", [33, 1024], BF16)
        VBL = sb("VBL", [33, 1024], BF16)
        WSB = sb("WSB", [128, L * 1024], BF16)
        BF = sb("BF", [128, L * 1024], F32)
        CM = sb("CM", [128, 128], F32)
        PC = sb("PC", [128, 80], F32)
        ONEB = sb("ONEB", [128, 128], BF16)
        ONEF = sb("ONEF", [128, 128], F32)
        PS = es.enter_context(nc.psum_tensor("PS", [128, 8, 512], F32))
        WSF = Z[:, :, :].rearrange("p c t -> p (c t)")[:, 0:L * 1024]
        BSB = BB[:, 5, 0:2 * L * 1024].bitcast(F32)

        for key in ("pe", "act", "dve", "pool", "sp", "cst", "xld", "ost", "pld", "dbg") + tuple(
                "w%d" % i for i in range(NWBUF)):
            P.add_sem(key, es.enter_context(nc.semaphore("s_" + key)))

        rX, rH, rO, rZ = R(8), R(8), R(8), R(8)
        rB = [R(8) for _ in range(6)]
        rS2 = Res()
        rW = R(NWBUF)
        rPB, rRB = Res(), Res()
        rTMP = R(2)
        rS1 = rTMP
        rZH = R(L)
        rT16, rST, rMV, rRSTD, rNMR, rEPSP = Res(), Res(), Res(), Res(), Res(), Res()
        rBT = R(8)
        rV = R(5)
        rPS = R(8)
        rC = Res()
        rWSF = None

        def Bv(i):
            return BB[:, i, :].rearrange("p (c t) -> p c t", c=8)

        U, GA, GB, NB_, PL, MS = (Bv(i) for i in range(6))
        SQ = PL
        rU, rGA, rGB, rN, rPL, rMS = rB
        rSQ = rPL
        NTM = BB[:, 3, :]
        ACTB = BB[:, 0:4, :].rearrange("p a (c t) -> p (a c) t", c=8)
        rACT = rB[0] + rB[1] + rB[2] + rB[3]
        VTM = O[:, :, :].rearrange("p c t -> p (c t)")

        psn = [0]

        def banks(n):
            b = psn[0]
            if b + n > 8:
                b = 0
            psn[0] = (b + n) % 8
            return list(range(b, b + n))

        stream = []
        for (t0, nt) in TILES:
            for l in range(L_RUN):
                for j in range(NBLK):
                    stream.append((l, j))
        wstate = {"issued": 0, "cons": 0}

        def blk_len(j):
            return 2048 if j in (10, 35) else BLK

        def w_issue():
            i = wstate["issued"]
            if i >= len(stream):
                return
            l, j = stream[i]
            b = i % NWBUF
            n = blk_len(j)
            P.dma("pool", "w%d" % b,
                  lambda e, b=b, l=l, j=j, n=n: e.dma_start(out=WB[:, b, 0:n], in_=wst[l, j, :, 0:n]),
                  writes=[rW[b]])
            wstate["issued"] += 1

        def w_next():
            i = wstate["cons"]
            wstate["cons"] += 1
            return i % NWBUF

        def w_done():
            w_issue()

        cst = []
        for (dst, src) in ((CV[:], cvec_d), (BIN[:], binfm_d), (WSF, wsT_d), (BSB, bsbc_d),
                           (CM[:], cmask_d), (PC[:], pcore_d)):
            nd = len(dst.shape)
            P.dma("sp", "cst", lambda e, dst=dst, src=src: e.dma_start(out=dst, in_=src), writes=[rC])
        for _ in range(NWBUF):
            w_issue()

        BVR = VTM[0:33, 0:1024]
        BVT = VTM[0:33, 1024:2048]
        P.op("dve", lambda e: e.memset(VTM[0:33, 0:2048], 0.0), writes=rO)
        for l in range(L):
            P.dma("sp", "cst", lambda e, l=l: e.dma_start(out=VTM[32 * l:32 * l + 1, 0:1024], in_=bvrow_d[l:l + 1, :]),
                  writes=[rC] + rO)
        P.op("dve", lambda e: e.memset(ONEB[:], 1.0), writes=[rC])
        P.op("dve", lambda e: e.memset(ONEF[:], 1.0), writes=[rC])
        P.op("dve", lambda e: e.memset(EPSC[:], EPS), writes=[rC])
        P.op("dve", lambda e: e.tensor_copy(out=VBH[:], in_=BVR), reads=[rC], writes=[rC])
        P.op("dve", lambda e: e.tensor_copy(out=BVT, in_=VBH[:]), reads=[rC], writes=[rC])
        P.op("dve", lambda e: e.tensor_tensor(out=BVT, in0=BVR, in1=BVT, op=ALU.subtract),
             reads=[rC], writes=[rC] + rO)
        P.op("dve", lambda e: e.tensor_copy(out=VBL[:], in_=BVT), reads=[rC], writes=[rC] + rO)
        for l in range(L):
            for h in range(8):
                sl = slice(l * 1024 + h * 128, l * 1024 + (h + 1) * 128)
                P.op("dve", lambda e, sl=sl: e.tensor_tensor(out=WSF[:, sl], in0=WSF[:, sl], in1=CM[:], op=ALU.mult),
                     reads=[rC], writes=rZ)
            P.op("dve", lambda e, l=l: e.tensor_copy(out=WSB[:, l * 1024:(l + 1) * 1024],
                                                     in_=WSF[:, l * 1024:(l + 1) * 1024]),
                 reads=rZ, writes=[rC])
            for hh in range(2):
                bk = banks(1)[0]
                P.group("pe", [lambda e, bk=bk, l=l, hh=hh: e.matmul(
                    PS[:, bk, :], ONEF[:], WSF[:, l * 1024 + hh * 512: l * 1024 + (hh + 1) * 512],
                    start=True, stop=True)], reads=[rC] + rZ, writes=[rPS[bk]])
                for h4 in range(4):
                    h = hh * 4 + h4
                    sl = slice(l * 1024 + h * 128, l * 1024 + (h + 1) * 128)
                    col = (l * 8 + V_LNB) * 8 + h
                    P.op("dve", lambda e, bk=bk, h4=h4, sl=sl, col=col: e.scalar_tensor_tensor(
                        out=BF[:, sl], in0=PS[:, bk, h4 * 128:(h4 + 1) * 128], scalar=CV[:, col:col + 1],
                        in1=BSB[:, sl], op0=ALU.mult, op1=ALU.add),
                        reads=[rPS[bk], rC] + rB[5], writes=[rC])
        rC_done = rC.w
        rC.const = True

        def cv(l, vi, c):
            col = (l * 8 + vi) * 8 + c
            return CV[:, col:col + 1]

        dump_toks = []

        def dump(nm, ap, res):
            if nm in dump_d and nm not in [d[0] for d in dump_toks]:
                dump_toks.append((nm, P.dma("sp", "dbg", lambda e: e.dma_start(out=dump_d[nm], in_=ap), reads=res)))

        def colblocks(ca, nt):
            cbs = []
            c = ca
            while c < nt:
                w = min(512, nt - c)
                cbs.append((c, w))
                c += w
            return cbs

        GF = BB[:, 0:2, :].rearrange("p a t -> p (a t)").bitcast(F32).rearrange("p (c t) -> p c t", c=8)
        rG = [[rB[oc // 4][2 * (oc % 4)], rB[oc // 4][2 * (oc % 4) + 1]] for oc in range(8)]

        def tile_layer(ti, t0, nt, l, X, rX, Z, rZ, prefetch):
            last = (l == L_RUN - 1)
            ca = HALO if (ti == 0 and last) else 0
            cbs = colblocks(ca, nt)
            cbs_all = colblocks(0, nt)
            nch = nt // 128
            clo = ca // 128

            def norm_R(sq_res, mode="rsqrt", cbl=None):
                for (c0, w) in (cbl or cbs):
                    bk = banks(1)[0]
                    for dc in range(8):
                        P.group("pe", [lambda e, bk=bk, dc=dc, c0=c0, w=w: e.matmul(
                            PS[:, bk, 0:w], ONEB[:], SQ[:, dc, c0:c0 + w], start=(dc == 0), stop=(dc == 7))],
                            reads=[rC, sq_res[dc]], writes=[rPS[bk]])
                    if mode == "epsp":
                        P.op("dve", lambda e, bk=bk, c0=c0, w=w: e.tensor_scalar(
                            out=EPSP[:, c0:c0 + w], in0=PS[:, bk, 0:w], scalar1=1.0 / D, scalar2=EPS,
                            op0=ALU.mult, op1=ALU.add), reads=[rPS[bk]], writes=[rEPSP])
                        P.op("dve", lambda e, c0=c0, w=w: e.scalar_tensor_tensor(
                            out=EPSP[:, c0:c0 + w], in0=EPSP[:, c0:c0 + w], scalar=EPS, in1=EPSP[:, c0:c0 + w],
                            op0=ALU.mult, op1=ALU.mult), reads=[rEPSP], writes=[rEPSP])
                        continue
                    if mode == "rsqrt_epsp":
                        P.op("dve", lambda e, bk=bk, c0=c0, w=w: e.scalar_tensor_tensor(
                            out=RB[:, c0:c0 + w], in0=PS[:, bk, 0:w], scalar=1.0 / D, in1=EPSP[:, c0:c0 + w],
                            op0=ALU.mult, op1=ALU.add), reads=[rPS[bk], rEPSP], writes=[rRB])
                        P.op("act", lambda e, c0=c0, w=w: e.activation(
                            out=RB[:, c0:c0 + w], in_=RB[:, c0:c0 + w], func=AF.Ln), reads=[rRB], writes=[rRB])
                    else:
                        P.op("act", lambda e, bk=bk, c0=c0, w=w: e.activation(
                            out=RB[:, c0:c0 + w], in_=PS[:, bk, 0:w], func=AF.Ln, bias=EPSC[:, 0:1], scale=1.0 / D),
                            reads=[rPS[bk], rC], writes=[rRB])
                    P.op("act", lambda e, c0=c0, w=w: e.activation(
                        out=RB[:, c0:c0 + w], in_=RB[:, c0:c0 + w], func=AF.Exp, scale=-0.5),
                        reads=[rRB], writes=[rRB])

            def squares_of_X(a, b_):
                for dc in range(8):
                    P.op("act", lambda e, dc=dc: e.activation(out=SQ[:, dc, a:b_], in_=X[:, dc, a:b_], func=AF.Square),
                         reads=[rX[dc]], writes=[rSQ[dc]])

            def residual(vi, hmode=None, hvi=None, gained=False):
                for dc in range(8):
                    P.op("dve", lambda e, dc=dc: e.tensor_tensor(
                        out=O[:, dc, ca:nt], in0=O[:, dc, ca:nt], in1=RB[:, ca:nt], op=ALU.mult),
                        reads=[rO[dc], rRB], writes=[rO[dc]])
                    if gained:
                        P.op("dve", lambda e, dc=dc: e.tensor_tensor(
                            out=X[:, dc, ca:nt], in0=X[:, dc, ca:nt], in1=O[:, dc, ca:nt], op=ALU.add),
                            reads=[rO[dc], rX[dc]], writes=[rX[dc]])
                    else:
                        P.op("dve", lambda e, dc=dc: e.scalar_tensor_tensor(
                            out=X[:, dc, ca:nt], in0=O[:, dc, ca:nt], scalar=cv(l, vi, dc), in1=X[:, dc, ca:nt],
                            op0=ALU.mult, op1=ALU.add), reads=[rO[dc], rX[dc], rC], writes=[rX[dc]])
                    if hmode == "gain":
                        P.op("act", lambda e, dc=dc: e.activation(
                            out=H[:, dc, ca:nt], in_=X[:, dc, ca:nt], func=AF.Identity, scale=cv(l, hvi, dc)),
                            reads=[rX[dc], rC], writes=[rH[dc]])
                    elif hmode == "copy":
                        P.op("act", lambda e, dc=dc: e.activation(out=H[:, dc, ca:nt], in_=X[:, dc, ca:nt], func=AF.Copy),
                             reads=[rX[dc]], writes=[rH[dc]])

            def fm_matmul(b, rhs, rhs_res, nk, evac, fis=range(4), kstride=512, cbl=None):
                cbl = cbl or cbs
                for fi in fis:
                    bks = banks(len(cbl))
                    fns = []
                    for k in range(nk):
                        for ci, (c0, w) in enumerate(cbl):
                            fns.append(lambda e, bk=bks[ci], k=k, fi=fi, c0=c0, w=w: e.matmul(
                                PS[:, bk, 0:w], WB[:, b, k * kstride + fi * 128: k * kstride + (fi + 1) * 128],
                                rhs[:, k, c0:c0 + w], start=(k == 0), stop=(k == nk - 1)))
                    P.group("pe", fns, reads=[rW[b]] + rhs_res, writes=[rPS[bk] for bk in bks])
                    for ci, (c0, w) in enumerate(cbl):
                        evac(fi, c0, w, bks[ci])

            P.dma("pool", "pld", lambda e: e.dma_start(out=PB[:, :, 0:nt], in_=pT[l, :, :, t0:t0 + nt]),
                  writes=[rPB])

            squares_of_X(0, nt)
            norm_R(rSQ, cbl=cbs_all)
            for dc in range(8):
                P.op("dve", lambda e, dc=dc: e.scalar_tensor_tensor(
                    out=H[:, dc, 0:nt], in0=X[:, dc, 0:nt], scalar=cv(l, V_PREMIX, dc), in1=RB[:, 0:nt],
                    op0=ALU.mult, op1=ALU.mult), reads=[rX[dc], rRB, rC], writes=[rH[dc]])

            specs = [(Z, rZ, AF.Identity, 0, 16, cbs_all), (U, rU, AF.Gelu_apprx_tanh, 8, 0, cbs),
                     (GA, rGA, AF.Sigmoid, 16, 0, cbs), (GB, rGB, AF.Sigmoid, 24, 0, cbs)]
            for (dst, dres, func, bcol, off, cbl) in specs:
                for half in range(2):
                    b = w_next()

                    def evac(fi, c0, w, bk, dst=dst, dres=dres, func=func, bcol=bcol, off=off, half=half):
                        fc = half * 4 + fi
                        col = l * 32 + bcol + fc
                        P.op("act", lambda e: e.activation(
                            out=dst[:, fc, off + c0: off + c0 + w], in_=PS[:, bk, 0:w], func=func,
                            bias=BIN[:, col:col + 1]), reads=[rPS[bk], rC], writes=[dres[fc]])
                    fm_matmul(b, H, rH, 8, evac, cbl=cbl)
                    w_done()

                if dst is Z:
                    W_ = 16 + nt
                    if ti == 0:
                        P.op("dve", lambda e: e.memset(Z[:, :, 0:16], 0.0), writes=rZ)
                        P.op("dve", lambda e: e.tensor_scalar(
                            out=Z[:, :, 16:16 + HALO], in0=Z[:, :, 16:16 + HALO], scalar1=PC[:, 0:1], scalar2=None,
                            op0=ALU.mult), reads=rZ + [rC], writes=rZ)
                    else:
                        P.op("dve", lambda e: e.tensor_copy(out=Z[:, :, 0:16], in_=ZH[:, l, :, :]),
                             reads=[rZH[l]], writes=rZ)
                    for g in range(4):
                        zs = Z[:, 2 * g:2 * g + 2, :]
                        zres = [rZ[2 * g], rZ[2 * g + 1]]
                        P.op("dve", lambda e, zs=zs: e.tensor_tensor(
                            out=S1[:, :, 1:W_], in0=zs[:, :, 1:W_], in1=zs[:, :, 0:W_ - 1], op=ALU.add),
                            reads=zres, writes=rS1)
                        cur, rcur = S1, rS1
                        if g >= 1:
                            P.op("dve", lambda e: e.tensor_tensor(
                                out=S2[:, :, 3:W_], in0=S1[:, :, 3:W_], in1=S1[:, :, 1:W_ - 2], op=ALU.add),
                                reads=rS1, writes=[rS2])
                            cur, rcur = S2, [rS2]
                        if g >= 2:
                            P.op("dve", lambda e: e.tensor_tensor(
                                out=S1[:, :, 7:W_], in0=S2[:, :, 7:W_], in1=S2[:, :, 3:W_ - 4], op=ALU.add),
                                reads=[rS2], writes=rS1)
                            cur, rcur = S1, rS1
                        if g >= 3:
                            P.op("dve", lambda e: e.tensor_tensor(
                                out=S2[:, :, 15:W_], in0=S1[:, :, 15:W_], in1=S1[:, :, 7:W_ - 8], op=ALU.add),
                                reads=rS1, writes=[rS2])
                            cur, rcur = S2, [rS2]
                        wdw = POOL_WINDOWS[g]
                        P.op("dve", lambda e, cur=cur, g=g, wdw=wdw: e.scalar_tensor_tensor(
                            out=PL[:, 2 * g:2 * g + 2, 0:nt], in0=cur[:, :, 16:16 + nt], scalar=1.0 / wdw,
                            in1=Z[:, 2 * g:2 * g + 2, 16:16 + nt], op0=ALU.mult, op1=ALU.subtract),
                            reads=rcur + zres, writes=[rPL[2 * g], rPL[2 * g + 1]])
                        if ti == 0:
                            for cc in range(2):
                                P.op("dve", lambda e, cur=cur, g=g, cc=cc: e.tensor_tensor(
                                    out=T16[:, cc, :], in0=cur[:, cc, 16 + HALO:32 + HALO],
                                    in1=PC[:, 1 + g * 16: 1 + (g + 1) * 16], op=ALU.mult),
                                    reads=rcur + [rC], writes=[rT16])
                            P.op("dve", lambda e, g=g: e.tensor_tensor(
                                out=PL[:, 2 * g:2 * g + 2, HALO:HALO + 16], in0=T16[:, :, :],
                                in1=Z[:, 2 * g:2 * g + 2, 16 + HALO:32 + HALO], op=ALU.subtract),
                                reads=[rT16] + zres, writes=[rPL[2 * g], rPL[2 * g + 1]])
                    dump("Z", Z[:, :, :], rZ)
                    dump("PL", PL, rPL)
                    dump("H", H[:, :, :], rH)
                    P.op("dve", lambda e: e.tensor_copy(out=ZH[:, l, :, :], in_=Z[:, :, nt:nt + 16]),
                         reads=rZ, writes=[rZH[l]])
                    if last and prefetch is not None:
                        pt0, pnt = prefetch
                        P.dma("sp", "xld", lambda e: e.dma_start(out=Z[:, :, 0:pnt], in_=xT[:, :, pt0:pt0 + pnt]),
                              writes=rZ)

            vb = [w_next(), w_next()]
            first_v = [True]
            for half in range(2):
                b = vb[half]
                for c in range(clo, nch):
                    bk = banks(1)[0]
                    fns = []
                    for k in range(8):
                        fns.append(lambda e, bk=bk, k=k, c=c, b=b: e.matmul(
                            PS[:, bk, :], H[:, k, c * 128:(c + 1) * 128], WB[:, b, k * 512:(k + 1) * 512],
                            start=(k == 0), stop=False))
                    vsl = slice(half * 512, (half + 1) * 512)
                    pr = slice(32 * l, 32 * l + 1)
                    fns.append(lambda e, bk=bk, vsl=vsl, pr=pr: e.matmul(
                        PS[:, bk, :], ONEB[pr, :], VBH[pr, vsl], start=False, stop=False))
                    fns.append(lambda e, bk=bk, vsl=vsl, pr=pr: e.matmul(
                        PS[:, bk, :], ONEB[pr, :], VBL[pr, vsl], start=False, stop=True))
                    P.group("pe", fns, reads=[rW[b], rC] + rH, writes=[rPS[bk]])
                    o0 = c * 1024 + half * 512
                    P.op("act", lambda e, bk=bk, o0=o0: e.activation(
                        out=VTM[:, o0:o0 + 512], in_=PS[:, bk, :], func=AF.Gelu_apprx_tanh),
                        reads=[rPS[bk]], writes=([rV[c]] + (rO if first_v[0] else [])))
                    first_v[0] = False
                w_done()
            for c in range(clo, nch):
                for half in range(2):
                    o0 = c * 1024 + half * 512
                    P.op("dve", lambda e, o0=o0, half=half: e.bn_stats(out=ST[:, half, :], in_=VTM[:, o0:o0 + 512]),
                         reads=[rV[c]], writes=[rST])
                P.op("dve", lambda e, c=c: e.bn_aggr(out=MVA[:, c, :], in_=ST[:, :, :].rearrange("p a b -> p (a b)")),
                     reads=[rST], writes=[rMV])

            b4 = w_next()

            def m4_groups(idx):
                for gi in idx:
                    g, dh = gi // 2, gi % 2
                    bks = banks(len(cbs))
                    fns = []
                    for cc in range(2):
                        for ci, (c0, w) in enumerate(cbs):
                            o0 = (g * 2 + cc) * 256 + dh * 128
                            fns.append(lambda e, bk=bks[ci], o0=o0, g=g, cc=cc, c0=c0, w=w: e.matmul(
                                PS[:, bk, 0:w], WB[:, b4, o0:o0 + 128], PL[:, 2 * g + cc, c0:c0 + w],
                                start=(cc == 0), stop=(cc == 1)))
                    P.group("pe", fns, reads=[rW[b4], rPL[2 * g], rPL[2 * g + 1]], writes=[rPS[bk] for bk in bks])
                    oc = 2 * g + dh
                    for ci, (c0, w) in enumerate(cbs):
                        if gi % 2 == 0:
                            P.op("dve", lambda e, bk=bks[ci], oc=oc, c0=c0, w=w: e.tensor_scalar(
                                out=MS[:, oc, c0:c0 + w], in0=PS[:, bk, 0:w], scalar1=cv(l, V_PSCALE, oc),
                                scalar2=None, op0=ALU.mult), reads=[rPS[bks[ci]], rC], writes=[rMS[oc]])
                        else:
                            P.op("act", lambda e, bk=bks[ci], oc=oc, c0=c0, w=w: e.activation(
                                out=MS[:, oc, c0:c0 + w], in_=PS[:, bk, 0:w], func=AF.Identity,
                                scale=cv(l, V_PSCALE, oc)), reads=[rPS[bks[ci]], rC], writes=[rMS[oc]])

            m4_groups(range(0, 3))
            P.op("act", lambda e: e.activation(out=RSTD[:, clo:nch], in_=MVA[:, clo:nch, 1], func=AF.Ln,
                                               bias=EPSC[:, 0:1], scale=1.0), reads=[rMV, rC], writes=[rRSTD])
            P.op("act", lambda e: e.activation(out=RSTD[:, clo:nch], in_=RSTD[:, clo:nch], func=AF.Exp, scale=-0.5),
                 reads=[rRSTD], writes=[rRSTD])
            P.op("dve", lambda e: e.scalar_tensor_tensor(
                out=NMR[:, clo:nch], in0=MVA[:, clo:nch, 0], scalar=-1.0, in1=RSTD[:, clo:nch],
                op0=ALU.mult, op1=ALU.mult), reads=[rMV, rRSTD], writes=[rNMR])
            for c in range(clo, nch):
                P.op("act", lambda e, c=c: e.activation(
                    out=NTM[:, c * 1024:(c + 1) * 1024], in_=VTM[:, c * 1024:(c + 1) * 1024], func=AF.Identity,
                    scale=RSTD[:, c:c + 1], bias=NMR[:, c:c + 1]), reads=rO + [rV[c], rRSTD, rNMR], writes=rN)
            m4_groups(range(3, 8))
            w_done()
            dump("VTM", VTM, rO)
            dump("NTM", NTM, rN)

            for h in range(8):
                for ci, (c0, w) in enumerate(cbs):
                    bk = banks(1)[0]
                    nck = w // 128
                    fns = []
                    for cc in range(nck):
                        c = c0 // 128 + cc
                        fns.append(lambda e, bk=bk, cc=cc, c=c, h=h: e.matmul(
                            PS[:, bk, cc * 128:(cc + 1) * 128], NTM[:, c * 1024 + h * 128: c * 1024 + (h + 1) * 128],
                            WSB[:, l * 1024 + h * 128: l * 1024 + (h + 1) * 128], start=True, stop=True))
                    P.group("pe", fns, reads=rN + [rC], writes=[rPS[bk]])
                    tb = (h * len(cbs) + ci) % 2
                    bf = BF[:, l * 1024 + h * 128: l * 1024 + (h + 1) * 128]
                    P.op("dve", lambda e, bk=bk, nck=nck, w=w, tb=tb, bf=bf, h=h: e.scalar_tensor_tensor(
                        out=TMP[:, tb, 0:w].rearrange("p (a t) -> p a t", a=nck),
                        in0=PS[:, bk, 0:w].rearrange("p (a t) -> p a t", a=nck),
                        scalar=cv(l, V_LNG, h),
                        in1=bf.unsqueeze(1).broadcast_to([128, nck, 128]),
                        op0=ALU.mult, op1=ALU.add), reads=[rPS[bk], rC], writes=[rTMP[tb]])
                    P.op("dve", lambda e, tb=tb, h=h, c0=c0, w=w: e.tensor_tensor(
                        out=U[:, h, c0:c0 + w], in0=TMP[:, tb, 0:w], in1=U[:, h, c0:c0 + w], op=ALU.mult),
                        reads=[rTMP[tb], rU[h]], writes=[rU[h]])
            dump("SG", U, rU)
            dump("MS", MS, rMS)

            for half in range(2):
                b = w_next()

                def evac(fi, c0, w, bk, half=half):
                    oc = half * 4 + fi
                    P.op("dve", lambda e: e.tensor_tensor(
                        out=GA[:, oc, c0:c0 + w], in0=PS[:, bk, 0:w], in1=GA[:, oc, c0:c0 + w], op=ALU.mult),
                        reads=[rPS[bk], rGA[oc]], writes=[rGA[oc]])
                fm_matmul(b, MS, rMS, 8, evac)
                w_done()
            dump("M1", GA, rGA)

            for half in range(2):
                b = w_next()

                def evac(fi, c0, w, bk, half=half):
                    oc = half * 4 + fi
                    P.op("dve", lambda e: e.tensor_tensor(
                        out=GB[:, oc, c0:c0 + w], in0=PS[:, bk, 0:w], in1=GB[:, oc, c0:c0 + w], op=ALU.mult),
                        reads=[rPS[bk], rGB[oc]], writes=[rGB[oc]])
                    P.op("dve", lambda e: e.tensor_tensor(
                        out=GB[:, oc, c0:c0 + w], in0=GB[:, oc, c0:c0 + w], in1=GA[:, oc, c0:c0 + w], op=ALU.add),
                        reads=[rGB[oc], rGA[oc]], writes=[rGB[oc]])
                fm_matmul(b, U, rU, 8, evac)
                w_done()

            def evac_O(oc, c0, w, bk, vi):
                P.op("act", lambda e: e.activation(out=SQ[:, oc, c0:c0 + w], in_=PS[:, bk, 0:w], func=AF.Square),
                     reads=[rPS[bk]], writes=[rSQ[oc], rBT[bk]])
                P.op("dve", lambda e: e.tensor_scalar(out=O[:, oc, c0:c0 + w], in0=PS[:, bk, 0:w],
                                                      scalar1=cv(l, vi, oc), scalar2=None, op0=ALU.mult),
                     reads=[rPS[bk], rBT[bk], rC], writes=[rO[oc]])

            for half in range(2):
                b = w_next()
                fm_matmul(b, GB, rGB, 8, lambda fi, c0, w, bk, half=half: evac_O(half * 4 + fi, c0, w, bk, V_POSTMIX))
                w_done()
            dump("MG", GB, rGB)
            dump("O1", O[:, :, :], rO)

            norm_R(rSQ)
            residual(V_POSTMIX, hmode="gain", hvi=V_PREFFN, gained=True)
            squares_of_X(ca, nt)
            norm_R(rSQ, mode="epsp")
            dump("X1", X[:, :, :], rX)

            for j in range(8):
                b = w_next()

                def evac(fi, c0, w, bk, j=j):
                    fc = j * 4 + fi
                    tb = fi % 2
                    P.op("act", lambda e: e.activation(out=TMP[:, tb, c0:c0 + w], in_=PS[:, bk, 0:w], func=AF.Relu),
                         reads=[rPS[bk]], writes=[rTMP[tb]])
                    P.op("dve", lambda e: e.tensor_tensor(
                        out=ACTB[:, fc, c0:c0 + w], in0=PS[:, bk, 0:w], in1=TMP[:, tb, c0:c0 + w], op=ALU.mult),
                        reads=[rPS[bk], rTMP[tb]], writes=[rACT[fc]])
                fm_matmul(b, H, rH, 8, evac)
                w_done()

            for j in range(8):
                b = w_next()
                fm_matmul(b, ACTB, rACT, 32, lambda fi, c0, w, bk, j=j: evac_O(j, c0, w, bk, V_POSTFFN),
                          fis=[0], kstride=128)
                w_done()

            norm_R(rSQ, mode="rsqrt_epsp")
            residual(V_POSTFFN, hmode="copy", gained=True)

            for half in range(2):
                b = w_next()

                def evac(fi, c0, w, bk, half=half):
                    oc = half * 4 + fi
                    P.op("act", lambda e: e.activation(
                        out=GF[:, oc, c0:c0 + w], in_=PS[:, bk, 0:w], func=AF.Sigmoid),
                        reads=[rPS[bk]], writes=rG[oc])
                fm_matmul(b, H, rH, 8, evac)
                w_done()

            bp = w_next()
            for oc in range(8):
                bks = banks(len(cbs))
                fns = []
                for kc in range(2):
                    for ci, (c0, w) in enumerate(cbs):
                        fns.append(lambda e, bk=bks[ci], kc=kc, oc=oc, c0=c0, w=w: e.matmul(
                            PS[:, bk, 0:w], WB[:, bp, kc * 1024 + oc * 128: kc * 1024 + (oc + 1) * 128],
                            PB[:, kc, c0:c0 + w], start=(kc == 0), stop=(kc == 1)))
                P.group("pe", fns, reads=[rW[bp], rPB], writes=[rPS[bk] for bk in bks])
                for ci, (c0, w) in enumerate(cbs):
                    bk = bks[ci]
                    P.op("dve", lambda e, bk=bk, oc=oc, c0=c0, w=w: e.tensor_tensor(
                        out=O[:, oc, c0:c0 + w], in0=PS[:, bk, 0:w], in1=GF[:, oc, c0:c0 + w], op=ALU.mult),
                        reads=[rPS[bk]] + rG[oc], writes=[rO[oc]])
                    P.op("act", lambda e, oc=oc, c0=c0, w=w: e.activation(
                        out=SQ[:, oc, c0:c0 + w], in_=O[:, oc, c0:c0 + w], func=AF.Square),
                        reads=[rO[oc]], writes=[rSQ[oc]])
            w_done()

            norm_R(rSQ)
            residual(V_POSTPLE)

        bufs = [(X, rX), (Z, rZ)]
        last_store = None
        P.dma("sp", "xld", lambda e: e.dma_start(out=X[:, :, 0:TILES[0][1]], in_=xT[:, :, 0:TILES[0][1]]), writes=rX)
        for ti, (t0, nt) in enumerate(TILES):
            (Xc, rXc), (Zc, rZc) = bufs[ti % 2], bufs[(ti + 1) % 2]
            prefetch = TILES[ti + 1] if ti + 1 < len(TILES) else None
            for l in range(L_RUN):
                tile_layer(ti, t0, nt, l, Xc, rXc, Zc, rZc, prefetch)
            s0 = HALO if ti == 0 else 0
            o0 = t0 + s0 - HALO
            n_out = nt - s0
            last_store = P.dma("sp", "ost", lambda e, s0=s0, o0=o0, n_out=n_out, Xc=Xc: e.dma_start(
                out=outT[:, :, o0:o0 + n_out], in_=Xc[:, :, s0:s0 + n_out]), reads=rXc)
        P.final_wait("sp", [last_store] + [d[1] for d in dump_toks])

        with nc.Block() as block:
            @block.tensor
            def _(e):
                for f in P.q["pe"]:
                    f(e)

            @block.scalar
            def _(e):
                for f in P.q["act"]:
                    f(e)

            @block.vector
            def _(e):
                for f in P.q["dve"]:
                    f(e)

            @block.gpsimd
            def _(e):
                for f in P.q["pool"]:
                    f(e)

            @block.sync
            def _(e):
                for f in P.q["sp"]:
                    f(e)
    return nc


def _fm8(v):
    return np.ascontiguousarray(v.reshape(8, 128).T)


def _blk_k512(W, col0):
    K = W.shape[0]
    return W[:, col0:col0 + 512].reshape(K // 128, 128, 512).transpose(1, 0, 2).reshape(128, -1)


def _build_wstream(inp):
    ws = np.zeros((L, NBLK, 128, BLK), np.float32)
    for l in range(L):
        w_in = inp["w_in"][l]
        order = [0, 512, 1024, 1536, 3072, 3584, 4096, 4608, 2048, 2560]
        for j, c0 in enumerate(order):
            ws[l, j] = _blk_k512(w_in, c0)
        ws[l, 10, :, :2048] = inp["pool_w"][l].reshape(4, 2, 128, 256).transpose(2, 0, 1, 3).reshape(128, 2048)
        for i, name in enumerate(("w_pa", "w_pb", "w_o")):
            for half in range(2):
                ws[l, 11 + 2 * i + half] = _blk_k512(inp[name][l], half * 512)
        for j in range(8):
            ws[l, 17 + j] = _blk_k512(inp["w_ff1"][l], j * 512)
        w2 = inp["w_ff2"][l]
        for j in range(8):
            ws[l, 25 + j] = w2[:, j * 128:(j + 1) * 128].reshape(32, 128, 128).transpose(1, 0, 2).reshape(128, BLK)
        for half in range(2):
            ws[l, 33 + half] = _blk_k512(inp["w_ple_gate"][l], half * 512)
        ws[l, 35, :, :2048] = inp["w_ple_proj"][l].reshape(2, 128, 1024).transpose(1, 0, 2).reshape(128, 2048)
    return ws


_NC_CACHE = {}


def make_in_maps(inp):
    x, p = inp["x"], inp["p"]
    B, S, _ = x.shape
    wst = _build_wstream(inp)
    cvec = np.zeros((128, L * 64), np.float32)
    names = ["pre_mix_g", "post_mix_g", "pre_ffn_g", "post_ffn_g", "post_ple_g", "pool_scale", "sgu_ln_g", "sgu_ln_b"]
    for l in range(L):
        for vi, nm in enumerate(names):
            cvec[:, (l * 8 + vi) * 8:(l * 8 + vi + 1) * 8] = _fm8(inp[nm][l])
    binfm = np.zeros((128, L * 32), np.float32)
    bvrow = np.zeros((L, 1024), np.float32)
    for l in range(L):
        b = inp["b_in"][l]
        for gi, c0 in enumerate((0, 1024, 3072, 4096)):
            binfm[:, l * 32 + gi * 8: l * 32 + (gi + 1) * 8] = _fm8(b[c0:c0 + 1024])
        bvrow[l] = b[2048:3072]
    wsT = np.ascontiguousarray(inp["sgu_w_s"].transpose(3, 0, 1, 2)).reshape(128, L * 1024)
    bsbc = np.ascontiguousarray(np.broadcast_to(inp["sgu_b_s"].reshape(1, L * 1024), (128, L * 1024)))
    si = np.arange(128)
    cmask = (si[:, None] <= si[None, :]).astype(np.float32)

    in_maps = []
    for c in range(NCORES):
        b, half = c // 2, c % 2
        s0 = half * OWN
        xt = np.zeros((NTOK, D), np.float32)
        pt = np.zeros((L, NTOK, 256), np.float32)
        xt[HALO:] = x[b, s0:s0 + OWN]
        pt[:, HALO:] = p[:, b, s0:s0 + OWN]
        pcore = np.zeros((128, 80), np.float32)
        if half == 1:
            xt[:HALO] = x[b, s0 - HALO:s0]
            pt[:, :HALO] = p[:, b, s0 - HALO:s0]
            pcore[:, 0] = 1.0
        for g, w in enumerate(POOL_WINDOWS):
            j = np.arange(16)
            cnt = np.minimum(j + 1, w) if half == 0 else np.full(16, w)
            pcore[:, 1 + g * 16: 1 + (g + 1) * 16] = (1.0 / cnt.astype(np.float32))[None, :]
        xTc = np.ascontiguousarray(xt.T.reshape(8, 128, NTOK).transpose(1, 0, 2))
        pTc = np.ascontiguousarray(pt.transpose(0, 2, 1).reshape(L, 2, 128, NTOK).transpose(0, 2, 1, 3))
        in_maps.append({"xT": xTc, "pT": pTc, "wst": wst, "cvec": cvec, "binfm": binfm, "bvrow": bvrow,
                        "wsT": wsT, "bsbc": bsbc, "cmask": cmask, "pcore": pcore})
    return in_maps


def kernel(**inputs):
    inp = {k: np.asarray(v, dtype=np.float32) for k, v in inputs.items()}
    B, S, _ = inp["x"].shape
    in_maps = make_in_maps(inp)
    if "nc" not in _NC_CACHE:
        _NC_CACHE["nc"] = build_nc()
    nc = _NC_CACHE["nc"]
    res = run_bass_kernel_spmd(nc, in_maps, core_ids=list(range(NCORES)))
    out = np.empty((B, S, D), np.float32)
    for c in range(NCORES):
        b, half = c // 2, c % 2
        o = np.asarray(res.results[c]["outT"], dtype=np.float32)
        out[b, half * OWN:(half + 1) * OWN, :] = o.transpose(1, 0, 2).reshape(D, OWN).T
    return out
```

```python
import numpy as np
import concourse.bass as bass
import concourse.mybir as mybir
from concourse.bass_utils import run_bass_kernel_spmd

F32 = mybir.dt.float32
BF16 = mybir.dt.bfloat16
AF = mybir.ActivationFunctionType
ALU = mybir.AluOpType

D = 1024
L = 2
NCORES = 8
OWN = 2048
HALO = 128
NTOK = OWN + HALO
TILES = [(0, 640), (640, 512), (1152, 512), (1664, 512)]
NTMAX = 640
ZW = 16 + NTMAX
EPS = 1e-6
NBLK = 36
NWBUF = 4
BLK = 4096
V_PREMIX, V_POSTMIX, V_PREFFN, V_POSTFFN, V_POSTPLE, V_PSCALE, V_LNG, V_LNB = range(8)
POOL_WINDOWS = (2, 4, 8, 16)
POOL_CHUNKS = ()
RES_ORDER = (0, 1, 2, 3, 4, 5, 6, 7)


class Res:
    __slots__ = ("w", "rs", "const")

    def __init__(self, const=False):
        self.w = None
        self.rs = {}
        self.const = const


class Prog:
    ENG = ("pe", "act", "dve", "pool", "sp")

    def __init__(self):
        self.q = {k: [] for k in self.ENG}
        self.semh = {}
        self.cnt = {}
        self.seen = {k: {} for k in self.ENG}

    def add_sem(self, key, handle):
        self.semh[key] = handle
        self.cnt[key] = 0

    def _wait(self, eng, toks):
        need = {}
        for t in toks:
            if t is None:
                continue
            key, val = t
            if key == "pe" and eng == "pe":
                continue
            if self.seen[eng].get(key, 0) >= val:
                continue
            if need.get(key, 0) < val:
                need[key] = val
        for key, val in need.items():
            self.seen[eng][key] = val
            s = self.semh[key]
            self.q[eng].append(lambda e, s=s, val=val: e.wait_ge(s, val))

    @staticmethod
    def _deps(reads, writes):
        deps = []
        for r in reads:
            deps.append(r.w)
        for w in writes:
            deps.append(w.w)
            deps.extend(w.rs.items())
        return deps

    @staticmethod
    def _commit(tok, reads, writes):
        for r in reads:
            if not r.const:
                if r.rs.get(tok[0], 0) < tok[1]:
                    r.rs[tok[0]] = tok[1]
        for w in writes:
            w.w = tok
            w.rs = {}

    def op(self, eng, fn, reads=(), writes=()):
        self._wait(eng, self._deps(reads, writes))
        self.cnt[eng] += 1
        tok = (eng, self.cnt[eng])
        s = self.semh[eng]
        self.q[eng].append(lambda e, fn=fn, s=s: fn(e).then_inc(s, 1))
        self._commit(tok, reads, writes)
        return tok

    def group(self, eng, fns, reads=(), writes=()):
        self._wait(eng, self._deps(reads, writes))
        self.cnt[eng] += 1
        tok = (eng, self.cnt[eng])
        s = self.semh[eng]
        for f in fns[:-1]:
            self.q[eng].append(f)
        last = fns[-1]
        self.q[eng].append(lambda e, fn=last, s=s: fn(e).then_inc(s, 1))
        self._commit(tok, reads, writes)
        return tok

    def dma(self, eng, semkey, fn, reads=(), writes=()):
        self._wait(eng, self._deps(reads, writes))
        self.cnt[semkey] += 16
        tok = (semkey, self.cnt[semkey])
        s = self.semh[semkey]
        self.q[eng].append(lambda e, fn=fn, s=s: fn(e).then_inc(s, 16))
        self._commit(tok, reads, writes)
        return tok

    def final_wait(self, eng, toks):
        self._wait(eng, toks)


def R(n, const=False):
    return [Res(const) for _ in range(n)]


def build_nc(TILES=TILES, L_RUN=L, NOUT=OWN, DUMPS=()):
    nc = bass.Bass("TRN2", target_bir_lowering=False)
    xT = nc.dram_tensor("xT", [128, 8, NTOK], F32, kind="ExternalInput").ap()
    pT = nc.dram_tensor("pT", [L, 128, 2, NTOK], F32, kind="ExternalInput").ap()
    wst = nc.dram_tensor("wst", [L, NBLK, 128, BLK], F32, kind="ExternalInput").ap()
    cvec_d = nc.dram_tensor("cvec", [128, L * 64], F32, kind="ExternalInput").ap()
    binfm_d = nc.dram_tensor("binfm", [128, L * 32], F32, kind="ExternalInput").ap()
    bvrow_d = nc.dram_tensor("bvrow", [L, 1024], F32, kind="ExternalInput").ap()
    wsT_d = nc.dram_tensor("wsT", [128, L * 1024], F32, kind="ExternalInput").ap()
    bsbc_d = nc.dram_tensor("bsbc", [128, L * 1024], F32, kind="ExternalInput").ap()
    cmask_d = nc.dram_tensor("cmask", [128, 128], F32, kind="ExternalInput").ap()
    pcore_d = nc.dram_tensor("pcore", [128, 80], F32, kind="ExternalInput").ap()
    outT = nc.dram_tensor("outT", [128, 8, NOUT], F32, kind="ExternalOutput").ap()

    dump_d = {}
    for (nm, shp, dt) in DUMPS:
        dump_d[nm] = nc.dram_tensor("dbg_" + nm, shp, dt, kind="ExternalOutput").ap()
    P = Prog()
    from contextlib import ExitStack
    with ExitStack() as es:
        def sb(name, shape, dt):
            return es.enter_context(nc.sbuf_tensor(name, shape, dt))

        X = sb("X", [128, 8, ZW], F32)
        H = sb("H", [128, 8, NTMAX], BF16)
        O = sb("O", [128, 8, NTMAX], F32)
        Z = sb("Z", [128, 8, ZW], F32)
        S2 = sb("S2", [128, 2, ZW], F32)
        BB = sb("BB", [128, 6, 8 * NTMAX], BF16)
        WB = sb("WB", [128, NWBUF, BLK], BF16)
        PB = sb("PB", [128, 2, NTMAX], BF16)
        RB = sb("RB", [128, NTMAX], F32)
        TMP = sb("TMP", [128, 2, ZW], F32)
        S1 = TMP
        ZH = sb("ZH", [128, L, 8, 16], F32)
        T16 = sb("T16", [128, 2, 16], F32)
        ST = sb("ST", [128, 2, 6], F32)
        MVA = sb("MVA", [128, 8, 2], F32)
        RSTD = sb("RSTD", [128, 8], F32)
        EPSC = sb("EPSC", [128, 1], F32)
        NMR = sb("NMR", [128, 8], F32)
        EPSP = sb("EPSP", [128, NTMAX], F32)
        CV = sb("CV", [128, L * 64], F32)
        BIN = sb("BIN", [128, L * 32], F32)
        VBH = sb("VBH", [33, 1024], BF16)
        VBL = sb("VBL", [33, 1024], BF16)
        WSB = sb("WSB", [128, L * 1024], BF16)
        BF = sb("BF", [128, L * 1024], F32)
        CM = sb("CM", [128, 128], F32)
        PC = sb("PC", [128, 80], F32)
        ONEB = sb("ONEB", [128, 128], BF16)
        ONEF = sb("ONEF", [128, 128], F32)
        PS = es.enter_context(nc.psum_tensor("PS", [128, 8, 512], F32))
        WSF = Z[:, :, :].rearrange("p c t -> p (c t)")[:, 0:L * 1024]
        BSB = BB[:, 5, 0:2 * L * 1024].bitcast(F32)

        for key in ("pe", "act", "dve", "pool", "sp", "cst", "xld", "ost", "pld", "dbg") + tuple(
                "w%d" % i for i in range(NWBUF)):
            P.add_sem(key, es.enter_context(nc.semaphore("s_" + key)))

        rX, rH, rO, rZ = R(8), R(8), R(8), R(8)
        rB = [R(8) for _ in range(6)]
        rS2 = Res()
        rW = R(NWBUF)
        rPB, rRB = Res(), Res()
        rTMP = R(2)
        rS1 = rTMP
        rZH = R(L)
        rT16, rST, rMV, rRSTD, rNMR, rEPSP = Res(), Res(), Res(), Res(), Res(), Res()
        rBT = R(8)
        rV = R(5)
        rPS = R(8)
        rC = Res()
        rWSF = None

        def Bv(i):
            return BB[:, i, :].rearrange("p (c t) -> p c t", c=8)

        U, GA, GB, NB_, PL, MS = (Bv(i) for i in range(6))
        SQ = PL
        rU, rGA, rGB, rN, rPL, rMS = rB
        rSQ = rPL
        NTM = BB[:, 3, :]
        ACTB = BB[:, 0:4, :].rearrange("p a (c t) -> p (a c) t", c=8)
        rACT = rB[0] + rB[1] + rB[2] + rB[3]
        VTM = O[:, :, :].rearrange("p c t -> p (c t)")

        psn = [0]

        def banks(n):
            b = psn[0]
            if b + n > 8:
                b = 0
            psn[0] = (b + n) % 8
            return list(range(b, b + n))

        stream = []
        for (t0, nt) in TILES:
            for l in range(L_RUN):
                for j in range(NBLK):
                    stream.append((l, j))
        wstate = {"issued": 0, "cons": 0}

        def blk_len(j):
            return 2048 if j in (10, 35) else BLK

        def w_issue():
            i = wstate["issued"]
            if i >= len(stream):
                return
            l, j = stream[i]
            b = i % NWBUF
            n = blk_len(j)
            P.dma("pool", "w%d" % b,
                  lambda e, b=b, l=l, j=j, n=n: e.dma_start(out=WB[:, b, 0:n], in_=wst[l, j, :, 0:n]),
                  writes=[rW[b]])
            wstate["issued"] += 1

        def w_next():
            i = wstate["cons"]
            wstate["cons"] += 1
            return i % NWBUF

        def w_done():
            w_issue()

        cst = []
        for (dst, src) in ((CV[:], cvec_d), (BIN[:], binfm_d), (WSF, wsT_d), (BSB, bsbc_d),
                           (CM[:], cmask_d), (PC[:], pcore_d)):
            nd = len(dst.shape)
            P.dma("sp", "cst", lambda e, dst=dst, src=src: e.dma_start(out=dst, in_=src), writes=[rC])
        for _ in range(NWBUF):
            w_issue()

        BVR = VTM[0:33, 0:1024]
        BVT = VTM[0:33, 1024:2048]
        P.op("dve", lambda e: e.memset(VTM[0:33, 0:2048], 0.0), writes=rO)
        for l in range(L):
            P.dma("sp", "cst", lambda e, l=l: e.dma_start(out=VTM[32 * l:32 * l + 1, 0:1024], in_=bvrow_d[l:l + 1, :]),
                  writes=[rC] + rO)
        P.op("dve", lambda e: e.memset(ONEB[:], 1.0), writes=[rC])
        P.op("dve", lambda e: e.memset(ONEF[:], 1.0), writes=[rC])
        P.op("dve", lambda e: e.memset(EPSC[:], EPS), writes=[rC])
        P.op("dve", lambda e: e.tensor_copy(out=VBH[:], in_=BVR), reads=[rC], writes=[rC])
        P.op("dve", lambda e: e.tensor_copy(out=BVT, in_=VBH[:]), reads=[rC], writes=[rC])
        P.op("dve", lambda e: e.tensor_tensor(out=BVT, in0=BVR, in1=BVT, op=ALU.subtract),
             reads=[rC], writes=[rC] + rO)
        P.op("dve", lambda e: e.tensor_copy(out=VBL[:], in_=BVT), reads=[rC], writes=[rC] + rO)
        for l in range(L):
            for h in range(8):
                sl = slice(l * 1024 + h * 128, l * 1024 + (h + 1) * 128)
                P.op("dve", lambda e, sl=sl: e.tensor_tensor(out=WSF[:, sl], in0=WSF[:, sl], in1=CM[:], op=ALU.mult),
                     reads=[rC], writes=rZ)
            P.op("dve", lambda e, l=l: e.tensor_copy(out=WSB[:, l * 1024:(l + 1) * 1024],
                                                     in_=WSF[:, l * 1024:(l + 1) * 1024]),
                 reads=rZ, writes=[rC])
            for hh in range(2):
                bk = banks(1)[0]
                P.group("pe", [lambda e, bk=bk, l=l, hh=hh: e.matmul(
                    PS[:, bk, :], ONEF[:], WSF[:, l * 1024 + hh * 512: l * 1024 + (hh + 1) * 512],
                    start=True, stop=True)], reads=[rC] + rZ, writes=[rPS[bk]])
                for h4 in range(4):
                    h = hh * 4 + h4
                    sl = slice(l * 1024 + h * 128, l * 1024 + (h + 1) * 128)
                    col = (l * 8 + V_LNB) * 8 + h
                    P.op("dve", lambda e, bk=bk, h4=h4, sl=sl, col=col: e.scalar_tensor_tensor(
                        out=BF[:, sl], in0=PS[:, bk, h4 * 128:(h4 + 1) * 128], scalar=CV[:, col:col + 1],
                        in1=BSB[:, sl], op0=ALU.mult, op1=ALU.add),
                        reads=[rPS[bk], rC] + rB[5], writes=[rC])
        rC_done = rC.w
        rC.const = True

        def cv(l, vi, c):
            col = (l * 8 + vi) * 8 + c
            return CV[:, col:col + 1]

        dump_toks = []

        def dump(nm, ap, res):
            if nm in dump_d and nm not in [d[0] for d in dump_toks]:
                dump_toks.append((nm, P.dma("sp", "dbg", lambda e: e.dma_start(out=dump_d[nm], in_=ap), reads=res)))

        def colblocks(ca, nt):
            cbs = []
            c = ca
            while c < nt:
                w = min(512, nt - c)
                cbs.append((c, w))
                c += w
            return cbs

        GF = BB[:, 0:2, :].rearrange("p a t -> p (a t)").bitcast(F32).rearrange("p (c t) -> p c t", c=8)
        rG = [[rB[oc // 4][2 * (oc % 4)], rB[oc // 4][2 * (oc % 4) + 1]] for oc in range(8)]

        def tile_layer(ti, t0, nt, l, X, rX, Z, rZ, prefetch):
            last = (l == L_RUN - 1)
            ca = HALO if (ti == 0 and last) else 0
            cbs = colblocks(ca, nt)
            cbs_all = colblocks(0, nt)
            nch = nt // 128
            clo = ca // 128

            def norm_R(sq_res, mode="rsqrt", cbl=None):
                for (c0, w) in (cbl or cbs):
                    bk = banks(1)[0]
                    for dc in range(8):
                        P.group("pe", [lambda e, bk=bk, dc=dc, c0=c0, w=w: e.matmul(
                            PS[:, bk, 0:w], ONEB[:], SQ[:, dc, c0:c0 + w], start=(dc == 0), stop=(dc == 7))],
                            reads=[rC, sq_res[dc]], writes=[rPS[bk]])
                    if mode == "epsp":
                        P.op("dve", lambda e, bk=bk, c0=c0, w=w: e.tensor_scalar(
                            out=EPSP[:, c0:c0 + w], in0=PS[:, bk, 0:w], scalar1=1.0 / D, scalar2=EPS,
                            op0=ALU.mult, op1=ALU.add), reads=[rPS[bk]], writes=[rEPSP])
                        P.op("dve", lambda e, c0=c0, w=w: e.scalar_tensor_tensor(
                            out=EPSP[:, c0:c0 + w], in0=EPSP[:, c0:c0 + w], scalar=EPS, in1=EPSP[:, c0:c0 + w],
                            op0=ALU.mult, op1=ALU.mult), reads=[rEPSP], writes=[rEPSP])
                        continue
                    if mode == "rsqrt_epsp":
                        P.op("dve", lambda e, bk=bk, c0=c0, w=w: e.scalar_tensor_tensor(
                            out=RB[:, c0:c0 + w], in0=PS[:, bk, 0:w], scalar=1.0 / D, in1=EPSP[:, c0:c0 + w],
                            op0=ALU.mult, op1=ALU.add), reads=[rPS[bk], rEPSP], writes=[rRB])
                        P.op("act", lambda e, c0=c0, w=w: e.activation(
                            out=RB[:, c0:c0 + w], in_=RB[:, c0:c0 + w], func=AF.Ln), reads=[rRB], writes=[rRB])
                    else:
                        P.op("act", lambda e, bk=bk, c0=c0, w=w: e.activation(
                            out=RB[:, c0:c0 + w], in_=PS[:, bk, 0:w], func=AF.Ln, bias=EPSC[:, 0:1], scale=1.0 / D),
                            reads=[rPS[bk], rC], writes=[rRB])
                    P.op("act", lambda e, c0=c0, w=w: e.activation(
                        out=RB[:, c0:c0 + w], in_=RB[:, c0:c0 + w], func=AF.Exp, scale=-0.5),
                        reads=[rRB], writes=[rRB])

            def squares_of_X(a, b_):
                for dc in range(8):
                    P.op("act", lambda e, dc=dc: e.activation(out=SQ[:, dc, a:b_], in_=X[:, dc, a:b_], func=AF.Square),
                         reads=[rX[dc]], writes=[rSQ[dc]])

            def residual(vi, hmode=None, hvi=None, gained=False):
                for dc in range(8):
                    P.op("dve", lambda e, dc=dc: e.tensor_tensor(
                        out=O[:, dc, ca:nt], in0=O[:, dc, ca:nt], in1=RB[:, ca:nt], op=ALU.mult),
                        reads=[rO[dc], rRB], writes=[rO[dc]])
                    if gained:
                        P.op("dve", lambda e, dc=dc: e.tensor_tensor(
                            out=X[:, dc, ca:nt], in0=X[:, dc, ca:nt], in1=O[:, dc, ca:nt], op=ALU.add),
                            reads=[rO[dc], rX[dc]], writes=[rX[dc]])
                    else:
                        P.op("dve", lambda e, dc=dc: e.scalar_tensor_tensor(
                            out=X[:, dc, ca:nt], in0=O[:, dc, ca:nt], scalar=cv(l, vi, dc), in1=X[:, dc, ca:nt],
                            op0=ALU.mult, op1=ALU.add), reads=[rO[dc], rX[dc], rC], writes=[rX[dc]])
                    if hmode == "gain":
                        P.op("act", lambda e, dc=dc: e.activation(
                            out=H[:, dc, ca:nt], in_=X[:, dc, ca:nt], func=AF.Identity, scale=cv(l, hvi, dc)),
                            reads=[rX[dc], rC], writes=[rH[dc]])
                    elif hmode == "copy":
                        P.op("act", lambda e, dc=dc: e.activation(out=H[:, dc, ca:nt], in_=X[:, dc, ca:nt], func=AF.Copy),
                             reads=[rX[dc]], writes=[rH[dc]])

            def fm_matmul(b, rhs, rhs_res, nk, evac, fis=range(4), kstride=512, cbl=None, kouter=False):
                cbl = cbl or cbs
                fis = list(fis)
                if kouter:
                    per = max(1, 4 // len(cbl))
                    for s0 in range(0, len(fis), per):
                        sub = fis[s0:s0 + per]
                        bkm = {fi: banks(len(cbl)) for fi in sub}
                        allb = [bk for fi in sub for bk in bkm[fi]]
                        for k in range(nk):
                            fns = []
                            for fi in sub:
                                for ci, (c0, w) in enumerate(cbl):
                                    fns.append(lambda e, bk=bkm[fi][ci], k=k, fi=fi, c0=c0, w=w: e.matmul(
                                        PS[:, bk, 0:w], WB[:, b, k * kstride + fi * 128: k * kstride + (fi + 1) * 128],
                                        rhs[:, k, c0:c0 + w], start=(k == 0), stop=(k == nk - 1)))
                            P.group("pe", fns, reads=[rW[b], rhs_res[k]], writes=[rPS[bk] for bk in allb])
                        for fi in sub:
                            for ci, (c0, w) in enumerate(cbl):
                                evac(fi, c0, w, bkm[fi][ci])
                    return
                for fi in fis:
                    bks = banks(len(cbl))
                    fns = []
                    for k in range(nk):
                        for ci, (c0, w) in enumerate(cbl):
                            fns.append(lambda e, bk=bks[ci], k=k, fi=fi, c0=c0, w=w: e.matmul(
                                PS[:, bk, 0:w], WB[:, b, k * kstride + fi * 128: k * kstride + (fi + 1) * 128],
                                rhs[:, k, c0:c0 + w], start=(k == 0), stop=(k == nk - 1)))
                    P.group("pe", fns, reads=[rW[b]] + rhs_res, writes=[rPS[bk] for bk in bks])
                    for ci, (c0, w) in enumerate(cbl):
                        evac(fi, c0, w, bks[ci])

            P.dma("pool", "pld", lambda e: e.dma_start(out=PB[:, :, 0:nt], in_=pT[l, :, :, t0:t0 + nt]),
                  writes=[rPB])

            squares_of_X(0, nt)
            norm_R(rSQ, cbl=cbs_all)
            for dc in range(8):
                P.op("dve", lambda e, dc=dc: e.scalar_tensor_tensor(
                    out=H[:, dc, 0:nt], in0=X[:, dc, 0:nt], scalar=cv(l, V_PREMIX, dc), in1=RB[:, 0:nt],
                    op0=ALU.mult, op1=ALU.mult), reads=[rX[dc], rRB, rC], writes=[rH[dc]])

            specs = [(Z, rZ, AF.Identity, 0, 16, cbs_all), (U, rU, AF.Gelu_apprx_tanh, 8, 0, cbs),
                     (GA, rGA, AF.Sigmoid, 16, 0, cbs), (GB, rGB, AF.Sigmoid, 24, 0, cbs)]
            for (dst, dres, func, bcol, off, cbl) in specs:
                for half in range(2):
                    b = w_next()

                    def evac(fi, c0, w, bk, dst=dst, dres=dres, func=func, bcol=bcol, off=off, half=half):
                        fc = half * 4 + fi
                        col = l * 32 + bcol + fc
                        P.op("act", lambda e: e.activation(
                            out=dst[:, fc, off + c0: off + c0 + w], in_=PS[:, bk, 0:w], func=func,
                            bias=BIN[:, col:col + 1]), reads=[rPS[bk], rC], writes=[dres[fc]])
                    fm_matmul(b, H, rH, 8, evac, cbl=cbl, kouter=(dst is Z and half == 0))
                    w_done()

                if dst is Z:
                    W_ = 16 + nt
                    if ti == 0:
                        P.op("dve", lambda e: e.memset(Z[:, :, 0:16], 0.0), writes=rZ)
                        P.op("dve", lambda e: e.tensor_scalar(
                            out=Z[:, :, 16:16 + HALO], in0=Z[:, :, 16:16 + HALO], scalar1=PC[:, 0:1], scalar2=None,
                            op0=ALU.mult), reads=rZ + [rC], writes=rZ)
                    else:
                        P.op("dve", lambda e: e.tensor_copy(out=Z[:, :, 0:16], in_=ZH[:, l, :, :]),
                             reads=[rZH[l]], writes=rZ)
                    for g in range(4):
                        zs = Z[:, 2 * g:2 * g + 2, :]
                        zres = [rZ[2 * g], rZ[2 * g + 1]]
                        P.op("dve", lambda e, zs=zs: e.tensor_tensor(
                            out=S1[:, :, 1:W_], in0=zs[:, :, 1:W_], in1=zs[:, :, 0:W_ - 1], op=ALU.add),
                            reads=zres, writes=rS1)
                        cur, rcur = S1, rS1
                        if g >= 1:
                            P.op("dve", lambda e: e.tensor_tensor(
                                out=S2[:, :, 3:W_], in0=S1[:, :, 3:W_], in1=S1[:, :, 1:W_ - 2], op=ALU.add),
                                reads=rS1, writes=[rS2])
                            cur, rcur = S2, [rS2]
                        if g >= 2:
                            P.op("dve", lambda e: e.tensor_tensor(
                                out=S1[:, :, 7:W_], in0=S2[:, :, 7:W_], in1=S2[:, :, 3:W_ - 4], op=ALU.add),
                                reads=[rS2], writes=rS1)
                            cur, rcur = S1, rS1
                        if g >= 3:
                            P.op("dve", lambda e: e.tensor_tensor(
                                out=S2[:, :, 15:W_], in0=S1[:, :, 15:W_], in1=S1[:, :, 7:W_ - 8], op=ALU.add),
                                reads=rS1, writes=[rS2])
                            cur, rcur = S2, [rS2]
                        wdw = POOL_WINDOWS[g]
                        P.op("dve", lambda e, cur=cur, g=g, wdw=wdw: e.scalar_tensor_tensor(
                            out=PL[:, 2 * g:2 * g + 2, 0:nt], in0=cur[:, :, 16:16 + nt], scalar=1.0 / wdw,
                            in1=Z[:, 2 * g:2 * g + 2, 16:16 + nt], op0=ALU.mult, op1=ALU.subtract),
                            reads=rcur + zres, writes=[rPL[2 * g], rPL[2 * g + 1]])
                        if ti == 0:
                            for cc in range(2):
                                P.op("dve", lambda e, cur=cur, g=g, cc=cc: e.tensor_tensor(
                                    out=T16[:, cc, :], in0=cur[:, cc, 16 + HALO:32 + HALO],
                                    in1=PC[:, 1 + g * 16: 1 + (g + 1) * 16], op=ALU.mult),
                                    reads=rcur + [rC], writes=[rT16])
                            P.op("dve", lambda e, g=g: e.tensor_tensor(
                                out=PL[:, 2 * g:2 * g + 2, HALO:HALO + 16], in0=T16[:, :, :],
                                in1=Z[:, 2 * g:2 * g + 2, 16 + HALO:32 + HALO], op=ALU.subtract),
                                reads=[rT16] + zres, writes=[rPL[2 * g], rPL[2 * g + 1]])
                    dump("Z", Z[:, :, :], rZ)
                    dump("PL", PL, rPL)
                    dump("H", H[:, :, :], rH)
                    P.op("dve", lambda e: e.tensor_copy(out=ZH[:, l, :, :], in_=Z[:, :, nt:nt + 16]),
                         reads=rZ, writes=[rZH[l]])
                    if last and prefetch is not None:
                        pt0, pnt = prefetch
                        P.dma("sp", "xld", lambda e: e.dma_start(out=Z[:, :, 0:pnt], in_=xT[:, :, pt0:pt0 + pnt]),
                              writes=rZ)

            vb = [w_next(), w_next()]
            first_v = [True]
            for half in range(2):
                b = vb[half]
                for c in range(clo, nch):
                    bk = banks(1)[0]
                    fns = []
                    for k in range(8):
                        fns.append(lambda e, bk=bk, k=k, c=c, b=b: e.matmul(
                            PS[:, bk, :], H[:, k, c * 128:(c + 1) * 128], WB[:, b, k * 512:(k + 1) * 512],
                            start=(k == 0), stop=False))
                    vsl = slice(half * 512, (half + 1) * 512)
                    pr = slice(32 * l, 32 * l + 1)
                    fns.append(lambda e, bk=bk, vsl=vsl, pr=pr: e.matmul(
                        PS[:, bk, :], ONEB[pr, :], VBH[pr, vsl], start=False, stop=False))
                    fns.append(lambda e, bk=bk, vsl=vsl, pr=pr: e.matmul(
                        PS[:, bk, :], ONEB[pr, :], VBL[pr, vsl], start=False, stop=True))
                    P.group("pe", fns, reads=[rW[b], rC] + rH, writes=[rPS[bk]])
                    o0 = c * 1024 + half * 512
                    P.op("act", lambda e, bk=bk, o0=o0: e.activation(
                        out=VTM[:, o0:o0 + 512], in_=PS[:, bk, :], func=AF.Gelu_apprx_tanh),
                        reads=[rPS[bk]], writes=([rV[c]] + (rO if first_v[0] else [])))
                    first_v[0] = False
                w_done()
            for c in range(clo, nch):
                for half in range(2):
                    o0 = c * 1024 + half * 512
                    P.op("dve", lambda e, o0=o0, half=half: e.bn_stats(out=ST[:, half, :], in_=VTM[:, o0:o0 + 512]),
                         reads=[rV[c]], writes=[rST])
                P.op("dve", lambda e, c=c: e.bn_aggr(out=MVA[:, c, :], in_=ST[:, :, :].rearrange("p a b -> p (a b)")),
                     reads=[rST], writes=[rMV])

            b4 = w_next()

            def m4_groups(idx):
                for gi in idx:
                    g, dh = gi // 2, gi % 2
                    bks = banks(len(cbs))
                    fns = []
                    for cc in range(2):
                        for ci, (c0, w) in enumerate(cbs):
                            o0 = (g * 2 + cc) * 256 + dh * 128
                            fns.append(lambda e, bk=bks[ci], o0=o0, g=g, cc=cc, c0=c0, w=w: e.matmul(
                                PS[:, bk, 0:w], WB[:, b4, o0:o0 + 128], PL[:, 2 * g + cc, c0:c0 + w],
                                start=(cc == 0), stop=(cc == 1)))
                    P.group("pe", fns, reads=[rW[b4], rPL[2 * g], rPL[2 * g + 1]], writes=[rPS[bk] for bk in bks])
                    oc = 2 * g + dh
                    for ci, (c0, w) in enumerate(cbs):
                        if gi % 2 == 0:
                            P.op("dve", lambda e, bk=bks[ci], oc=oc, c0=c0, w=w: e.tensor_scalar(
                                out=MS[:, oc, c0:c0 + w], in0=PS[:, bk, 0:w], scalar1=cv(l, V_PSCALE, oc),
                                scalar2=None, op0=ALU.mult), reads=[rPS[bks[ci]], rC], writes=[rMS[oc]])
                        else:
                            P.op("act", lambda e, bk=bks[ci], oc=oc, c0=c0, w=w: e.activation(
                                out=MS[:, oc, c0:c0 + w], in_=PS[:, bk, 0:w], func=AF.Identity,
                                scale=cv(l, V_PSCALE, oc)), reads=[rPS[bks[ci]], rC], writes=[rMS[oc]])

            m4_groups(range(0, 3))
            P.op("act", lambda e: e.activation(out=RSTD[:, clo:nch], in_=MVA[:, clo:nch, 1], func=AF.Ln,
                                               bias=EPSC[:, 0:1], scale=1.0), reads=[rMV, rC], writes=[rRSTD])
            P.op("act", lambda e: e.activation(out=RSTD[:, clo:nch], in_=RSTD[:, clo:nch], func=AF.Exp, scale=-0.5),
                 reads=[rRSTD], writes=[rRSTD])
            P.op("dve", lambda e: e.scalar_tensor_tensor(
                out=NMR[:, clo:nch], in0=MVA[:, clo:nch, 0], scalar=-1.0, in1=RSTD[:, clo:nch],
                op0=ALU.mult, op1=ALU.mult), reads=[rMV, rRSTD], writes=[rNMR])
            for c in range(clo, nch):
                P.op("act", lambda e, c=c: e.activation(
                    out=NTM[:, c * 1024:(c + 1) * 1024], in_=VTM[:, c * 1024:(c + 1) * 1024], func=AF.Identity,
                    scale=RSTD[:, c:c + 1], bias=NMR[:, c:c + 1]), reads=rO + [rV[c], rRSTD, rNMR], writes=rN)
            m4_groups(range(3, 8))
            w_done()
            dump("VTM", VTM, rO)
            dump("NTM", NTM, rN)

            for h in range(8):
                for ci, (c0, w) in enumerate(cbs):
                    bk = banks(1)[0]
                    nck = w // 128
                    fns = []
                    for cc in range(nck):
                        c = c0 // 128 + cc
                        fns.append(lambda e, bk=bk, cc=cc, c=c, h=h: e.matmul(
                            PS[:, bk, cc * 128:(cc + 1) * 128], NTM[:, c * 1024 + h * 128: c * 1024 + (h + 1) * 128],
                            WSB[:, l * 1024 + h * 128: l * 1024 + (h + 1) * 128], start=True, stop=True))
                    P.group("pe", fns, reads=rN + [rC], writes=[rPS[bk]])
                    tb = (h * len(cbs) + ci) % 2
                    bf = BF[:, l * 1024 + h * 128: l * 1024 + (h + 1) * 128]
                    P.op("dve", lambda e, bk=bk, nck=nck, w=w, tb=tb, bf=bf, h=h: e.scalar_tensor_tensor(
                        out=TMP[:, tb, 0:w].rearrange("p (a t) -> p a t", a=nck),
                        in0=PS[:, bk, 0:w].rearrange("p (a t) -> p a t", a=nck),
                        scalar=cv(l, V_LNG, h),
                        in1=bf.unsqueeze(1).broadcast_to([128, nck, 128]),
                        op0=ALU.mult, op1=ALU.add), reads=[rPS[bk], rC], writes=[rTMP[tb]])
                    P.op("dve", lambda e, tb=tb, h=h, c0=c0, w=w: e.tensor_tensor(
                        out=U[:, h, c0:c0 + w], in0=TMP[:, tb, 0:w], in1=U[:, h, c0:c0 + w], op=ALU.mult),
                        reads=[rTMP[tb], rU[h]], writes=[rU[h]])
            dump("SG", U, rU)
            dump("MS", MS, rMS)

            for half in range(2):
                b = w_next()

                def evac(fi, c0, w, bk, half=half):
                    oc = half * 4 + fi
                    P.op("dve", lambda e: e.tensor_tensor(
                        out=GA[:, oc, c0:c0 + w], in0=PS[:, bk, 0:w], in1=GA[:, oc, c0:c0 + w], op=ALU.mult),
                        reads=[rPS[bk], rGA[oc]], writes=[rGA[oc]])
                fm_matmul(b, MS, rMS, 8, evac)
                w_done()
            dump("M1", GA, rGA)

            for half in range(2):
                b = w_next()

                def evac(fi, c0, w, bk, half=half):
                    oc = half * 4 + fi
                    P.op("dve", lambda e: e.tensor_tensor(
                        out=GB[:, oc, c0:c0 + w], in0=PS[:, bk, 0:w], in1=GB[:, oc, c0:c0 + w], op=ALU.mult),
                        reads=[rPS[bk], rGB[oc]], writes=[rGB[oc]])
                    P.op("dve", lambda e: e.tensor_tensor(
                        out=GB[:, oc, c0:c0 + w], in0=GB[:, oc, c0:c0 + w], in1=GA[:, oc, c0:c0 + w], op=ALU.add),
                        reads=[rGB[oc], rGA[oc]], writes=[rGB[oc]])
                fm_matmul(b, U, rU, 8, evac)
                w_done()

            def evac_O(oc, c0, w, bk, vi):
                P.op("act", lambda e: e.activation(out=SQ[:, oc, c0:c0 + w], in_=PS[:, bk, 0:w], func=AF.Square),
                     reads=[rPS[bk]], writes=[rSQ[oc], rBT[bk]])
                P.op("dve", lambda e: e.tensor_scalar(out=O[:, oc, c0:c0 + w], in0=PS[:, bk, 0:w],
                                                      scalar1=cv(l, vi, oc), scalar2=None, op0=ALU.mult),
                     reads=[rPS[bk], rBT[bk], rC], writes=[rO[oc]])

            for half in range(2):
                b = w_next()
                fm_matmul(b, GB, rGB, 8, lambda fi, c0, w, bk, half=half: evac_O(half * 4 + fi, c0, w, bk, V_POSTMIX))
                w_done()
            dump("MG", GB, rGB)
            dump("O1", O[:, :, :], rO)

            norm_R(rSQ)
            residual(V_POSTMIX, hmode="gain", hvi=V_PREFFN, gained=True)
            dump("X1", X[:, :, :], rX)

            for j in range(8):
                b = w_next()

                def evac(fi, c0, w, bk, j=j):
                    fc = j * 4 + fi
                    tb = fi % 2
                    P.op("act", lambda e: e.activation(out=TMP[:, tb, c0:c0 + w], in_=PS[:, bk, 0:w], func=AF.Relu),
                         reads=[rPS[bk]], writes=[rTMP[tb]])
                    P.op("dve", lambda e: e.tensor_tensor(
                        out=ACTB[:, fc, c0:c0 + w], in0=PS[:, bk, 0:w], in1=TMP[:, tb, c0:c0 + w], op=ALU.mult),
                        reads=[rPS[bk], rTMP[tb]], writes=[rACT[fc]])
                fm_matmul(b, H, rH, 8, evac, kouter=(j == 0))
                w_done()
                if j == 0:
                    squares_of_X(ca, nt)
                    norm_R(rSQ, mode="epsp")

            for j in range(8):
                b = w_next()
                fm_matmul(b, ACTB, rACT, 32, lambda fi, c0, w, bk, j=j: evac_O(j, c0, w, bk, V_POSTFFN),
                          fis=[0], kstride=128)
                w_done()

            norm_R(rSQ, mode="rsqrt_epsp")
            residual(V_POSTFFN, hmode="copy", gained=True)

            for half in range(2):
                b = w_next()

                def evac(fi, c0, w, bk, half=half):
                    oc = half * 4 + fi
                    P.op("act", lambda e: e.activation(
                        out=GF[:, oc, c0:c0 + w], in_=PS[:, bk, 0:w], func=AF.Sigmoid),
                        reads=[rPS[bk]], writes=rG[oc])
                fm_matmul(b, H, rH, 8, evac, kouter=(half == 0))
                w_done()

            bp = w_next()
            for oc in range(8):
                bks = banks(len(cbs))
                fns = []
                for kc in range(2):
                    for ci, (c0, w) in enumerate(cbs):
                        fns.append(lambda e, bk=bks[ci], kc=kc, oc=oc, c0=c0, w=w: e.matmul(
                            PS[:, bk, 0:w], WB[:, bp, kc * 1024 + oc * 128: kc * 1024 + (oc + 1) * 128],
                            PB[:, kc, c0:c0 + w], start=(kc == 0), stop=(kc == 1)))
                P.group("pe", fns, reads=[rW[bp], rPB], writes=[rPS[bk] for bk in bks])
                for ci, (c0, w) in enumerate(cbs):
                    bk = bks[ci]
                    P.op("dve", lambda e, bk=bk, oc=oc, c0=c0, w=w: e.tensor_tensor(
                        out=O[:, oc, c0:c0 + w], in0=PS[:, bk, 0:w], in1=GF[:, oc, c0:c0 + w], op=ALU.mult),
                        reads=[rPS[bk]] + rG[oc], writes=[rO[oc]])
                    P.op("act", lambda e, oc=oc, c0=c0, w=w: e.activation(
                        out=SQ[:, oc, c0:c0 + w], in_=O[:, oc, c0:c0 + w], func=AF.Square),
                        reads=[rO[oc]], writes=[rSQ[oc]])
            w_done()

            norm_R(rSQ)
            residual(V_POSTPLE)

        bufs = [(X, rX), (Z, rZ)]
        last_store = None
        P.dma("sp", "xld", lambda e: e.dma_start(out=X[:, :, 0:TILES[0][1]], in_=xT[:, :, 0:TILES[0][1]]), writes=rX)
        for ti, (t0, nt) in enumerate(TILES):
            (Xc, rXc), (Zc, rZc) = bufs[ti % 2], bufs[(ti + 1) % 2]
            prefetch = TILES[ti + 1] if ti + 1 < len(TILES) else None
            for l in range(L_RUN):
                tile_layer(ti, t0, nt, l, Xc, rXc, Zc, rZc, prefetch)
            s0 = HALO if ti == 0 else 0
            o0 = t0 + s0 - HALO
            n_out = nt - s0
            last_store = P.dma("sp", "ost", lambda e, s0=s0, o0=o0, n_out=n_out, Xc=Xc: e.dma_start(
                out=outT[:, :, o0:o0 + n_out], in_=Xc[:, :, s0:s0 + n_out]), reads=rXc)
        P.final_wait("sp", [last_store] + [d[1] for d in dump_toks])

        with nc.Block() as block:
            @block.tensor
            def _(e):
                for f in P.q["pe"]:
                    f(e)

            @block.scalar
            def _(e):
                for f in P.q["act"]:
                    f(e)

            @block.vector
            def _(e):
                for f in P.q["dve"]:
                    f(e)

            @block.gpsimd
            def _(e):
                for f in P.q["pool"]:
                    f(e)

            @block.sync
            def _(e):
                for f in P.q["sp"]:
                    f(e)
    return nc


def _fm8(v):
    return np.ascontiguousarray(v.reshape(8, 128).T)


def _blk_k512(W, col0):
    K = W.shape[0]
    return W[:, col0:col0 + 512].reshape(K // 128, 128, 512).transpose(1, 0, 2).reshape(128, -1)


def _build_wstream(inp):
    ws = np.zeros((L, NBLK, 128, BLK), np.float32)
    for l in range(L):
        w_in = inp["w_in"][l]
        order = [0, 512, 1024, 1536, 3072, 3584, 4096, 4608, 2048, 2560]
        for j, c0 in enumerate(order):
            ws[l, j] = _blk_k512(w_in, c0)
        ws[l, 10, :, :2048] = inp["pool_w"][l].reshape(4, 2, 128, 256).transpose(2, 0, 1, 3).reshape(128, 2048)
        for i, name in enumerate(("w_pa", "w_pb", "w_o")):
            for half in range(2):
                ws[l, 11 + 2 * i + half] = _blk_k512(inp[name][l], half * 512)
        for j in range(8):
            ws[l, 17 + j] = _blk_k512(inp["w_ff1"][l], j * 512)
        w2 = inp["w_ff2"][l]
        for j in range(8):
            ws[l, 25 + j] = w2[:, j * 128:(j + 1) * 128].reshape(32, 128, 128).transpose(1, 0, 2).reshape(128, BLK)
        for half in range(2):
            ws[l, 33 + half] = _blk_k512(inp["w_ple_gate"][l], half * 512)
        ws[l, 35, :, :2048] = inp["w_ple_proj"][l].reshape(2, 128, 1024).transpose(1, 0, 2).reshape(128, 2048)
    return ws


_NC_CACHE = {}


def make_in_maps(inp):
    x, p = inp["x"], inp["p"]
    B, S, _ = x.shape
    wst = _build_wstream(inp)
    cvec = np.zeros((128, L * 64), np.float32)
    names = ["pre_mix_g", "post_mix_g", "pre_ffn_g", "post_ffn_g", "post_ple_g", "pool_scale", "sgu_ln_g", "sgu_ln_b"]
    for l in range(L):
        for vi, nm in enumerate(names):
            cvec[:, (l * 8 + vi) * 8:(l * 8 + vi + 1) * 8] = _fm8(inp[nm][l])
    binfm = np.zeros((128, L * 32), np.float32)
    bvrow = np.zeros((L, 1024), np.float32)
    for l in range(L):
        b = inp["b_in"][l]
        for gi, c0 in enumerate((0, 1024, 3072, 4096)):
            binfm[:, l * 32 + gi * 8: l * 32 + (gi + 1) * 8] = _fm8(b[c0:c0 + 1024])
        bvrow[l] = b[2048:3072]
    wsT = np.ascontiguousarray(inp["sgu_w_s"].transpose(3, 0, 1, 2)).reshape(128, L * 1024)
    bsbc = np.ascontiguousarray(np.broadcast_to(inp["sgu_b_s"].reshape(1, L * 1024), (128, L * 1024)))
    si = np.arange(128)
    cmask = (si[:, None] <= si[None, :]).astype(np.float32)

    in_maps = []
    for c in range(NCORES):
        b, half = c // 2, c % 2
        s0 = half * OWN
        xt = np.zeros((NTOK, D), np.float32)
        pt = np.zeros((L, NTOK, 256), np.float32)
        xt[HALO:] = x[b, s0:s0 + OWN]
        pt[:, HALO:] = p[:, b, s0:s0 + OWN]
        pcore = np.zeros((128, 80), np.float32)
        if half == 1:
            xt[:HALO] = x[b, s0 - HALO:s0]
            pt[:, :HALO] = p[:, b, s0 - HALO:s0]
            pcore[:, 0] = 1.0
        for g, w in enumerate(POOL_WINDOWS):
            j = np.arange(16)
            cnt = np.minimum(j + 1, w) if half == 0 else np.full(16, w)
            pcore[:, 1 + g * 16: 1 + (g + 1) * 16] = (1.0 / cnt.astype(np.float32))[None, :]
        xTc = np.ascontiguousarray(xt.T.reshape(8, 128, NTOK).transpose(1, 0, 2))
        pTc = np.ascontiguousarray(pt.transpose(0, 2, 1).reshape(L, 2, 128, NTOK).transpose(0, 2, 1, 3))
        in_maps.append({"xT": xTc, "pT": pTc, "wst": wst, "cvec": cvec, "binfm": binfm, "bvrow": bvrow,
                        "wsT": wsT, "bsbc": bsbc, "cmask": cmask, "pcore": pcore})
    return in_maps


def kernel(**inputs):
    inp = {k: np.asarray(v, dtype=np.float32) for k, v in inputs.items()}
    B, S, _ = inp["x"].shape
    in_maps = make_in_maps(inp)
    if "nc" not in _NC_CACHE:
        _NC_CACHE["nc"] = build_nc()
    nc = _NC_CACHE["nc"]
    res = run_bass_kernel_spmd(nc, in_maps, core_ids=list(range(NCORES)))
    out = np.empty((B, S, D), np.float32)
    for c in range(NCORES):
        b, half = c // 2, c % 2
        o = np.asarray(res.results[c]["outT"], dtype=np.float32)
        out[b, half * OWN:(half + 1) * OWN, :] = o.transpose(1, 0, 2).reshape(D, OWN).T
    return out
```

```python
import numpy as np
import concourse.bass as bass
import concourse.mybir as mybir
from concourse.bass_utils import run_bass_kernel_spmd

F32 = mybir.dt.float32
BF16 = mybir.dt.bfloat16
AF = mybir.ActivationFunctionType
ALU = mybir.AluOpType

D = 1024
L = 2
NCORES = 8
OWN = 2048
HALO = 128
NTOK = OWN + HALO
TILES = [(0, 640), (640, 512), (1152, 512), (1664, 512)]
NTMAX = 640
ZW = 16 + NTMAX
EPS = 1e-6
NBLK = 36
NWBUF = 4
BLK = 4096
V_PREMIX, V_POSTMIX, V_PREFFN, V_POSTFFN, V_POSTPLE, V_PSCALE, V_LNG, V_LNB = range(8)
POOL_WINDOWS = (2, 4, 8, 16)
POOL_CHUNKS = ()
RES_ORDER = (0, 1, 2, 3, 4, 5, 6, 7)


class Res:
    __slots__ = ("w", "rs", "const")

    def __init__(self, const=False):
        self.w = None
        self.rs = {}
        self.const = const


class Prog:
    ENG = ("pe", "act", "dve", "pool", "sp")

    def __init__(self):
        self.q = {k: [] for k in self.ENG}
        self.semh = {}
        self.cnt = {}
        self.seen = {k: {} for k in self.ENG}

    def add_sem(self, key, handle):
        self.semh[key] = handle
        self.cnt[key] = 0

    def _wait(self, eng, toks):
        need = {}
        for t in toks:
            if t is None:
                continue
            key, val = t
            if key == "pe" and eng == "pe":
                continue
            if self.seen[eng].get(key, 0) >= val:
                continue
            if need.get(key, 0) < val:
                need[key] = val
        for key, val in need.items():
            self.seen[eng][key] = val
            s = self.semh[key]
            self.q[eng].append(lambda e, s=s, val=val: e.wait_ge(s, val))

    @staticmethod
    def _deps(reads, writes):
        deps = []
        for r in reads:
            deps.append(r.w)
        for w in writes:
            deps.append(w.w)
            deps.extend(w.rs.items())
        return deps

    @staticmethod
    def _commit(tok, reads, writes):
        for r in reads:
            if not r.const:
                if r.rs.get(tok[0], 0) < tok[1]:
                    r.rs[tok[0]] = tok[1]
        for w in writes:
            w.w = tok
            w.rs = {}

    def op(self, eng, fn, reads=(), writes=()):
        self._wait(eng, self._deps(reads, writes))
        self.cnt[eng] += 1
        tok = (eng, self.cnt[eng])
        s = self.semh[eng]
        self.q[eng].append(lambda e, fn=fn, s=s: fn(e).then_inc(s, 1))
        self._commit(tok, reads, writes)
        return tok

    def group(self, eng, fns, reads=(), writes=()):
        self._wait(eng, self._deps(reads, writes))
        self.cnt[eng] += 1
        tok = (eng, self.cnt[eng])
        s = self.semh[eng]
        for f in fns[:-1]:
            self.q[eng].append(f)
        last = fns[-1]
        self.q[eng].append(lambda e, fn=last, s=s: fn(e).then_inc(s, 1))
        self._commit(tok, reads, writes)
        return tok

    def dma(self, eng, semkey, fn, reads=(), writes=()):
        self._wait(eng, self._deps(reads, writes))
        self.cnt[semkey] += 16
        tok = (semkey, self.cnt[semkey])
        s = self.semh[semkey]
        self.q[eng].append(lambda e, fn=fn, s=s: fn(e).then_inc(s, 16))
        self._commit(tok, reads, writes)
        return tok

    def final_wait(self, eng, toks):
        self._wait(eng, toks)


def R(n, const=False):
    return [Res(const) for _ in range(n)]


def build_nc(TILES=TILES, L_RUN=L, NOUT=OWN, DUMPS=()):
    nc = bass.Bass("TRN2", target_bir_lowering=False)
    xT = nc.dram_tensor("xT", [128, 8, NTOK], F32, kind="ExternalInput").ap()
    pT = nc.dram_tensor("pT", [L, 128, 2, NTOK], F32, kind="ExternalInput").ap()
    wst = nc.dram_tensor("wst", [L, NBLK, 128, BLK], F32, kind="ExternalInput").ap()
    cvec_d = nc.dram_tensor("cvec", [128, L * 64], F32, kind="ExternalInput").ap()
    binfm_d = nc.dram_tensor("binfm", [128, L * 32], F32, kind="ExternalInput").ap()
    bvrow_d = nc.dram_tensor("bvrow", [L, 1024], F32, kind="ExternalInput").ap()
    wsT_d = nc.dram_tensor("wsT", [128, L * 1024], F32, kind="ExternalInput").ap()
    bsbc_d = nc.dram_tensor("bsbc", [128, L * 1024], F32, kind="ExternalInput").ap()
    cmask_d = nc.dram_tensor("cmask", [128, 128], F32, kind="ExternalInput").ap()
    pcore_d = nc.dram_tensor("pcore", [128, 80], F32, kind="ExternalInput").ap()
    outT = nc.dram_tensor("outT", [128, 8, NOUT], F32, kind="ExternalOutput").ap()

    dump_d = {}
    for (nm, shp, dt) in DUMPS:
        dump_d[nm] = nc.dram_tensor("dbg_" + nm, shp, dt, kind="ExternalOutput").ap()
    P = Prog()
    from contextlib import ExitStack
    with ExitStack() as es:
        def sb(name, shape, dt):
            return es.enter_context(nc.sbuf_tensor(name, shape, dt))

        X = sb("X", [128, 8, ZW], F32)
        H = sb("H", [128, 8, NTMAX], BF16)
        O = sb("O", [128, 8, NTMAX], F32)
        Z = sb("Z", [128, 8, ZW], F32)
        S2 = sb("S2", [128, 2, ZW], F32)
        BB = sb("BB", [128, 6, 8 * NTMAX], BF16)
        WB = sb("WB", [128, NWBUF, BLK], BF16)
        PB = sb("PB", [128, 2, NTMAX], BF16)
        RB = sb("RB", [128, NTMAX], F32)
        TMP = sb("TMP", [128, 2, ZW], F32)
        S1 = TMP
        ZH = sb("ZH", [128, L, 8, 16], F32)
        T16 = sb("T16", [128, 2, 16], F32)
        ST = sb("ST", [128, 2, 6], F32)
        MVA = sb("MVA", [128, 8, 2], F32)
        RSTD = sb("RSTD", [128, 8], F32)
        EPSC = sb("EPSC", [128, 1], F32)
        NMR = sb("NMR", [128, 8], F32)
        EPSP = sb("EPSP", [128, NTMAX], F32)
        CV = sb("CV", [128, L * 64], F32)
        BIN = sb("BIN", [128, L * 32], F32)
        VBH = sb("VBH", [33, 1024], BF16)
        VBL = sb("VBL", [33, 1024], BF16)
        WSB = sb("WSB", [128, L * 1024], BF16)
        BF = sb("BF", [128, L * 1024], F32)
        CM = sb("CM", [128, 128], F32)
        PC = sb("PC", [128, 80], F32)
        ONEB = sb("ONEB", [128, 128], BF16)
        ONEF = sb("ONEF", [128, 128], F32)
        PS = es.enter_context(nc.psum_tensor("PS", [128, 8, 512], F32))
        WSF = Z[:, :, :].rearrange("p c t -> p (c t)")[:, 0:L * 1024]
        BSB = BB[:, 5, 0:2 * L * 1024].bitcast(F32)

        for key in ("pe", "act", "dve", "pool", "sp", "cst", "xld", "ost", "pld", "dbg") + tuple(
                "w%d" % i for i in range(NWBUF)):
            P.add_sem(key, es.enter_context(nc.semaphore("s_" + key)))

        rX, rH, rO, rZ = R(8), R(8), R(8), R(8)
        rB = [R(8) for _ in range(6)]
        rS2 = Res()
        rW = R(NWBUF)
        rPB, rRB = Res(), Res()
        rTMP = R(2)
        rS1 = rTMP
        rZH = R(L)
        rT16, rST, rMV, rRSTD, rNMR, rEPSP = Res(), Res(), Res(), Res(), Res(), Res()
        rBT = R(8)
        rV = R(5)
        rPS = R(8)
        rC = Res()
        rWSF = None

        def Bv(i):
            return BB[:, i, :].rearrange("p (c t) -> p c t", c=8)

        U, GA, GB, NB_, PL, MS = (Bv(i) for i in range(6))
        SQ = PL
        rU, rGA, rGB, rN, rPL, rMS = rB
        rSQ = rPL
        NTM = BB[:, 3, :]
        ACTB = BB[:, 0:4, :].rearrange("p a (c t) -> p (a c) t", c=8)
        rACT = rB[0] + rB[1] + rB[2] + rB[3]
        VTM = O[:, :, :].rearrange("p c t -> p (c t)")

        psn = [0]

        def banks(n):
            b = psn[0]
            if b + n > 8:
                b = 0
            psn[0] = (b + n) % 8
            return list(range(b, b + n))

        stream = []
        for (t0, nt) in TILES:
            for l in range(L_RUN):
                for j in range(NBLK):
                    stream.append((l, j))
        wstate = {"issued": 0, "cons": 0}

        def blk_len(j):
            return 2048 if j in (10, 35) else BLK

        def w_issue():
            i = wstate["issued"]
            if i >= len(stream):
                return
            l, j = stream[i]
            b = i % NWBUF
            n = blk_len(j)
            P.dma("pool", "w%d" % b,
                  lambda e, b=b, l=l, j=j, n=n: e.dma_start(out=WB[:, b, 0:n], in_=wst[l, j, :, 0:n]),
                  writes=[rW[b]])
            wstate["issued"] += 1

        def w_next():
            i = wstate["cons"]
            wstate["cons"] += 1
            return i % NWBUF

        def w_done():
            w_issue()

        cst = []
        for (dst, src) in ((CV[:], cvec_d), (BIN[:], binfm_d), (WSF, wsT_d), (BSB, bsbc_d),
                           (CM[:], cmask_d), (PC[:], pcore_d)):
            nd = len(dst.shape)
            P.dma("sp", "cst", lambda e, dst=dst, src=src: e.dma_start(out=dst, in_=src), writes=[rC])
        for _ in range(NWBUF):
            w_issue()

        BVR = VTM[0:33, 0:1024]
        BVT = VTM[0:33, 1024:2048]
        P.op("dve", lambda e: e.memset(VTM[0:33, 0:2048], 0.0), writes=rO)
        for l in range(L):
            P.dma("sp", "cst", lambda e, l=l: e.dma_start(out=VTM[32 * l:32 * l + 1, 0:1024], in_=bvrow_d[l:l + 1, :]),
                  writes=[rC] + rO)
        P.op("dve", lambda e: e.memset(ONEB[:], 1.0), writes=[rC])
        P.op("dve", lambda e: e.memset(ONEF[:], 1.0), writes=[rC])
        P.op("dve", lambda e: e.memset(EPSC[:], EPS), writes=[rC])
        P.op("dve", lambda e: e.tensor_copy(out=VBH[:], in_=BVR), reads=[rC], writes=[rC])
        P.op("dve", lambda e: e.tensor_copy(out=BVT, in_=VBH[:]), reads=[rC], writes=[rC])
        P.op("dve", lambda e: e.tensor_tensor(out=BVT, in0=BVR, in1=BVT, op=ALU.subtract),
             reads=[rC], writes=[rC] + rO)
        P.op("dve", lambda e: e.tensor_copy(out=VBL[:], in_=BVT), reads=[rC], writes=[rC] + rO)
        for l in range(L):
            for h in range(8):
                sl = slice(l * 1024 + h * 128, l * 1024 + (h + 1) * 128)
                P.op("dve", lambda e, sl=sl: e.tensor_tensor(out=WSF[:, sl], in0=WSF[:, sl], in1=CM[:], op=ALU.mult),
                     reads=[rC], writes=rZ)
            P.op("dve", lambda e, l=l: e.tensor_copy(out=WSB[:, l * 1024:(l + 1) * 1024],
                                                     in_=WSF[:, l * 1024:(l + 1) * 1024]),
                 reads=rZ, writes=[rC])
            for hh in range(2):
                bk = banks(1)[0]
                P.group("pe", [lambda e, bk=bk, l=l, hh=hh: e.matmul(
                    PS[:, bk, :], ONEF[:], WSF[:, l * 1024 + hh * 512: l * 1024 + (hh + 1) * 512],
                    start=True, stop=True)], reads=[rC] + rZ, writes=[rPS[bk]])
                for h4 in range(4):
                    h = hh * 4 + h4
                    sl = slice(l * 1024 + h * 128, l * 1024 + (h + 1) * 128)
                    col = (l * 8 + V_LNB) * 8 + h
                    P.op("dve", lambda e, bk=bk, h4=h4, sl=sl, col=col: e.scalar_tensor_tensor(
                        out=BF[:, sl], in0=PS[:, bk, h4 * 128:(h4 + 1) * 128], scalar=CV[:, col:col + 1],
                        in1=BSB[:, sl], op0=ALU.mult, op1=ALU.add),
                        reads=[rPS[bk], rC] + rB[5], writes=[rC])
        rC_done = rC.w
        rC.const = True

        def cv(l, vi, c):
            col = (l * 8 + vi) * 8 + c
            return CV[:, col:col + 1]

        dump_toks = []

        def dump(nm, ap, res):
            if nm in dump_d and nm not in [d[0] for d in dump_toks]:
                dump_toks.append((nm, P.dma("sp", "dbg", lambda e: e.dma_start(out=dump_d[nm], in_=ap), reads=res)))

        def colblocks(ca, nt):
            cbs = []
            c = ca
            while c < nt:
                w = min(512, nt - c)
                cbs.append((c, w))
                c += w
            return cbs

        GF = BB[:, 0:2, :].rearrange("p a t -> p (a t)").bitcast(F32).rearrange("p (c t) -> p c t", c=8)
        rG = [[rB[oc // 4][2 * (oc % 4)], rB[oc // 4][2 * (oc % 4) + 1]] for oc in range(8)]

        def tile_layer(ti, t0, nt, l, X, rX, Z, rZ, prefetch):
            last = (l == L_RUN - 1)
            ca = HALO if (ti == 0 and last) else 0
            cbs = colblocks(ca, nt)
            cbs_all = colblocks(0, nt)
            nch = nt // 128
            clo = ca // 128

            def norm_R(sq_res, mode="rsqrt", cbl=None):
                for (c0, w) in (cbl or cbs):
                    bk = banks(1)[0]
                    for dc in range(8):
                        P.group("pe", [lambda e, bk=bk, dc=dc, c0=c0, w=w: e.matmul(
                            PS[:, bk, 0:w], ONEB[:], SQ[:, dc, c0:c0 + w], start=(dc == 0), stop=(dc == 7))],
                            reads=[rC, sq_res[dc]], writes=[rPS[bk]])
                    if mode == "epsp":
                        P.op("dve", lambda e, bk=bk, c0=c0, w=w: e.tensor_scalar(
                            out=EPSP[:, c0:c0 + w], in0=PS[:, bk, 0:w], scalar1=1.0 / D, scalar2=EPS,
                            op0=ALU.mult, op1=ALU.add), reads=[rPS[bk]], writes=[rEPSP])
                        P.op("dve", lambda e, c0=c0, w=w: e.scalar_tensor_tensor(
                            out=EPSP[:, c0:c0 + w], in0=EPSP[:, c0:c0 + w], scalar=EPS, in1=EPSP[:, c0:c0 + w],
                            op0=ALU.mult, op1=ALU.mult), reads=[rEPSP], writes=[rEPSP])
                        continue
                    if mode == "rsqrt_epsp":
                        P.op("dve", lambda e, bk=bk, c0=c0, w=w: e.scalar_tensor_tensor(
                            out=RB[:, c0:c0 + w], in0=PS[:, bk, 0:w], scalar=1.0 / D, in1=EPSP[:, c0:c0 + w],
                            op0=ALU.mult, op1=ALU.add), reads=[rPS[bk], rEPSP], writes=[rRB])
                        P.op("act", lambda e, c0=c0, w=w: e.activation(
                            out=RB[:, c0:c0 + w], in_=RB[:, c0:c0 + w], func=AF.Ln), reads=[rRB], writes=[rRB])
                    else:
                        P.op("act", lambda e, bk=bk, c0=c0, w=w: e.activation(
                            out=RB[:, c0:c0 + w], in_=PS[:, bk, 0:w], func=AF.Ln, bias=EPSC[:, 0:1], scale=1.0 / D),
                            reads=[rPS[bk], rC], writes=[rRB])
                    P.op("act", lambda e, c0=c0, w=w: e.activation(
                        out=RB[:, c0:c0 + w], in_=RB[:, c0:c0 + w], func=AF.Exp, scale=-0.5),
                        reads=[rRB], writes=[rRB])

            def squares_of_X(a, b_):
                for dc in range(8):
                    P.op("act", lambda e, dc=dc: e.activation(out=SQ[:, dc, a:b_], in_=X[:, dc, a:b_], func=AF.Square),
                         reads=[rX[dc]], writes=[rSQ[dc]])

            def residual(vi, hmode=None, hvi=None, gained=False):
                for dc in range(8):
                    P.op("dve", lambda e, dc=dc: e.tensor_tensor(
                        out=O[:, dc, ca:nt], in0=O[:, dc, ca:nt], in1=RB[:, ca:nt], op=ALU.mult),
                        reads=[rO[dc], rRB], writes=[rO[dc]])
                    if gained:
                        P.op("dve", lambda e, dc=dc: e.tensor_tensor(
                            out=X[:, dc, ca:nt], in0=X[:, dc, ca:nt], in1=O[:, dc, ca:nt], op=ALU.add),
                            reads=[rO[dc], rX[dc]], writes=[rX[dc]])
                    else:
                        P.op("dve", lambda e, dc=dc: e.scalar_tensor_tensor(
                            out=X[:, dc, ca:nt], in0=O[:, dc, ca:nt], scalar=cv(l, vi, dc), in1=X[:, dc, ca:nt],
                            op0=ALU.mult, op1=ALU.add), reads=[rO[dc], rX[dc], rC], writes=[rX[dc]])
                    if hmode == "gain":
                        P.op("act", lambda e, dc=dc: e.activation(
                            out=H[:, dc, ca:nt], in_=X[:, dc, ca:nt], func=AF.Identity, scale=cv(l, hvi, dc)),
                            reads=[rX[dc], rC], writes=[rH[dc]])
                    elif hmode == "copy":
                        P.op("act", lambda e, dc=dc: e.activation(out=H[:, dc, ca:nt], in_=X[:, dc, ca:nt], func=AF.Copy),
                             reads=[rX[dc]], writes=[rH[dc]])

            def fm_matmul(b, rhs, rhs_res, nk, evac, fis=range(4), kstride=512, cbl=None, kouter=False):
                cbl = cbl or cbs
                fis = list(fis)
                if kouter:
                    per = max(1, 4 // len(cbl))
                    for s0 in range(0, len(fis), per):
                        sub = fis[s0:s0 + per]
                        bkm = {fi: banks(len(cbl)) for fi in sub}
                        allb = [bk for fi in sub for bk in bkm[fi]]
                        for k in range(nk):
                            fns = []
                            for fi in sub:
                                for ci, (c0, w) in enumerate(cbl):
                                    fns.append(lambda e, bk=bkm[fi][ci], k=k, fi=fi, c0=c0, w=w: e.matmul(
                                        PS[:, bk, 0:w], WB[:, b, k * kstride + fi * 128: k * kstride + (fi + 1) * 128],
                                        rhs[:, k, c0:c0 + w], start=(k == 0), stop=(k == nk - 1)))
                            P.group("pe", fns, reads=[rW[b], rhs_res[k]], writes=[rPS[bk] for bk in allb])
                        for fi in sub:
                            for ci, (c0, w) in enumerate(cbl):
                                evac(fi, c0, w, bkm[fi][ci])
                    return
                for fi in fis:
                    bks = banks(len(cbl))
                    fns = []
                    for k in range(nk):
                        for ci, (c0, w) in enumerate(cbl):
                            fns.append(lambda e, bk=bks[ci], k=k, fi=fi, c0=c0, w=w: e.matmul(
                                PS[:, bk, 0:w], WB[:, b, k * kstride + fi * 128: k * kstride + (fi + 1) * 128],
                                rhs[:, k, c0:c0 + w], start=(k == 0), stop=(k == nk - 1)))
                    P.group("pe", fns, reads=[rW[b]] + rhs_res, writes=[rPS[bk] for bk in bks])
                    for ci, (c0, w) in enumerate(cbl):
                        evac(fi, c0, w, bks[ci])

            P.dma("pool", "pld", lambda e: e.dma_start(out=PB[:, :, 0:nt], in_=pT[l, :, :, t0:t0 + nt]),
                  writes=[rPB])

            squares_of_X(0, nt)
            norm_R(rSQ, cbl=cbs_all)
            for dc in range(8):
                P.op("dve", lambda e, dc=dc: e.scalar_tensor_tensor(
                    out=H[:, dc, 0:nt], in0=X[:, dc, 0:nt], scalar=cv(l, V_PREMIX, dc), in1=RB[:, 0:nt],
                    op0=ALU.mult, op1=ALU.mult), reads=[rX[dc], rRB, rC], writes=[rH[dc]])

            def w_in_group(dst, dres, func, bcol, off, cbl, first):
                for half in range(2):
                    b = w_next()

                    def evac(fi, c0, w, bk, half=half):
                        fc = half * 4 + fi
                        col = l * 32 + bcol + fc
                        P.op("act", lambda e: e.activation(
                            out=dst[:, fc, off + c0: off + c0 + w], in_=PS[:, bk, 0:w], func=func,
                            bias=BIN[:, col:col + 1]), reads=[rPS[bk], rC], writes=[dres[fc]])
                    fm_matmul(b, H, rH, 8, evac, cbl=cbl, kouter=(first and half == 0))
                    w_done()

            w_in_group(Z, rZ, AF.Identity, 0, 16, cbs_all, True)
            W_ = 16 + nt

            def pool_prep():
                if ti == 0:
                    P.op("dve", lambda e: e.memset(Z[:, :, 0:16], 0.0), writes=rZ)
                    P.op("dve", lambda e: e.tensor_scalar(
                        out=Z[:, :, 16:16 + HALO], in0=Z[:, :, 16:16 + HALO], scalar1=PC[:, 0:1], scalar2=None,
                        op0=ALU.mult), reads=rZ + [rC], writes=rZ)
                else:
                    P.op("dve", lambda e: e.tensor_copy(out=Z[:, :, 0:16], in_=ZH[:, l, :, :]),
                         reads=[rZH[l]], writes=rZ)

            def pool_groups(gs):
                for g in gs:
                    zs = Z[:, 2 * g:2 * g + 2, :]
                    zres = [rZ[2 * g], rZ[2 * g + 1]]
                    P.op("dve", lambda e, zs=zs: e.tensor_tensor(
                        out=S1[:, :, 1:W_], in0=zs[:, :, 1:W_], in1=zs[:, :, 0:W_ - 1], op=ALU.add),
                        reads=zres, writes=rS1)
                    cur, rcur = S1, rS1
                    if g >= 1:
                        P.op("dve", lambda e: e.tensor_tensor(
                            out=S2[:, :, 3:W_], in0=S1[:, :, 3:W_], in1=S1[:, :, 1:W_ - 2], op=ALU.add),
                            reads=rS1, writes=[rS2])
                        cur, rcur = S2, [rS2]
                    if g >= 2:
                        P.op("dve", lambda e: e.tensor_tensor(
                            out=S1[:, :, 7:W_], in0=S2[:, :, 7:W_], in1=S2[:, :, 3:W_ - 4], op=ALU.add),
                            reads=[rS2], writes=rS1)
                        cur, rcur = S1, rS1
                    if g >= 3:
                        P.op("dve", lambda e: e.tensor_tensor(
                            out=S2[:, :, 15:W_], in0=S1[:, :, 15:W_], in1=S1[:, :, 7:W_ - 8], op=ALU.add),
                            reads=rS1, writes=[rS2])
                        cur, rcur = S2, [rS2]
                    wdw = POOL_WINDOWS[g]
                    P.op("dve", lambda e, cur=cur, g=g, wdw=wdw: e.scalar_tensor_tensor(
                        out=PL[:, 2 * g:2 * g + 2, 0:nt], in0=cur[:, :, 16:16 + nt], scalar=1.0 / wdw,
                        in1=Z[:, 2 * g:2 * g + 2, 16:16 + nt], op0=ALU.mult, op1=ALU.subtract),
                        reads=rcur + zres, writes=[rPL[2 * g], rPL[2 * g + 1]])
                    if ti == 0:
                        for cc in range(2):
                            P.op("dve", lambda e, cur=cur, g=g, cc=cc: e.tensor_tensor(
                                out=T16[:, cc, :], in0=cur[:, cc, 16 + HALO:32 + HALO],
                                in1=PC[:, 1 + g * 16: 1 + (g + 1) * 16], op=ALU.mult),
                                reads=rcur + [rC], writes=[rT16])
                        P.op("dve", lambda e, g=g: e.tensor_tensor(
                            out=PL[:, 2 * g:2 * g + 2, HALO:HALO + 16], in0=T16[:, :, :],
                            in1=Z[:, 2 * g:2 * g + 2, 16 + HALO:32 + HALO], op=ALU.subtract),
                            reads=[rT16] + zres, writes=[rPL[2 * g], rPL[2 * g + 1]])

            def pool_finish():
                dump("Z", Z[:, :, :], rZ)
                dump("PL", PL, rPL)
                dump("H", H[:, :, :], rH)
                P.op("dve", lambda e: e.tensor_copy(out=ZH[:, l, :, :], in_=Z[:, :, nt:nt + 16]),
                     reads=rZ, writes=[rZH[l]])
                if last and prefetch is not None:
                    pt0, pnt = prefetch
                    P.dma("sp", "xld", lambda e: e.dma_start(out=Z[:, :, 0:pnt], in_=xT[:, :, pt0:pt0 + pnt]),
                          writes=rZ)


            pool_prep()
            pool_groups((0, 1))

            vb = [w_next(), w_next()]
            first_v = [True]
            for half in range(2):
                b = vb[half]
                for c in range(clo, nch):
                    bk = banks(1)[0]
                    fns = []
                    for k in range(8):
                        fns.append(lambda e, bk=bk, k=k, c=c, b=b: e.matmul(
                            PS[:, bk, :], H[:, k, c * 128:(c + 1) * 128], WB[:, b, k * 512:(k + 1) * 512],
                            start=(k == 0), stop=False))
                    vsl = slice(half * 512, (half + 1) * 512)
                    pr = slice(32 * l, 32 * l + 1)
                    fns.append(lambda e, bk=bk, vsl=vsl, pr=pr: e.matmul(
                        PS[:, bk, :], ONEB[pr, :], VBH[pr, vsl], start=False, stop=False))
                    fns.append(lambda e, bk=bk, vsl=vsl, pr=pr: e.matmul(
                        PS[:, bk, :], ONEB[pr, :], VBL[pr, vsl], start=False, stop=True))
                    P.group("pe", fns, reads=[rW[b], rC] + rH, writes=[rPS[bk]])
                    o0 = c * 1024 + half * 512
                    P.op("act", lambda e, bk=bk, o0=o0: e.activation(
                        out=VTM[:, o0:o0 + 512], in_=PS[:, bk, :], func=AF.Gelu_apprx_tanh),
                        reads=[rPS[bk]], writes=([rV[c]] + (rO if first_v[0] else [])))
                    first_v[0] = False
                w_done()
            for c in range(clo, nch):
                for half in range(2):
                    o0 = c * 1024 + half * 512
                    P.op("dve", lambda e, o0=o0, half=half: e.bn_stats(out=ST[:, half, :], in_=VTM[:, o0:o0 + 512]),
                         reads=[rV[c]], writes=[rST])
                P.op("dve", lambda e, c=c: e.bn_aggr(out=MVA[:, c, :], in_=ST[:, :, :].rearrange("p a b -> p (a b)")),
                     reads=[rST], writes=[rMV])

            P.op("act", lambda e: e.activation(out=RSTD[:, clo:nch], in_=MVA[:, clo:nch, 1], func=AF.Ln,
                                               bias=EPSC[:, 0:1], scale=1.0), reads=[rMV, rC], writes=[rRSTD])
            P.op("act", lambda e: e.activation(out=RSTD[:, clo:nch], in_=RSTD[:, clo:nch], func=AF.Exp, scale=-0.5),
                 reads=[rRSTD], writes=[rRSTD])
            P.op("dve", lambda e: e.scalar_tensor_tensor(
                out=NMR[:, clo:nch], in0=MVA[:, clo:nch, 0], scalar=-1.0, in1=RSTD[:, clo:nch],
                op0=ALU.mult, op1=ALU.mult), reads=[rMV, rRSTD], writes=[rNMR])
            for c in range(clo, nch):
                P.op("act", lambda e, c=c: e.activation(
                    out=NTM[:, c * 1024:(c + 1) * 1024], in_=VTM[:, c * 1024:(c + 1) * 1024], func=AF.Identity,
                    scale=RSTD[:, c:c + 1], bias=NMR[:, c:c + 1]), reads=rO + [rV[c], rRSTD, rNMR], writes=rN)
            w_in_group(U, rU, AF.Gelu_apprx_tanh, 8, 0, cbs, False)

            for h in range(8):
                for ci, (c0, w) in enumerate(cbs):
                    bk = banks(1)[0]
                    nck = w // 128
                    fns = []
                    for cc in range(nck):
                        c = c0 // 128 + cc
                        fns.append(lambda e, bk=bk, cc=cc, c=c, h=h: e.matmul(
                            PS[:, bk, cc * 128:(cc + 1) * 128], NTM[:, c * 1024 + h * 128: c * 1024 + (h + 1) * 128],
                            WSB[:, l * 1024 + h * 128: l * 1024 + (h + 1) * 128], start=True, stop=True))
                    P.group("pe", fns, reads=rN + [rC], writes=[rPS[bk]])
                    tb = (h * len(cbs) + ci) % 2
                    bf = BF[:, l * 1024 + h * 128: l * 1024 + (h + 1) * 128]
                    P.op("dve", lambda e, bk=bk, nck=nck, w=w, tb=tb, bf=bf, h=h: e.scalar_tensor_tensor(
                        out=TMP[:, tb, 0:w].rearrange("p (a t) -> p a t", a=nck),
                        in0=PS[:, bk, 0:w].rearrange("p (a t) -> p a t", a=nck),
                        scalar=cv(l, V_LNG, h),
                        in1=bf.unsqueeze(1).broadcast_to([128, nck, 128]),
                        op0=ALU.mult, op1=ALU.add), reads=[rPS[bk], rC], writes=[rTMP[tb]])
                    P.op("dve", lambda e, tb=tb, h=h, c0=c0, w=w: e.tensor_tensor(
                        out=U[:, h, c0:c0 + w], in0=TMP[:, tb, 0:w], in1=U[:, h, c0:c0 + w], op=ALU.mult),
                        reads=[rTMP[tb], rU[h]], writes=[rU[h]])
            dump("SG", U, rU)
            dump("MS", MS, rMS)

            pool_groups((2, 3))
            pool_finish()
            w_in_group(GA, rGA, AF.Sigmoid, 16, 0, cbs, False)
            w_in_group(GB, rGB, AF.Sigmoid, 24, 0, cbs, False)

            b4 = w_next()

            def m4_groups(idx):
                for gi in idx:
                    g, dh = gi // 2, gi % 2
                    bks = banks(len(cbs))
                    fns = []
                    for cc in range(2):
                        for ci, (c0, w) in enumerate(cbs):
                            o0 = (g * 2 + cc) * 256 + dh * 128
                            fns.append(lambda e, bk=bks[ci], o0=o0, g=g, cc=cc, c0=c0, w=w: e.matmul(
                                PS[:, bk, 0:w], WB[:, b4, o0:o0 + 128], PL[:, 2 * g + cc, c0:c0 + w],
                                start=(cc == 0), stop=(cc == 1)))
                    P.group("pe", fns, reads=[rW[b4], rPL[2 * g], rPL[2 * g + 1]], writes=[rPS[bk] for bk in bks])
                    oc = 2 * g + dh
                    for ci, (c0, w) in enumerate(cbs):
                        if gi % 2 == 0:
                            P.op("dve", lambda e, bk=bks[ci], oc=oc, c0=c0, w=w: e.tensor_scalar(
                                out=MS[:, oc, c0:c0 + w], in0=PS[:, bk, 0:w], scalar1=cv(l, V_PSCALE, oc),
                                scalar2=None, op0=ALU.mult), reads=[rPS[bks[ci]], rC], writes=[rMS[oc]])
                        else:
                            P.op("act", lambda e, bk=bks[ci], oc=oc, c0=c0, w=w: e.activation(
                                out=MS[:, oc, c0:c0 + w], in_=PS[:, bk, 0:w], func=AF.Identity,
                                scale=cv(l, V_PSCALE, oc)), reads=[rPS[bks[ci]], rC], writes=[rMS[oc]])

            m4_groups(range(0, 8))
            w_done()
            dump("VTM", VTM, rO)
            dump("NTM", NTM, rN)

            for half in range(2):
                b = w_next()

                def evac(fi, c0, w, bk, half=half):
                    oc = half * 4 + fi
                    P.op("dve", lambda e: e.tensor_tensor(
                        out=GA[:, oc, c0:c0 + w], in0=PS[:, bk, 0:w], in1=GA[:, oc, c0:c0 + w], op=ALU.mult),
                        reads=[rPS[bk], rGA[oc]], writes=[rGA[oc]])
                fm_matmul(b, MS, rMS, 8, evac)
                w_done()
            dump("M1", GA, rGA)

            for half in range(2):
                b = w_next()

                def evac(fi, c0, w, bk, half=half):
                    oc = half * 4 + fi
                    P.op("dve", lambda e: e.tensor_tensor(
                        out=GB[:, oc, c0:c0 + w], in0=PS[:, bk, 0:w], in1=GB[:, oc, c0:c0 + w], op=ALU.mult),
                        reads=[rPS[bk], rGB[oc]], writes=[rGB[oc]])
                    P.op("dve", lambda e: e.tensor_tensor(
                        out=GB[:, oc, c0:c0 + w], in0=GB[:, oc, c0:c0 + w], in1=GA[:, oc, c0:c0 + w], op=ALU.add),
                        reads=[rGB[oc], rGA[oc]], writes=[rGB[oc]])
                fm_matmul(b, U, rU, 8, evac)
                w_done()

            def evac_O(oc, c0, w, bk, vi):
                P.op("act", lambda e: e.activation(out=SQ[:, oc, c0:c0 + w], in_=PS[:, bk, 0:w], func=AF.Square),
                     reads=[rPS[bk]], writes=[rSQ[oc], rBT[bk]])
                P.op("dve", lambda e: e.tensor_scalar(out=O[:, oc, c0:c0 + w], in0=PS[:, bk, 0:w],
                                                      scalar1=cv(l, vi, oc), scalar2=None, op0=ALU.mult),
                     reads=[rPS[bk], rBT[bk], rC], writes=[rO[oc]])

            for half in range(2):
                b = w_next()
                fm_matmul(b, GB, rGB, 8, lambda fi, c0, w, bk, half=half: evac_O(half * 4 + fi, c0, w, bk, V_POSTMIX))
                w_done()
            dump("MG", GB, rGB)
            dump("O1", O[:, :, :], rO)

            norm_R(rSQ)
            residual(V_POSTMIX, hmode="gain", hvi=V_PREFFN, gained=True)
            dump("X1", X[:, :, :], rX)

            for j in range(8):
                b = w_next()

                def evac(fi, c0, w, bk, j=j):
                    fc = j * 4 + fi
                    tb = fi % 2
                    P.op("act", lambda e: e.activation(out=TMP[:, tb, c0:c0 + w], in_=PS[:, bk, 0:w], func=AF.Relu),
                         reads=[rPS[bk]], writes=[rTMP[tb]])
                    P.op("dve", lambda e: e.tensor_tensor(
                        out=ACTB[:, fc, c0:c0 + w], in0=PS[:, bk, 0:w], in1=TMP[:, tb, c0:c0 + w], op=ALU.mult),
                        reads=[rPS[bk], rTMP[tb]], writes=[rACT[fc]])
                fm_matmul(b, H, rH, 8, evac, kouter=(j == 0))
                w_done()
                if j == 0:
                    squares_of_X(ca, nt)
                    norm_R(rSQ, mode="epsp")

            for j in range(8):
                b = w_next()
                fm_matmul(b, ACTB, rACT, 32, lambda fi, c0, w, bk, j=j: evac_O(j, c0, w, bk, V_POSTFFN),
                          fis=[0], kstride=128)
                w_done()

            norm_R(rSQ, mode="rsqrt_epsp")
            residual(V_POSTFFN, hmode="copy", gained=True)

            for half in range(2):
                b = w_next()

                def evac(fi, c0, w, bk, half=half):
                    oc = half * 4 + fi
                    P.op("act", lambda e: e.activation(
                        out=GF[:, oc, c0:c0 + w], in_=PS[:, bk, 0:w], func=AF.Sigmoid),
                        reads=[rPS[bk]], writes=rG[oc])
                fm_matmul(b, H, rH, 8, evac, kouter=(half == 0))
                w_done()

            bp = w_next()
            for oc in range(8):
                bks = banks(len(cbs))
                fns = []
                for kc in range(2):
                    for ci, (c0, w) in enumerate(cbs):
                        fns.append(lambda e, bk=bks[ci], kc=kc, oc=oc, c0=c0, w=w: e.matmul(
                            PS[:, bk, 0:w], WB[:, bp, kc * 1024 + oc * 128: kc * 1024 + (oc + 1) * 128],
                            PB[:, kc, c0:c0 + w], start=(kc == 0), stop=(kc == 1)))
                P.group("pe", fns, reads=[rW[bp], rPB], writes=[rPS[bk] for bk in bks])
                for ci, (c0, w) in enumerate(cbs):
                    bk = bks[ci]
                    P.op("dve", lambda e, bk=bk, oc=oc, c0=c0, w=w: e.tensor_tensor(
                        out=O[:, oc, c0:c0 + w], in0=PS[:, bk, 0:w], in1=GF[:, oc, c0:c0 + w], op=ALU.mult),
                        reads=[rPS[bk]] + rG[oc], writes=[rO[oc]])
                    P.op("act", lambda e, oc=oc, c0=c0, w=w: e.activation(
                        out=SQ[:, oc, c0:c0 + w], in_=O[:, oc, c0:c0 + w], func=AF.Square),
                        reads=[rO[oc]], writes=[rSQ[oc]])
            w_done()

            norm_R(rSQ)
            residual(V_POSTPLE)

        bufs = [(X, rX), (Z, rZ)]
        last_store = None
        P.dma("sp", "xld", lambda e: e.dma_start(out=X[:, :, 0:TILES[0][1]], in_=xT[:, :, 0:TILES[0][1]]), writes=rX)
        for ti, (t0, nt) in enumerate(TILES):
            (Xc, rXc), (Zc, rZc) = bufs[ti % 2], bufs[(ti + 1) % 2]
            prefetch = TILES[ti + 1] if ti + 1 < len(TILES) else None
            for l in range(L_RUN):
                tile_layer(ti, t0, nt, l, Xc, rXc, Zc, rZc, prefetch)
            s0 = HALO if ti == 0 else 0
            o0 = t0 + s0 - HALO
            n_out = nt - s0
            last_store = P.dma("sp", "ost", lambda e, s0=s0, o0=o0, n_out=n_out, Xc=Xc: e.dma_start(
                out=outT[:, :, o0:o0 + n_out], in_=Xc[:, :, s0:s0 + n_out]), reads=rXc)
        P.final_wait("sp", [last_store] + [d[1] for d in dump_toks])

        with nc.Block() as block:
            @block.tensor
            def _(e):
                for f in P.q["pe"]:
                    f(e)

            @block.scalar
            def _(e):
                for f in P.q["act"]:
                    f(e)

            @block.vector
            def _(e):
                for f in P.q["dve"]:
                    f(e)

            @block.gpsimd
            def _(e):
                for f in P.q["pool"]:
                    f(e)

            @block.sync
            def _(e):
                for f in P.q["sp"]:
                    f(e)
    return nc


def _fm8(v):
    return np.ascontiguousarray(v.reshape(8, 128).T)


def _blk_k512(W, col0):
    K = W.shape[0]
    return W[:, col0:col0 + 512].reshape(K // 128, 128, 512).transpose(1, 0, 2).reshape(128, -1)


def _build_wstream(inp):
    ws = np.zeros((L, NBLK, 128, BLK), np.float32)
    for l in range(L):
        w_in = inp["w_in"][l]
        order = [0, 512, 2048, 2560, 1024, 1536, 3072, 3584, 4096, 4608]
        for j, c0 in enumerate(order):
            ws[l, j] = _blk_k512(w_in, c0)
        ws[l, 10, :, :2048] = inp["pool_w"][l].reshape(4, 2, 128, 256).transpose(2, 0, 1, 3).reshape(128, 2048)
        for i, name in enumerate(("w_pa", "w_pb", "w_o")):
            for half in range(2):
                ws[l, 11 + 2 * i + half] = _blk_k512(inp[name][l], half * 512)
        for j in range(8):
            ws[l, 17 + j] = _blk_k512(inp["w_ff1"][l], j * 512)
        w2 = inp["w_ff2"][l]
        for j in range(8):
            ws[l, 25 + j] = w2[:, j * 128:(j + 1) * 128].reshape(32, 128, 128).transpose(1, 0, 2).reshape(128, BLK)
        for half in range(2):
            ws[l, 33 + half] = _blk_k512(inp["w_ple_gate"][l], half * 512)
        ws[l, 35, :, :2048] = inp["w_ple_proj"][l].reshape(2, 128, 1024).transpose(1, 0, 2).reshape(128, 2048)
    return ws


_NC_CACHE = {}


def make_in_maps(inp):
    x, p = inp["x"], inp["p"]
    B, S, _ = x.shape
    wst = _build_wstream(inp)
    cvec = np.zeros((128, L * 64), np.float32)
    names = ["pre_mix_g", "post_mix_g", "pre_ffn_g", "post_ffn_g", "post_ple_g", "pool_scale", "sgu_ln_g", "sgu_ln_b"]
    for l in range(L):
        for vi, nm in enumerate(names):
            cvec[:, (l * 8 + vi) * 8:(l * 8 + vi + 1) * 8] = _fm8(inp[nm][l])
    binfm = np.zeros((128, L * 32), np.float32)
    bvrow = np.zeros((L, 1024), np.float32)
    for l in range(L):
        b = inp["b_in"][l]
        for gi, c0 in enumerate((0, 1024, 3072, 4096)):
            binfm[:, l * 32 + gi * 8: l * 32 + (gi + 1) * 8] = _fm8(b[c0:c0 + 1024])
        bvrow[l] = b[2048:3072]
    wsT = np.ascontiguousarray(inp["sgu_w_s"].transpose(3, 0, 1, 2)).reshape(128, L * 1024)
    bsbc = np.ascontiguousarray(np.broadcast_to(inp["sgu_b_s"].reshape(1, L * 1024), (128, L * 1024)))
    si = np.arange(128)
    cmask = (si[:, None] <= si[None, :]).astype(np.float32)

    in_maps = []
    for c in range(NCORES):
        b, half = c // 2, c % 2
        s0 = half * OWN
        xt = np.zeros((NTOK, D), np.float32)
        pt = np.zeros((L, NTOK, 256), np.float32)
        xt[HALO:] = x[b, s0:s0 + OWN]
        pt[:, HALO:] = p[:, b, s0:s0 + OWN]
        pcore = np.zeros((128, 80), np.float32)
        if half == 1:
            xt[:HALO] = x[b, s0 - HALO:s0]
            pt[:, :HALO] = p[:, b, s0 - HALO:s0]
            pcore[:, 0] = 1.0
        for g, w in enumerate(POOL_WINDOWS):
            j = np.arange(16)
            cnt = np.minimum(j + 1, w) if half == 0 else np.full(16, w)
            pcore[:, 1 + g * 16: 1 + (g + 1) * 16] = (1.0 / cnt.astype(np.float32))[None, :]
        xTc = np.ascontiguousarray(xt.T.reshape(8, 128, NTOK).transpose(1, 0, 2))
        pTc = np.ascontiguousarray(pt.transpose(0, 2, 1).reshape(L, 2, 128, NTOK).transpose(0, 2, 1, 3))
        in_maps.append({"xT": xTc, "pT": pTc, "wst": wst, "cvec": cvec, "binfm": binfm, "bvrow": bvrow,
                        "wsT": wsT, "bsbc": bsbc, "cmask": cmask, "pcore": pcore})
    return in_maps


def kernel(**inputs):
    inp = {k: np.asarray(v, dtype=np.float32) for k, v in inputs.items()}
    B, S, _ = inp["x"].shape
    in_maps = make_in_maps(inp)
    if "nc" not in _NC_CACHE:
        _NC_CACHE["nc"] = build_nc()
    nc = _NC_CACHE["nc"]
    res = run_bass_kernel_spmd(nc, in_maps, core_ids=list(range(NCORES)))
    out = np.empty((B, S, D), np.float32)
    for c in range(NCORES):
        b, half = c // 2, c % 2
        o = np.asarray(res.results[c]["outT"], dtype=np.float32)
        out[b, half * OWN:(half + 1) * OWN, :] = o.transpose(1, 0, 2).reshape(D, OWN).T
    return out
```

```python
import numpy as np
import concourse.bass as bass
import concourse.mybir as mybir
from concourse.bass_utils import run_bass_kernel_spmd

F32 = mybir.dt.float32
BF16 = mybir.dt.bfloat16
AF = mybir.ActivationFunctionType
ALU = mybir.AluOpType

D = 1024
L = 2
NCORES = 8
OWN = 2048
HALO = 128
NTOK = OWN + HALO
TILES = [(0, 640), (640, 512), (1152, 512), (1664, 512)]
NTMAX = 640
ZW = 16 + NTMAX
EPS = 1e-6
NBLK = 36
NWBUF = 4
BLK = 4096
V_PREMIX, V_POSTMIX, V_PREFFN, V_POSTFFN, V_POSTPLE, V_PSCALE, V_LNG, V_LNB = range(8)
POOL_WINDOWS = (2, 4, 8, 16)
POOL_CHUNKS = ()
RES_ORDER = (0, 1, 2, 3, 4, 5, 6, 7)


class Res:
    __slots__ = ("w", "rs", "const")

    def __init__(self, const=False):
        self.w = None
        self.rs = {}
        self.const = const


class Prog:
    ENG = ("pe", "act", "dve", "pool", "sp")

    def __init__(self):
        self.q = {k: [] for k in self.ENG}
        self.semh = {}
        self.cnt = {}
        self.seen = {k: {} for k in self.ENG}

    def add_sem(self, key, handle):
        self.semh[key] = handle
        self.cnt[key] = 0

    def _wait(self, eng, toks):
        need = {}
        for t in toks:
            if t is None:
                continue
            key, val = t
            if key == "pe" and eng == "pe":
                continue
            if self.seen[eng].get(key, 0) >= val:
                continue
            if need.get(key, 0) < val:
                need[key] = val
        for key, val in need.items():
            self.seen[eng][key] = val
            s = self.semh[key]
            self.q[eng].append(lambda e, s=s, val=val: e.wait_ge(s, val))

    @staticmethod
    def _deps(reads, writes):
        deps = []
        for r in reads:
            deps.append(r.w)
        for w in writes:
            deps.append(w.w)
            deps.extend(w.rs.items())
        return deps

    @staticmethod
    def _commit(tok, reads, writes):
        for r in reads:
            if not r.const:
                if r.rs.get(tok[0], 0) < tok[1]:
                    r.rs[tok[0]] = tok[1]
        for w in writes:
            w.w = tok
            w.rs = {}

    def op(self, eng, fn, reads=(), writes=()):
        self._wait(eng, self._deps(reads, writes))
        self.cnt[eng] += 1
        tok = (eng, self.cnt[eng])
        s = self.semh[eng]
        self.q[eng].append(lambda e, fn=fn, s=s: fn(e).then_inc(s, 1))
        self._commit(tok, reads, writes)
        return tok

    def group(self, eng, fns, reads=(), writes=()):
        self._wait(eng, self._deps(reads, writes))
        self.cnt[eng] += 1
        tok = (eng, self.cnt[eng])
        s = self.semh[eng]
        for f in fns[:-1]:
            self.q[eng].append(f)
        last = fns[-1]
        self.q[eng].append(lambda e, fn=last, s=s: fn(e).then_inc(s, 1))
        self._commit(tok, reads, writes)
        return tok

    def dma(self, eng, semkey, fn, reads=(), writes=()):
        self._wait(eng, self._deps(reads, writes))
        self.cnt[semkey] += 16
        tok = (semkey, self.cnt[semkey])
        s = self.semh[semkey]
        self.q[eng].append(lambda e, fn=fn, s=s: fn(e).then_inc(s, 16))
        self._commit(tok, reads, writes)
        return tok

    def final_wait(self, eng, toks):
        self._wait(eng, toks)


def R(n, const=False):
    return [Res(const) for _ in range(n)]


def build_nc(TILES=TILES, L_RUN=L, NOUT=OWN, DUMPS=()):
    nc = bass.Bass("TRN2", target_bir_lowering=False)
    xT = nc.dram_tensor("xT", [128, 8, NTOK], F32, kind="ExternalInput").ap()
    pT = nc.dram_tensor("pT", [L, 128, 2, NTOK], F32, kind="ExternalInput").ap()
    wst = nc.dram_tensor("wst", [L, NBLK, 128, BLK], F32, kind="ExternalInput").ap()
    cvec_d = nc.dram_tensor("cvec", [128, L * 64], F32, kind="ExternalInput").ap()
    binfm_d = nc.dram_tensor("binfm", [128, L * 32], F32, kind="ExternalInput").ap()
    bvrow_d = nc.dram_tensor("bvrow", [L, 1024], F32, kind="ExternalInput").ap()
    wsT_d = nc.dram_tensor("wsT", [128, L * 1024], F32, kind="ExternalInput").ap()
    bsbc_d = nc.dram_tensor("bsbc", [128, L * 1024], F32, kind="ExternalInput").ap()
    cmask_d = nc.dram_tensor("cmask", [128, 128], F32, kind="ExternalInput").ap()
    pcore_d = nc.dram_tensor("pcore", [128, 80], F32, kind="ExternalInput").ap()
    outT = nc.dram_tensor("outT", [128, 8, NOUT], F32, kind="ExternalOutput").ap()

    dump_d = {}
    for (nm, shp, dt) in DUMPS:
        dump_d[nm] = nc.dram_tensor("dbg_" + nm, shp, dt, kind="ExternalOutput").ap()
    P = Prog()
    from contextlib import ExitStack
    with ExitStack() as es:
        def sb(name, shape, dt):
            return es.enter_context(nc.sbuf_tensor(name, shape, dt))

        X = sb("X", [128, 8, ZW], F32)
        H = sb("H", [128, 8, NTMAX], BF16)
        O = sb("O", [128, 8, NTMAX], F32)
        Z = sb("Z", [128, 8, ZW], F32)
        S2 = sb("S2", [128, 2, ZW], F32)
        BB = sb("BB", [128, 6, 8 * NTMAX], BF16)
        WB = sb("WB", [128, NWBUF, BLK], BF16)
        PB = sb("PB", [128, 2, NTMAX], BF16)
        RB = sb("RB", [128, NTMAX], F32)
        TMP = sb("TMP", [128, 2, ZW], F32)
        S1 = TMP
        ZH = sb("ZH", [128, L, 8, 16], F32)
        T16 = sb("T16", [128, 2, 16], F32)
        ST = sb("ST", [128, 2, 6], F32)
        MVA = sb("MVA", [128, 8, 2], F32)
        RSTD = sb("RSTD", [128, 8], F32)
        EPSC = sb("EPSC", [128, 1], F32)
        NMR = sb("NMR", [128, 8], F32)
        EPSP = sb("EPSP", [128, NTMAX], F32)
        CV = sb("CV", [128, L * 64], F32)
        BIN = sb("BIN", [128, L * 32], F32)
        VBH = sb("VBH", [33, 1024], BF16)
        VBL = sb("VBL", [33, 1024], BF16)
        VB2 = sb("VB2", [2, L, 1024], BF16)
        WSB = sb("WSB", [128, L * 1024], BF16)
        BF = sb("BF", [128, L * 1024], F32)
        CM = sb("CM", [128, 128], F32)
        PC = sb("PC", [128, 80], F32)
        ONEB = sb("ONEB", [128, 128], BF16)
        ONEF = sb("ONEF", [128, 128], F32)
        PS = es.enter_context(nc.psum_tensor("PS", [128, 8, 512], F32))
        WSF = BB[:, 3, 0:2 * L * 1024].bitcast(F32)
        BSB = BB[:, 5, 0:2 * L * 1024].bitcast(F32)

        for key in ("pe", "act", "dve", "pool", "sp", "cst", "xld", "ost", "pld", "dbg") + tuple(
                "w%d" % i for i in range(NWBUF)):
            P.add_sem(key, es.enter_context(nc.semaphore("s_" + key)))

        rX, rH, rO, rZ = R(8), R(8), R(8), R(8)
        rB = [R(8) for _ in range(6)]
        rS2 = Res()
        rW = R(NWBUF)
        rPB, rRB = Res(), Res()
        rTMP = R(2)
        rS1 = rTMP
        rZH = R(L)
        rT16, rST, rMV, rRSTD, rNMR, rEPSP = Res(), Res(), Res(), Res(), Res(), Res()
        rBT = R(8)
        rV = R(5)
        rPS = R(8)
        rC = Res()
        rVB2 = Res()
        rWSB, rBF = Res(), Res()

        def Bv(i):
            return BB[:, i, :].rearrange("p (c t) -> p c t", c=8)

        U, GA, GB, NB_, PL, MS = (Bv(i) for i in range(6))
        SQ = PL
        rU, rGA, rGB, rN, rPL, rMS = rB
        rSQ = rPL
        NTM = BB[:, 3, :]
        ACTB = BB[:, 0:4, :].rearrange("p a (c t) -> p (a c) t", c=8)
        rACT = rB[0] + rB[1] + rB[2] + rB[3]
        VTM = O[:, :, :].rearrange("p c t -> p (c t)")

        psn = [0]

        def banks(n):
            b = psn[0]
            if b + n > 8:
                b = 0
            psn[0] = (b + n) % 8
            return list(range(b, b + n))

        stream = []
        for (t0, nt) in TILES:
            for l in range(L_RUN):
                for j in range(NBLK):
                    stream.append((l, j))
        wstate = {"issued": 0, "cons": 0}

        def blk_len(j):
            return 2048 if j in (10, 35) else BLK

        def w_issue():
            i = wstate["issued"]
            if i >= len(stream):
                return
            l, j = stream[i]
            b = i % NWBUF
            n = blk_len(j)
            P.dma("pool", "w%d" % b,
                  lambda e, b=b, l=l, j=j, n=n: e.dma_start(out=WB[:, b, 0:n], in_=wst[l, j, :, 0:n]),
                  writes=[rW[b]])
            wstate["issued"] += 1

        def w_next():
            i = wstate["cons"]
            wstate["cons"] += 1
            return i % NWBUF

        def w_done():
            w_issue()

        cst = []
        for (dst, src) in ((CV[:], cvec_d), (BIN[:], binfm_d), (WSF, wsT_d), (BSB, bsbc_d),
                           (CM[:], cmask_d), (PC[:], pcore_d)):
            nd = len(dst.shape)
            P.dma("sp", "cst", lambda e, dst=dst, src=src: e.dma_start(out=dst, in_=src), writes=[rC])
        for _ in range(NWBUF):
            w_issue()

        BVR = VTM[0:33, 0:1024]
        BVT = VTM[0:33, 1024:2048]
        P.op("dve", lambda e: e.memset(VTM[0:33, 0:2048], 0.0), writes=rO)
        for l in range(L):
            P.dma("sp", "cst", lambda e, l=l: e.dma_start(out=VTM[32 * l:32 * l + 1, 0:1024], in_=bvrow_d[l:l + 1, :]),
                  writes=[rC] + rO)
        P.op("dve", lambda e: e.memset(ONEB[:], 1.0), writes=[rC])
        P.op("dve", lambda e: e.memset(ONEF[:], 1.0), writes=[rC])
        P.op("dve", lambda e: e.memset(EPSC[:], EPS), writes=[rC])
        P.op("dve", lambda e: e.tensor_copy(out=VBH[:], in_=BVR), reads=[rC], writes=[rC])
        P.op("dve", lambda e: e.tensor_copy(out=BVT, in_=VBH[:]), reads=[rC], writes=[rC])
        P.op("dve", lambda e: e.tensor_tensor(out=BVT, in0=BVR, in1=BVT, op=ALU.subtract),
             reads=[rC], writes=[rC] + rO)
        P.op("dve", lambda e: e.tensor_copy(out=VBL[:], in_=BVT), reads=[rC], writes=[rC] + rO)
        for l in range(L):
            P.dma("sp", "cst", lambda e, l=l: e.dma_start(out=VB2[0:1, l, :], in_=VBH[32 * l:32 * l + 1, :]),
                  reads=[rC], writes=[rVB2])
            P.dma("sp", "cst", lambda e, l=l: e.dma_start(out=VB2[1:2, l, :], in_=VBL[32 * l:32 * l + 1, :]),
                  reads=[rC], writes=[rVB2])
        def setup_sgu():
            for l in range(L):
                for h in range(8):
                    sl = slice(l * 1024 + h * 128, l * 1024 + (h + 1) * 128)
                    P.op("dve", lambda e, sl=sl: e.tensor_tensor(out=WSF[:, sl], in0=WSF[:, sl], in1=CM[:], op=ALU.mult),
                         reads=[rC], writes=rB[3])
                P.op("dve", lambda e, l=l: e.tensor_copy(out=WSB[:, l * 1024:(l + 1) * 1024],
                                                         in_=WSF[:, l * 1024:(l + 1) * 1024]),
                     reads=rB[3], writes=[rWSB])
                for hh in range(2):
                    bk = banks(1)[0]
                    P.group("pe", [lambda e, bk=bk, l=l, hh=hh: e.matmul(
                        PS[:, bk, :], ONEF[:], WSF[:, l * 1024 + hh * 512: l * 1024 + (hh + 1) * 512],
                        start=True, stop=True)], reads=[rC] + rB[3], writes=[rPS[bk]])
                    for h4 in range(4):
                        h = hh * 4 + h4
                        sl = slice(l * 1024 + h * 128, l * 1024 + (h + 1) * 128)
                        col = (l * 8 + V_LNB) * 8 + h
                        P.op("dve", lambda e, bk=bk, h4=h4, sl=sl, col=col: e.scalar_tensor_tensor(
                            out=BF[:, sl], in0=PS[:, bk, h4 * 128:(h4 + 1) * 128], scalar=CV[:, col:col + 1],
                            in1=BSB[:, sl], op0=ALU.mult, op1=ALU.add),
                            reads=[rPS[bk], rC] + rB[5], writes=[rBF])

        setup_sgu()
        rWSB.const = True
        rBF.const = True
        rC_done = rC.w
        rC.const = True

        def cv(l, vi, c):
            col = (l * 8 + vi) * 8 + c
            return CV[:, col:col + 1]

        dump_toks = []

        def dump(nm, ap, res):
            if nm in dump_d and nm not in [d[0] for d in dump_toks]:
                dump_toks.append((nm, P.dma("sp", "dbg", lambda e: e.dma_start(out=dump_d[nm], in_=ap), reads=res)))

        def colblocks(ca, nt):
            cbs = []
            c = ca
            while c < nt:
                w = min(512, nt - c)
                cbs.append((c, w))
                c += w
            return cbs

        GF = BB[:, 0:2, :].rearrange("p a t -> p (a t)").bitcast(F32).rearrange("p (c t) -> p c t", c=8)
        rG = [[rB[oc // 4][2 * (oc % 4)], rB[oc // 4][2 * (oc % 4) + 1]] for oc in range(8)]

        def tile_layer(ti, t0, nt, l, X, rX, Z, rZ, prefetch):
            last = (l == L_RUN - 1)
            ca = HALO if (ti == 0 and last) else 0
            cbs = colblocks(ca, nt)
            cbs_all = colblocks(0, nt)
            nch = nt // 128
            clo = ca // 128

            def norm_R(sq_res, mode="rsqrt", cbl=None):
                for (c0, w) in (cbl or cbs):
                    bk = banks(1)[0]
                    for dc in range(8):
                        P.group("pe", [lambda e, bk=bk, dc=dc, c0=c0, w=w: e.matmul(
                            PS[:, bk, 0:w], ONEB[:], SQ[:, dc, c0:c0 + w], start=(dc == 0), stop=(dc == 7))],
                            reads=[rC, sq_res[dc]], writes=[rPS[bk]])
                    if mode == "epsp":
                        P.op("dve", lambda e, bk=bk, c0=c0, w=w: e.tensor_scalar(
                            out=EPSP[:, c0:c0 + w], in0=PS[:, bk, 0:w], scalar1=1.0 / D, scalar2=EPS,
                            op0=ALU.mult, op1=ALU.add), reads=[rPS[bk]], writes=[rEPSP])
                        P.op("dve", lambda e, c0=c0, w=w: e.scalar_tensor_tensor(
                            out=EPSP[:, c0:c0 + w], in0=EPSP[:, c0:c0 + w], scalar=EPS, in1=EPSP[:, c0:c0 + w],
                            op0=ALU.mult, op1=ALU.mult), reads=[rEPSP], writes=[rEPSP])
                        continue
                    if mode == "rsqrt_epsp":
                        P.op("dve", lambda e, bk=bk, c0=c0, w=w: e.scalar_tensor_tensor(
                            out=RB[:, c0:c0 + w], in0=PS[:, bk, 0:w], scalar=1.0 / D, in1=EPSP[:, c0:c0 + w],
                            op0=ALU.mult, op1=ALU.add), reads=[rPS[bk], rEPSP], writes=[rRB])
                        P.op("act", lambda e, c0=c0, w=w: e.activation(
                            out=RB[:, c0:c0 + w], in_=RB[:, c0:c0 + w], func=AF.Ln), reads=[rRB], writes=[rRB])
                    else:
                        P.op("act", lambda e, bk=bk, c0=c0, w=w: e.activation(
                            out=RB[:, c0:c0 + w], in_=PS[:, bk, 0:w], func=AF.Ln, bias=EPSC[:, 0:1], scale=1.0 / D),
                            reads=[rPS[bk], rC], writes=[rRB])
                    P.op("act", lambda e, c0=c0, w=w: e.activation(
                        out=RB[:, c0:c0 + w], in_=RB[:, c0:c0 + w], func=AF.Exp, scale=-0.5),
                        reads=[rRB], writes=[rRB])

            def squares_of_X(a, b_):
                for dc in range(8):
                    P.op("act", lambda e, dc=dc: e.activation(out=SQ[:, dc, a:b_], in_=X[:, dc, a:b_], func=AF.Square),
                         reads=[rX[dc]], writes=[rSQ[dc]])

            def residual(vi, hmode=None, hvi=None, gained=False):
                n = nt - ca
                step = 2 if gained else 1
                for d0 in range(0, 8, step):
                    dcs = list(range(d0, d0 + step))
                    if gained:
                        P.op("dve", lambda e, d0=d0: e.tensor_tensor(
                            out=O[:, d0:d0 + 2, ca:nt], in0=O[:, d0:d0 + 2, ca:nt],
                            in1=RB[:, ca:nt].unsqueeze(1).broadcast_to([128, 2, n]), op=ALU.mult),
                            reads=[rO[d] for d in dcs] + [rRB], writes=[rO[d] for d in dcs])
                        P.op("dve", lambda e, d0=d0: e.tensor_tensor(
                            out=X[:, d0:d0 + 2, ca:nt], in0=X[:, d0:d0 + 2, ca:nt], in1=O[:, d0:d0 + 2, ca:nt],
                            op=ALU.add), reads=[rO[d] for d in dcs] + [rX[d] for d in dcs],
                            writes=[rX[d] for d in dcs])
                    else:
                        dc = d0
                        P.op("dve", lambda e, dc=dc: e.tensor_tensor(
                            out=O[:, dc, ca:nt], in0=O[:, dc, ca:nt], in1=RB[:, ca:nt], op=ALU.mult),
                            reads=[rO[dc], rRB], writes=[rO[dc]])
                        P.op("dve", lambda e, dc=dc: e.scalar_tensor_tensor(
                            out=X[:, dc, ca:nt], in0=O[:, dc, ca:nt], scalar=cv(l, vi, dc), in1=X[:, dc, ca:nt],
                            op0=ALU.mult, op1=ALU.add), reads=[rO[dc], rX[dc], rC], writes=[rX[dc]])
                    for dc in dcs:
                        if hmode == "gain":
                            P.op("act", lambda e, dc=dc: e.activation(
                                out=H[:, dc, ca:nt], in_=X[:, dc, ca:nt], func=AF.Identity, scale=cv(l, hvi, dc)),
                                reads=[rX[dc], rC], writes=[rH[dc]])
                        elif hmode == "copy":
                            P.op("act", lambda e, dc=dc: e.activation(
                                out=H[:, dc, ca:nt], in_=X[:, dc, ca:nt], func=AF.Copy),
                                reads=[rX[dc]], writes=[rH[dc]])

            def fm_matmul(b, rhs, rhs_res, nk, evac, fis=range(4), kstride=512, cbl=None, kouter=False):
                cbl = cbl or cbs
                fis = list(fis)
                if kouter:
                    per = max(1, 4 // len(cbl))
                    for s0 in range(0, len(fis), per):
                        sub = fis[s0:s0 + per]
                        bkm = {fi: banks(len(cbl)) for fi in sub}
                        allb = [bk for fi in sub for bk in bkm[fi]]
                        for k in range(nk):
                            fns = []
                            for fi in sub:
                                for ci, (c0, w) in enumerate(cbl):
                                    fns.append(lambda e, bk=bkm[fi][ci], k=k, fi=fi, c0=c0, w=w: e.matmul(
                                        PS[:, bk, 0:w], WB[:, b, k * kstride + fi * 128: k * kstride + (fi + 1) * 128],
                                        rhs[:, k, c0:c0 + w], start=(k == 0), stop=(k == nk - 1)))
                            P.group("pe", fns, reads=[rW[b], rhs_res[k]], writes=[rPS[bk] for bk in allb])
                        for fi in sub:
                            for ci, (c0, w) in enumerate(cbl):
                                evac(fi, c0, w, bkm[fi][ci])
                    return
                for fi in fis:
                    bks = banks(len(cbl))
                    fns = []
                    for k in range(nk):
                        for ci, (c0, w) in enumerate(cbl):
                            fns.append(lambda e, bk=bks[ci], k=k, fi=fi, c0=c0, w=w: e.matmul(
                                PS[:, bk, 0:w], WB[:, b, k * kstride + fi * 128: k * kstride + (fi + 1) * 128],
                                rhs[:, k, c0:c0 + w], start=(k == 0), stop=(k == nk - 1)))
                    P.group("pe", fns, reads=[rW[b]] + rhs_res, writes=[rPS[bk] for bk in bks])
                    for ci, (c0, w) in enumerate(cbl):
                        evac(fi, c0, w, bks[ci])

            P.dma("pool", "pld", lambda e: e.dma_start(out=PB[:, :, 0:nt], in_=pT[l, :, :, t0:t0 + nt]),
                  writes=[rPB])

            squares_of_X(0, nt)
            norm_R(rSQ, cbl=cbs_all)
            for dc in range(8):
                P.op("dve", lambda e, dc=dc: e.scalar_tensor_tensor(
                    out=H[:, dc, 0:nt], in0=X[:, dc, 0:nt], scalar=cv(l, V_PREMIX, dc), in1=RB[:, 0:nt],
                    op0=ALU.mult, op1=ALU.mult), reads=[rX[dc], rRB, rC], writes=[rH[dc]])

            def w_in_group(dst, dres, func, bcol, off, cbl, first):
                for half in range(2):
                    b = w_next()

                    def evac(fi, c0, w, bk, half=half):
                        fc = half * 4 + fi
                        col = l * 32 + bcol + fc
                        P.op("act", lambda e: e.activation(
                            out=dst[:, fc, off + c0: off + c0 + w], in_=PS[:, bk, 0:w], func=func,
                            bias=BIN[:, col:col + 1]), reads=[rPS[bk], rC], writes=[dres[fc]])
                    fm_matmul(b, H, rH, 8, evac, cbl=cbl, kouter=(first and half == 0))
                    w_done()

            w_in_group(Z, rZ, AF.Identity, 0, 16, cbs_all, True)
            W_ = 16 + nt

            def pool_prep():
                if ti == 0:
                    P.op("dve", lambda e: e.memset(Z[:, :, 0:16], 0.0), writes=rZ)
                    P.op("dve", lambda e: e.tensor_scalar(
                        out=Z[:, :, 16:16 + HALO], in0=Z[:, :, 16:16 + HALO], scalar1=PC[:, 0:1], scalar2=None,
                        op0=ALU.mult), reads=rZ + [rC], writes=rZ)
                else:
                    P.op("dve", lambda e: e.tensor_copy(out=Z[:, :, 0:16], in_=ZH[:, l, :, :]),
                         reads=[rZH[l]], writes=rZ)

            def pool_groups(gs):
                for g in gs:
                    zs = Z[:, 2 * g:2 * g + 2, :]
                    zres = [rZ[2 * g], rZ[2 * g + 1]]
                    P.op("dve", lambda e, zs=zs: e.tensor_tensor(
                        out=S1[:, :, 1:W_], in0=zs[:, :, 1:W_], in1=zs[:, :, 0:W_ - 1], op=ALU.add),
                        reads=zres, writes=rS1)
                    cur, rcur = S1, rS1
                    if g >= 1:
                        P.op("dve", lambda e: e.tensor_tensor(
                            out=S2[:, :, 3:W_], in0=S1[:, :, 3:W_], in1=S1[:, :, 1:W_ - 2], op=ALU.add),
                            reads=rS1, writes=[rS2])
                        cur, rcur = S2, [rS2]
                    if g >= 2:
                        P.op("dve", lambda e: e.tensor_tensor(
                            out=S1[:, :, 7:W_], in0=S2[:, :, 7:W_], in1=S2[:, :, 3:W_ - 4], op=ALU.add),
                            reads=[rS2], writes=rS1)
                        cur, rcur = S1, rS1
                    if g >= 3:
                        P.op("dve", lambda e: e.tensor_tensor(
                            out=S2[:, :, 15:W_], in0=S1[:, :, 15:W_], in1=S1[:, :, 7:W_ - 8], op=ALU.add),
                            reads=rS1, writes=[rS2])
                        cur, rcur = S2, [rS2]
                    wdw = POOL_WINDOWS[g]
                    P.op("dve", lambda e, cur=cur, g=g, wdw=wdw: e.scalar_tensor_tensor(
                        out=PL[:, 2 * g:2 * g + 2, 0:nt], in0=cur[:, :, 16:16 + nt], scalar=1.0 / wdw,
                        in1=Z[:, 2 * g:2 * g + 2, 16:16 + nt], op0=ALU.mult, op1=ALU.subtract),
                        reads=rcur + zres, writes=[rPL[2 * g], rPL[2 * g + 1]])
                    if ti == 0:
                        for cc in range(2):
                            P.op("dve", lambda e, cur=cur, g=g, cc=cc: e.tensor_tensor(
                                out=T16[:, cc, :], in0=cur[:, cc, 16 + HALO:32 + HALO],
                                in1=PC[:, 1 + g * 16: 1 + (g + 1) * 16], op=ALU.mult),
                                reads=rcur + [rC], writes=[rT16])
                        P.op("dve", lambda e, g=g: e.tensor_tensor(
                            out=PL[:, 2 * g:2 * g + 2, HALO:HALO + 16], in0=T16[:, :, :],
                            in1=Z[:, 2 * g:2 * g + 2, 16 + HALO:32 + HALO], op=ALU.subtract),
                            reads=[rT16] + zres, writes=[rPL[2 * g], rPL[2 * g + 1]])

            def pool_finish():
                dump("Z", Z[:, :, :], rZ)
                dump("PL", PL, rPL)
                dump("H", H[:, :, :], rH)
                P.op("dve", lambda e: e.tensor_copy(out=ZH[:, l, :, :], in_=Z[:, :, nt:nt + 16]),
                     reads=rZ, writes=[rZH[l]])
                if last and prefetch is not None:
                    pt0, pnt = prefetch
                    P.dma("sp", "xld", lambda e: e.dma_start(out=Z[:, :, 0:pnt], in_=xT[:, :, pt0:pt0 + pnt]),
                          writes=rZ)


            pool_prep()
            pool_groups((0, 1))

            vb = [w_next(), w_next()]
            first_v = [True]
            for half in range(2):
                b = vb[half]
                for c in range(clo, nch):
                    bk = banks(1)[0]
                    fns = []
                    for k in range(8):
                        fns.append(lambda e, bk=bk, k=k, c=c, b=b: e.matmul(
                            PS[:, bk, :], H[:, k, c * 128:(c + 1) * 128], WB[:, b, k * 512:(k + 1) * 512],
                            start=(k == 0), stop=False))
                    vsl = slice(half * 512, (half + 1) * 512)
                    fns.append(lambda e, bk=bk, vsl=vsl: e.matmul(
                        PS[:, bk, :], ONEB[0:2, :], VB2[0:2, l, vsl], start=False, stop=True))
                    P.group("pe", fns, reads=[rW[b], rC, rVB2] + rH, writes=[rPS[bk]])
                    o0 = c * 1024 + half * 512
                    P.op("act", lambda e, bk=bk, o0=o0: e.activation(
                        out=VTM[:, o0:o0 + 512], in_=PS[:, bk, :], func=AF.Gelu_apprx_tanh),
                        reads=[rPS[bk]], writes=([rV[c]] + (rO if first_v[0] else [])))
                    first_v[0] = False
                w_done()
            for c in range(clo, nch):
                for half in range(2):
                    o0 = c * 1024 + half * 512
                    P.op("dve", lambda e, o0=o0, half=half: e.bn_stats(out=ST[:, half, :], in_=VTM[:, o0:o0 + 512]),
                         reads=[rV[c]], writes=[rST])
                P.op("dve", lambda e, c=c: e.bn_aggr(out=MVA[:, c, :], in_=ST[:, :, :].rearrange("p a b -> p (a b)")),
                     reads=[rST], writes=[rMV])

            P.op("act", lambda e: e.activation(out=RSTD[:, clo:nch], in_=MVA[:, clo:nch, 1], func=AF.Ln,
                                               bias=EPSC[:, 0:1], scale=1.0), reads=[rMV, rC], writes=[rRSTD])
            P.op("act", lambda e: e.activation(out=RSTD[:, clo:nch], in_=RSTD[:, clo:nch], func=AF.Exp, scale=-0.5),
                 reads=[rRSTD], writes=[rRSTD])
            P.op("dve", lambda e: e.scalar_tensor_tensor(
                out=NMR[:, clo:nch], in0=MVA[:, clo:nch, 0], scalar=-1.0, in1=RSTD[:, clo:nch],
                op0=ALU.mult, op1=ALU.mult), reads=[rMV, rRSTD], writes=[rNMR])
            for c in range(clo, nch):
                P.op("act", lambda e, c=c: e.activation(
                    out=NTM[:, c * 1024:(c + 1) * 1024], in_=VTM[:, c * 1024:(c + 1) * 1024], func=AF.Identity,
                    scale=RSTD[:, c:c + 1], bias=NMR[:, c:c + 1]), reads=rO + [rV[c], rRSTD, rNMR], writes=rN)
            w_in_group(U, rU, AF.Gelu_apprx_tanh, 8, 0, cbs, False)

            for h in range(8):
                for ci, (c0, w) in enumerate(cbs):
                    bk = banks(1)[0]
                    nck = w // 128
                    fns = []
                    for cc in range(nck):
                        c = c0 // 128 + cc
                        fns.append(lambda e, bk=bk, cc=cc, c=c, h=h: e.matmul(
                            PS[:, bk, cc * 128:(cc + 1) * 128], NTM[:, c * 1024 + h * 128: c * 1024 + (h + 1) * 128],
                            WSB[:, l * 1024 + h * 128: l * 1024 + (h + 1) * 128], start=True, stop=True))
                    P.group("pe", fns, reads=rN + [rC, rWSB], writes=[rPS[bk]])
                    tb = (h * len(cbs) + ci) % 2
                    bf = BF[:, l * 1024 + h * 128: l * 1024 + (h + 1) * 128]
                    P.op("dve", lambda e, bk=bk, nck=nck, w=w, tb=tb, bf=bf, h=h: e.scalar_tensor_tensor(
                        out=TMP[:, tb, 0:w].rearrange("p (a t) -> p a t", a=nck),
                        in0=PS[:, bk, 0:w].rearrange("p (a t) -> p a t", a=nck),
                        scalar=cv(l, V_LNG, h),
                        in1=bf.unsqueeze(1).broadcast_to([128, nck, 128]),
                        op0=ALU.mult, op1=ALU.add), reads=[rPS[bk], rC, rBF], writes=[rTMP[tb]])
                    P.op("dve", lambda e, tb=tb, h=h, c0=c0, w=w: e.tensor_tensor(
                        out=U[:, h, c0:c0 + w], in0=TMP[:, tb, 0:w], in1=U[:, h, c0:c0 + w], op=ALU.mult),
                        reads=[rTMP[tb], rU[h]], writes=[rU[h]])
            dump("SG", U, rU)
            dump("MS", MS, rMS)

            pool_groups((2, 3))
            pool_finish()
            w_in_group(GA, rGA, AF.Sigmoid, 16, 0, cbs, False)
            w_in_group(GB, rGB, AF.Sigmoid, 24, 0, cbs, False)

            b4 = w_next()

            def m4_groups(idx):
                for gi in idx:
                    g, dh = gi // 2, gi % 2
                    bks = banks(len(cbs))
                    fns = []
                    for cc in range(2):
                        for ci, (c0, w) in enumerate(cbs):
                            o0 = (g * 2 + cc) * 256 + dh * 128
                            fns.append(lambda e, bk=bks[ci], o0=o0, g=g, cc=cc, c0=c0, w=w: e.matmul(
                                PS[:, bk, 0:w], WB[:, b4, o0:o0 + 128], PL[:, 2 * g + cc, c0:c0 + w],
                                start=(cc == 0), stop=(cc == 1)))
                    P.group("pe", fns, reads=[rW[b4], rPL[2 * g], rPL[2 * g + 1]], writes=[rPS[bk] for bk in bks])
                    oc = 2 * g + dh
                    for ci, (c0, w) in enumerate(cbs):
                        if gi % 2 == 0:
                            P.op("dve", lambda e, bk=bks[ci], oc=oc, c0=c0, w=w: e.tensor_scalar(
                                out=MS[:, oc, c0:c0 + w], in0=PS[:, bk, 0:w], scalar1=cv(l, V_PSCALE, oc),
                                scalar2=None, op0=ALU.mult), reads=[rPS[bks[ci]], rC], writes=[rMS[oc]])
                        else:
                            P.op("act", lambda e, bk=bks[ci], oc=oc, c0=c0, w=w: e.activation(
                                out=MS[:, oc, c0:c0 + w], in_=PS[:, bk, 0:w], func=AF.Identity,
                                scale=cv(l, V_PSCALE, oc)), reads=[rPS[bks[ci]], rC], writes=[rMS[oc]])

            m4_groups(range(0, 8))
            w_done()
            dump("VTM", VTM, rO)
            dump("NTM", NTM, rN)

            for half in range(2):
                b = w_next()

                def evac(fi, c0, w, bk, half=half):
                    oc = half * 4 + fi
                    P.op("dve", lambda e: e.tensor_tensor(
                        out=GA[:, oc, c0:c0 + w], in0=PS[:, bk, 0:w], in1=GA[:, oc, c0:c0 + w], op=ALU.mult),
                        reads=[rPS[bk], rGA[oc]], writes=[rGA[oc]])
                fm_matmul(b, MS, rMS, 8, evac)
                w_done()
            dump("M1", GA, rGA)

            for half in range(2):
                b = w_next()

                def evac(fi, c0, w, bk, half=half):
                    oc = half * 4 + fi
                    P.op("dve", lambda e: e.tensor_tensor(
                        out=GB[:, oc, c0:c0 + w], in0=PS[:, bk, 0:w], in1=GB[:, oc, c0:c0 + w], op=ALU.mult),
                        reads=[rPS[bk], rGB[oc]], writes=[rGB[oc]])
                    P.op("dve", lambda e: e.tensor_tensor(
                        out=GB[:, oc, c0:c0 + w], in0=GB[:, oc, c0:c0 + w], in1=GA[:, oc, c0:c0 + w], op=ALU.add),
                        reads=[rGB[oc], rGA[oc]], writes=[rGB[oc]])
                fm_matmul(b, U, rU, 8, evac)
                w_done()

            def evac_O(oc, c0, w, bk, vi):
                P.op("act", lambda e: e.activation(out=SQ[:, oc, c0:c0 + w], in_=PS[:, bk, 0:w], func=AF.Square),
                     reads=[rPS[bk]], writes=[rSQ[oc], rBT[bk]])
                P.op("dve", lambda e: e.tensor_scalar(out=O[:, oc, c0:c0 + w], in0=PS[:, bk, 0:w],
                                                      scalar1=cv(l, vi, oc), scalar2=None, op0=ALU.mult),
                     reads=[rPS[bk], rBT[bk], rC], writes=[rO[oc]])

            for half in range(2):
                b = w_next()
                fm_matmul(b, GB, rGB, 8, lambda fi, c0, w, bk, half=half: evac_O(half * 4 + fi, c0, w, bk, V_POSTMIX))
                w_done()
            dump("MG", GB, rGB)
            dump("O1", O[:, :, :], rO)

            norm_R(rSQ)
            residual(V_POSTMIX, hmode="gain", hvi=V_PREFFN, gained=True)
            dump("X1", X[:, :, :], rX)

            for j in range(8):
                b = w_next()

                def evac(fi, c0, w, bk, j=j):
                    fc = j * 4 + fi
                    tb = fi % 2
                    P.op("act", lambda e: e.activation(out=TMP[:, tb, c0:c0 + w], in_=PS[:, bk, 0:w], func=AF.Relu),
                         reads=[rPS[bk]], writes=[rTMP[tb]])
                    P.op("dve", lambda e: e.tensor_tensor(
                        out=ACTB[:, fc, c0:c0 + w], in0=PS[:, bk, 0:w], in1=TMP[:, tb, c0:c0 + w], op=ALU.mult),
                        reads=[rPS[bk], rTMP[tb]], writes=[rACT[fc]])
                fm_matmul(b, H, rH, 8, evac, kouter=(j == 0))
                w_done()
                if j == 0:
                    squares_of_X(ca, nt)
                    norm_R(rSQ, mode="epsp")

            for j in range(8):
                b = w_next()
                fm_matmul(b, ACTB, rACT, 32, lambda fi, c0, w, bk, j=j: evac_O(j, c0, w, bk, V_POSTFFN),
                          fis=[0], kstride=128)
                w_done()

            norm_R(rSQ, mode="rsqrt_epsp")
            residual(V_POSTFFN, hmode="copy", gained=True)

            for half in range(2):
                b = w_next()

                def evac(fi, c0, w, bk, half=half):
                    oc = half * 4 + fi
                    P.op("act", lambda e: e.activation(
                        out=GF[:, oc, c0:c0 + w], in_=PS[:, bk, 0:w], func=AF.Sigmoid),
                        reads=[rPS[bk]], writes=rG[oc])
                fm_matmul(b, H, rH, 8, evac, kouter=(half == 0))
                w_done()

            bp = w_next()
            for oc in range(8):
                bks = banks(len(cbs))
                fns = []
                for kc in range(2):
                    for ci, (c0, w) in enumerate(cbs):
                        fns.append(lambda e, bk=bks[ci], kc=kc, oc=oc, c0=c0, w=w: e.matmul(
                            PS[:, bk, 0:w], WB[:, bp, kc * 1024 + oc * 128: kc * 1024 + (oc + 1) * 128],
                            PB[:, kc, c0:c0 + w], start=(kc == 0), stop=(kc == 1)))
                P.group("pe", fns, reads=[rW[bp], rPB], writes=[rPS[bk] for bk in bks])
                for ci, (c0, w) in enumerate(cbs):
                    bk = bks[ci]
                    P.op("dve", lambda e, bk=bk, oc=oc, c0=c0, w=w: e.tensor_tensor(
                        out=O[:, oc, c0:c0 + w], in0=PS[:, bk, 0:w], in1=GF[:, oc, c0:c0 + w], op=ALU.mult),
                        reads=[rPS[bk]] + rG[oc], writes=[rO[oc]])
                    P.op("act", lambda e, oc=oc, c0=c0, w=w: e.activation(
                        out=SQ[:, oc, c0:c0 + w], in_=O[:, oc, c0:c0 + w], func=AF.Square),
                        reads=[rO[oc]], writes=[rSQ[oc]])
            w_done()

            norm_R(rSQ)
            residual(V_POSTPLE)

        bufs = [(X, rX), (Z, rZ)]
        last_store = None
        P.dma("sp", "xld", lambda e: e.dma_start(out=X[:, :, 0:TILES[0][1]], in_=xT[:, :, 0:TILES[0][1]]), writes=rX)
        for ti, (t0, nt) in enumerate(TILES):
            (Xc, rXc), (Zc, rZc) = bufs[ti % 2], bufs[(ti + 1) % 2]
            prefetch = TILES[ti + 1] if ti + 1 < len(TILES) else None
            for l in range(L_RUN):
                tile_layer(ti, t0, nt, l, Xc, rXc, Zc, rZc, prefetch)
            s0 = HALO if ti == 0 else 0
            o0 = t0 + s0 - HALO
            n_out = nt - s0
            for dc in range(8):
                last_store = P.dma("sp", "ost", lambda e, s0=s0, o0=o0, n_out=n_out, Xc=Xc, dc=dc: e.dma_start(
                    out=outT[:, dc, o0:o0 + n_out], in_=Xc[:, dc, s0:s0 + n_out]), reads=[rXc[dc]])
            for r_ in rXc:
                r_.rs["ost"] = P.cnt["ost"]
        P.final_wait("sp", [last_store] + [d[1] for d in dump_toks])

        with nc.Block() as block:
            @block.tensor
            def _(e):
                for f in P.q["pe"]:
                    f(e)

            @block.scalar
            def _(e):
                for f in P.q["act"]:
                    f(e)

            @block.vector
            def _(e):
                for f in P.q["dve"]:
                    f(e)

            @block.gpsimd
            def _(e):
                for f in P.q["pool"]:
                    f(e)

            @block.sync
            def _(e):
                for f in P.q["sp"]:
                    f(e)
    return nc


def _fm8(v):
    return np.ascontiguousarray(v.reshape(8, 128).T)


def _blk_k512(W, col0):
    K = W.shape[0]
    return W[:, col0:col0 + 512].reshape(K // 128, 128, 512).transpose(1, 0, 2).reshape(128, -1)


def _build_wstream(inp):
    ws = np.zeros((L, NBLK, 128, BLK), np.float32)
    for l in range(L):
        w_in = inp["w_in"][l]
        order = [0, 512, 2048, 2560, 1024, 1536, 3072, 3584, 4096, 4608]
        for j, c0 in enumerate(order):
            ws[l, j] = _blk_k512(w_in, c0)
        ws[l, 10, :, :2048] = inp["pool_w"][l].reshape(4, 2, 128, 256).transpose(2, 0, 1, 3).reshape(128, 2048)
        for i, name in enumerate(("w_pa", "w_pb", "w_o")):
            for half in range(2):
                ws[l, 11 + 2 * i + half] = _blk_k512(inp[name][l], half * 512)
        for j in range(8):
            ws[l, 17 + j] = _blk_k512(inp["w_ff1"][l], j * 512)
        w2 = inp["w_ff2"][l]
        for j in range(8):
            ws[l, 25 + j] = w2[:, j * 128:(j + 1) * 128].reshape(32, 128, 128).transpose(1, 0, 2).reshape(128, BLK)
        for half in range(2):
            ws[l, 33 + half] = _blk_k512(inp["w_ple_gate"][l], half * 512)
        ws[l, 35, :, :2048] = inp["w_ple_proj"][l].reshape(2, 128, 1024).transpose(1, 0, 2).reshape(128, 2048)
    return ws


_NC_CACHE = {}


def make_in_maps(inp):
    x, p = inp["x"], inp["p"]
    B, S, _ = x.shape
    wst = _build_wstream(inp)
    cvec = np.zeros((128, L * 64), np.float32)
    names = ["pre_mix_g", "post_mix_g", "pre_ffn_g", "post_ffn_g", "post_ple_g", "pool_scale", "sgu_ln_g", "sgu_ln_b"]
    for l in range(L):
        for vi, nm in enumerate(names):
            cvec[:, (l * 8 + vi) * 8:(l * 8 + vi + 1) * 8] = _fm8(inp[nm][l])
    binfm = np.zeros((128, L * 32), np.float32)
    bvrow = np.zeros((L, 1024), np.float32)
    for l in range(L):
        b = inp["b_in"][l]
        for gi, c0 in enumerate((0, 1024, 3072, 4096)):
            binfm[:, l * 32 + gi * 8: l * 32 + (gi + 1) * 8] = _fm8(b[c0:c0 + 1024])
        bvrow[l] = b[2048:3072]
    wsT = np.ascontiguousarray(inp["sgu_w_s"].transpose(3, 0, 1, 2)).reshape(128, L * 1024)
    bsbc = np.ascontiguousarray(np.broadcast_to(inp["sgu_b_s"].reshape(1, L * 1024), (128, L * 1024)))
    si = np.arange(128)
    cmask = (si[:, None] <= si[None, :]).astype(np.float32)

    in_maps = []
    for c in range(NCORES):
        b, half = c // 2, c % 2
        s0 = half * OWN
        xt = np.zeros((NTOK, D), np.float32)
        pt = np.zeros((L, NTOK, 256), np.float32)
        xt[HALO:] = x[b, s0:s0 + OWN]
        pt[:, HALO:] = p[:, b, s0:s0 + OWN]
        pcore = np.zeros((128, 80), np.float32)
        if half == 1:
            xt[:HALO] = x[b, s0 - HALO:s0]
            pt[:, :HALO] = p[:, b, s0 - HALO:s0]
            pcore[:, 0] = 1.0
        for g, w in enumerate(POOL_WINDOWS):
            j = np.arange(16)
            cnt = np.minimum(j + 1, w) if half == 0 else np.full(16, w)
            pcore[:, 1 + g * 16: 1 + (g + 1) * 16] = (1.0 / cnt.astype(np.float32))[None, :]
        xTc = np.ascontiguousarray(xt.T.reshape(8, 128, NTOK).transpose(1, 0, 2))
        pTc = np.ascontiguousarray(pt.transpose(0, 2, 1).reshape(L, 2, 128, NTOK).transpose(0, 2, 1, 3))
        in_maps.append({"xT": xTc, "pT": pTc, "wst": wst, "cvec": cvec, "binfm": binfm, "bvrow": bvrow,
                        "wsT": wsT, "bsbc": bsbc, "cmask": cmask, "pcore": pcore})
    return in_maps


def kernel(**inputs):
    inp = {k: np.asarray(v, dtype=np.float32) for k, v in inputs.items()}
    B, S, _ = inp["x"].shape
    in_maps = make_in_maps(inp)
    if "nc" not in _NC_CACHE:
        _NC_CACHE["nc"] = build_nc()
    nc = _NC_CACHE["nc"]
    res = run_bass_kernel_spmd(nc, in_maps, core_ids=list(range(NCORES)))
    out = np.empty((B, S, D), np.float32)
    for c in range(NCORES):
        b, half = c // 2, c % 2
        o = np.asarray(res.results[c]["outT"], dtype=np.float32)
        out[b, half * OWN:(half + 1) * OWN, :] = o.transpose(1, 0, 2).reshape(D, OWN).T
    return out
```

```python
import numpy as np
import concourse.bass as bass
import concourse.mybir as mybir
from concourse.bass_utils import run_bass_kernel_spmd

F32 = mybir.dt.float32
BF16 = mybir.dt.bfloat16
AF = mybir.ActivationFunctionType
ALU = mybir.AluOpType

D = 1024
L = 2
NCORES = 8
OWN = 2048
HALO = 128
NTOK = OWN + HALO
TILES = [(0, 640), (640, 512), (1152, 512), (1664, 512)]
NTMAX = 640
ZW = 16 + NTMAX
EPS = 1e-6
NBLK = 36
NWBUF = 4
BLK = 4096
V_PREMIX, V_POSTMIX, V_PREFFN, V_POSTFFN, V_POSTPLE, V_PSCALE, V_LNG, V_LNB = range(8)
POOL_WINDOWS = (2, 4, 8, 16)
POOL_CHUNKS = ()
RES_ORDER = (0, 1, 2, 3, 4, 5, 6, 7)


class Res:
    __slots__ = ("w", "rs", "const")

    def __init__(self, const=False):
        self.w = None
        self.rs = {}
        self.const = const


class Prog:
    ENG = ("pe", "act", "dve", "pool", "sp")

    def __init__(self):
        self.q = {k: [] for k in self.ENG}
        self.semh = {}
        self.cnt = {}
        self.seen = {k: {} for k in self.ENG}

    def add_sem(self, key, handle):
        self.semh[key] = handle
        self.cnt[key] = 0

    def _wait(self, eng, toks):
        need = {}
        for t in toks:
            if t is None:
                continue
            key, val = t
            if key == "pe" and eng == "pe":
                continue
            if self.seen[eng].get(key, 0) >= val:
                continue
            if need.get(key, 0) < val:
                need[key] = val
        for key, val in need.items():
            self.seen[eng][key] = val
            s = self.semh[key]
            self.q[eng].append(lambda e, s=s, val=val: e.wait_ge(s, val))

    @staticmethod
    def _deps(reads, writes):
        deps = []
        for r in reads:
            deps.append(r.w)
        for w in writes:
            deps.append(w.w)
            deps.extend(w.rs.items())
        return deps

    @staticmethod
    def _commit(tok, reads, writes):
        for r in reads:
            if not r.const:
                if r.rs.get(tok[0], 0) < tok[1]:
                    r.rs[tok[0]] = tok[1]
        for w in writes:
            w.w = tok
            w.rs = {}

    def op(self, eng, fn, reads=(), writes=()):
        self._wait(eng, self._deps(reads, writes))
        self.cnt[eng] += 1
        tok = (eng, self.cnt[eng])
        s = self.semh[eng]
        self.q[eng].append(lambda e, fn=fn, s=s: fn(e).then_inc(s, 1))
        self._commit(tok, reads, writes)
        return tok

    def group(self, eng, fns, reads=(), writes=()):
        self._wait(eng, self._deps(reads, writes))
        self.cnt[eng] += 1
        tok = (eng, self.cnt[eng])
        s = self.semh[eng]
        for f in fns[:-1]:
            self.q[eng].append(f)
        last = fns[-1]
        self.q[eng].append(lambda e, fn=last, s=s: fn(e).then_inc(s, 1))
        self._commit(tok, reads, writes)
        return tok

    def dma(self, eng, semkey, fn, reads=(), writes=()):
        self._wait(eng, self._deps(reads, writes))
        self.cnt[semkey] += 16
        tok = (semkey, self.cnt[semkey])
        s = self.semh[semkey]
        self.q[eng].append(lambda e, fn=fn, s=s: fn(e).then_inc(s, 16))
        self._commit(tok, reads, writes)
        return tok

    def final_wait(self, eng, toks):
        self._wait(eng, toks)


def R(n, const=False):
    return [Res(const) for _ in range(n)]


def build_nc(TILES=TILES, L_RUN=L, NOUT=OWN, DUMPS=()):
    nc = bass.Bass("TRN2", target_bir_lowering=False)
    xT = nc.dram_tensor("xT", [128, 8, NTOK], F32, kind="ExternalInput").ap()
    pT = nc.dram_tensor("pT", [L, 128, 2, NTOK], F32, kind="ExternalInput").ap()
    wst = nc.dram_tensor("wst", [L, NBLK, 128, BLK], F32, kind="ExternalInput").ap()
    cvec_d = nc.dram_tensor("cvec", [128, L * 64], F32, kind="ExternalInput").ap()
    binfm_d = nc.dram_tensor("binfm", [128, L * 32], F32, kind="ExternalInput").ap()
    bvrow_d = nc.dram_tensor("bvrow", [L, 1024], F32, kind="ExternalInput").ap()
    wsT_d = nc.dram_tensor("wsT", [128, L * 1024], F32, kind="ExternalInput").ap()
    bsbc_d = nc.dram_tensor("bsbc", [128, L * 1024], F32, kind="ExternalInput").ap()
    cmask_d = nc.dram_tensor("cmask", [128, 128], F32, kind="ExternalInput").ap()
    pcore_d = nc.dram_tensor("pcore", [128, 80], F32, kind="ExternalInput").ap()
    outT = nc.dram_tensor("outT", [128, 8, NOUT], F32, kind="ExternalOutput").ap()

    dump_d = {}
    for (nm, shp, dt) in DUMPS:
        dump_d[nm] = nc.dram_tensor("dbg_" + nm, shp, dt, kind="ExternalOutput").ap()
    P = Prog()
    from contextlib import ExitStack
    with ExitStack() as es:
        def sb(name, shape, dt):
            return es.enter_context(nc.sbuf_tensor(name, shape, dt))

        X = sb("X", [128, 8, ZW], F32)
        H = sb("H", [128, 8, NTMAX], BF16)
        O = sb("O", [128, 8, NTMAX], F32)
        Z = sb("Z", [128, 8, ZW], F32)
        S2 = sb("S2", [128, 2, ZW], F32)
        BB = sb("BB", [128, 6, 8 * NTMAX], BF16)
        WB = sb("WB", [128, NWBUF, BLK], BF16)
        PB = sb("PB", [128, 2, NTMAX], BF16)
        RB = sb("RB", [128, NTMAX], F32)
        TMP = sb("TMP", [128, 2, ZW], F32)
        S1 = TMP
        ZH = sb("ZH", [128, L, 8, 16], F32)
        T16 = sb("T16", [128, 2, 16], F32)
        ST = sb("ST", [128, 2, 6], F32)
        MVA = sb("MVA", [128, 8, 2], F32)
        RSTD = sb("RSTD", [128, 8], F32)
        EPSC = sb("EPSC", [128, 1], F32)
        NMR = sb("NMR", [128, 8], F32)
        EPSP = sb("EPSP", [128, NTMAX], F32)
        CV = sb("CV", [128, L * 64], F32)
        BIN = sb("BIN", [128, L * 32], F32)
        VBH = sb("VBH", [33, 1024], BF16)
        VBL = sb("VBL", [33, 1024], BF16)
        VB2 = sb("VB2", [2, L, 1024], BF16)
        WSB = sb("WSB", [128, L * 1024], BF16)
        BF = sb("BF", [128, L * 1024], F32)
        CM = sb("CM", [128, 128], F32)
        PC = sb("PC", [128, 80], F32)
        ONEB = sb("ONEB", [128, 128], BF16)
        ONEF = sb("ONEF", [128, 128], F32)
        PS = es.enter_context(nc.psum_tensor("PS", [128, 8, 512], F32))
        WSF = BB[:, 3, 0:2 * L * 1024].bitcast(F32)
        BSB = BB[:, 5, 0:2 * L * 1024].bitcast(F32)

        for key in ("pe", "act", "dve", "pool", "sp", "cst", "xld", "ost", "pld", "dbg") + tuple(
                "w%d" % i for i in range(NWBUF)):
            P.add_sem(key, es.enter_context(nc.semaphore("s_" + key)))

        rX, rH, rO, rZ = R(8), R(8), R(8), R(8)
        rB = [R(8) for _ in range(6)]
        rS2 = Res()
        rW = R(NWBUF)
        rPB, rRB = Res(), Res()
        rTMP = R(2)
        rS1 = rTMP
        rZH = R(L)
        rT16, rST, rMV, rRSTD, rNMR, rEPSP = Res(), Res(), Res(), Res(), Res(), Res()
        rBT = R(8)
        rV = R(5)
        rPS = R(8)
        rC = Res()
        rVB2 = Res()
        rWSB, rBF = Res(), Res()

        def Bv(i):
            return BB[:, i, :].rearrange("p (c t) -> p c t", c=8)

        U, GA, GB, NB_, PL, MS = (Bv(i) for i in range(6))
        SQ = PL
        rU, rGA, rGB, rN, rPL, rMS = rB
        rSQ = rPL
        NTM = BB[:, 3, :]
        ACTB = BB[:, 0:4, :].rearrange("p a (c t) -> p (a c) t", c=8)
        rACT = rB[0] + rB[1] + rB[2] + rB[3]
        VTM = O[:, :, :].rearrange("p c t -> p (c t)")

        psn = [0]

        def banks(n):
            b = psn[0]
            if b + n > 8:
                b = 0
            psn[0] = (b + n) % 8
            return list(range(b, b + n))

        stream = []
        for (t0, nt) in TILES:
            for l in range(L_RUN):
                for j in range(NBLK):
                    stream.append((l, j))
        wstate = {"issued": 0, "cons": 0}

        def blk_len(j):
            return 2048 if j in (10, 35) else BLK

        def w_issue():
            i = wstate["issued"]
            if i >= len(stream):
                return
            l, j = stream[i]
            b = i % NWBUF
            n = blk_len(j)
            P.dma("pool", "w%d" % b,
                  lambda e, b=b, l=l, j=j, n=n: e.dma_start(out=WB[:, b, 0:n], in_=wst[l, j, :, 0:n]),
                  writes=[rW[b]])
            wstate["issued"] += 1

        def w_next():
            i = wstate["cons"]
            wstate["cons"] += 1
            return i % NWBUF

        def w_done():
            w_issue()

        cst = []
        for (dst, src) in ((CV[:], cvec_d), (BIN[:], binfm_d), (WSF, wsT_d), (BSB, bsbc_d),
                           (CM[:], cmask_d), (PC[:], pcore_d)):
            nd = len(dst.shape)
            P.dma("sp", "cst", lambda e, dst=dst, src=src: e.dma_start(out=dst, in_=src), writes=[rC])
        for _ in range(NWBUF):
            w_issue()

        BVR = VTM[0:33, 0:1024]
        BVT = VTM[0:33, 1024:2048]
        P.op("dve", lambda e: e.memset(VTM[0:33, 0:2048], 0.0), writes=rO)
        for l in range(L):
            P.dma("sp", "cst", lambda e, l=l: e.dma_start(out=VTM[32 * l:32 * l + 1, 0:1024], in_=bvrow_d[l:l + 1, :]),
                  writes=[rC] + rO)
        P.op("dve", lambda e: e.memset(ONEB[:], 1.0), writes=[rC])
        P.op("dve", lambda e: e.memset(ONEF[:], 1.0), writes=[rC])
        P.op("dve", lambda e: e.memset(EPSC[:], EPS), writes=[rC])
        P.op("dve", lambda e: e.tensor_copy(out=VBH[:], in_=BVR), reads=[rC], writes=[rC])
        P.op("dve", lambda e: e.tensor_copy(out=BVT, in_=VBH[:]), reads=[rC], writes=[rC])
        P.op("dve", lambda e: e.tensor_tensor(out=BVT, in0=BVR, in1=BVT, op=ALU.subtract),
             reads=[rC], writes=[rC] + rO)
        P.op("dve", lambda e: e.tensor_copy(out=VBL[:], in_=BVT), reads=[rC], writes=[rC] + rO)
        for l in range(L):
            P.dma("sp", "cst", lambda e, l=l: e.dma_start(out=VB2[0:1, l, :], in_=VBH[32 * l:32 * l + 1, :]),
                  reads=[rC], writes=[rVB2])
            P.dma("sp", "cst", lambda e, l=l: e.dma_start(out=VB2[1:2, l, :], in_=VBL[32 * l:32 * l + 1, :]),
                  reads=[rC], writes=[rVB2])
        def setup_sgu():
            for l in range(L):
                for h in range(8):
                    sl = slice(l * 1024 + h * 128, l * 1024 + (h + 1) * 128)
                    P.op("dve", lambda e, sl=sl: e.tensor_tensor(out=WSF[:, sl], in0=WSF[:, sl], in1=CM[:], op=ALU.mult),
                         reads=[rC], writes=rB[3])
                P.op("dve", lambda e, l=l: e.tensor_copy(out=WSB[:, l * 1024:(l + 1) * 1024],
                                                         in_=WSF[:, l * 1024:(l + 1) * 1024]),
                     reads=rB[3], writes=[rWSB])
                for hh in range(2):
                    bk = banks(1)[0]
                    P.group("pe", [lambda e, bk=bk, l=l, hh=hh: e.matmul(
                        PS[:, bk, :], ONEF[:], WSF[:, l * 1024 + hh * 512: l * 1024 + (hh + 1) * 512],
                        start=True, stop=True)], reads=[rC] + rB[3], writes=[rPS[bk]])
                    for h4 in range(4):
                        h = hh * 4 + h4
                        sl = slice(l * 1024 + h * 128, l * 1024 + (h + 1) * 128)
                        col = (l * 8 + V_LNB) * 8 + h
                        P.op("dve", lambda e, bk=bk, h4=h4, sl=sl, col=col: e.scalar_tensor_tensor(
                            out=BF[:, sl], in0=PS[:, bk, h4 * 128:(h4 + 1) * 128], scalar=CV[:, col:col + 1],
                            in1=BSB[:, sl], op0=ALU.mult, op1=ALU.add),
                            reads=[rPS[bk], rC] + rB[5], writes=[rBF])

        setup_sgu()
        rWSB.const = True
        rBF.const = True
        rC_done = rC.w
        rC.const = True

        def cv(l, vi, c):
            col = (l * 8 + vi) * 8 + c
            return CV[:, col:col + 1]

        dump_toks = []

        def dump(nm, ap, res):
            if nm in dump_d and nm not in [d[0] for d in dump_toks]:
                dump_toks.append((nm, P.dma("sp", "dbg", lambda e: e.dma_start(out=dump_d[nm], in_=ap), reads=res)))

        def colblocks(ca, nt):
            cbs = []
            c = ca
            while c < nt:
                w = min(512, nt - c)
                cbs.append((c, w))
                c += w
            return cbs

        GF = BB[:, 0:2, :].rearrange("p a t -> p (a t)").bitcast(F32).rearrange("p (c t) -> p c t", c=8)
        rG = [[rB[oc // 4][2 * (oc % 4)], rB[oc // 4][2 * (oc % 4) + 1]] for oc in range(8)]

        def tile_layer(ti, t0, nt, l, X, rX, Z, rZ, prefetch):
            last = (l == L_RUN - 1)
            ca = HALO if (ti == 0 and last) else 0
            cbs = colblocks(ca, nt)
            cbs_all = colblocks(0, nt)
            nch = nt // 128
            clo = ca // 128

            def norm_R(sq_res, mode="rsqrt", cbl=None):
                for (c0, w) in (cbl or cbs):
                    bk = banks(1)[0]
                    for dc in range(8):
                        P.group("pe", [lambda e, bk=bk, dc=dc, c0=c0, w=w: e.matmul(
                            PS[:, bk, 0:w], ONEB[:], SQ[:, dc, c0:c0 + w], start=(dc == 0), stop=(dc == 7))],
                            reads=[rC, sq_res[dc]], writes=[rPS[bk]])
                    if mode == "epsp":
                        P.op("dve", lambda e, bk=bk, c0=c0, w=w: e.tensor_scalar(
                            out=EPSP[:, c0:c0 + w], in0=PS[:, bk, 0:w], scalar1=1.0 / D, scalar2=EPS,
                            op0=ALU.mult, op1=ALU.add), reads=[rPS[bk]], writes=[rEPSP])
                        P.op("dve", lambda e, c0=c0, w=w: e.scalar_tensor_tensor(
                            out=EPSP[:, c0:c0 + w], in0=EPSP[:, c0:c0 + w], scalar=EPS, in1=EPSP[:, c0:c0 + w],
                            op0=ALU.mult, op1=ALU.mult), reads=[rEPSP], writes=[rEPSP])
                        continue
                    if mode == "rsqrt_epsp":
                        P.op("dve", lambda e, bk=bk, c0=c0, w=w: e.scalar_tensor_tensor(
                            out=RB[:, c0:c0 + w], in0=PS[:, bk, 0:w], scalar=1.0 / D, in1=EPSP[:, c0:c0 + w],
                            op0=ALU.mult, op1=ALU.add), reads=[rPS[bk], rEPSP], writes=[rRB])
                        P.op("act", lambda e, c0=c0, w=w: e.activation(
                            out=RB[:, c0:c0 + w], in_=RB[:, c0:c0 + w], func=AF.Ln), reads=[rRB], writes=[rRB])
                    else:
                        P.op("act", lambda e, bk=bk, c0=c0, w=w: e.activation(
                            out=RB[:, c0:c0 + w], in_=PS[:, bk, 0:w], func=AF.Ln, bias=EPSC[:, 0:1], scale=1.0 / D),
                            reads=[rPS[bk], rC], writes=[rRB])
                    P.op("act", lambda e, c0=c0, w=w: e.activation(
                        out=RB[:, c0:c0 + w], in_=RB[:, c0:c0 + w], func=AF.Exp, scale=-0.5),
                        reads=[rRB], writes=[rRB])

            def squares_of_X(a, b_):
                for dc in range(8):
                    P.op("act", lambda e, dc=dc: e.activation(out=SQ[:, dc, a:b_], in_=X[:, dc, a:b_], func=AF.Square),
                         reads=[rX[dc]], writes=[rSQ[dc]])

            def residual(vi, hmode=None, hvi=None, gained=False):
                n = nt - ca
                step = 2 if gained else 1
                for d0 in range(0, 8, step):
                    dcs = list(range(d0, d0 + step))
                    if gained:
                        P.op("dve", lambda e, d0=d0: e.tensor_tensor(
                            out=O[:, d0:d0 + 2, ca:nt], in0=O[:, d0:d0 + 2, ca:nt],
                            in1=RB[:, ca:nt].unsqueeze(1).broadcast_to([128, 2, n]), op=ALU.mult),
                            reads=[rO[d] for d in dcs] + [rRB], writes=[rO[d] for d in dcs])
                        P.op("dve", lambda e, d0=d0: e.tensor_tensor(
                            out=X[:, d0:d0 + 2, ca:nt], in0=X[:, d0:d0 + 2, ca:nt], in1=O[:, d0:d0 + 2, ca:nt],
                            op=ALU.add), reads=[rO[d] for d in dcs] + [rX[d] for d in dcs],
                            writes=[rX[d] for d in dcs])
                    else:
                        dc = d0
                        P.op("dve", lambda e, dc=dc: e.tensor_tensor(
                            out=O[:, dc, ca:nt], in0=O[:, dc, ca:nt], in1=RB[:, ca:nt], op=ALU.mult),
                            reads=[rO[dc], rRB], writes=[rO[dc]])
                        P.op("dve", lambda e, dc=dc: e.scalar_tensor_tensor(
                            out=X[:, dc, ca:nt], in0=O[:, dc, ca:nt], scalar=cv(l, vi, dc), in1=X[:, dc, ca:nt],
                            op0=ALU.mult, op1=ALU.add), reads=[rO[dc], rX[dc], rC], writes=[rX[dc]])
                    for dc in dcs:
                        if hmode == "gain":
                            P.op("act", lambda e, dc=dc: e.activation(
                                out=H[:, dc, ca:nt], in_=X[:, dc, ca:nt], func=AF.Identity, scale=cv(l, hvi, dc)),
                                reads=[rX[dc], rC], writes=[rH[dc]])
                        elif hmode == "copy":
                            P.op("act", lambda e, dc=dc: e.activation(
                                out=H[:, dc, ca:nt], in_=X[:, dc, ca:nt], func=AF.Copy),
                                reads=[rX[dc]], writes=[rH[dc]])

            def fm_matmul(b, rhs, rhs_res, nk, evac, fis=range(4), kstride=512, cbl=None, kouter=False, perk=False):
                cbl = cbl or cbs
                fis = list(fis)
                if kouter:
                    per = max(1, 4 // len(cbl))
                    for s0 in range(0, len(fis), per):
                        sub = fis[s0:s0 + per]
                        bkm = {fi: banks(len(cbl)) for fi in sub}
                        allb = [bk for fi in sub for bk in bkm[fi]]
                        for k in range(nk):
                            fns = []
                            for fi in sub:
                                for ci, (c0, w) in enumerate(cbl):
                                    fns.append(lambda e, bk=bkm[fi][ci], k=k, fi=fi, c0=c0, w=w: e.matmul(
                                        PS[:, bk, 0:w], WB[:, b, k * kstride + fi * 128: k * kstride + (fi + 1) * 128],
                                        rhs[:, k, c0:c0 + w], start=(k == 0), stop=(k == nk - 1)))
                            P.group("pe", fns, reads=[rW[b], rhs_res[k]], writes=[rPS[bk] for bk in allb])
                        for fi in sub:
                            for ci, (c0, w) in enumerate(cbl):
                                evac(fi, c0, w, bkm[fi][ci])
                    return
                for idx, fi in enumerate(fis):
                    bks = banks(len(cbl))
                    fns = []
                    for k in range(nk):
                        fk = []
                        for ci, (c0, w) in enumerate(cbl):
                            fk.append(lambda e, bk=bks[ci], k=k, fi=fi, c0=c0, w=w: e.matmul(
                                PS[:, bk, 0:w], WB[:, b, k * kstride + fi * 128: k * kstride + (fi + 1) * 128],
                                rhs[:, k, c0:c0 + w], start=(k == 0), stop=(k == nk - 1)))
                        if perk and idx == 0:
                            P.group("pe", fk, reads=[rW[b], rhs_res[k]], writes=[rPS[bk] for bk in bks])
                        else:
                            fns.extend(fk)
                    if fns:
                        P.group("pe", fns, reads=[rW[b]] + rhs_res, writes=[rPS[bk] for bk in bks])
                    for ci, (c0, w) in enumerate(cbl):
                        evac(fi, c0, w, bks[ci])

            P.dma("pool", "pld", lambda e: e.dma_start(out=PB[:, :, 0:nt], in_=pT[l, :, :, t0:t0 + nt]),
                  writes=[rPB])

            squares_of_X(0, nt)
            norm_R(rSQ, cbl=cbs_all)
            for dc in range(8):
                P.op("dve", lambda e, dc=dc: e.scalar_tensor_tensor(
                    out=H[:, dc, 0:nt], in0=X[:, dc, 0:nt], scalar=cv(l, V_PREMIX, dc), in1=RB[:, 0:nt],
                    op0=ALU.mult, op1=ALU.mult), reads=[rX[dc], rRB, rC], writes=[rH[dc]])

            def w_in_group(dst, dres, func, bcol, off, cbl, first):
                for half in range(2):
                    b = w_next()

                    def evac(fi, c0, w, bk, half=half):
                        fc = half * 4 + fi
                        col = l * 32 + bcol + fc
                        P.op("act", lambda e: e.activation(
                            out=dst[:, fc, off + c0: off + c0 + w], in_=PS[:, bk, 0:w], func=func,
                            bias=BIN[:, col:col + 1]), reads=[rPS[bk], rC], writes=[dres[fc]])
                    fm_matmul(b, H, rH, 8, evac, cbl=cbl, kouter=(first and half == 0))
                    w_done()

            w_in_group(Z, rZ, AF.Identity, 0, 16, cbs_all, True)
            W_ = 16 + nt

            def pool_prep():
                if ti == 0:
                    P.op("dve", lambda e: e.memset(Z[:, :, 0:16], 0.0), writes=rZ)
                    P.op("dve", lambda e: e.tensor_scalar(
                        out=Z[:, :, 16:16 + HALO], in0=Z[:, :, 16:16 + HALO], scalar1=PC[:, 0:1], scalar2=None,
                        op0=ALU.mult), reads=rZ + [rC], writes=rZ)
                else:
                    P.op("dve", lambda e: e.tensor_copy(out=Z[:, :, 0:16], in_=ZH[:, l, :, :]),
                         reads=[rZH[l]], writes=rZ)

            def pool_groups(gs):
                for g in gs:
                    zs = Z[:, 2 * g:2 * g + 2, :]
                    zres = [rZ[2 * g], rZ[2 * g + 1]]
                    P.op("dve", lambda e, zs=zs: e.tensor_tensor(
                        out=S1[:, :, 1:W_], in0=zs[:, :, 1:W_], in1=zs[:, :, 0:W_ - 1], op=ALU.add),
                        reads=zres, writes=rS1)
                    cur, rcur = S1, rS1
                    if g >= 1:
                        P.op("dve", lambda e: e.tensor_tensor(
                            out=S2[:, :, 3:W_], in0=S1[:, :, 3:W_], in1=S1[:, :, 1:W_ - 2], op=ALU.add),
                            reads=rS1, writes=[rS2])
                        cur, rcur = S2, [rS2]
                    if g >= 2:
                        P.op("dve", lambda e: e.tensor_tensor(
                            out=S1[:, :, 7:W_], in0=S2[:, :, 7:W_], in1=S2[:, :, 3:W_ - 4], op=ALU.add),
                            reads=[rS2], writes=rS1)
                        cur, rcur = S1, rS1
                    if g >= 3:
                        P.op("dve", lambda e: e.tensor_tensor(
                            out=S2[:, :, 15:W_], in0=S1[:, :, 15:W_], in1=S1[:, :, 7:W_ - 8], op=ALU.add),
                            reads=rS1, writes=[rS2])
                        cur, rcur = S2, [rS2]
                    wdw = POOL_WINDOWS[g]
                    P.op("dve", lambda e, cur=cur, g=g, wdw=wdw: e.scalar_tensor_tensor(
                        out=PL[:, 2 * g:2 * g + 2, 0:nt], in0=cur[:, :, 16:16 + nt], scalar=1.0 / wdw,
                        in1=Z[:, 2 * g:2 * g + 2, 16:16 + nt], op0=ALU.mult, op1=ALU.subtract),
                        reads=rcur + zres, writes=[rPL[2 * g], rPL[2 * g + 1]])
                    if ti == 0:
                        for cc in range(2):
                            P.op("dve", lambda e, cur=cur, g=g, cc=cc: e.tensor_tensor(
                                out=T16[:, cc, :], in0=cur[:, cc, 16 + HALO:32 + HALO],
                                in1=PC[:, 1 + g * 16: 1 + (g + 1) * 16], op=ALU.mult),
                                reads=rcur + [rC], writes=[rT16])
                        P.op("dve", lambda e, g=g: e.tensor_tensor(
                            out=PL[:, 2 * g:2 * g + 2, HALO:HALO + 16], in0=T16[:, :, :],
                            in1=Z[:, 2 * g:2 * g + 2, 16 + HALO:32 + HALO], op=ALU.subtract),
                            reads=[rT16] + zres, writes=[rPL[2 * g], rPL[2 * g + 1]])

            def pool_finish():
                dump("Z", Z[:, :, :], rZ)
                dump("PL", PL, rPL)
                dump("H", H[:, :, :], rH)
                P.op("dve", lambda e: e.tensor_copy(out=ZH[:, l, :, :], in_=Z[:, :, nt:nt + 16]),
                     reads=rZ, writes=[rZH[l]])
                if last and prefetch is not None:
                    pt0, pnt = prefetch
                    P.dma("sp", "xld", lambda e: e.dma_start(out=Z[:, :, 0:pnt], in_=xT[:, :, pt0:pt0 + pnt]),
                          writes=rZ)


            pool_prep()
            pool_groups((0, 1))

            vb = [w_next(), w_next()]
            first_v = [True]
            for half in range(2):
                b = vb[half]
                for c in range(clo, nch):
                    bk = banks(1)[0]
                    fns = []
                    for k in range(8):
                        fns.append(lambda e, bk=bk, k=k, c=c, b=b: e.matmul(
                            PS[:, bk, :], H[:, k, c * 128:(c + 1) * 128], WB[:, b, k * 512:(k + 1) * 512],
                            start=(k == 0), stop=False))
                    vsl = slice(half * 512, (half + 1) * 512)
                    fns.append(lambda e, bk=bk, vsl=vsl: e.matmul(
                        PS[:, bk, :], ONEB[0:2, :], VB2[0:2, l, vsl], start=False, stop=True))
                    P.group("pe", fns, reads=[rW[b], rC, rVB2] + rH, writes=[rPS[bk]])
                    o0 = c * 1024 + half * 512
                    P.op("act", lambda e, bk=bk, o0=o0: e.activation(
                        out=VTM[:, o0:o0 + 512], in_=PS[:, bk, :], func=AF.Gelu_apprx_tanh),
                        reads=[rPS[bk]], writes=([rV[c]] + (rO if first_v[0] else [])))
                    first_v[0] = False
                w_done()
            for c in range(clo, nch):
                for half in range(2):
                    o0 = c * 1024 + half * 512
                    P.op("dve", lambda e, o0=o0, half=half: e.bn_stats(out=ST[:, half, :], in_=VTM[:, o0:o0 + 512]),
                         reads=[rV[c]], writes=[rST])
                P.op("dve", lambda e, c=c: e.bn_aggr(out=MVA[:, c, :], in_=ST[:, :, :].rearrange("p a b -> p (a b)")),
                     reads=[rST], writes=[rMV])

            P.op("act", lambda e: e.activation(out=RSTD[:, clo:nch], in_=MVA[:, clo:nch, 1], func=AF.Ln,
                                               bias=EPSC[:, 0:1], scale=1.0), reads=[rMV, rC], writes=[rRSTD])
            P.op("act", lambda e: e.activation(out=RSTD[:, clo:nch], in_=RSTD[:, clo:nch], func=AF.Exp, scale=-0.5),
                 reads=[rRSTD], writes=[rRSTD])
            P.op("dve", lambda e: e.scalar_tensor_tensor(
                out=NMR[:, clo:nch], in0=MVA[:, clo:nch, 0], scalar=-1.0, in1=RSTD[:, clo:nch],
                op0=ALU.mult, op1=ALU.mult), reads=[rMV, rRSTD], writes=[rNMR])
            for c in range(clo, nch):
                P.op("act", lambda e, c=c: e.activation(
                    out=NTM[:, c * 1024:(c + 1) * 1024], in_=VTM[:, c * 1024:(c + 1) * 1024], func=AF.Identity,
                    scale=RSTD[:, c:c + 1], bias=NMR[:, c:c + 1]), reads=rO + [rV[c], rRSTD, rNMR], writes=rN)
            w_in_group(U, rU, AF.Gelu_apprx_tanh, 8, 0, cbs, False)

            for h in range(8):
                for ci, (c0, w) in enumerate(cbs):
                    bk = banks(1)[0]
                    nck = w // 128
                    fns = []
                    for cc in range(nck):
                        c = c0 // 128 + cc
                        fns.append(lambda e, bk=bk, cc=cc, c=c, h=h: e.matmul(
                            PS[:, bk, cc * 128:(cc + 1) * 128], NTM[:, c * 1024 + h * 128: c * 1024 + (h + 1) * 128],
                            WSB[:, l * 1024 + h * 128: l * 1024 + (h + 1) * 128], start=True, stop=True))
                    P.group("pe", fns, reads=rN + [rC, rWSB], writes=[rPS[bk]])
                    tb = (h * len(cbs) + ci) % 2
                    bf = BF[:, l * 1024 + h * 128: l * 1024 + (h + 1) * 128]
                    P.op("dve", lambda e, bk=bk, nck=nck, w=w, tb=tb, bf=bf, h=h: e.scalar_tensor_tensor(
                        out=TMP[:, tb, 0:w].rearrange("p (a t) -> p a t", a=nck),
                        in0=PS[:, bk, 0:w].rearrange("p (a t) -> p a t", a=nck),
                        scalar=cv(l, V_LNG, h),
                        in1=bf.unsqueeze(1).broadcast_to([128, nck, 128]),
                        op0=ALU.mult, op1=ALU.add), reads=[rPS[bk], rC, rBF], writes=[rTMP[tb]])
                    P.op("dve", lambda e, tb=tb, h=h, c0=c0, w=w: e.tensor_tensor(
                        out=U[:, h, c0:c0 + w], in0=TMP[:, tb, 0:w], in1=U[:, h, c0:c0 + w], op=ALU.mult),
                        reads=[rTMP[tb], rU[h]], writes=[rU[h]])
            dump("SG", U, rU)
            dump("MS", MS, rMS)

            pool_groups((2, 3))
            pool_finish()
            w_in_group(GA, rGA, AF.Sigmoid, 16, 0, cbs, False)
            w_in_group(GB, rGB, AF.Sigmoid, 24, 0, cbs, False)

            b4 = w_next()

            def m4_groups(idx):
                for gi in idx:
                    g, dh = gi // 2, gi % 2
                    bks = banks(len(cbs))
                    fns = []
                    for cc in range(2):
                        for ci, (c0, w) in enumerate(cbs):
                            o0 = (g * 2 + cc) * 256 + dh * 128
                            fns.append(lambda e, bk=bks[ci], o0=o0, g=g, cc=cc, c0=c0, w=w: e.matmul(
                                PS[:, bk, 0:w], WB[:, b4, o0:o0 + 128], PL[:, 2 * g + cc, c0:c0 + w],
                                start=(cc == 0), stop=(cc == 1)))
                    P.group("pe", fns, reads=[rW[b4], rPL[2 * g], rPL[2 * g + 1]], writes=[rPS[bk] for bk in bks])
                    oc = 2 * g + dh
                    for ci, (c0, w) in enumerate(cbs):
                        if gi % 2 == 0:
                            P.op("dve", lambda e, bk=bks[ci], oc=oc, c0=c0, w=w: e.tensor_scalar(
                                out=MS[:, oc, c0:c0 + w], in0=PS[:, bk, 0:w], scalar1=cv(l, V_PSCALE, oc),
                                scalar2=None, op0=ALU.mult), reads=[rPS[bks[ci]], rC], writes=[rMS[oc]])
                        else:
                            P.op("act", lambda e, bk=bks[ci], oc=oc, c0=c0, w=w: e.activation(
                                out=MS[:, oc, c0:c0 + w], in_=PS[:, bk, 0:w], func=AF.Identity,
                                scale=cv(l, V_PSCALE, oc)), reads=[rPS[bks[ci]], rC], writes=[rMS[oc]])

            m4_groups(range(0, 8))
            w_done()
            dump("VTM", VTM, rO)
            dump("NTM", NTM, rN)

            for half in range(2):
                b = w_next()

                def evac(fi, c0, w, bk, half=half):
                    oc = half * 4 + fi
                    P.op("dve", lambda e: e.tensor_tensor(
                        out=GA[:, oc, c0:c0 + w], in0=PS[:, bk, 0:w], in1=GA[:, oc, c0:c0 + w], op=ALU.mult),
                        reads=[rPS[bk], rGA[oc]], writes=[rGA[oc]])
                fm_matmul(b, MS, rMS, 8, evac, perk=(half == 0))
                w_done()
            dump("M1", GA, rGA)

            for half in range(2):
                b = w_next()

                def evac(fi, c0, w, bk, half=half):
                    oc = half * 4 + fi
                    P.op("dve", lambda e: e.tensor_tensor(
                        out=GB[:, oc, c0:c0 + w], in0=PS[:, bk, 0:w], in1=GB[:, oc, c0:c0 + w], op=ALU.mult),
                        reads=[rPS[bk], rGB[oc]], writes=[rGB[oc]])
                    P.op("dve", lambda e: e.tensor_tensor(
                        out=GB[:, oc, c0:c0 + w], in0=GB[:, oc, c0:c0 + w], in1=GA[:, oc, c0:c0 + w], op=ALU.add),
                        reads=[rGB[oc], rGA[oc]], writes=[rGB[oc]])
                fm_matmul(b, U, rU, 8, evac, perk=(half == 0))
                w_done()

            def evac_O(oc, c0, w, bk, vi):
                P.op("act", lambda e: e.activation(out=SQ[:, oc, c0:c0 + w], in_=PS[:, bk, 0:w], func=AF.Square),
                     reads=[rPS[bk]], writes=[rSQ[oc], rBT[bk]])
                P.op("dve", lambda e: e.tensor_scalar(out=O[:, oc, c0:c0 + w], in0=PS[:, bk, 0:w],
                                                      scalar1=cv(l, vi, oc), scalar2=None, op0=ALU.mult),
                     reads=[rPS[bk], rBT[bk], rC], writes=[rO[oc]])

            for half in range(2):
                b = w_next()
                fm_matmul(b, GB, rGB, 8, lambda fi, c0, w, bk, half=half: evac_O(half * 4 + fi, c0, w, bk, V_POSTMIX),
                          perk=(half == 0))
                w_done()
            dump("MG", GB, rGB)
            dump("O1", O[:, :, :], rO)

            norm_R(rSQ)
            residual(V_POSTMIX, hmode="gain", hvi=V_PREFFN, gained=True)
            dump("X1", X[:, :, :], rX)

            for j in range(8):
                b = w_next()

                def evac(fi, c0, w, bk, j=j):
                    fc = j * 4 + fi
                    tb = fi % 2
                    P.op("act", lambda e: e.activation(out=TMP[:, tb, c0:c0 + w], in_=PS[:, bk, 0:w], func=AF.Relu),
                         reads=[rPS[bk]], writes=[rTMP[tb]])
                    P.op("dve", lambda e: e.tensor_tensor(
                        out=ACTB[:, fc, c0:c0 + w], in0=PS[:, bk, 0:w], in1=TMP[:, tb, c0:c0 + w], op=ALU.mult),
                        reads=[rPS[bk], rTMP[tb]], writes=[rACT[fc]])
                fm_matmul(b, H, rH, 8, evac, kouter=(j == 0))
                w_done()
                if j == 0:
                    squares_of_X(ca, nt)
                    norm_R(rSQ, mode="epsp")

            for j in range(8):
                b = w_next()
                fm_matmul(b, ACTB, rACT, 32, lambda fi, c0, w, bk, j=j: evac_O(j, c0, w, bk, V_POSTFFN),
                          fis=[0], kstride=128, perk=(j == 0))
                w_done()

            norm_R(rSQ, mode="rsqrt_epsp")
            residual(V_POSTFFN, hmode="copy", gained=True)

            for half in range(2):
                b = w_next()

                def evac(fi, c0, w, bk, half=half):
                    oc = half * 4 + fi
                    P.op("act", lambda e: e.activation(
                        out=GF[:, oc, c0:c0 + w], in_=PS[:, bk, 0:w], func=AF.Sigmoid),
                        reads=[rPS[bk]], writes=rG[oc])
                fm_matmul(b, H, rH, 8, evac, kouter=(half == 0))
                w_done()

            bp = w_next()
            for oc in range(8):
                bks = banks(len(cbs))
                fns = []
                for kc in range(2):
                    for ci, (c0, w) in enumerate(cbs):
                        fns.append(lambda e, bk=bks[ci], kc=kc, oc=oc, c0=c0, w=w: e.matmul(
                            PS[:, bk, 0:w], WB[:, bp, kc * 1024 + oc * 128: kc * 1024 + (oc + 1) * 128],
                            PB[:, kc, c0:c0 + w], start=(kc == 0), stop=(kc == 1)))
                P.group("pe", fns, reads=[rW[bp], rPB], writes=[rPS[bk] for bk in bks])
                for ci, (c0, w) in enumerate(cbs):
                    bk = bks[ci]
                    P.op("dve", lambda e, bk=bk, oc=oc, c0=c0, w=w: e.tensor_tensor(
                        out=O[:, oc, c0:c0 + w], in0=PS[:, bk, 0:w], in1=GF[:, oc, c0:c0 + w], op=ALU.mult),
                        reads=[rPS[bk]] + rG[oc], writes=[rO[oc]])
                    P.op("act", lambda e, oc=oc, c0=c0, w=w: e.activation(
                        out=SQ[:, oc, c0:c0 + w], in_=O[:, oc, c0:c0 + w], func=AF.Square),
                        reads=[rO[oc]], writes=[rSQ[oc]])
            w_done()

            norm_R(rSQ)
            residual(V_POSTPLE)

        bufs = [(X, rX), (Z, rZ)]
        last_store = None
        P.dma("sp", "xld", lambda e: e.dma_start(out=X[:, :, 0:TILES[0][1]], in_=xT[:, :, 0:TILES[0][1]]), writes=rX)
        for ti, (t0, nt) in enumerate(TILES):
            (Xc, rXc), (Zc, rZc) = bufs[ti % 2], bufs[(ti + 1) % 2]
            prefetch = TILES[ti + 1] if ti + 1 < len(TILES) else None
            for l in range(L_RUN):
                tile_layer(ti, t0, nt, l, Xc, rXc, Zc, rZc, prefetch)
            s0 = HALO if ti == 0 else 0
            o0 = t0 + s0 - HALO
            n_out = nt - s0
            for dc in range(8):
                last_store = P.dma("sp", "ost", lambda e, s0=s0, o0=o0, n_out=n_out, Xc=Xc, dc=dc: e.dma_start(
                    out=outT[:, dc, o0:o0 + n_out], in_=Xc[:, dc, s0:s0 + n_out]), reads=[rXc[dc]])
            for r_ in rXc:
                r_.rs["ost"] = P.cnt["ost"]
        P.final_wait("sp", [last_store] + [d[1] for d in dump_toks])

        with nc.Block() as block:
            @block.tensor
            def _(e):
                for f in P.q["pe"]:
                    f(e)

            @block.scalar
            def _(e):
                for f in P.q["act"]:
                    f(e)

            @block.vector
            def _(e):
                for f in P.q["dve"]:
                    f(e)

            @block.gpsimd
            def _(e):
                for f in P.q["pool"]:
                    f(e)

            @block.sync
            def _(e):
                for f in P.q["sp"]:
                    f(e)
    return nc


def _fm8(v):
    return np.ascontiguousarray(v.reshape(8, 128).T)


def _blk_k512(W, col0):
    K = W.shape[0]
    return W[:, col0:col0 + 512].reshape(K // 128, 128, 512).transpose(1, 0, 2).reshape(128, -1)


def _build_wstream(inp):
    ws = np.zeros((L, NBLK, 128, BLK), np.float32)
    for l in range(L):
        w_in = inp["w_in"][l]
        order = [0, 512, 2048, 2560, 1024, 1536, 3072, 3584, 4096, 4608]
        for j, c0 in enumerate(order):
            ws[l, j] = _blk_k512(w_in, c0)
        ws[l, 10, :, :2048] = inp["pool_w"][l].reshape(4, 2, 128, 256).transpose(2, 0, 1, 3).reshape(128, 2048)
        for i, name in enumerate(("w_pa", "w_pb", "w_o")):
            for half in range(2):
                ws[l, 11 + 2 * i + half] = _blk_k512(inp[name][l], half * 512)
        for j in range(8):
            ws[l, 17 + j] = _blk_k512(inp["w_ff1"][l], j * 512)
        w2 = inp["w_ff2"][l]
        for j in range(8):
            ws[l, 25 + j] = w2[:, j * 128:(j + 1) * 128].reshape(32, 128, 128).transpose(1, 0, 2).reshape(128, BLK)
        for half in range(2):
            ws[l, 33 + half] = _blk_k512(inp["w_ple_gate"][l], half * 512)
        ws[l, 35, :, :2048] = inp["w_ple_proj"][l].reshape(2, 128, 1024).transpose(1, 0, 2).reshape(128, 2048)
    return ws


_NC_CACHE = {}


def make_in_maps(inp):
    x, p = inp["x"], inp["p"]
    B, S, _ = x.shape
    wst = _build_wstream(inp)
    cvec = np.zeros((128, L * 64), np.float32)
    names = ["pre_mix_g", "post_mix_g", "pre_ffn_g", "post_ffn_g", "post_ple_g", "pool_scale", "sgu_ln_g", "sgu_ln_b"]
    for l in range(L):
        for vi, nm in enumerate(names):
            cvec[:, (l * 8 + vi) * 8:(l * 8 + vi + 1) * 8] = _fm8(inp[nm][l])
    binfm = np.zeros((128, L * 32), np.float32)
    bvrow = np.zeros((L, 1024), np.float32)
    for l in range(L):
        b = inp["b_in"][l]
        for gi, c0 in enumerate((0, 1024, 3072, 4096)):
            binfm[:, l * 32 + gi * 8: l * 32 + (gi + 1) * 8] = _fm8(b[c0:c0 + 1024])
        bvrow[l] = b[2048:3072]
    wsT = np.ascontiguousarray(inp["sgu_w_s"].transpose(3, 0, 1, 2)).reshape(128, L * 1024)
    bsbc = np.ascontiguousarray(np.broadcast_to(inp["sgu_b_s"].reshape(1, L * 1024), (128, L * 1024)))
    si = np.arange(128)
    cmask = (si[:, None] <= si[None, :]).astype(np.float32)

    in_maps = []
    for c in range(NCORES):
        b, half = c // 2, c % 2
        s0 = half * OWN
        xt = np.zeros((NTOK, D), np.float32)
        pt = np.zeros((L, NTOK, 256), np.float32)
        xt[HALO:] = x[b, s0:s0 + OWN]
        pt[:, HALO:] = p[:, b, s0:s0 + OWN]
        pcore = np.zeros((128, 80), np.float32)
        if half == 1:
            xt[:HALO] = x[b, s0 - HALO:s0]
            pt[:, :HALO] = p[:, b, s0 - HALO:s0]
            pcore[:, 0] = 1.0
        for g, w in enumerate(POOL_WINDOWS):
            j = np.arange(16)
            cnt = np.minimum(j + 1, w) if half == 0 else np.full(16, w)
            pcore[:, 1 + g * 16: 1 + (g + 1) * 16] = (1.0 / cnt.astype(np.float32))[None, :]
        xTc = np.ascontiguousarray(xt.T.reshape(8, 128, NTOK).transpose(1, 0, 2))
        pTc = np.ascontiguousarray(pt.transpose(0, 2, 1).reshape(L, 2, 128, NTOK).transpose(0, 2, 1, 3))
        in_maps.append({"xT": xTc, "pT": pTc, "wst": wst, "cvec": cvec, "binfm": binfm, "bvrow": bvrow,
                        "wsT": wsT, "bsbc": bsbc, "cmask": cmask, "pcore": pcore})
    return in_maps


def kernel(**inputs):
    inp = {k: np.asarray(v, dtype=np.float32) for k, v in inputs.items()}
    B, S, _ = inp["x"].shape
    in_maps = make_in_maps(inp)
    if "nc" not in _NC_CACHE:
        _NC_CACHE["nc"] = build_nc()
    nc = _NC_CACHE["nc"]
    res = run_bass_kernel_spmd(nc, in_maps, core_ids=list(range(NCORES)))
    out = np.empty((B, S, D), np.float32)
    for c in range(NCORES):
        b, half = c // 2, c % 2
        o = np.asarray(res.results[c]["outT"], dtype=np.float32)
        out[b, half * OWN:(half + 1) * OWN, :] = o.transpose(1, 0, 2).reshape(D, OWN).T
    return out
```

```python
import numpy as np
import concourse.bass as bass
import concourse.mybir as mybir
from concourse.bass_utils import run_bass_kernel_spmd

F32 = mybir.dt.float32
BF16 = mybir.dt.bfloat16
AF = mybir.ActivationFunctionType
ALU = mybir.AluOpType

D = 1024
L = 2
NCORES = 8
OWN = 2048
HALO = 128
NTOK = OWN + HALO
TILES = [(0, 640), (640, 512), (1152, 512), (1664, 512)]
NTMAX = 640
ZW = 16 + NTMAX
EPS = 1e-6
NBLK = 36
NWBUF = 4
BLK = 4096
V_PREMIX, V_POSTMIX, V_PREFFN, V_POSTFFN, V_POSTPLE, V_PSCALE, V_LNG, V_LNB = range(8)
POOL_WINDOWS = (2, 4, 8, 16)
POOL_CHUNKS = ()
RES_ORDER = (0, 1, 2, 3, 4, 5, 6, 7)


class Res:
    __slots__ = ("w", "rs", "const")

    def __init__(self, const=False):
        self.w = None
        self.rs = {}
        self.const = const


class Prog:
    ENG = ("pe", "act", "dve", "pool", "sp")

    def __init__(self):
        self.q = {k: [] for k in self.ENG}
        self.semh = {}
        self.cnt = {}
        self.seen = {k: {} for k in self.ENG}

    def add_sem(self, key, handle):
        self.semh[key] = handle
        self.cnt[key] = 0

    def _wait(self, eng, toks):
        need = {}
        for t in toks:
            if t is None:
                continue
            key, val = t
            if key == "pe" and eng == "pe":
                continue
            if self.seen[eng].get(key, 0) >= val:
                continue
            if need.get(key, 0) < val:
                need[key] = val
        for key, val in need.items():
            self.seen[eng][key] = val
            s = self.semh[key]
            self.q[eng].append(lambda e, s=s, val=val: e.wait_ge(s, val))

    @staticmethod
    def _deps(reads, writes):
        deps = []
        for r in reads:
            deps.append(r.w)
        for w in writes:
            deps.append(w.w)
            deps.extend(w.rs.items())
        return deps

    @staticmethod
    def _commit(tok, reads, writes):
        for r in reads:
            if not r.const:
                if r.rs.get(tok[0], 0) < tok[1]:
                    r.rs[tok[0]] = tok[1]
        for w in writes:
            w.w = tok
            w.rs = {}

    def op(self, eng, fn, reads=(), writes=()):
        self._wait(eng, self._deps(reads, writes))
        self.cnt[eng] += 1
        tok = (eng, self.cnt[eng])
        s = self.semh[eng]
        self.q[eng].append(lambda e, fn=fn, s=s: fn(e).then_inc(s, 1))
        self._commit(tok, reads, writes)
        return tok

    def group(self, eng, fns, reads=(), writes=()):
        self._wait(eng, self._deps(reads, writes))
        self.cnt[eng] += 1
        tok = (eng, self.cnt[eng])
        s = self.semh[eng]
        for f in fns[:-1]:
            self.q[eng].append(f)
        last = fns[-1]
        self.q[eng].append(lambda e, fn=last, s=s: fn(e).then_inc(s, 1))
        self._commit(tok, reads, writes)
        return tok

    def dma(self, eng, semkey, fn, reads=(), writes=()):
        self._wait(eng, self._deps(reads, writes))
        self.cnt[semkey] += 16
        tok = (semkey, self.cnt[semkey])
        s = self.semh[semkey]
        self.q[eng].append(lambda e, fn=fn, s=s: fn(e).then_inc(s, 16))
        self._commit(tok, reads, writes)
        return tok

    def final_wait(self, eng, toks):
        self._wait(eng, toks)


def R(n, const=False):
    return [Res(const) for _ in range(n)]


def build_nc(TILES=TILES, L_RUN=L, NOUT=OWN, DUMPS=()):
    nc = bass.Bass("TRN2", target_bir_lowering=False)
    xT = nc.dram_tensor("xT", [128, 8, NTOK], F32, kind="ExternalInput").ap()
    pT = nc.dram_tensor("pT", [L, 128, 2, NTOK], F32, kind="ExternalInput").ap()
    wst = nc.dram_tensor("wst", [L, NBLK, 128, BLK], F32, kind="ExternalInput").ap()
    cvec_d = nc.dram_tensor("cvec", [128, L * 64], F32, kind="ExternalInput").ap()
    binfm_d = nc.dram_tensor("binfm", [128, L * 32], F32, kind="ExternalInput").ap()
    bvrow_d = nc.dram_tensor("bvrow", [L, 1024], F32, kind="ExternalInput").ap()
    wsT_d = nc.dram_tensor("wsT", [128, L * 1024], F32, kind="ExternalInput").ap()
    bsbc_d = nc.dram_tensor("bsbc", [128, L * 1024], F32, kind="ExternalInput").ap()
    cmask_d = nc.dram_tensor("cmask", [128, 128], F32, kind="ExternalInput").ap()
    pcore_d = nc.dram_tensor("pcore", [128, 80], F32, kind="ExternalInput").ap()
    outT = nc.dram_tensor("outT", [128, 8, NOUT], F32, kind="ExternalOutput").ap()

    dump_d = {}
    for (nm, shp, dt) in DUMPS:
        dump_d[nm] = nc.dram_tensor("dbg_" + nm, shp, dt, kind="ExternalOutput").ap()
    P = Prog()
    from contextlib import ExitStack
    with ExitStack() as es:
        def sb(name, shape, dt):
            return es.enter_context(nc.sbuf_tensor(name, shape, dt))

        X = sb("X", [128, 8, ZW], F32)
        H = sb("H", [128, 8, NTMAX], BF16)
        O = sb("O", [128, 8, NTMAX], F32)
        Z = sb("Z", [128, 8, ZW], F32)
        S2 = sb("S2", [128, 2, ZW], F32)
        BB = sb("BB", [128, 6, 8 * NTMAX], BF16)
        WB = sb("WB", [128, NWBUF, BLK], BF16)
        PB = sb("PB", [128, 2, NTMAX], BF16)
        RB = sb("RB", [128, NTMAX], F32)
        TMP = sb("TMP", [128, 2, ZW], F32)
        S1 = TMP
        ZH = sb("ZH", [128, L, 8, 16], F32)
        T16 = sb("T16", [128, 2, 16], F32)
        ST = sb("ST", [128, 2, 6], F32)
        MVA = sb("MVA", [128, 8, 2], F32)
        RSTD = sb("RSTD", [128, 8], F32)
        EPSC = sb("EPSC", [128, 1], F32)
        NMR = sb("NMR", [128, 8], F32)
        EPSP = sb("EPSP", [128, NTMAX], F32)
        CV = sb("CV", [128, L * 64], F32)
        BIN = sb("BIN", [128, L * 32], F32)
        VBH = sb("VBH", [33, 1024], BF16)
        VBL = sb("VBL", [33, 1024], BF16)
        VB2 = sb("VB2", [2, L, 1024], BF16)
        WSB = sb("WSB", [128, L * 1024], BF16)
        BF = sb("BF", [128, L * 1024], F32)
        CM = sb("CM", [128, 128], F32)
        PC = sb("PC", [128, 80], F32)
        ONEB = sb("ONEB", [128, 128], BF16)
        ONEF = sb("ONEF", [128, 128], F32)
        PS = es.enter_context(nc.psum_tensor("PS", [128, 8, 512], F32))
        WSF = BB[:, 1, 0:2 * L * 1024].bitcast(F32)
        BSB = BB[:, 5, 0:2 * L * 1024].bitcast(F32)

        for key in ("pe", "act", "dve", "pool", "sp", "cst", "xld", "ost", "pld", "dbg") + tuple(
                "w%d" % i for i in range(NWBUF)):
            P.add_sem(key, es.enter_context(nc.semaphore("s_" + key)))

        rX, rH, rO, rZ = R(8), R(8), R(8), R(8)
        rB = [R(8) for _ in range(6)]
        rS2 = Res()
        rW = R(NWBUF)
        rPB, rRB = Res(), Res()
        rTMP = R(2)
        rS1 = rTMP
        rZH = R(L)
        rT16, rST, rMV, rRSTD, rNMR, rEPSP = Res(), Res(), Res(), Res(), Res(), Res()
        rBT = R(8)
        rV = R(5)
        rPS = R(8)
        rC = Res()
        rVB2 = Res()
        rWSB, rBF = Res(), Res()

        def Bv(i):
            return BB[:, i, :].rearrange("p (c t) -> p c t", c=8)

        U, GA, GB, NB_, PL, MS = (Bv(i) for i in range(6))
        SQ = PL
        rU, rGA, rGB, rN, rPL, rMS = rB
        rSQ = rPL
        NTM = BB[:, 3, :]
        ACTB = BB[:, 0:4, :].rearrange("p a (c t) -> p (a c) t", c=8)
        rACT = rB[0] + rB[1] + rB[2] + rB[3]
        VTM = O[:, :, :].rearrange("p c t -> p (c t)")

        psn = [0]

        def banks(n):
            b = psn[0]
            if b + n > 8:
                b = 0
            psn[0] = (b + n) % 8
            return list(range(b, b + n))

        stream = []
        for (t0, nt) in TILES:
            for l in range(L_RUN):
                for j in range(NBLK):
                    stream.append((l, j))
        wstate = {"issued": 0, "cons": 0}

        def blk_len(j):
            return 2048 if j in (10, 35) else BLK

        def w_issue(extra_reads=()):
            i = wstate["issued"]
            if i >= len(stream):
                return
            l, j = stream[i]
            b = i % NWBUF
            n = blk_len(j)
            P.dma("pool", "w%d" % b,
                  lambda e, b=b, l=l, j=j, n=n: e.dma_start(out=WB[:, b, 0:n], in_=wst[l, j, :, 0:n]),
                  reads=list(extra_reads), writes=[rW[b]])
            wstate["issued"] += 1

        def w_next():
            i = wstate["cons"]
            wstate["cons"] += 1
            return i % NWBUF

        def w_done():
            w_issue()

        for (dst, src) in ((CV[:], cvec_d), (BIN[:], binfm_d), (CM[:], cmask_d), (PC[:], pcore_d)):
            P.dma("sp", "cst", lambda e, dst=dst, src=src: e.dma_start(out=dst, in_=src), writes=[rC])
        P.dma("sp", "xld", lambda e: e.dma_start(out=X[:, :, 0:TILES[0][1]], in_=xT[:, :, 0:TILES[0][1]]), writes=rX)
        w_issue()
        P.op("dve", lambda e: e.memset(ONEB[:], 1.0), writes=[rC])
        P.op("dve", lambda e: e.memset(ONEF[:], 1.0), writes=[rC])
        P.op("dve", lambda e: e.memset(EPSC[:], EPS), writes=[rC])
        rC.const = True

        BVR = VTM[0:33, 0:1024]
        BVT = VTM[0:33, 1024:2048]

        def setup_late():
            for _ in range(NWBUF - 1):
                w_issue(extra_reads=rX)
            for (dst, src) in ((WSF, wsT_d), (BSB, bsbc_d)):
                P.dma("sp", "cst", lambda e, dst=dst, src=src: e.dma_start(out=dst, in_=src),
                      writes=rB[1] + rB[5])
            P.op("dve", lambda e: e.memset(VTM[0:33, 0:2048], 0.0), writes=rO)
            for l in range(L):
                P.dma("sp", "cst", lambda e, l=l: e.dma_start(out=VTM[32 * l:32 * l + 1, 0:1024],
                                                              in_=bvrow_d[l:l + 1, :]), writes=rO)
            P.op("dve", lambda e: e.tensor_copy(out=VBH[:], in_=BVR), reads=rO, writes=[rVB2])
            P.op("dve", lambda e: e.tensor_copy(out=BVT, in_=VBH[:]), reads=[rVB2], writes=rO)
            P.op("dve", lambda e: e.tensor_tensor(out=BVT, in0=BVR, in1=BVT, op=ALU.subtract),
                 reads=rO, writes=rO)
            P.op("dve", lambda e: e.tensor_copy(out=VBL[:], in_=BVT), reads=rO, writes=[rVB2])
            for l in range(L):
                P.dma("sp", "cst", lambda e, l=l: e.dma_start(out=VB2[0:1, l, :], in_=VBH[32 * l:32 * l + 1, :]),
                      reads=[rVB2], writes=[rVB2])
                P.dma("sp", "cst", lambda e, l=l: e.dma_start(out=VB2[1:2, l, :], in_=VBL[32 * l:32 * l + 1, :]),
                      reads=[rVB2], writes=[rVB2])
            sgu_mask()

        def sgu_mask():
            for l in range(L):
                for h in range(8):
                    sl = slice(l * 1024 + h * 128, l * 1024 + (h + 1) * 128)
                    P.op("dve", lambda e, sl=sl: e.tensor_tensor(out=WSF[:, sl], in0=WSF[:, sl], in1=CM[:], op=ALU.mult),
                         reads=[rC] + rB[1], writes=rB[1])
                P.op("dve", lambda e, l=l: e.tensor_copy(out=WSB[:, l * 1024:(l + 1) * 1024],
                                                         in_=WSF[:, l * 1024:(l + 1) * 1024]),
                     reads=rB[1], writes=[rWSB])

        def sgu_bfull():
            for l in range(L):
                for hh in range(2):
                    bk = banks(1)[0]
                    P.group("pe", [lambda e, bk=bk, l=l, hh=hh: e.matmul(
                        PS[:, bk, :], ONEF[:], WSF[:, l * 1024 + hh * 512: l * 1024 + (hh + 1) * 512],
                        start=True, stop=True)], reads=[rC] + rB[1], writes=[rPS[bk]])
                    for h4 in range(4):
                        h = hh * 4 + h4
                        sl = slice(l * 1024 + h * 128, l * 1024 + (h + 1) * 128)
                        col = (l * 8 + V_LNB) * 8 + h
                        P.op("dve", lambda e, bk=bk, h4=h4, sl=sl, col=col: e.scalar_tensor_tensor(
                            out=BF[:, sl], in0=PS[:, bk, h4 * 128:(h4 + 1) * 128], scalar=CV[:, col:col + 1],
                            in1=BSB[:, sl], op0=ALU.mult, op1=ALU.add),
                            reads=[rPS[bk], rC] + rB[5], writes=[rBF])
            rWSB.const = True
            rBF.const = True

        def cv(l, vi, c):
            col = (l * 8 + vi) * 8 + c
            return CV[:, col:col + 1]

        dump_toks = []

        def dump(nm, ap, res):
            if nm in dump_d and nm not in [d[0] for d in dump_toks]:
                dump_toks.append((nm, P.dma("sp", "dbg", lambda e: e.dma_start(out=dump_d[nm], in_=ap), reads=res)))

        def colblocks(ca, nt):
            cbs = []
            c = ca
            while c < nt:
                w = min(512, nt - c)
                cbs.append((c, w))
                c += w
            return cbs

        GF = BB[:, 0:2, :].rearrange("p a t -> p (a t)").bitcast(F32).rearrange("p (c t) -> p c t", c=8)
        rG = [[rB[oc // 4][2 * (oc % 4)], rB[oc // 4][2 * (oc % 4) + 1]] for oc in range(8)]

        def tile_layer(ti, t0, nt, l, X, rX, Z, rZ, prefetch):
            last = (l == L_RUN - 1)
            ca = HALO if (ti == 0 and last) else 0
            cbs = colblocks(ca, nt)
            cbs_all = colblocks(0, nt)
            nch = nt // 128
            clo = ca // 128

            def norm_R(sq_res, mode="rsqrt", cbl=None):
                for (c0, w) in (cbl or cbs):
                    bk = banks(1)[0]
                    for dc in range(8):
                        P.group("pe", [lambda e, bk=bk, dc=dc, c0=c0, w=w: e.matmul(
                            PS[:, bk, 0:w], ONEB[:], SQ[:, dc, c0:c0 + w], start=(dc == 0), stop=(dc == 7))],
                            reads=[rC, sq_res[dc]], writes=[rPS[bk]])
                    if mode == "epsp":
                        P.op("dve", lambda e, bk=bk, c0=c0, w=w: e.tensor_scalar(
                            out=EPSP[:, c0:c0 + w], in0=PS[:, bk, 0:w], scalar1=1.0 / D, scalar2=EPS,
                            op0=ALU.mult, op1=ALU.add), reads=[rPS[bk]], writes=[rEPSP])
                        P.op("dve", lambda e, c0=c0, w=w: e.scalar_tensor_tensor(
                            out=EPSP[:, c0:c0 + w], in0=EPSP[:, c0:c0 + w], scalar=EPS, in1=EPSP[:, c0:c0 + w],
                            op0=ALU.mult, op1=ALU.mult), reads=[rEPSP], writes=[rEPSP])
                        continue
                    if mode == "rsqrt_epsp":
                        P.op("dve", lambda e, bk=bk, c0=c0, w=w: e.scalar_tensor_tensor(
                            out=RB[:, c0:c0 + w], in0=PS[:, bk, 0:w], scalar=1.0 / D, in1=EPSP[:, c0:c0 + w],
                            op0=ALU.mult, op1=ALU.add), reads=[rPS[bk], rEPSP], writes=[rRB])
                        P.op("act", lambda e, c0=c0, w=w: e.activation(
                            out=RB[:, c0:c0 + w], in_=RB[:, c0:c0 + w], func=AF.Ln), reads=[rRB], writes=[rRB])
                    else:
                        P.op("act", lambda e, bk=bk, c0=c0, w=w: e.activation(
                            out=RB[:, c0:c0 + w], in_=PS[:, bk, 0:w], func=AF.Ln, bias=EPSC[:, 0:1], scale=1.0 / D),
                            reads=[rPS[bk], rC], writes=[rRB])
                    P.op("act", lambda e, c0=c0, w=w: e.activation(
                        out=RB[:, c0:c0 + w], in_=RB[:, c0:c0 + w], func=AF.Exp, scale=-0.5),
                        reads=[rRB], writes=[rRB])

            def squares_of_X(a, b_):
                for dc in range(8):
                    P.op("act", lambda e, dc=dc: e.activation(out=SQ[:, dc, a:b_], in_=X[:, dc, a:b_], func=AF.Square),
                         reads=[rX[dc]], writes=[rSQ[dc]])

            def residual(vi, hmode=None, hvi=None, gained=False):
                n = nt - ca
                step = 2 if gained else 1
                for d0 in range(0, 8, step):
                    dcs = list(range(d0, d0 + step))
                    if gained:
                        P.op("dve", lambda e, d0=d0: e.tensor_tensor(
                            out=O[:, d0:d0 + 2, ca:nt], in0=O[:, d0:d0 + 2, ca:nt],
                            in1=RB[:, ca:nt].unsqueeze(1).broadcast_to([128, 2, n]), op=ALU.mult),
                            reads=[rO[d] for d in dcs] + [rRB], writes=[rO[d] for d in dcs])
                        P.op("dve", lambda e, d0=d0: e.tensor_tensor(
                            out=X[:, d0:d0 + 2, ca:nt], in0=X[:, d0:d0 + 2, ca:nt], in1=O[:, d0:d0 + 2, ca:nt],
                            op=ALU.add), reads=[rO[d] for d in dcs] + [rX[d] for d in dcs],
                            writes=[rX[d] for d in dcs])
                    else:
                        dc = d0
                        P.op("dve", lambda e, dc=dc: e.tensor_tensor(
                            out=O[:, dc, ca:nt], in0=O[:, dc, ca:nt], in1=RB[:, ca:nt], op=ALU.mult),
                            reads=[rO[dc], rRB], writes=[rO[dc]])
                        P.op("dve", lambda e, dc=dc: e.scalar_tensor_tensor(
                            out=X[:, dc, ca:nt], in0=O[:, dc, ca:nt], scalar=cv(l, vi, dc), in1=X[:, dc, ca:nt],
                            op0=ALU.mult, op1=ALU.add), reads=[rO[dc], rX[dc], rC], writes=[rX[dc]])
                    for dc in dcs:
                        if hmode == "gain":
                            P.op("act", lambda e, dc=dc: e.activation(
                                out=H[:, dc, ca:nt], in_=X[:, dc, ca:nt], func=AF.Identity, scale=cv(l, hvi, dc)),
                                reads=[rX[dc], rC], writes=[rH[dc]])
                        elif hmode == "copy":
                            P.op("act", lambda e, dc=dc: e.activation(
                                out=H[:, dc, ca:nt], in_=X[:, dc, ca:nt], func=AF.Copy),
                                reads=[rX[dc]], writes=[rH[dc]])

            def fm_matmul(b, rhs, rhs_res, nk, evac, fis=range(4), kstride=512, cbl=None, kouter=False, perk=False):
                cbl = cbl or cbs
                fis = list(fis)
                if kouter:
                    per = max(1, 4 // len(cbl))
                    for s0 in range(0, len(fis), per):
                        sub = fis[s0:s0 + per]
                        bkm = {fi: banks(len(cbl)) for fi in sub}
                        allb = [bk for fi in sub for bk in bkm[fi]]
                        for k in range(nk):
                            fns = []
                            for fi in sub:
                                for ci, (c0, w) in enumerate(cbl):
                                    fns.append(lambda e, bk=bkm[fi][ci], k=k, fi=fi, c0=c0, w=w: e.matmul(
                                        PS[:, bk, 0:w], WB[:, b, k * kstride + fi * 128: k * kstride + (fi + 1) * 128],
                                        rhs[:, k, c0:c0 + w], start=(k == 0), stop=(k == nk - 1)))
                            P.group("pe", fns, reads=[rW[b], rhs_res[k]], writes=[rPS[bk] for bk in allb])
                        for fi in sub:
                            for ci, (c0, w) in enumerate(cbl):
                                evac(fi, c0, w, bkm[fi][ci])
                    return
                for idx, fi in enumerate(fis):
                    bks = banks(len(cbl))
                    fns = []
                    for k in range(nk):
                        fk = []
                        for ci, (c0, w) in enumerate(cbl):
                            fk.append(lambda e, bk=bks[ci], k=k, fi=fi, c0=c0, w=w: e.matmul(
                                PS[:, bk, 0:w], WB[:, b, k * kstride + fi * 128: k * kstride + (fi + 1) * 128],
                                rhs[:, k, c0:c0 + w], start=(k == 0), stop=(k == nk - 1)))
                        if perk and idx == 0:
                            P.group("pe", fk, reads=[rW[b], rhs_res[k]], writes=[rPS[bk] for bk in bks])
                        else:
                            fns.extend(fk)
                    if fns:
                        P.group("pe", fns, reads=[rW[b]] + rhs_res, writes=[rPS[bk] for bk in bks])
                    for ci, (c0, w) in enumerate(cbl):
                        evac(fi, c0, w, bks[ci])

            P.dma("pool", "pld", lambda e: e.dma_start(out=PB[:, :, 0:nt], in_=pT[l, :, :, t0:t0 + nt]),
                  writes=[rPB])

            squares_of_X(0, nt)
            norm_R(rSQ, cbl=cbs_all)
            for dc in range(8):
                P.op("dve", lambda e, dc=dc: e.scalar_tensor_tensor(
                    out=H[:, dc, 0:nt], in0=X[:, dc, 0:nt], scalar=cv(l, V_PREMIX, dc), in1=RB[:, 0:nt],
                    op0=ALU.mult, op1=ALU.mult), reads=[rX[dc], rRB, rC], writes=[rH[dc]])

            if ti == 0 and l == 0:
                setup_late()
            def w_in_group(dst, dres, func, bcol, off, cbl, first):
                for half in range(2):
                    b = w_next()

                    def evac(fi, c0, w, bk, half=half):
                        fc = half * 4 + fi
                        col = l * 32 + bcol + fc
                        P.op("act", lambda e: e.activation(
                            out=dst[:, fc, off + c0: off + c0 + w], in_=PS[:, bk, 0:w], func=func,
                            bias=BIN[:, col:col + 1]), reads=[rPS[bk], rC], writes=[dres[fc]])
                    fm_matmul(b, H, rH, 8, evac, cbl=cbl, kouter=(first and half == 0))
                    w_done()

            w_in_group(Z, rZ, AF.Identity, 0, 16, cbs_all, True)
            W_ = 16 + nt

            def pool_prep():
                if ti == 0:
                    P.op("dve", lambda e: e.memset(Z[:, :, 0:16], 0.0), writes=rZ)
                    P.op("dve", lambda e: e.tensor_scalar(
                        out=Z[:, :, 16:16 + HALO], in0=Z[:, :, 16:16 + HALO], scalar1=PC[:, 0:1], scalar2=None,
                        op0=ALU.mult), reads=rZ + [rC], writes=rZ)
                else:
                    P.op("dve", lambda e: e.tensor_copy(out=Z[:, :, 0:16], in_=ZH[:, l, :, :]),
                         reads=[rZH[l]], writes=rZ)

            def pool_groups(gs):
                for g in gs:
                    zs = Z[:, 2 * g:2 * g + 2, :]
                    zres = [rZ[2 * g], rZ[2 * g + 1]]
                    P.op("dve", lambda e, zs=zs: e.tensor_tensor(
                        out=S1[:, :, 1:W_], in0=zs[:, :, 1:W_], in1=zs[:, :, 0:W_ - 1], op=ALU.add),
                        reads=zres, writes=rS1)
                    cur, rcur = S1, rS1
                    if g >= 1:
                        P.op("dve", lambda e: e.tensor_tensor(
                            out=S2[:, :, 3:W_], in0=S1[:, :, 3:W_], in1=S1[:, :, 1:W_ - 2], op=ALU.add),
                            reads=rS1, writes=[rS2])
                        cur, rcur = S2, [rS2]
                    if g >= 2:
                        P.op("dve", lambda e: e.tensor_tensor(
                            out=S1[:, :, 7:W_], in0=S2[:, :, 7:W_], in1=S2[:, :, 3:W_ - 4], op=ALU.add),
                            reads=[rS2], writes=rS1)
                        cur, rcur = S1, rS1
                    if g >= 3:
                        P.op("dve", lambda e: e.tensor_tensor(
                            out=S2[:, :, 15:W_], in0=S1[:, :, 15:W_], in1=S1[:, :, 7:W_ - 8], op=ALU.add),
                            reads=rS1, writes=[rS2])
                        cur, rcur = S2, [rS2]
                    wdw = POOL_WINDOWS[g]
                    P.op("dve", lambda e, cur=cur, g=g, wdw=wdw: e.scalar_tensor_tensor(
                        out=PL[:, 2 * g:2 * g + 2, 0:nt], in0=cur[:, :, 16:16 + nt], scalar=1.0 / wdw,
                        in1=Z[:, 2 * g:2 * g + 2, 16:16 + nt], op0=ALU.mult, op1=ALU.subtract),
                        reads=rcur + zres, writes=[rPL[2 * g], rPL[2 * g + 1]])
                    if ti == 0:
                        for cc in range(2):
                            P.op("dve", lambda e, cur=cur, g=g, cc=cc: e.tensor_tensor(
                                out=T16[:, cc, :], in0=cur[:, cc, 16 + HALO:32 + HALO],
                                in1=PC[:, 1 + g * 16: 1 + (g + 1) * 16], op=ALU.mult),
                                reads=rcur + [rC], writes=[rT16])
                        P.op("dve", lambda e, g=g: e.tensor_tensor(
                            out=PL[:, 2 * g:2 * g + 2, HALO:HALO + 16], in0=T16[:, :, :],
                            in1=Z[:, 2 * g:2 * g + 2, 16 + HALO:32 + HALO], op=ALU.subtract),
                            reads=[rT16] + zres, writes=[rPL[2 * g], rPL[2 * g + 1]])

            def pool_finish():
                dump("Z", Z[:, :, :], rZ)
                dump("PL", PL, rPL)
                dump("H", H[:, :, :], rH)
                P.op("dve", lambda e: e.tensor_copy(out=ZH[:, l, :, :], in_=Z[:, :, nt:nt + 16]),
                     reads=rZ, writes=[rZH[l]])
                if last and prefetch is not None:
                    pt0, pnt = prefetch
                    P.dma("sp", "xld", lambda e: e.dma_start(out=Z[:, :, 0:pnt], in_=xT[:, :, pt0:pt0 + pnt]),
                          writes=rZ)


            pool_prep()
            pool_groups((0, 1))

            vb = [w_next(), w_next()]
            first_v = [True]
            for half in range(2):
                b = vb[half]
                for c in range(clo, nch):
                    bk = banks(1)[0]
                    fns = []
                    for k in range(8):
                        fns.append(lambda e, bk=bk, k=k, c=c, b=b: e.matmul(
                            PS[:, bk, :], H[:, k, c * 128:(c + 1) * 128], WB[:, b, k * 512:(k + 1) * 512],
                            start=(k == 0), stop=False))
                    vsl = slice(half * 512, (half + 1) * 512)
                    fns.append(lambda e, bk=bk, vsl=vsl: e.matmul(
                        PS[:, bk, :], ONEB[0:2, :], VB2[0:2, l, vsl], start=False, stop=True))
                    P.group("pe", fns, reads=[rW[b], rC, rVB2] + rH, writes=[rPS[bk]])
                    o0 = c * 1024 + half * 512
                    P.op("act", lambda e, bk=bk, o0=o0: e.activation(
                        out=VTM[:, o0:o0 + 512], in_=PS[:, bk, :], func=AF.Gelu_apprx_tanh),
                        reads=[rPS[bk]], writes=([rV[c]] + (rO if first_v[0] else [])))
                    first_v[0] = False
                w_done()
            for c in range(clo, nch):
                for half in range(2):
                    o0 = c * 1024 + half * 512
                    P.op("dve", lambda e, o0=o0, half=half: e.bn_stats(out=ST[:, half, :], in_=VTM[:, o0:o0 + 512]),
                         reads=[rV[c]], writes=[rST])
                P.op("dve", lambda e, c=c: e.bn_aggr(out=MVA[:, c, :], in_=ST[:, :, :].rearrange("p a b -> p (a b)")),
                     reads=[rST], writes=[rMV])

            P.op("act", lambda e: e.activation(out=RSTD[:, clo:nch], in_=MVA[:, clo:nch, 1], func=AF.Ln,
                                               bias=EPSC[:, 0:1], scale=1.0), reads=[rMV, rC], writes=[rRSTD])
            P.op("act", lambda e: e.activation(out=RSTD[:, clo:nch], in_=RSTD[:, clo:nch], func=AF.Exp, scale=-0.5),
                 reads=[rRSTD], writes=[rRSTD])
            P.op("dve", lambda e: e.scalar_tensor_tensor(
                out=NMR[:, clo:nch], in0=MVA[:, clo:nch, 0], scalar=-1.0, in1=RSTD[:, clo:nch],
                op0=ALU.mult, op1=ALU.mult), reads=[rMV, rRSTD], writes=[rNMR])
            for c in range(clo, nch):
                P.op("act", lambda e, c=c: e.activation(
                    out=NTM[:, c * 1024:(c + 1) * 1024], in_=VTM[:, c * 1024:(c + 1) * 1024], func=AF.Identity,
                    scale=RSTD[:, c:c + 1], bias=NMR[:, c:c + 1]), reads=rO + [rV[c], rRSTD, rNMR], writes=rN)
            w_in_group(U, rU, AF.Gelu_apprx_tanh, 8, 0, cbs, False)
            if ti == 0 and l == 0:
                sgu_bfull()

            for h in range(8):
                for ci, (c0, w) in enumerate(cbs):
                    bk = banks(1)[0]
                    nck = w // 128
                    fns = []
                    for cc in range(nck):
                        c = c0 // 128 + cc
                        fns.append(lambda e, bk=bk, cc=cc, c=c, h=h: e.matmul(
                            PS[:, bk, cc * 128:(cc + 1) * 128], NTM[:, c * 1024 + h * 128: c * 1024 + (h + 1) * 128],
                            WSB[:, l * 1024 + h * 128: l * 1024 + (h + 1) * 128], start=True, stop=True))
                    P.group("pe", fns, reads=rN + [rC, rWSB], writes=[rPS[bk]])
                    tb = (h * len(cbs) + ci) % 2
                    bf = BF[:, l * 1024 + h * 128: l * 1024 + (h + 1) * 128]
                    P.op("dve", lambda e, bk=bk, nck=nck, w=w, tb=tb, bf=bf, h=h: e.scalar_tensor_tensor(
                        out=TMP[:, tb, 0:w].rearrange("p (a t) -> p a t", a=nck),
                        in0=PS[:, bk, 0:w].rearrange("p (a t) -> p a t", a=nck),
                        scalar=cv(l, V_LNG, h),
                        in1=bf.unsqueeze(1).broadcast_to([128, nck, 128]),
                        op0=ALU.mult, op1=ALU.add), reads=[rPS[bk], rC, rBF], writes=[rTMP[tb]])
                    P.op("dve", lambda e, tb=tb, h=h, c0=c0, w=w: e.tensor_tensor(
                        out=U[:, h, c0:c0 + w], in0=TMP[:, tb, 0:w], in1=U[:, h, c0:c0 + w], op=ALU.mult),
                        reads=[rTMP[tb], rU[h]], writes=[rU[h]])
            dump("SG", U, rU)
            dump("MS", MS, rMS)

            pool_groups((2, 3))
            pool_finish()
            w_in_group(GA, rGA, AF.Sigmoid, 16, 0, cbs, False)
            w_in_group(GB, rGB, AF.Sigmoid, 24, 0, cbs, False)

            b4 = w_next()

            def m4_groups(idx):
                for gi in idx:
                    g, dh = gi // 2, gi % 2
                    bks = banks(len(cbs))
                    fns = []
                    for cc in range(2):
                        for ci, (c0, w) in enumerate(cbs):
                            o0 = (g * 2 + cc) * 256 + dh * 128
                            fns.append(lambda e, bk=bks[ci], o0=o0, g=g, cc=cc, c0=c0, w=w: e.matmul(
                                PS[:, bk, 0:w], WB[:, b4, o0:o0 + 128], PL[:, 2 * g + cc, c0:c0 + w],
                                start=(cc == 0), stop=(cc == 1)))
                    P.group("pe", fns, reads=[rW[b4], rPL[2 * g], rPL[2 * g + 1]], writes=[rPS[bk] for bk in bks])
                    oc = 2 * g + dh
                    for ci, (c0, w) in enumerate(cbs):
                        if gi % 2 == 0:
                            P.op("dve", lambda e, bk=bks[ci], oc=oc, c0=c0, w=w: e.tensor_scalar(
                                out=MS[:, oc, c0:c0 + w], in0=PS[:, bk, 0:w], scalar1=cv(l, V_PSCALE, oc),
                                scalar2=None, op0=ALU.mult), reads=[rPS[bks[ci]], rC], writes=[rMS[oc]])
                        else:
                            P.op("act", lambda e, bk=bks[ci], oc=oc, c0=c0, w=w: e.activation(
                                out=MS[:, oc, c0:c0 + w], in_=PS[:, bk, 0:w], func=AF.Identity,
                                scale=cv(l, V_PSCALE, oc)), reads=[rPS[bks[ci]], rC], writes=[rMS[oc]])

            m4_groups(range(0, 8))
            w_done()
            dump("VTM", VTM, rO)
            dump("NTM", NTM, rN)

            for half in range(2):
                b = w_next()

                def evac(fi, c0, w, bk, half=half):
                    oc = half * 4 + fi
                    P.op("dve", lambda e: e.tensor_tensor(
                        out=GA[:, oc, c0:c0 + w], in0=PS[:, bk, 0:w], in1=GA[:, oc, c0:c0 + w], op=ALU.mult),
                        reads=[rPS[bk], rGA[oc]], writes=[rGA[oc]])
                fm_matmul(b, MS, rMS, 8, evac, perk=(half == 0))
                w_done()
            dump("M1", GA, rGA)

            for half in range(2):
                b = w_next()

                def evac(fi, c0, w, bk, half=half):
                    oc = half * 4 + fi
                    P.op("dve", lambda e: e.tensor_tensor(
                        out=GB[:, oc, c0:c0 + w], in0=PS[:, bk, 0:w], in1=GB[:, oc, c0:c0 + w], op=ALU.mult),
                        reads=[rPS[bk], rGB[oc]], writes=[rGB[oc]])
                    P.op("dve", lambda e: e.tensor_tensor(
                        out=GB[:, oc, c0:c0 + w], in0=GB[:, oc, c0:c0 + w], in1=GA[:, oc, c0:c0 + w], op=ALU.add),
                        reads=[rGB[oc], rGA[oc]], writes=[rGB[oc]])
                fm_matmul(b, U, rU, 8, evac, perk=(half == 0))
                w_done()

            def evac_O(oc, c0, w, bk, vi):
                P.op("act", lambda e: e.activation(out=SQ[:, oc, c0:c0 + w], in_=PS[:, bk, 0:w], func=AF.Square),
                     reads=[rPS[bk]], writes=[rSQ[oc], rBT[bk]])
                P.op("dve", lambda e: e.tensor_scalar(out=O[:, oc, c0:c0 + w], in0=PS[:, bk, 0:w],
                                                      scalar1=cv(l, vi, oc), scalar2=None, op0=ALU.mult),
                     reads=[rPS[bk], rBT[bk], rC], writes=[rO[oc]])

            for half in range(2):
                b = w_next()
                fm_matmul(b, GB, rGB, 8, lambda fi, c0, w, bk, half=half: evac_O(half * 4 + fi, c0, w, bk, V_POSTMIX),
                          perk=(half == 0))
                w_done()
            dump("MG", GB, rGB)
            dump("O1", O[:, :, :], rO)

            norm_R(rSQ)
            residual(V_POSTMIX, hmode="gain", hvi=V_PREFFN, gained=True)
            dump("X1", X[:, :, :], rX)

            for j in range(8):
                b = w_next()

                def evac(fi, c0, w, bk, j=j):
                    fc = j * 4 + fi
                    tb = fi % 2
                    P.op("act", lambda e: e.activation(out=TMP[:, tb, c0:c0 + w], in_=PS[:, bk, 0:w], func=AF.Relu),
                         reads=[rPS[bk]], writes=[rTMP[tb]])
                    P.op("dve", lambda e: e.tensor_tensor(
                        out=ACTB[:, fc, c0:c0 + w], in0=PS[:, bk, 0:w], in1=TMP[:, tb, c0:c0 + w], op=ALU.mult),
                        reads=[rPS[bk], rTMP[tb]], writes=[rACT[fc]])
                fm_matmul(b, H, rH, 8, evac, kouter=(j == 0))
                w_done()
                if j == 0:
                    squares_of_X(ca, nt)
                    norm_R(rSQ, mode="epsp")

            for j in range(8):
                b = w_next()
                fm_matmul(b, ACTB, rACT, 32, lambda fi, c0, w, bk, j=j: evac_O(j, c0, w, bk, V_POSTFFN),
                          fis=[0], kstride=128, perk=(j == 0))
                w_done()

            norm_R(rSQ, mode="rsqrt_epsp")
            residual(V_POSTFFN, hmode="copy", gained=True)

            for half in range(2):
                b = w_next()

                def evac(fi, c0, w, bk, half=half):
                    oc = half * 4 + fi
                    P.op("act", lambda e: e.activation(
                        out=GF[:, oc, c0:c0 + w], in_=PS[:, bk, 0:w], func=AF.Sigmoid),
                        reads=[rPS[bk]], writes=rG[oc])
                fm_matmul(b, H, rH, 8, evac, kouter=(half == 0))
                w_done()

            bp = w_next()
            for oc in range(8):
                bks = banks(len(cbs))
                fns = []
                for kc in range(2):
                    for ci, (c0, w) in enumerate(cbs):
                        fns.append(lambda e, bk=bks[ci], kc=kc, oc=oc, c0=c0, w=w: e.matmul(
                            PS[:, bk, 0:w], WB[:, bp, kc * 1024 + oc * 128: kc * 1024 + (oc + 1) * 128],
                            PB[:, kc, c0:c0 + w], start=(kc == 0), stop=(kc == 1)))
                P.group("pe", fns, reads=[rW[bp], rPB], writes=[rPS[bk] for bk in bks])
                for ci, (c0, w) in enumerate(cbs):
                    bk = bks[ci]
                    P.op("dve", lambda e, bk=bk, oc=oc, c0=c0, w=w: e.tensor_tensor(
                        out=O[:, oc, c0:c0 + w], in0=PS[:, bk, 0:w], in1=GF[:, oc, c0:c0 + w], op=ALU.mult),
                        reads=[rPS[bk]] + rG[oc], writes=[rO[oc]])
                    P.op("act", lambda e, oc=oc, c0=c0, w=w: e.activation(
                        out=SQ[:, oc, c0:c0 + w], in_=O[:, oc, c0:c0 + w], func=AF.Square),
                        reads=[rO[oc]], writes=[rSQ[oc]])
            w_done()

            norm_R(rSQ)
            residual(V_POSTPLE)

        bufs = [(X, rX), (Z, rZ)]
        last_store = None
        for ti, (t0, nt) in enumerate(TILES):
            (Xc, rXc), (Zc, rZc) = bufs[ti % 2], bufs[(ti + 1) % 2]
            prefetch = TILES[ti + 1] if ti + 1 < len(TILES) else None
            for l in range(L_RUN):
                tile_layer(ti, t0, nt, l, Xc, rXc, Zc, rZc, prefetch)
            s0 = HALO if ti == 0 else 0
            o0 = t0 + s0 - HALO
            n_out = nt - s0
            for dc in range(8):
                last_store = P.dma("sp", "ost", lambda e, s0=s0, o0=o0, n_out=n_out, Xc=Xc, dc=dc: e.dma_start(
                    out=outT[:, dc, o0:o0 + n_out], in_=Xc[:, dc, s0:s0 + n_out]), reads=[rXc[dc]])
            for r_ in rXc:
                r_.rs["ost"] = P.cnt["ost"]
        P.final_wait("sp", [last_store] + [d[1] for d in dump_toks])

        with nc.Block() as block:
            @block.tensor
            def _(e):
                for f in P.q["pe"]:
                    f(e)

            @block.scalar
            def _(e):
                for f in P.q["act"]:
                    f(e)

            @block.vector
            def _(e):
                for f in P.q["dve"]:
                    f(e)

            @block.gpsimd
            def _(e):
                for f in P.q["pool"]:
                    f(e)

            @block.sync
            def _(e):
                for f in P.q["sp"]:
                    f(e)
    return nc


def _fm8(v):
    return np.ascontiguousarray(v.reshape(8, 128).T)


def _blk_k512(W, col0):
    K = W.shape[0]
    return W[:, col0:col0 + 512].reshape(K // 128, 128, 512).transpose(1, 0, 2).reshape(128, -1)


def _build_wstream(inp):
    ws = np.zeros((L, NBLK, 128, BLK), np.float32)
    for l in range(L):
        w_in = inp["w_in"][l]
        order = [0, 512, 2048, 2560, 1024, 1536, 3072, 3584, 4096, 4608]
        for j, c0 in enumerate(order):
            ws[l, j] = _blk_k512(w_in, c0)
        ws[l, 10, :, :2048] = inp["pool_w"][l].reshape(4, 2, 128, 256).transpose(2, 0, 1, 3).reshape(128, 2048)
        for i, name in enumerate(("w_pa", "w_pb", "w_o")):
            for half in range(2):
                ws[l, 11 + 2 * i + half] = _blk_k512(inp[name][l], half * 512)
        for j in range(8):
            ws[l, 17 + j] = _blk_k512(inp["w_ff1"][l], j * 512)
        w2 = inp["w_ff2"][l]
        for j in range(8):
            ws[l, 25 + j] = w2[:, j * 128:(j + 1) * 128].reshape(32, 128, 128).transpose(1, 0, 2).reshape(128, BLK)
        for half in range(2):
            ws[l, 33 + half] = _blk_k512(inp["w_ple_gate"][l], half * 512)
        ws[l, 35, :, :2048] = inp["w_ple_proj"][l].reshape(2, 128, 1024).transpose(1, 0, 2).reshape(128, 2048)
    return ws


_NC_CACHE = {}


def make_in_maps(inp):
    x, p = inp["x"], inp["p"]
    B, S, _ = x.shape
    wst = _build_wstream(inp)
    cvec = np.zeros((128, L * 64), np.float32)
    names = ["pre_mix_g", "post_mix_g", "pre_ffn_g", "post_ffn_g", "post_ple_g", "pool_scale", "sgu_ln_g", "sgu_ln_b"]
    for l in range(L):
        for vi, nm in enumerate(names):
            cvec[:, (l * 8 + vi) * 8:(l * 8 + vi + 1) * 8] = _fm8(inp[nm][l])
    binfm = np.zeros((128, L * 32), np.float32)
    bvrow = np.zeros((L, 1024), np.float32)
    for l in range(L):
        b = inp["b_in"][l]
        for gi, c0 in enumerate((0, 1024, 3072, 4096)):
            binfm[:, l * 32 + gi * 8: l * 32 + (gi + 1) * 8] = _fm8(b[c0:c0 + 1024])
        bvrow[l] = b[2048:3072]
    wsT = np.ascontiguousarray(inp["sgu_w_s"].transpose(3, 0, 1, 2)).reshape(128, L * 1024)
    bsbc = np.ascontiguousarray(np.broadcast_to(inp["sgu_b_s"].reshape(1, L * 1024), (128, L * 1024)))
    si = np.arange(128)
    cmask = (si[:, None] <= si[None, :]).astype(np.float32)

    in_maps = []
    for c in range(NCORES):
        b, half = c // 2, c % 2
        s0 = half * OWN
        xt = np.zeros((NTOK, D), np.float32)
        pt = np.zeros((L, NTOK, 256), np.float32)
        xt[HALO:] = x[b, s0:s0 + OWN]
        pt[:, HALO:] = p[:, b, s0:s0 + OWN]
        pcore = np.zeros((128, 80), np.float32)
        if half == 1:
            xt[:HALO] = x[b, s0 - HALO:s0]
            pt[:, :HALO] = p[:, b, s0 - HALO:s0]
            pcore[:, 0] = 1.0
        for g, w in enumerate(POOL_WINDOWS):
            j = np.arange(16)
            cnt = np.minimum(j + 1, w) if half == 0 else np.full(16, w)
            pcore[:, 1 + g * 16: 1 + (g + 1) * 16] = (1.0 / cnt.astype(np.float32))[None, :]
        xTc = np.ascontiguousarray(xt.T.reshape(8, 128, NTOK).transpose(1, 0, 2))
        pTc = np.ascontiguousarray(pt.transpose(0, 2, 1).reshape(L, 2, 128, NTOK).transpose(0, 2, 1, 3))
        in_maps.append({"xT": xTc, "pT": pTc, "wst": wst, "cvec": cvec, "binfm": binfm, "bvrow": bvrow,
                        "wsT": wsT, "bsbc": bsbc, "cmask": cmask, "pcore": pcore})
    return in_maps


def kernel(**inputs):
    inp = {k: np.asarray(v, dtype=np.float32) for k, v in inputs.items()}
    B, S, _ = inp["x"].shape
    in_maps = make_in_maps(inp)
    if "nc" not in _NC_CACHE:
        _NC_CACHE["nc"] = build_nc()
    nc = _NC_CACHE["nc"]
    res = run_bass_kernel_spmd(nc, in_maps, core_ids=list(range(NCORES)))
    out = np.empty((B, S, D), np.float32)
    for c in range(NCORES):
        b, half = c // 2, c % 2
        o = np.asarray(res.results[c]["outT"], dtype=np.float32)
        out[b, half * OWN:(half + 1) * OWN, :] = o.transpose(1, 0, 2).reshape(D, OWN).T
    return out
```

```python
import numpy as np
import concourse.bass as bass
import concourse.mybir as mybir
from concourse.bass_utils import run_bass_kernel_spmd

F32 = mybir.dt.float32
BF16 = mybir.dt.bfloat16
AF = mybir.ActivationFunctionType
ALU = mybir.AluOpType

D = 1024
L = 2
NCORES = 8
OWN = 2048
HALO = 128
NTOK = OWN + HALO
TILES = [(0, 640), (640, 512), (1152, 512), (1664, 512)]
NTMAX = 640
ZW = 16 + NTMAX
EPS = 1e-6
NBLK = 36
NWBUF = 4
BLK = 4096
V_PREMIX, V_POSTMIX, V_PREFFN, V_POSTFFN, V_POSTPLE, V_PSCALE, V_LNG, V_LNB = range(8)
POOL_WINDOWS = (2, 4, 8, 16)
POOL_CHUNKS = ()
RES_ORDER = (0, 1, 2, 3, 4, 5, 6, 7)


class Res:
    __slots__ = ("w", "rs", "const")

    def __init__(self, const=False):
        self.w = None
        self.rs = {}
        self.const = const


class Prog:
    ENG = ("pe", "act", "dve", "pool", "sp")

    def __init__(self):
        self.q = {k: [] for k in self.ENG}
        self.semh = {}
        self.cnt = {}
        self.seen = {k: {} for k in self.ENG}

    def add_sem(self, key, handle):
        self.semh[key] = handle
        self.cnt[key] = 0

    def _wait(self, eng, toks):
        need = {}
        for t in toks:
            if t is None:
                continue
            key, val = t
            if key == "pe" and eng == "pe":
                continue
            if self.seen[eng].get(key, 0) >= val:
                continue
            if need.get(key, 0) < val:
                need[key] = val
        for key, val in need.items():
            self.seen[eng][key] = val
            s = self.semh[key]
            self.q[eng].append(lambda e, s=s, val=val: e.wait_ge(s, val))

    @staticmethod
    def _deps(reads, writes):
        deps = []
        for r in reads:
            deps.append(r.w)
        for w in writes:
            deps.append(w.w)
            deps.extend(w.rs.items())
        return deps

    @staticmethod
    def _commit(tok, reads, writes):
        for r in reads:
            if not r.const:
                if r.rs.get(tok[0], 0) < tok[1]:
                    r.rs[tok[0]] = tok[1]
        for w in writes:
            w.w = tok
            w.rs = {}

    def op(self, eng, fn, reads=(), writes=()):
        self._wait(eng, self._deps(reads, writes))
        self.cnt[eng] += 1
        tok = (eng, self.cnt[eng])
        s = self.semh[eng]
        self.q[eng].append(lambda e, fn=fn, s=s: fn(e).then_inc(s, 1))
        self._commit(tok, reads, writes)
        return tok

    def group(self, eng, fns, reads=(), writes=()):
        self._wait(eng, self._deps(reads, writes))
        self.cnt[eng] += 1
        tok = (eng, self.cnt[eng])
        s = self.semh[eng]
        for f in fns[:-1]:
            self.q[eng].append(f)
        last = fns[-1]
        self.q[eng].append(lambda e, fn=last, s=s: fn(e).then_inc(s, 1))
        self._commit(tok, reads, writes)
        return tok

    def dma(self, eng, semkey, fn, reads=(), writes=()):
        self._wait(eng, self._deps(reads, writes))
        self.cnt[semkey] += 16
        tok = (semkey, self.cnt[semkey])
        s = self.semh[semkey]
        self.q[eng].append(lambda e, fn=fn, s=s: fn(e).then_inc(s, 16))
        self._commit(tok, reads, writes)
        return tok

    def final_wait(self, eng, toks):
        self._wait(eng, toks)


def R(n, const=False):
    return [Res(const) for _ in range(n)]


def build_nc(TILES=TILES, L_RUN=L, NOUT=OWN, DUMPS=()):
    nc = bass.Bass("TRN2", target_bir_lowering=False)
    xT = nc.dram_tensor("xT", [128, 8, NTOK], F32, kind="ExternalInput").ap()
    pT = nc.dram_tensor("pT", [L, 128, 2, NTOK], F32, kind="ExternalInput").ap()
    wst = nc.dram_tensor("wst", [L, NBLK, 128, BLK], F32, kind="ExternalInput").ap()
    cvec_d = nc.dram_tensor("cvec", [128, L * 64], F32, kind="ExternalInput").ap()
    binfm_d = nc.dram_tensor("binfm", [128, L * 32], F32, kind="ExternalInput").ap()
    bvrow_d = nc.dram_tensor("bvrow", [L, 1024], F32, kind="ExternalInput").ap()
    wsT_d = nc.dram_tensor("wsT", [128, L * 1024], F32, kind="ExternalInput").ap()
    bsbc_d = nc.dram_tensor("bsbc", [128, L * 1024], F32, kind="ExternalInput").ap()
    cmask_d = nc.dram_tensor("cmask", [128, 128], F32, kind="ExternalInput").ap()
    pcore_d = nc.dram_tensor("pcore", [128, 80], F32, kind="ExternalInput").ap()
    outT = nc.dram_tensor("outT", [128, 8, NOUT], F32, kind="ExternalOutput").ap()

    dump_d = {}
    for (nm, shp, dt) in DUMPS:
        dump_d[nm] = nc.dram_tensor("dbg_" + nm, shp, dt, kind="ExternalOutput").ap()
    P = Prog()
    from contextlib import ExitStack
    with ExitStack() as es:
        def sb(name, shape, dt):
            return es.enter_context(nc.sbuf_tensor(name, shape, dt))

        X = sb("X", [128, 8, ZW], F32)
        H = sb("H", [128, 8, NTMAX], BF16)
        O = sb("O", [128, 8, NTMAX], F32)
        Z = sb("Z", [128, 8, ZW], F32)
        S2 = sb("S2", [128, 2, ZW], F32)
        BB = sb("BB", [128, 6, 8 * NTMAX], BF16)
        WB = sb("WB", [128, NWBUF, BLK], BF16)
        PB = sb("PB", [128, 2, NTMAX], BF16)
        RB = sb("RB", [128, NTMAX], F32)
        TMP = sb("TMP", [128, 2, ZW], F32)
        S1 = TMP
        ZH = sb("ZH", [128, L, 8, 16], F32)
        T16 = sb("T16", [128, 2, 16], F32)
        ST = sb("ST", [128, 2, 6], F32)
        MVA = sb("MVA", [128, 8, 2], F32)
        RSTD = sb("RSTD", [128, 8], F32)
        EPSC = sb("EPSC", [128, 1], F32)
        NMR = sb("NMR", [128, 8], F32)
        DUM = sb("DUM", [128, 2], F32)
        EPSP = sb("EPSP", [128, NTMAX], F32)
        CV = sb("CV", [128, L * 64], F32)
        BIN = sb("BIN", [128, L * 32], F32)
        VBH = sb("VBH", [33, 1024], BF16)
        VBL = sb("VBL", [33, 1024], BF16)
        VB2 = sb("VB2", [2, L, 1024], BF16)
        WSB = sb("WSB", [128, L * 1024], BF16)
        BF = sb("BF", [128, L * 1024], F32)
        CM = sb("CM", [128, 128], F32)
        PC = sb("PC", [128, 80], F32)
        ONEB = sb("ONEB", [128, 128], BF16)
        ONEF = sb("ONEF", [128, 128], F32)
        PS = es.enter_context(nc.psum_tensor("PS", [128, 8, 512], F32))
        WSF = BB[:, 1, 0:2 * L * 1024].bitcast(F32)
        BSB = BB[:, 5, 0:2 * L * 1024].bitcast(F32)

        for key in ("pe", "act", "dve", "pool", "sp", "cst", "xld", "ost", "pld", "dbg") + tuple(
                "w%d" % i for i in range(NWBUF)):
            P.add_sem(key, es.enter_context(nc.semaphore("s_" + key)))

        rX, rH, rO, rZ = R(8), R(8), R(8), R(8)
        rB = [R(8) for _ in range(6)]
        rS2 = Res()
        rW = R(NWBUF)
        rPB, rRB = Res(), Res()
        rTMP = R(2)
        rS1 = rTMP
        rZH = R(L)
        rT16, rST, rMV, rRSTD, rNMR, rEPSP = Res(), Res(), Res(), Res(), Res(), Res()
        rBT = R(8)
        rDUM = Res()
        rV = R(5)
        rPS = R(8)
        rC = Res()
        rVB2 = Res()
        rWSB, rBF = Res(), Res()

        def Bv(i):
            return BB[:, i, :].rearrange("p (c t) -> p c t", c=8)

        U, GA, GB, NB_, PL, MS = (Bv(i) for i in range(6))
        SQ = PL
        rU, rGA, rGB, rN, rPL, rMS = rB
        rSQ = rPL
        NTM = BB[:, 3, :]
        ACTB = BB[:, 0:4, :].rearrange("p a (c t) -> p (a c) t", c=8)
        rACT = rB[0] + rB[1] + rB[2] + rB[3]
        VTM = O[:, :, :].rearrange("p c t -> p (c t)")

        psn = [0]

        def banks(n):
            b = psn[0]
            if b + n > 8:
                b = 0
            psn[0] = (b + n) % 8
            return list(range(b, b + n))

        stream = []
        for (t0, nt) in TILES:
            for l in range(L_RUN):
                for j in range(NBLK):
                    stream.append((l, j))
        wstate = {"issued": 0, "cons": 0}

        def blk_len(j):
            return 2048 if j in (10, 35) else BLK

        def w_issue(extra_reads=()):
            i = wstate["issued"]
            if i >= len(stream):
                return
            l, j = stream[i]
            b = i % NWBUF
            n = blk_len(j)
            P.dma("pool", "w%d" % b,
                  lambda e, b=b, l=l, j=j, n=n: e.dma_start(out=WB[:, b, 0:n], in_=wst[l, j, :, 0:n]),
                  reads=list(extra_reads), writes=[rW[b]])
            wstate["issued"] += 1

        def w_next():
            i = wstate["cons"]
            wstate["cons"] += 1
            return i % NWBUF

        def w_done():
            w_issue()

        for (dst, src) in ((CV[:], cvec_d), (BIN[:], binfm_d), (CM[:], cmask_d), (PC[:], pcore_d)):
            P.dma("sp", "cst", lambda e, dst=dst, src=src: e.dma_start(out=dst, in_=src), writes=[rC])
        P.dma("sp", "xld", lambda e: e.dma_start(out=X[:, :, 0:TILES[0][1]], in_=xT[:, :, 0:TILES[0][1]]), writes=rX)
        w_issue()
        P.op("dve", lambda e: e.memset(ONEB[:], 1.0), writes=[rC])
        P.op("dve", lambda e: e.memset(ONEF[:], 1.0), writes=[rC])
        P.op("dve", lambda e: e.memset(EPSC[:], EPS), writes=[rC])
        rC.const = True

        BVR = VTM[0:33, 0:1024]
        BVT = VTM[0:33, 1024:2048]

        def setup_late():
            for _ in range(NWBUF - 1):
                w_issue(extra_reads=rX)
            for (dst, src) in ((WSF, wsT_d), (BSB, bsbc_d)):
                P.dma("sp", "cst", lambda e, dst=dst, src=src: e.dma_start(out=dst, in_=src),
                      writes=rB[1] + rB[5])
            P.op("dve", lambda e: e.memset(VTM[0:33, 0:2048], 0.0), writes=rO)
            for l in range(L):
                P.dma("sp", "cst", lambda e, l=l: e.dma_start(out=VTM[32 * l:32 * l + 1, 0:1024],
                                                              in_=bvrow_d[l:l + 1, :]), writes=rO)
            P.op("dve", lambda e: e.tensor_copy(out=VBH[:], in_=BVR), reads=rO, writes=[rVB2])
            P.op("dve", lambda e: e.tensor_copy(out=BVT, in_=VBH[:]), reads=[rVB2], writes=rO)
            P.op("dve", lambda e: e.tensor_tensor(out=BVT, in0=BVR, in1=BVT, op=ALU.subtract),
                 reads=rO, writes=rO)
            P.op("dve", lambda e: e.tensor_copy(out=VBL[:], in_=BVT), reads=rO, writes=[rVB2])
            for l in range(L):
                P.dma("sp", "cst", lambda e, l=l: e.dma_start(out=VB2[0:1, l, :], in_=VBH[32 * l:32 * l + 1, :]),
                      reads=[rVB2], writes=[rVB2])
                P.dma("sp", "cst", lambda e, l=l: e.dma_start(out=VB2[1:2, l, :], in_=VBL[32 * l:32 * l + 1, :]),
                      reads=[rVB2], writes=[rVB2])
            sgu_mask()

        def sgu_mask():
            for l in range(L):
                for h in range(8):
                    sl = slice(l * 1024 + h * 128, l * 1024 + (h + 1) * 128)
                    P.op("dve", lambda e, sl=sl: e.tensor_tensor(out=WSF[:, sl], in0=WSF[:, sl], in1=CM[:], op=ALU.mult),
                         reads=[rC] + rB[1], writes=rB[1])
                P.op("dve", lambda e, l=l: e.tensor_copy(out=WSB[:, l * 1024:(l + 1) * 1024],
                                                         in_=WSF[:, l * 1024:(l + 1) * 1024]),
                     reads=rB[1], writes=[rWSB])

        def sgu_bfull():
            for l in range(L):
                for hh in range(2):
                    bk = banks(1)[0]
                    P.group("pe", [lambda e, bk=bk, l=l, hh=hh: e.matmul(
                        PS[:, bk, :], ONEF[:], WSF[:, l * 1024 + hh * 512: l * 1024 + (hh + 1) * 512],
                        start=True, stop=True)], reads=[rC] + rB[1], writes=[rPS[bk]])
                    for h4 in range(4):
                        h = hh * 4 + h4
                        sl = slice(l * 1024 + h * 128, l * 1024 + (h + 1) * 128)
                        col = (l * 8 + V_LNB) * 8 + h
                        P.op("dve", lambda e, bk=bk, h4=h4, sl=sl, col=col: e.scalar_tensor_tensor(
                            out=BF[:, sl], in0=PS[:, bk, h4 * 128:(h4 + 1) * 128], scalar=CV[:, col:col + 1],
                            in1=BSB[:, sl], op0=ALU.mult, op1=ALU.add),
                            reads=[rPS[bk], rC] + rB[5], writes=[rBF])
            rWSB.const = True
            rBF.const = True

        def preload_ln_table():
            P.op("act", lambda e: e.activation(out=DUM[:, 0:1], in_=EPSC[:, 0:1], func=AF.Ln), reads=[rC], writes=[rDUM])

        def cv(l, vi, c):
            col = (l * 8 + vi) * 8 + c
            return CV[:, col:col + 1]

        dump_toks = []

        def dump(nm, ap, res):
            if nm in dump_d and nm not in [d[0] for d in dump_toks]:
                dump_toks.append((nm, P.dma("sp", "dbg", lambda e: e.dma_start(out=dump_d[nm], in_=ap), reads=res)))

        def colblocks(ca, nt):
            cbs = []
            c = ca
            while c < nt:
                w = min(512, nt - c)
                cbs.append((c, w))
                c += w
            return cbs

        GF = BB[:, 0:2, :].rearrange("p a t -> p (a t)").bitcast(F32).rearrange("p (c t) -> p c t", c=8)
        rG = [[rB[oc // 4][2 * (oc % 4)], rB[oc // 4][2 * (oc % 4) + 1]] for oc in range(8)]

        def tile_layer(ti, t0, nt, l, X, rX, Z, rZ, prefetch):
            last = (l == L_RUN - 1)
            ca = HALO if (ti == 0 and last) else 0
            cbs = colblocks(ca, nt)
            cbs_all = colblocks(0, nt)
            nch = nt // 128
            clo = ca // 128

            def norm_R(sq_res, mode="rsqrt", cbl=None):
                for (c0, w) in (cbl or cbs):
                    bk = banks(1)[0]
                    for dc in range(8):
                        P.group("pe", [lambda e, bk=bk, dc=dc, c0=c0, w=w: e.matmul(
                            PS[:, bk, 0:w], ONEB[:], SQ[:, dc, c0:c0 + w], start=(dc == 0), stop=(dc == 7))],
                            reads=[rC, sq_res[dc]], writes=[rPS[bk]])
                    if mode == "epsp":
                        P.op("dve", lambda e, bk=bk, c0=c0, w=w: e.tensor_scalar(
                            out=EPSP[:, c0:c0 + w], in0=PS[:, bk, 0:w], scalar1=1.0 / D, scalar2=EPS,
                            op0=ALU.mult, op1=ALU.add), reads=[rPS[bk]], writes=[rEPSP])
                        P.op("dve", lambda e, c0=c0, w=w: e.scalar_tensor_tensor(
                            out=EPSP[:, c0:c0 + w], in0=EPSP[:, c0:c0 + w], scalar=EPS, in1=EPSP[:, c0:c0 + w],
                            op0=ALU.mult, op1=ALU.mult), reads=[rEPSP], writes=[rEPSP])
                        continue
                    if mode == "rsqrt_epsp":
                        P.op("dve", lambda e, bk=bk, c0=c0, w=w: e.scalar_tensor_tensor(
                            out=RB[:, c0:c0 + w], in0=PS[:, bk, 0:w], scalar=1.0 / D, in1=EPSP[:, c0:c0 + w],
                            op0=ALU.mult, op1=ALU.add), reads=[rPS[bk], rEPSP], writes=[rRB])
                        P.op("act", lambda e, c0=c0, w=w: e.activation(
                            out=RB[:, c0:c0 + w], in_=RB[:, c0:c0 + w], func=AF.Ln), reads=[rRB], writes=[rRB])
                    else:
                        P.op("act", lambda e, bk=bk, c0=c0, w=w: e.activation(
                            out=RB[:, c0:c0 + w], in_=PS[:, bk, 0:w], func=AF.Ln, bias=EPSC[:, 0:1], scale=1.0 / D),
                            reads=[rPS[bk], rC], writes=[rRB])
                    P.op("act", lambda e, c0=c0, w=w: e.activation(
                        out=RB[:, c0:c0 + w], in_=RB[:, c0:c0 + w], func=AF.Exp, scale=-0.5),
                        reads=[rRB], writes=[rRB])

            def squares_of_X(a, b_):
                for dc in range(8):
                    P.op("act", lambda e, dc=dc: e.activation(out=SQ[:, dc, a:b_], in_=X[:, dc, a:b_], func=AF.Square),
                         reads=[rX[dc]], writes=[rSQ[dc]])

            def residual(vi, hmode=None, hvi=None, gained=False):
                n = nt - ca
                step = 2 if gained else 1
                for d0 in range(0, 8, step):
                    dcs = list(range(d0, d0 + step))
                    if gained:
                        P.op("dve", lambda e, d0=d0: e.tensor_tensor(
                            out=O[:, d0:d0 + 2, ca:nt], in0=O[:, d0:d0 + 2, ca:nt],
                            in1=RB[:, ca:nt].unsqueeze(1).broadcast_to([128, 2, n]), op=ALU.mult),
                            reads=[rO[d] for d in dcs] + [rRB], writes=[rO[d] for d in dcs])
                        P.op("dve", lambda e, d0=d0: e.tensor_tensor(
                            out=X[:, d0:d0 + 2, ca:nt], in0=X[:, d0:d0 + 2, ca:nt], in1=O[:, d0:d0 + 2, ca:nt],
                            op=ALU.add), reads=[rO[d] for d in dcs] + [rX[d] for d in dcs],
                            writes=[rX[d] for d in dcs])
                    else:
                        dc = d0
                        P.op("dve", lambda e, dc=dc: e.tensor_tensor(
                            out=O[:, dc, ca:nt], in0=O[:, dc, ca:nt], in1=RB[:, ca:nt], op=ALU.mult),
                            reads=[rO[dc], rRB], writes=[rO[dc]])
                        P.op("dve", lambda e, dc=dc: e.scalar_tensor_tensor(
                            out=X[:, dc, ca:nt], in0=O[:, dc, ca:nt], scalar=cv(l, vi, dc), in1=X[:, dc, ca:nt],
                            op0=ALU.mult, op1=ALU.add), reads=[rO[dc], rX[dc], rC], writes=[rX[dc]])
                    for dc in dcs:
                        if hmode == "gain":
                            P.op("act", lambda e, dc=dc: e.activation(
                                out=H[:, dc, ca:nt], in_=X[:, dc, ca:nt], func=AF.Identity, scale=cv(l, hvi, dc)),
                                reads=[rX[dc], rC], writes=[rH[dc]])
                        elif hmode == "copy":
                            P.op("act", lambda e, dc=dc: e.activation(
                                out=H[:, dc, ca:nt], in_=X[:, dc, ca:nt], func=AF.Copy),
                                reads=[rX[dc]], writes=[rH[dc]])

            def fm_matmul(b, rhs, rhs_res, nk, evac, fis=range(4), kstride=512, cbl=None, kouter=False, perk=False):
                cbl = cbl or cbs
                fis = list(fis)
                if kouter:
                    per = max(1, 4 // len(cbl))
                    for s0 in range(0, len(fis), per):
                        sub = fis[s0:s0 + per]
                        bkm = {fi: banks(len(cbl)) for fi in sub}
                        allb = [bk for fi in sub for bk in bkm[fi]]
                        for k in range(nk):
                            fns = []
                            for fi in sub:
                                for ci, (c0, w) in enumerate(cbl):
                                    fns.append(lambda e, bk=bkm[fi][ci], k=k, fi=fi, c0=c0, w=w: e.matmul(
                                        PS[:, bk, 0:w], WB[:, b, k * kstride + fi * 128: k * kstride + (fi + 1) * 128],
                                        rhs[:, k, c0:c0 + w], start=(k == 0), stop=(k == nk - 1)))
                            P.group("pe", fns, reads=[rW[b], rhs_res[k]], writes=[rPS[bk] for bk in allb])
                        for fi in sub:
                            for ci, (c0, w) in enumerate(cbl):
                                evac(fi, c0, w, bkm[fi][ci])
                    return
                for idx, fi in enumerate(fis):
                    bks = banks(len(cbl))
                    fns = []
                    for k in range(nk):
                        fk = []
                        for ci, (c0, w) in enumerate(cbl):
                            fk.append(lambda e, bk=bks[ci], k=k, fi=fi, c0=c0, w=w: e.matmul(
                                PS[:, bk, 0:w], WB[:, b, k * kstride + fi * 128: k * kstride + (fi + 1) * 128],
                                rhs[:, k, c0:c0 + w], start=(k == 0), stop=(k == nk - 1)))
                        if perk and idx == 0:
                            P.group("pe", fk, reads=[rW[b], rhs_res[k]], writes=[rPS[bk] for bk in bks])
                        else:
                            fns.extend(fk)
                    if fns:
                        P.group("pe", fns, reads=[rW[b]] + rhs_res, writes=[rPS[bk] for bk in bks])
                    for ci, (c0, w) in enumerate(cbl):
                        evac(fi, c0, w, bks[ci])

            P.dma("pool", "pld", lambda e: e.dma_start(out=PB[:, :, 0:nt], in_=pT[l, :, :, t0:t0 + nt]),
                  writes=[rPB])

            squares_of_X(0, nt)
            norm_R(rSQ, cbl=cbs_all)
            for dc in range(8):
                P.op("dve", lambda e, dc=dc: e.scalar_tensor_tensor(
                    out=H[:, dc, 0:nt], in0=X[:, dc, 0:nt], scalar=cv(l, V_PREMIX, dc), in1=RB[:, 0:nt],
                    op0=ALU.mult, op1=ALU.mult), reads=[rX[dc], rRB, rC], writes=[rH[dc]])

            if ti == 0 and l == 0:
                setup_late()
            def w_in_group(dst, dres, func, bcol, off, cbl, first):
                for half in range(2):
                    b = w_next()

                    def evac(fi, c0, w, bk, half=half):
                        fc = half * 4 + fi
                        col = l * 32 + bcol + fc
                        P.op("act", lambda e: e.activation(
                            out=dst[:, fc, off + c0: off + c0 + w], in_=PS[:, bk, 0:w], func=func,
                            bias=BIN[:, col:col + 1]), reads=[rPS[bk], rC], writes=[dres[fc]])
                    fm_matmul(b, H, rH, 8, evac, cbl=cbl, kouter=(first and half == 0))
                    w_done()

            w_in_group(Z, rZ, AF.Identity, 0, 16, cbs_all, True)
            W_ = 16 + nt

            def pool_prep():
                if ti == 0:
                    P.op("dve", lambda e: e.memset(Z[:, :, 0:16], 0.0), writes=rZ)
                    P.op("dve", lambda e: e.tensor_scalar(
                        out=Z[:, :, 16:16 + HALO], in0=Z[:, :, 16:16 + HALO], scalar1=PC[:, 0:1], scalar2=None,
                        op0=ALU.mult), reads=rZ + [rC], writes=rZ)
                else:
                    P.op("dve", lambda e: e.tensor_copy(out=Z[:, :, 0:16], in_=ZH[:, l, :, :]),
                         reads=[rZH[l]], writes=rZ)

            def pool_groups(gs):
                for g in gs:
                    zs = Z[:, 2 * g:2 * g + 2, :]
                    zres = [rZ[2 * g], rZ[2 * g + 1]]
                    P.op("dve", lambda e, zs=zs: e.tensor_tensor(
                        out=S1[:, :, 1:W_], in0=zs[:, :, 1:W_], in1=zs[:, :, 0:W_ - 1], op=ALU.add),
                        reads=zres, writes=rS1)
                    cur, rcur = S1, rS1
                    if g >= 1:
                        P.op("dve", lambda e: e.tensor_tensor(
                            out=S2[:, :, 3:W_], in0=S1[:, :, 3:W_], in1=S1[:, :, 1:W_ - 2], op=ALU.add),
                            reads=rS1, writes=[rS2])
                        cur, rcur = S2, [rS2]
                    if g >= 2:
                        P.op("dve", lambda e: e.tensor_tensor(
                            out=S1[:, :, 7:W_], in0=S2[:, :, 7:W_], in1=S2[:, :, 3:W_ - 4], op=ALU.add),
                            reads=[rS2], writes=rS1)
                        cur, rcur = S1, rS1
                    if g >= 3:
                        P.op("dve", lambda e: e.tensor_tensor(
                            out=S2[:, :, 15:W_], in0=S1[:, :, 15:W_], in1=S1[:, :, 7:W_ - 8], op=ALU.add),
                            reads=rS1, writes=[rS2])
                        cur, rcur = S2, [rS2]
                    wdw = POOL_WINDOWS[g]
                    P.op("dve", lambda e, cur=cur, g=g, wdw=wdw: e.scalar_tensor_tensor(
                        out=PL[:, 2 * g:2 * g + 2, 0:nt], in0=cur[:, :, 16:16 + nt], scalar=1.0 / wdw,
                        in1=Z[:, 2 * g:2 * g + 2, 16:16 + nt], op0=ALU.mult, op1=ALU.subtract),
                        reads=rcur + zres, writes=[rPL[2 * g], rPL[2 * g + 1]])
                    if ti == 0:
                        for cc in range(2):
                            P.op("dve", lambda e, cur=cur, g=g, cc=cc: e.tensor_tensor(
                                out=T16[:, cc, :], in0=cur[:, cc, 16 + HALO:32 + HALO],
                                in1=PC[:, 1 + g * 16: 1 + (g + 1) * 16], op=ALU.mult),
                                reads=rcur + [rC], writes=[rT16])
                        P.op("dve", lambda e, g=g: e.tensor_tensor(
                            out=PL[:, 2 * g:2 * g + 2, HALO:HALO + 16], in0=T16[:, :, :],
                            in1=Z[:, 2 * g:2 * g + 2, 16 + HALO:32 + HALO], op=ALU.subtract),
                            reads=[rT16] + zres, writes=[rPL[2 * g], rPL[2 * g + 1]])

            def pool_finish():
                dump("Z", Z[:, :, :], rZ)
                dump("PL", PL, rPL)
                dump("H", H[:, :, :], rH)
                P.op("dve", lambda e: e.tensor_copy(out=ZH[:, l, :, :], in_=Z[:, :, nt:nt + 16]),
                     reads=rZ, writes=[rZH[l]])
                if last and prefetch is not None:
                    pt0, pnt = prefetch
                    P.dma("sp", "xld", lambda e: e.dma_start(out=Z[:, :, 0:pnt], in_=xT[:, :, pt0:pt0 + pnt]),
                          writes=rZ)


            pool_prep()
            pool_groups((0, 1))

            vb = [w_next(), w_next()]
            first_v = [True]
            for half in range(2):
                b = vb[half]
                for c in range(clo, nch):
                    bk = banks(1)[0]
                    fns = []
                    for k in range(8):
                        fns.append(lambda e, bk=bk, k=k, c=c, b=b: e.matmul(
                            PS[:, bk, :], H[:, k, c * 128:(c + 1) * 128], WB[:, b, k * 512:(k + 1) * 512],
                            start=(k == 0), stop=False))
                    vsl = slice(half * 512, (half + 1) * 512)
                    fns.append(lambda e, bk=bk, vsl=vsl: e.matmul(
                        PS[:, bk, :], ONEB[0:2, :], VB2[0:2, l, vsl], start=False, stop=True))
                    P.group("pe", fns, reads=[rW[b], rC, rVB2] + rH, writes=[rPS[bk]])
                    o0 = c * 1024 + half * 512
                    P.op("act", lambda e, bk=bk, o0=o0: e.activation(
                        out=VTM[:, o0:o0 + 512], in_=PS[:, bk, :], func=AF.Gelu_apprx_tanh),
                        reads=[rPS[bk]], writes=([rV[c]] + (rO if first_v[0] else [])))
                    first_v[0] = False
                w_done()
            for c in range(clo, nch):
                for half in range(2):
                    o0 = c * 1024 + half * 512
                    P.op("dve", lambda e, o0=o0, half=half: e.bn_stats(out=ST[:, half, :], in_=VTM[:, o0:o0 + 512]),
                         reads=[rV[c]], writes=[rST])
                P.op("dve", lambda e, c=c: e.bn_aggr(out=MVA[:, c, :], in_=ST[:, :, :].rearrange("p a b -> p (a b)")),
                     reads=[rST], writes=[rMV])

            P.op("act", lambda e: e.activation(out=RSTD[:, clo:nch], in_=MVA[:, clo:nch, 1], func=AF.Ln,
                                               bias=EPSC[:, 0:1], scale=1.0), reads=[rMV, rC], writes=[rRSTD])
            P.op("act", lambda e: e.activation(out=RSTD[:, clo:nch], in_=RSTD[:, clo:nch], func=AF.Exp, scale=-0.5),
                 reads=[rRSTD], writes=[rRSTD])
            P.op("dve", lambda e: e.scalar_tensor_tensor(
                out=NMR[:, clo:nch], in0=MVA[:, clo:nch, 0], scalar=-1.0, in1=RSTD[:, clo:nch],
                op0=ALU.mult, op1=ALU.mult), reads=[rMV, rRSTD], writes=[rNMR])
            for c in range(clo, nch):
                P.op("act", lambda e, c=c: e.activation(
                    out=NTM[:, c * 1024:(c + 1) * 1024], in_=VTM[:, c * 1024:(c + 1) * 1024], func=AF.Identity,
                    scale=RSTD[:, c:c + 1], bias=NMR[:, c:c + 1]), reads=rO + [rV[c], rRSTD, rNMR], writes=rN)
            w_in_group(U, rU, AF.Gelu_apprx_tanh, 8, 0, cbs, False)
            if ti == 0 and l == 0:
                sgu_bfull()

            for h in range(8):
                for ci, (c0, w) in enumerate(cbs):
                    bk = banks(1)[0]
                    nck = w // 128
                    fns = []
                    for cc in range(nck):
                        c = c0 // 128 + cc
                        fns.append(lambda e, bk=bk, cc=cc, c=c, h=h: e.matmul(
                            PS[:, bk, cc * 128:(cc + 1) * 128], NTM[:, c * 1024 + h * 128: c * 1024 + (h + 1) * 128],
                            WSB[:, l * 1024 + h * 128: l * 1024 + (h + 1) * 128], start=True, stop=True))
                    P.group("pe", fns, reads=rN + [rC, rWSB], writes=[rPS[bk]])
                    tb = (h * len(cbs) + ci) % 2
                    bf = BF[:, l * 1024 + h * 128: l * 1024 + (h + 1) * 128]
                    P.op("dve", lambda e, bk=bk, nck=nck, w=w, tb=tb, bf=bf, h=h: e.scalar_tensor_tensor(
                        out=TMP[:, tb, 0:w].rearrange("p (a t) -> p a t", a=nck),
                        in0=PS[:, bk, 0:w].rearrange("p (a t) -> p a t", a=nck),
                        scalar=cv(l, V_LNG, h),
                        in1=bf.unsqueeze(1).broadcast_to([128, nck, 128]),
                        op0=ALU.mult, op1=ALU.add), reads=[rPS[bk], rC, rBF], writes=[rTMP[tb]])
                    P.op("dve", lambda e, tb=tb, h=h, c0=c0, w=w: e.tensor_tensor(
                        out=U[:, h, c0:c0 + w], in0=TMP[:, tb, 0:w], in1=U[:, h, c0:c0 + w], op=ALU.mult),
                        reads=[rTMP[tb], rU[h]], writes=[rU[h]])
            dump("SG", U, rU)
            dump("MS", MS, rMS)

            pool_groups((2, 3))
            pool_finish()
            w_in_group(GA, rGA, AF.Sigmoid, 16, 0, cbs, False)
            w_in_group(GB, rGB, AF.Sigmoid, 24, 0, cbs, False)
            preload_ln_table()

            b4 = w_next()

            def m4_groups(idx):
                for gi in idx:
                    g, dh = gi // 2, gi % 2
                    bks = banks(len(cbs))
                    fns = []
                    for cc in range(2):
                        for ci, (c0, w) in enumerate(cbs):
                            o0 = (g * 2 + cc) * 256 + dh * 128
                            fns.append(lambda e, bk=bks[ci], o0=o0, g=g, cc=cc, c0=c0, w=w: e.matmul(
                                PS[:, bk, 0:w], WB[:, b4, o0:o0 + 128], PL[:, 2 * g + cc, c0:c0 + w],
                                start=(cc == 0), stop=(cc == 1)))
                    P.group("pe", fns, reads=[rW[b4], rPL[2 * g], rPL[2 * g + 1]], writes=[rPS[bk] for bk in bks])
                    oc = 2 * g + dh
                    for ci, (c0, w) in enumerate(cbs):
                        if gi % 2 == 0:
                            P.op("dve", lambda e, bk=bks[ci], oc=oc, c0=c0, w=w: e.tensor_scalar(
                                out=MS[:, oc, c0:c0 + w], in0=PS[:, bk, 0:w], scalar1=cv(l, V_PSCALE, oc),
                                scalar2=None, op0=ALU.mult), reads=[rPS[bks[ci]], rC], writes=[rMS[oc]])
                        else:
                            P.op("act", lambda e, bk=bks[ci], oc=oc, c0=c0, w=w: e.activation(
                                out=MS[:, oc, c0:c0 + w], in_=PS[:, bk, 0:w], func=AF.Identity,
                                scale=cv(l, V_PSCALE, oc)), reads=[rPS[bks[ci]], rC], writes=[rMS[oc]])

            m4_groups(range(0, 8))
            w_done()
            dump("VTM", VTM, rO)
            dump("NTM", NTM, rN)

            for half in range(2):
                b = w_next()

                def evac(fi, c0, w, bk, half=half):
                    oc = half * 4 + fi
                    P.op("dve", lambda e: e.tensor_tensor(
                        out=GA[:, oc, c0:c0 + w], in0=PS[:, bk, 0:w], in1=GA[:, oc, c0:c0 + w], op=ALU.mult),
                        reads=[rPS[bk], rGA[oc]], writes=[rGA[oc]])
                fm_matmul(b, MS, rMS, 8, evac, perk=(half == 0))
                w_done()
            dump("M1", GA, rGA)

            for half in range(2):
                b = w_next()

                def evac(fi, c0, w, bk, half=half):
                    oc = half * 4 + fi
                    P.op("dve", lambda e: e.tensor_tensor(
                        out=GB[:, oc, c0:c0 + w], in0=PS[:, bk, 0:w], in1=GB[:, oc, c0:c0 + w], op=ALU.mult),
                        reads=[rPS[bk], rGB[oc]], writes=[rGB[oc]])
                    P.op("dve", lambda e: e.tensor_tensor(
                        out=GB[:, oc, c0:c0 + w], in0=GB[:, oc, c0:c0 + w], in1=GA[:, oc, c0:c0 + w], op=ALU.add),
                        reads=[rGB[oc], rGA[oc]], writes=[rGB[oc]])
                fm_matmul(b, U, rU, 8, evac, perk=(half == 0))
                w_done()

            def evac_O(oc, c0, w, bk, vi):
                P.op("act", lambda e: e.activation(out=SQ[:, oc, c0:c0 + w], in_=PS[:, bk, 0:w], func=AF.Square),
                     reads=[rPS[bk]], writes=[rSQ[oc], rBT[bk]])
                P.op("dve", lambda e: e.tensor_scalar(out=O[:, oc, c0:c0 + w], in0=PS[:, bk, 0:w],
                                                      scalar1=cv(l, vi, oc), scalar2=None, op0=ALU.mult),
                     reads=[rPS[bk], rBT[bk], rC], writes=[rO[oc]])

            for half in range(2):
                b = w_next()
                fm_matmul(b, GB, rGB, 8, lambda fi, c0, w, bk, half=half: evac_O(half * 4 + fi, c0, w, bk, V_POSTMIX),
                          perk=(half == 0))
                w_done()
            dump("MG", GB, rGB)
            dump("O1", O[:, :, :], rO)

            norm_R(rSQ)
            residual(V_POSTMIX, hmode="gain", hvi=V_PREFFN, gained=True)
            dump("X1", X[:, :, :], rX)

            for j in range(8):
                b = w_next()

                def evac(fi, c0, w, bk, j=j):
                    fc = j * 4 + fi
                    tb = fi % 2
                    P.op("act", lambda e: e.activation(out=TMP[:, tb, c0:c0 + w], in_=PS[:, bk, 0:w], func=AF.Relu),
                         reads=[rPS[bk]], writes=[rTMP[tb]])
                    P.op("dve", lambda e: e.tensor_tensor(
                        out=ACTB[:, fc, c0:c0 + w], in0=PS[:, bk, 0:w], in1=TMP[:, tb, c0:c0 + w], op=ALU.mult),
                        reads=[rPS[bk], rTMP[tb]], writes=[rACT[fc]])
                fm_matmul(b, H, rH, 8, evac, kouter=(j == 0))
                w_done()
                if j == 0:
                    squares_of_X(ca, nt)
                    norm_R(rSQ, mode="epsp")

            for j in range(8):
                b = w_next()
                fm_matmul(b, ACTB, rACT, 32, lambda fi, c0, w, bk, j=j: evac_O(j, c0, w, bk, V_POSTFFN),
                          fis=[0], kstride=128, perk=(j == 0))
                w_done()

            norm_R(rSQ, mode="rsqrt_epsp")
            residual(V_POSTFFN, hmode="copy", gained=True)

            for half in range(2):
                b = w_next()

                def evac(fi, c0, w, bk, half=half):
                    oc = half * 4 + fi
                    P.op("act", lambda e: e.activation(
                        out=GF[:, oc, c0:c0 + w], in_=PS[:, bk, 0:w], func=AF.Sigmoid),
                        reads=[rPS[bk]], writes=rG[oc])
                fm_matmul(b, H, rH, 8, evac, kouter=(half == 0))
                w_done()

            preload_ln_table()
            bp = w_next()
            for oc in range(8):
                bks = banks(len(cbs))
                fns = []
                for kc in range(2):
                    for ci, (c0, w) in enumerate(cbs):
                        fns.append(lambda e, bk=bks[ci], kc=kc, oc=oc, c0=c0, w=w: e.matmul(
                            PS[:, bk, 0:w], WB[:, bp, kc * 1024 + oc * 128: kc * 1024 + (oc + 1) * 128],
                            PB[:, kc, c0:c0 + w], start=(kc == 0), stop=(kc == 1)))
                P.group("pe", fns, reads=[rW[bp], rPB], writes=[rPS[bk] for bk in bks])
                for ci, (c0, w) in enumerate(cbs):
                    bk = bks[ci]
                    P.op("dve", lambda e, bk=bk, oc=oc, c0=c0, w=w: e.tensor_tensor(
                        out=O[:, oc, c0:c0 + w], in0=PS[:, bk, 0:w], in1=GF[:, oc, c0:c0 + w], op=ALU.mult),
                        reads=[rPS[bk]] + rG[oc], writes=[rO[oc]])
                    P.op("act", lambda e, oc=oc, c0=c0, w=w: e.activation(
                        out=SQ[:, oc, c0:c0 + w], in_=O[:, oc, c0:c0 + w], func=AF.Square),
                        reads=[rO[oc]], writes=[rSQ[oc]])
            w_done()

            norm_R(rSQ)
            residual(V_POSTPLE)

        bufs = [(X, rX), (Z, rZ)]
        last_store = None
        for ti, (t0, nt) in enumerate(TILES):
            (Xc, rXc), (Zc, rZc) = bufs[ti % 2], bufs[(ti + 1) % 2]
            prefetch = TILES[ti + 1] if ti + 1 < len(TILES) else None
            for l in range(L_RUN):
                tile_layer(ti, t0, nt, l, Xc, rXc, Zc, rZc, prefetch)
            s0 = HALO if ti == 0 else 0
            o0 = t0 + s0 - HALO
            n_out = nt - s0
            for dc in range(8):
                last_store = P.dma("sp", "ost", lambda e, s0=s0, o0=o0, n_out=n_out, Xc=Xc, dc=dc: e.dma_start(
                    out=outT[:, dc, o0:o0 + n_out], in_=Xc[:, dc, s0:s0 + n_out]), reads=[rXc[dc]])
            for r_ in rXc:
                r_.rs["ost"] = P.cnt["ost"]
        P.final_wait("sp", [last_store] + [d[1] for d in dump_toks])

        with nc.Block() as block:
            @block.tensor
            def _(e):
                for f in P.q["pe"]:
                    f(e)

            @block.scalar
            def _(e):
                for f in P.q["act"]:
                    f(e)

            @block.vector
            def _(e):
                for f in P.q["dve"]:
                    f(e)

            @block.gpsimd
            def _(e):
                for f in P.q["pool"]:
                    f(e)

            @block.sync
            def _(e):
                for f in P.q["sp"]:
                    f(e)
    return nc


def _fm8(v):
    return np.ascontiguousarray(v.reshape(8, 128).T)


def _blk_k512(W, col0):
    K = W.shape[0]
    return W[:, col0:col0 + 512].reshape(K // 128, 128, 512).transpose(1, 0, 2).reshape(128, -1)


def _build_wstream(inp):
    ws = np.zeros((L, NBLK, 128, BLK), np.float32)
    for l in range(L):
        w_in = inp["w_in"][l]
        order = [0, 512, 2048, 2560, 1024, 1536, 3072, 3584, 4096, 4608]
        for j, c0 in enumerate(order):
            ws[l, j] = _blk_k512(w_in, c0)
        ws[l, 10, :, :2048] = inp["pool_w"][l].reshape(4, 2, 128, 256).transpose(2, 0, 1, 3).reshape(128, 2048)
        for i, name in enumerate(("w_pa", "w_pb", "w_o")):
            for half in range(2):
                ws[l, 11 + 2 * i + half] = _blk_k512(inp[name][l], half * 512)
        for j in range(8):
            ws[l, 17 + j] = _blk_k512(inp["w_ff1"][l], j * 512)
        w2 = inp["w_ff2"][l]
        for j in range(8):
            ws[l, 25 + j] = w2[:, j * 128:(j + 1) * 128].reshape(32, 128, 128).transpose(1, 0, 2).reshape(128, BLK)
        for half in range(2):
            ws[l, 33 + half] = _blk_k512(inp["w_ple_gate"][l], half * 512)
        ws[l, 35, :, :2048] = inp["w_ple_proj"][l].reshape(2, 128, 1024).transpose(1, 0, 2).reshape(128, 2048)
    return ws


_NC_CACHE = {}


def make_in_maps(inp):
    x, p = inp["x"], inp["p"]
    B, S, _ = x.shape
    wst = _build_wstream(inp)
    cvec = np.zeros((128, L * 64), np.float32)
    names = ["pre_mix_g", "post_mix_g", "pre_ffn_g", "post_ffn_g", "post_ple_g", "pool_scale", "sgu_ln_g", "sgu_ln_b"]
    for l in range(L):
        for vi, nm in enumerate(names):
            cvec[:, (l * 8 + vi) * 8:(l * 8 + vi + 1) * 8] = _fm8(inp[nm][l])
    binfm = np.zeros((128, L * 32), np.float32)
    bvrow = np.zeros((L, 1024), np.float32)
    for l in range(L):
        b = inp["b_in"][l]
        for gi, c0 in enumerate((0, 1024, 3072, 4096)):
            binfm[:, l * 32 + gi * 8: l * 32 + (gi + 1) * 8] = _fm8(b[c0:c0 + 1024])
        bvrow[l] = b[2048:3072]
    wsT = np.ascontiguousarray(inp["sgu_w_s"].transpose(3, 0, 1, 2)).reshape(128, L * 1024)
    bsbc = np.ascontiguousarray(np.broadcast_to(inp["sgu_b_s"].reshape(1, L * 1024), (128, L * 1024)))
    si = np.arange(128)
    cmask = (si[:, None] <= si[None, :]).astype(np.float32)

    in_maps = []
    for c in range(NCORES):
        b, half = c // 2, c % 2
        s0 = half * OWN
        xt = np.zeros((NTOK, D), np.float32)
        pt = np.zeros((L, NTOK, 256), np.float32)
        xt[HALO:] = x[b, s0:s0 + OWN]
        pt[:, HALO:] = p[:, b, s0:s0 + OWN]
        pcore = np.zeros((128, 80), np.float32)
        if half == 1:
            xt[:HALO] = x[b, s0 - HALO:s0]
            pt[:, :HALO] = p[:, b, s0 - HALO:s0]
            pcore[:, 0] = 1.0
        for g, w in enumerate(POOL_WINDOWS):
            j = np.arange(16)
            cnt = np.minimum(j + 1, w) if half == 0 else np.full(16, w)
            pcore[:, 1 + g * 16: 1 + (g + 1) * 16] = (1.0 / cnt.astype(np.float32))[None, :]
        xTc = np.ascontiguousarray(xt.T.reshape(8, 128, NTOK).transpose(1, 0, 2))
        pTc = np.ascontiguousarray(pt.transpose(0, 2, 1).reshape(L, 2, 128, NTOK).transpose(0, 2, 1, 3))
        in_maps.append({"xT": xTc, "pT": pTc, "wst": wst, "cvec": cvec, "binfm": binfm, "bvrow": bvrow,
                        "wsT": wsT, "bsbc": bsbc, "cmask": cmask, "pcore": pcore})
    return in_maps


def kernel(**inputs):
    inp = {k: np.asarray(v, dtype=np.float32) for k, v in inputs.items()}
    B, S, _ = inp["x"].shape
    in_maps = make_in_maps(inp)
    if "nc" not in _NC_CACHE:
        _NC_CACHE["nc"] = build_nc()
    nc = _NC_CACHE["nc"]
    res = run_bass_kernel_spmd(nc, in_maps, core_ids=list(range(NCORES)))
    out = np.empty((B, S, D), np.float32)
    for c in range(NCORES):
        b, half = c // 2, c % 2
        o = np.asarray(res.results[c]["outT"], dtype=np.float32)
        out[b, half * OWN:(half + 1) * OWN, :] = o.transpose(1, 0, 2).reshape(D, OWN).T
    return out
```

```python
import numpy as np
import concourse.bass as bass
import concourse.mybir as mybir
from concourse.bass_utils import run_bass_kernel_spmd

F32 = mybir.dt.float32
BF16 = mybir.dt.bfloat16
AF = mybir.ActivationFunctionType
ALU = mybir.AluOpType

D = 1024
L = 2
NCORES = 8
OWN = 2048
HALO = 128
NTOK = OWN + HALO
TILES = [(0, 640), (640, 512), (1152, 512), (1664, 512)]
NTMAX = 640
ZW = 16 + NTMAX
EPS = 1e-6
NBLK = 36
NWBUF = 4
BLK = 4096
V_PREMIX, V_POSTMIX, V_PREFFN, V_POSTFFN, V_POSTPLE, V_PSCALE, V_LNG, V_LNB = range(8)
POOL_WINDOWS = (2, 4, 8, 16)
POOL_CHUNKS = ()
RES_ORDER = (0, 1, 2, 3, 4, 5, 6, 7)


class Res:
    __slots__ = ("w", "rs", "const")

    def __init__(self, const=False):
        self.w = None
        self.rs = {}
        self.const = const


class Prog:
    ENG = ("pe", "act", "dve", "pool", "sp")

    def __init__(self):
        self.q = {k: [] for k in self.ENG}
        self.semh = {}
        self.cnt = {}
        self.seen = {k: {} for k in self.ENG}

    def add_sem(self, key, handle):
        self.semh[key] = handle
        self.cnt[key] = 0

    def _wait(self, eng, toks):
        need = {}
        for t in toks:
            if t is None:
                continue
            key, val = t
            if key == "pe" and eng == "pe":
                continue
            if self.seen[eng].get(key, 0) >= val:
                continue
            if need.get(key, 0) < val:
                need[key] = val
        for key, val in need.items():
            self.seen[eng][key] = val
            s = self.semh[key]
            self.q[eng].append(lambda e, s=s, val=val: e.wait_ge(s, val))

    @staticmethod
    def _deps(reads, writes):
        deps = []
        for r in reads:
            deps.append(r.w)
        for w in writes:
            deps.append(w.w)
            deps.extend(w.rs.items())
        return deps

    @staticmethod
    def _commit(tok, reads, writes):
        for r in reads:
            if not r.const:
                if r.rs.get(tok[0], 0) < tok[1]:
                    r.rs[tok[0]] = tok[1]
        for w in writes:
            w.w = tok
            w.rs = {}

    def op(self, eng, fn, reads=(), writes=()):
        self._wait(eng, self._deps(reads, writes))
        self.cnt[eng] += 1
        tok = (eng, self.cnt[eng])
        s = self.semh[eng]
        self.q[eng].append(lambda e, fn=fn, s=s: fn(e).then_inc(s, 1))
        self._commit(tok, reads, writes)
        return tok

    def group(self, eng, fns, reads=(), writes=()):
        self._wait(eng, self._deps(reads, writes))
        self.cnt[eng] += 1
        tok = (eng, self.cnt[eng])
        s = self.semh[eng]
        for f in fns[:-1]:
            self.q[eng].append(f)
        last = fns[-1]
        self.q[eng].append(lambda e, fn=last, s=s: fn(e).then_inc(s, 1))
        self._commit(tok, reads, writes)
        return tok

    def dma(self, eng, semkey, fn, reads=(), writes=()):
        self._wait(eng, self._deps(reads, writes))
        self.cnt[semkey] += 16
        tok = (semkey, self.cnt[semkey])
        s = self.semh[semkey]
        self.q[eng].append(lambda e, fn=fn, s=s: fn(e).then_inc(s, 16))
        self._commit(tok, reads, writes)
        return tok

    def final_wait(self, eng, toks):
        self._wait(eng, toks)


def R(n, const=False):
    return [Res(const) for _ in range(n)]


def build_nc(TILES=TILES, L_RUN=L, NOUT=OWN, DUMPS=()):
    nc = bass.Bass("TRN2", target_bir_lowering=False)
    xT = nc.dram_tensor("xT", [128, 8, NTOK], F32, kind="ExternalInput").ap()
    pT = nc.dram_tensor("pT", [L, 128, 2, NTOK], F32, kind="ExternalInput").ap()
    wst = nc.dram_tensor("wst", [L, NBLK, 128, BLK], F32, kind="ExternalInput").ap()
    cvec_d = nc.dram_tensor("cvec", [128, L * 64], F32, kind="ExternalInput").ap()
    binfm_d = nc.dram_tensor("binfm", [128, L * 32], F32, kind="ExternalInput").ap()
    bvrow_d = nc.dram_tensor("bvrow", [L, 1024], F32, kind="ExternalInput").ap()
    wsT_d = nc.dram_tensor("wsT", [128, L * 1024], F32, kind="ExternalInput").ap()
    bsbc_d = nc.dram_tensor("bsbc", [128, L * 1024], F32, kind="ExternalInput").ap()
    cmask_d = nc.dram_tensor("cmask", [128, 128], F32, kind="ExternalInput").ap()
    pcore_d = nc.dram_tensor("pcore", [128, 80], F32, kind="ExternalInput").ap()
    outT = nc.dram_tensor("outT", [128, 8, NOUT], F32, kind="ExternalOutput").ap()

    dump_d = {}
    for (nm, shp, dt) in DUMPS:
        dump_d[nm] = nc.dram_tensor("dbg_" + nm, shp, dt, kind="ExternalOutput").ap()
    P = Prog()
    from contextlib import ExitStack
    with ExitStack() as es:
        def sb(name, shape, dt):
            return es.enter_context(nc.sbuf_tensor(name, shape, dt))

        X = sb("X", [128, 8, ZW], F32)
        H = sb("H", [128, 8, NTMAX], BF16)
        O = sb("O", [128, 8, NTMAX], F32)
        Z = sb("Z", [128, 8, ZW], F32)
        S2 = sb("S2", [128, 2, ZW], F32)
        BB = sb("BB", [128, 6, 8 * NTMAX], BF16)
        WB = sb("WB", [128, NWBUF, BLK], BF16)
        PB = sb("PB", [128, 2, NTMAX], BF16)
        RB = sb("RB", [128, NTMAX], F32)
        TMP = sb("TMP", [128, 2, ZW], F32)
        S1 = TMP
        ZH = sb("ZH", [128, L, 8, 16], F32)
        T16 = sb("T16", [128, 2, 16], F32)
        ST = sb("ST", [128, 2, 6], F32)
        MVA = sb("MVA", [128, 8, 2], F32)
        RSTD = sb("RSTD", [128, 8], F32)
        EPSC = sb("EPSC", [128, 1], F32)
        NMR = sb("NMR", [128, 8], F32)
        DUM = sb("DUM", [128, 2], F32)
        EPSP = sb("EPSP", [128, NTMAX], F32)
        CV = sb("CV", [128, L * 64], F32)
        BIN = sb("BIN", [128, L * 32], F32)
        VBH = sb("VBH", [33, 1024], BF16)
        VBL = sb("VBL", [33, 1024], BF16)
        VB2 = sb("VB2", [2, L, 1024], BF16)
        WSB = sb("WSB", [128, L * 1024], BF16)
        BF = sb("BF", [128, L * 1024], F32)
        CM = sb("CM", [128, 128], F32)
        PC = sb("PC", [128, 80], F32)
        ONEB = sb("ONEB", [128, 128], BF16)
        ONEF = sb("ONEF", [128, 128], F32)
        PS = es.enter_context(nc.psum_tensor("PS", [128, 8, 512], F32))
        WSF = BB[:, 1, 0:2 * L * 1024].bitcast(F32)
        BSB = BB[:, 5, 0:2 * L * 1024].bitcast(F32)

        for key in ("pe", "act", "dve", "pool", "sp", "cst", "cstB", "cstC", "cstD", "xld", "ost", "pld", "dbg") + tuple(
                "w%d" % i for i in range(NWBUF)):
            P.add_sem(key, es.enter_context(nc.semaphore("s_" + key)))

        rX, rH, rO, rZ = R(8), R(8), R(8), R(8)
        rB = [R(8) for _ in range(6)]
        rS2 = Res()
        rW = R(NWBUF)
        rPB, rRB = Res(), Res()
        rTMP = R(2)
        rS1 = rTMP
        rZH = R(L)
        rT16, rST, rMV, rRSTD, rNMR, rEPSP = Res(), Res(), Res(), Res(), Res(), Res()
        rBT = R(8)
        rDUM = Res()
        rV = R(5)
        rPS = R(8)
        rC = Res()
        rVB2 = Res()
        rWSB, rBF = Res(), Res()

        def Bv(i):
            return BB[:, i, :].rearrange("p (c t) -> p c t", c=8)

        U, GA, GB, NB_, PL, MS = (Bv(i) for i in range(6))
        SQ = PL
        rU, rGA, rGB, rN, rPL, rMS = rB
        rSQ = rPL
        NTM = BB[:, 3, :]
        ACTB = BB[:, 0:4, :].rearrange("p a (c t) -> p (a c) t", c=8)
        rACT = rB[0] + rB[1] + rB[2] + rB[3]
        VTM = O[:, :, :].rearrange("p c t -> p (c t)")

        psn = [0]

        def banks(n):
            b = psn[0]
            if b + n > 8:
                b = 0
            psn[0] = (b + n) % 8
            return list(range(b, b + n))

        stream = []
        for (t0, nt) in TILES:
            for l in range(L_RUN):
                for j in range(NBLK):
                    stream.append((l, j))
        wstate = {"issued": 0, "cons": 0}

        def blk_len(j):
            return 2048 if j in (10, 35) else BLK

        def w_issue(extra_reads=()):
            i = wstate["issued"]
            if i >= len(stream):
                return
            l, j = stream[i]
            b = i % NWBUF
            n = blk_len(j)
            P.dma("pool", "w%d" % b,
                  lambda e, b=b, l=l, j=j, n=n: e.dma_start(out=WB[:, b, 0:n], in_=wst[l, j, :, 0:n]),
                  reads=list(extra_reads), writes=[rW[b]])
            wstate["issued"] += 1

        def w_next():
            i = wstate["cons"]
            wstate["cons"] += 1
            return i % NWBUF

        def w_done():
            w_issue()

        for (dst, src) in ((CV[:], cvec_d), (BIN[:], binfm_d), (CM[:], cmask_d), (PC[:], pcore_d)):
            P.dma("sp", "cst", lambda e, dst=dst, src=src: e.dma_start(out=dst, in_=src), writes=[rC])
        P.dma("sp", "xld", lambda e: e.dma_start(out=X[:, :, 0:TILES[0][1]], in_=xT[:, :, 0:TILES[0][1]]), writes=rX)
        w_issue()
        P.op("dve", lambda e: e.memset(ONEB[:], 1.0), writes=[rC])
        P.op("dve", lambda e: e.memset(ONEF[:], 1.0), writes=[rC])
        P.op("dve", lambda e: e.memset(EPSC[:], EPS), writes=[rC])
        rC.const = True

        BVR = VTM[0:33, 0:1024]
        BVT = VTM[0:33, 1024:2048]

        def setup_late():
            for _ in range(NWBUF - 1):
                w_issue(extra_reads=rX)
            for (dst, src) in ((WSF, wsT_d), (BSB, bsbc_d)):
                P.dma("sp", "cstB", lambda e, dst=dst, src=src: e.dma_start(out=dst, in_=src),
                      writes=rB[1] + rB[5])
            P.op("dve", lambda e: e.memset(VTM[0:33, 0:2048], 0.0), writes=rO)
            for l in range(L):
                P.dma("sp", "cstC", lambda e, l=l: e.dma_start(out=VTM[32 * l:32 * l + 1, 0:1024],
                                                               in_=bvrow_d[l:l + 1, :]), writes=rO)
            P.op("dve", lambda e: e.tensor_copy(out=VBH[:], in_=BVR), reads=rO, writes=[rVB2])
            P.op("dve", lambda e: e.tensor_copy(out=BVT, in_=VBH[:]), reads=[rVB2], writes=rO)
            P.op("dve", lambda e: e.tensor_tensor(out=BVT, in0=BVR, in1=BVT, op=ALU.subtract),
                 reads=rO, writes=rO)
            P.op("dve", lambda e: e.tensor_copy(out=VBL[:], in_=BVT), reads=rO, writes=[rVB2])
            for l in range(L):
                P.dma("sp", "cstD", lambda e, l=l: e.dma_start(out=VB2[0:1, l, :], in_=VBH[32 * l:32 * l + 1, :]),
                      reads=[rVB2], writes=[rVB2])
                P.dma("sp", "cstD", lambda e, l=l: e.dma_start(out=VB2[1:2, l, :], in_=VBL[32 * l:32 * l + 1, :]),
                      reads=[rVB2], writes=[rVB2])
            sgu_mask()

        def sgu_mask():
            for l in range(L):
                for h in range(8):
                    sl = slice(l * 1024 + h * 128, l * 1024 + (h + 1) * 128)
                    P.op("dve", lambda e, sl=sl: e.tensor_tensor(out=WSF[:, sl], in0=WSF[:, sl], in1=CM[:], op=ALU.mult),
                         reads=[rC] + rB[1], writes=rB[1])
                P.op("dve", lambda e, l=l: e.tensor_copy(out=WSB[:, l * 1024:(l + 1) * 1024],
                                                         in_=WSF[:, l * 1024:(l + 1) * 1024]),
                     reads=rB[1], writes=[rWSB])

        def sgu_bfull():
            for l in range(L):
                for hh in range(2):
                    bk = banks(1)[0]
                    P.group("pe", [lambda e, bk=bk, l=l, hh=hh: e.matmul(
                        PS[:, bk, :], ONEF[:], WSF[:, l * 1024 + hh * 512: l * 1024 + (hh + 1) * 512],
                        start=True, stop=True)], reads=[rC] + rB[1], writes=[rPS[bk]])
                    for h4 in range(4):
                        h = hh * 4 + h4
                        sl = slice(l * 1024 + h * 128, l * 1024 + (h + 1) * 128)
                        col = (l * 8 + V_LNB) * 8 + h
                        P.op("dve", lambda e, bk=bk, h4=h4, sl=sl, col=col: e.scalar_tensor_tensor(
                            out=BF[:, sl], in0=PS[:, bk, h4 * 128:(h4 + 1) * 128], scalar=CV[:, col:col + 1],
                            in1=BSB[:, sl], op0=ALU.mult, op1=ALU.add),
                            reads=[rPS[bk], rC] + rB[5], writes=[rBF])
            rWSB.const = True
            rBF.const = True

        def preload_ln_table():
            P.op("act", lambda e: e.activation(out=DUM[:, 0:1], in_=EPSC[:, 0:1], func=AF.Ln), reads=[rC], writes=[rDUM])

        def cv(l, vi, c):
            col = (l * 8 + vi) * 8 + c
            return CV[:, col:col + 1]

        dump_toks = []

        def dump(nm, ap, res):
            if nm in dump_d and nm not in [d[0] for d in dump_toks]:
                dump_toks.append((nm, P.dma("sp", "dbg", lambda e: e.dma_start(out=dump_d[nm], in_=ap), reads=res)))

        def colblocks(ca, nt):
            cbs = []
            c = ca
            while c < nt:
                w = min(512, nt - c)
                cbs.append((c, w))
                c += w
            return cbs

        GF = BB[:, 0:2, :].rearrange("p a t -> p (a t)").bitcast(F32).rearrange("p (c t) -> p c t", c=8)
        rG = [[rB[oc // 4][2 * (oc % 4)], rB[oc // 4][2 * (oc % 4) + 1]] for oc in range(8)]

        def tile_layer(ti, t0, nt, l, X, rX, Z, rZ, prefetch):
            last = (l == L_RUN - 1)
            ca = HALO if (ti == 0 and last) else 0
            cbs = colblocks(ca, nt)
            cbs_all = colblocks(0, nt)
            nch = nt // 128
            clo = ca // 128

            def norm_R(sq_res, mode="rsqrt", cbl=None):
                for (c0, w) in (cbl or cbs):
                    bk = banks(1)[0]
                    for dc in range(8):
                        P.group("pe", [lambda e, bk=bk, dc=dc, c0=c0, w=w: e.matmul(
                            PS[:, bk, 0:w], ONEB[:], SQ[:, dc, c0:c0 + w], start=(dc == 0), stop=(dc == 7))],
                            reads=[rC, sq_res[dc]], writes=[rPS[bk]])
                    if mode == "epsp":
                        P.op("dve", lambda e, bk=bk, c0=c0, w=w: e.tensor_scalar(
                            out=EPSP[:, c0:c0 + w], in0=PS[:, bk, 0:w], scalar1=1.0 / D, scalar2=EPS,
                            op0=ALU.mult, op1=ALU.add), reads=[rPS[bk]], writes=[rEPSP])
                        P.op("dve", lambda e, c0=c0, w=w: e.scalar_tensor_tensor(
                            out=EPSP[:, c0:c0 + w], in0=EPSP[:, c0:c0 + w], scalar=EPS, in1=EPSP[:, c0:c0 + w],
                            op0=ALU.mult, op1=ALU.mult), reads=[rEPSP], writes=[rEPSP])
                        continue
                    if mode == "rsqrt_epsp":
                        P.op("dve", lambda e, bk=bk, c0=c0, w=w: e.scalar_tensor_tensor(
                            out=RB[:, c0:c0 + w], in0=PS[:, bk, 0:w], scalar=1.0 / D, in1=EPSP[:, c0:c0 + w],
                            op0=ALU.mult, op1=ALU.add), reads=[rPS[bk], rEPSP], writes=[rRB])
                        P.op("act", lambda e, c0=c0, w=w: e.activation(
                            out=RB[:, c0:c0 + w], in_=RB[:, c0:c0 + w], func=AF.Ln), reads=[rRB], writes=[rRB])
                    else:
                        P.op("act", lambda e, bk=bk, c0=c0, w=w: e.activation(
                            out=RB[:, c0:c0 + w], in_=PS[:, bk, 0:w], func=AF.Ln, bias=EPSC[:, 0:1], scale=1.0 / D),
                            reads=[rPS[bk], rC], writes=[rRB])
                    P.op("act", lambda e, c0=c0, w=w: e.activation(
                        out=RB[:, c0:c0 + w], in_=RB[:, c0:c0 + w], func=AF.Exp, scale=-0.5),
                        reads=[rRB], writes=[rRB])

            def squares_of_X(a, b_):
                for dc in range(8):
                    P.op("act", lambda e, dc=dc: e.activation(out=SQ[:, dc, a:b_], in_=X[:, dc, a:b_], func=AF.Square),
                         reads=[rX[dc]], writes=[rSQ[dc]])

            def residual(vi, hmode=None, hvi=None, gained=False):
                n = nt - ca
                step = 2
                for d0 in range(0, 8, step):
                    dcs = list(range(d0, d0 + step))
                    if gained:
                        P.op("dve", lambda e, d0=d0: e.tensor_tensor(
                            out=O[:, d0:d0 + 2, ca:nt], in0=O[:, d0:d0 + 2, ca:nt],
                            in1=RB[:, ca:nt].unsqueeze(1).broadcast_to([128, 2, n]), op=ALU.mult),
                            reads=[rO[d] for d in dcs] + [rRB], writes=[rO[d] for d in dcs])
                        P.op("dve", lambda e, d0=d0: e.tensor_tensor(
                            out=X[:, d0:d0 + 2, ca:nt], in0=X[:, d0:d0 + 2, ca:nt], in1=O[:, d0:d0 + 2, ca:nt],
                            op=ALU.add), reads=[rO[d] for d in dcs] + [rX[d] for d in dcs],
                            writes=[rX[d] for d in dcs])
                    else:
                        P.op("dve", lambda e, d0=d0: e.tensor_tensor(
                            out=O[:, d0:d0 + 2, ca:nt], in0=O[:, d0:d0 + 2, ca:nt],
                            in1=RB[:, ca:nt].unsqueeze(1).broadcast_to([128, 2, n]), op=ALU.mult),
                            reads=[rO[d] for d in dcs] + [rRB], writes=[rO[d] for d in dcs])
                        for dc in dcs:
                            P.op("dve", lambda e, dc=dc: e.scalar_tensor_tensor(
                                out=X[:, dc, ca:nt], in0=O[:, dc, ca:nt], scalar=cv(l, vi, dc), in1=X[:, dc, ca:nt],
                                op0=ALU.mult, op1=ALU.add), reads=[rO[dc], rX[dc], rC], writes=[rX[dc]])
                    for dc in dcs:
                        if hmode == "gain":
                            P.op("act", lambda e, dc=dc: e.activation(
                                out=H[:, dc, ca:nt], in_=X[:, dc, ca:nt], func=AF.Identity, scale=cv(l, hvi, dc)),
                                reads=[rX[dc], rC], writes=[rH[dc]])
                        elif hmode == "copy":
                            P.op("act", lambda e, dc=dc: e.activation(
                                out=H[:, dc, ca:nt], in_=X[:, dc, ca:nt], func=AF.Copy),
                                reads=[rX[dc]], writes=[rH[dc]])

            def fm_matmul(b, rhs, rhs_res, nk, evac, fis=range(4), kstride=512, cbl=None, kouter=False, perk=False):
                cbl = cbl or cbs
                fis = list(fis)
                if kouter:
                    per = max(1, 4 // len(cbl))
                    for s0 in range(0, len(fis), per):
                        sub = fis[s0:s0 + per]
                        bkm = {fi: banks(len(cbl)) for fi in sub}
                        allb = [bk for fi in sub for bk in bkm[fi]]
                        for k in range(nk):
                            fns = []
                            for fi in sub:
                                for ci, (c0, w) in enumerate(cbl):
                                    fns.append(lambda e, bk=bkm[fi][ci], k=k, fi=fi, c0=c0, w=w: e.matmul(
                                        PS[:, bk, 0:w], WB[:, b, k * kstride + fi * 128: k * kstride + (fi + 1) * 128],
                                        rhs[:, k, c0:c0 + w], start=(k == 0), stop=(k == nk - 1)))
                            P.group("pe", fns, reads=[rW[b], rhs_res[k]], writes=[rPS[bk] for bk in allb])
                        for fi in sub:
                            for ci, (c0, w) in enumerate(cbl):
                                evac(fi, c0, w, bkm[fi][ci])
                    return
                for idx, fi in enumerate(fis):
                    bks = banks(len(cbl))
                    fns = []
                    for k in range(nk):
                        fk = []
                        for ci, (c0, w) in enumerate(cbl):
                            fk.append(lambda e, bk=bks[ci], k=k, fi=fi, c0=c0, w=w: e.matmul(
                                PS[:, bk, 0:w], WB[:, b, k * kstride + fi * 128: k * kstride + (fi + 1) * 128],
                                rhs[:, k, c0:c0 + w], start=(k == 0), stop=(k == nk - 1)))
                        if perk and idx == 0:
                            P.group("pe", fk, reads=[rW[b], rhs_res[k]], writes=[rPS[bk] for bk in bks])
                        else:
                            fns.extend(fk)
                    if fns:
                        P.group("pe", fns, reads=[rW[b]] + rhs_res, writes=[rPS[bk] for bk in bks])
                    for ci, (c0, w) in enumerate(cbl):
                        evac(fi, c0, w, bks[ci])

            P.dma("pool", "pld", lambda e: e.dma_start(out=PB[:, :, 0:nt], in_=pT[l, :, :, t0:t0 + nt]),
                  writes=[rPB])

            squares_of_X(0, nt)
            norm_R(rSQ, cbl=cbs_all)
            for dc in range(8):
                P.op("dve", lambda e, dc=dc: e.scalar_tensor_tensor(
                    out=H[:, dc, 0:nt], in0=X[:, dc, 0:nt], scalar=cv(l, V_PREMIX, dc), in1=RB[:, 0:nt],
                    op0=ALU.mult, op1=ALU.mult), reads=[rX[dc], rRB, rC], writes=[rH[dc]])

            if ti == 0 and l == 0:
                setup_late()
            def w_in_group(dst, dres, func, bcol, off, cbl, first):
                for half in range(2):
                    b = w_next()

                    def evac(fi, c0, w, bk, half=half):
                        fc = half * 4 + fi
                        col = l * 32 + bcol + fc
                        P.op("act", lambda e: e.activation(
                            out=dst[:, fc, off + c0: off + c0 + w], in_=PS[:, bk, 0:w], func=func,
                            bias=BIN[:, col:col + 1]), reads=[rPS[bk], rC], writes=[dres[fc]])
                    fm_matmul(b, H, rH, 8, evac, cbl=cbl, kouter=(first and half == 0))
                    w_done()

            w_in_group(Z, rZ, AF.Identity, 0, 16, cbs_all, True)
            W_ = 16 + nt

            def pool_prep():
                if ti == 0:
                    P.op("dve", lambda e: e.memset(Z[:, :, 0:16], 0.0), writes=rZ)
                    P.op("dve", lambda e: e.tensor_scalar(
                        out=Z[:, :, 16:16 + HALO], in0=Z[:, :, 16:16 + HALO], scalar1=PC[:, 0:1], scalar2=None,
                        op0=ALU.mult), reads=rZ + [rC], writes=rZ)
                else:
                    P.op("dve", lambda e: e.tensor_copy(out=Z[:, :, 0:16], in_=ZH[:, l, :, :]),
                         reads=[rZH[l]], writes=rZ)

            def pool_groups(gs):
                for g in gs:
                    zs = Z[:, 2 * g:2 * g + 2, :]
                    zres = [rZ[2 * g], rZ[2 * g + 1]]
                    P.op("dve", lambda e, zs=zs: e.tensor_tensor(
                        out=S1[:, :, 1:W_], in0=zs[:, :, 1:W_], in1=zs[:, :, 0:W_ - 1], op=ALU.add),
                        reads=zres, writes=rS1)
                    cur, rcur = S1, rS1
                    if g >= 1:
                        P.op("dve", lambda e: e.tensor_tensor(
                            out=S2[:, :, 3:W_], in0=S1[:, :, 3:W_], in1=S1[:, :, 1:W_ - 2], op=ALU.add),
                            reads=rS1, writes=[rS2])
                        cur, rcur = S2, [rS2]
                    if g >= 2:
                        P.op("dve", lambda e: e.tensor_tensor(
                            out=S1[:, :, 7:W_], in0=S2[:, :, 7:W_], in1=S2[:, :, 3:W_ - 4], op=ALU.add),
                            reads=[rS2], writes=rS1)
                        cur, rcur = S1, rS1
                    if g >= 3:
                        P.op("dve", lambda e: e.tensor_tensor(
                            out=S2[:, :, 15:W_], in0=S1[:, :, 15:W_], in1=S1[:, :, 7:W_ - 8], op=ALU.add),
                            reads=rS1, writes=[rS2])
                        cur, rcur = S2, [rS2]
                    wdw = POOL_WINDOWS[g]
                    P.op("dve", lambda e, cur=cur, g=g, wdw=wdw: e.scalar_tensor_tensor(
                        out=PL[:, 2 * g:2 * g + 2, 0:nt], in0=cur[:, :, 16:16 + nt], scalar=1.0 / wdw,
                        in1=Z[:, 2 * g:2 * g + 2, 16:16 + nt], op0=ALU.mult, op1=ALU.subtract),
                        reads=rcur + zres, writes=[rPL[2 * g], rPL[2 * g + 1]])
                    if ti == 0:
                        for cc in range(2):
                            P.op("dve", lambda e, cur=cur, g=g, cc=cc: e.tensor_tensor(
                                out=T16[:, cc, :], in0=cur[:, cc, 16 + HALO:32 + HALO],
                                in1=PC[:, 1 + g * 16: 1 + (g + 1) * 16], op=ALU.mult),
                                reads=rcur + [rC], writes=[rT16])
                        P.op("dve", lambda e, g=g: e.tensor_tensor(
                            out=PL[:, 2 * g:2 * g + 2, HALO:HALO + 16], in0=T16[:, :, :],
                            in1=Z[:, 2 * g:2 * g + 2, 16 + HALO:32 + HALO], op=ALU.subtract),
                            reads=[rT16] + zres, writes=[rPL[2 * g], rPL[2 * g + 1]])

            def pool_finish():
                dump("Z", Z[:, :, :], rZ)
                dump("PL", PL, rPL)
                dump("H", H[:, :, :], rH)
                P.op("dve", lambda e: e.tensor_copy(out=ZH[:, l, :, :], in_=Z[:, :, nt:nt + 16]),
                     reads=rZ, writes=[rZH[l]])
                if last and prefetch is not None:
                    pt0, pnt = prefetch
                    P.dma("sp", "xld", lambda e: e.dma_start(out=Z[:, :, 0:pnt], in_=xT[:, :, pt0:pt0 + pnt]),
                          writes=rZ)


            pool_prep()
            pool_groups((0, 1))

            vb = [w_next(), w_next()]
            first_v = [True]
            for half in range(2):
                b = vb[half]
                for c in range(clo, nch):
                    bk = banks(1)[0]
                    fns = []
                    for k in range(8):
                        fns.append(lambda e, bk=bk, k=k, c=c, b=b: e.matmul(
                            PS[:, bk, :], H[:, k, c * 128:(c + 1) * 128], WB[:, b, k * 512:(k + 1) * 512],
                            start=(k == 0), stop=False))
                    vsl = slice(half * 512, (half + 1) * 512)
                    fns.append(lambda e, bk=bk, vsl=vsl: e.matmul(
                        PS[:, bk, :], ONEB[0:2, :], VB2[0:2, l, vsl], start=False, stop=True))
                    P.group("pe", fns, reads=[rW[b], rC, rVB2] + rH, writes=[rPS[bk]])
                    o0 = c * 1024 + half * 512
                    P.op("act", lambda e, bk=bk, o0=o0: e.activation(
                        out=VTM[:, o0:o0 + 512], in_=PS[:, bk, :], func=AF.Gelu_apprx_tanh),
                        reads=[rPS[bk]], writes=([rV[c]] + (rO if first_v[0] else [])))
                    first_v[0] = False
                w_done()
            for c in range(clo, nch):
                for half in range(2):
                    o0 = c * 1024 + half * 512
                    P.op("dve", lambda e, o0=o0, half=half: e.bn_stats(out=ST[:, half, :], in_=VTM[:, o0:o0 + 512]),
                         reads=[rV[c]], writes=[rST])
                P.op("dve", lambda e, c=c: e.bn_aggr(out=MVA[:, c, :], in_=ST[:, :, :].rearrange("p a b -> p (a b)")),
                     reads=[rST], writes=[rMV])

            P.op("act", lambda e: e.activation(out=RSTD[:, clo:nch], in_=MVA[:, clo:nch, 1], func=AF.Ln,
                                               bias=EPSC[:, 0:1], scale=1.0), reads=[rMV, rC], writes=[rRSTD])
            P.op("act", lambda e: e.activation(out=RSTD[:, clo:nch], in_=RSTD[:, clo:nch], func=AF.Exp, scale=-0.5),
                 reads=[rRSTD], writes=[rRSTD])
            P.op("dve", lambda e: e.scalar_tensor_tensor(
                out=NMR[:, clo:nch], in0=MVA[:, clo:nch, 0], scalar=-1.0, in1=RSTD[:, clo:nch],
                op0=ALU.mult, op1=ALU.mult), reads=[rMV, rRSTD], writes=[rNMR])
            for c in range(clo, nch):
                P.op("act", lambda e, c=c: e.activation(
                    out=NTM[:, c * 1024:(c + 1) * 1024], in_=VTM[:, c * 1024:(c + 1) * 1024], func=AF.Identity,
                    scale=RSTD[:, c:c + 1], bias=NMR[:, c:c + 1]), reads=rO + [rV[c], rRSTD, rNMR], writes=rN)
            w_in_group(U, rU, AF.Gelu_apprx_tanh, 8, 0, cbs, False)
            if ti == 0 and l == 0:
                sgu_bfull()

            for h in range(8):
                for ci, (c0, w) in enumerate(cbs):
                    bk = banks(1)[0]
                    nck = w // 128
                    fns = []
                    for cc in range(nck):
                        c = c0 // 128 + cc
                        fns.append(lambda e, bk=bk, cc=cc, c=c, h=h: e.matmul(
                            PS[:, bk, cc * 128:(cc + 1) * 128], NTM[:, c * 1024 + h * 128: c * 1024 + (h + 1) * 128],
                            WSB[:, l * 1024 + h * 128: l * 1024 + (h + 1) * 128], start=True, stop=True))
                    P.group("pe", fns, reads=rN + [rC, rWSB], writes=[rPS[bk]])
                    tb = (h * len(cbs) + ci) % 2
                    bf = BF[:, l * 1024 + h * 128: l * 1024 + (h + 1) * 128]
                    P.op("dve", lambda e, bk=bk, nck=nck, w=w, tb=tb, bf=bf, h=h: e.scalar_tensor_tensor(
                        out=TMP[:, tb, 0:w].rearrange("p (a t) -> p a t", a=nck),
                        in0=PS[:, bk, 0:w].rearrange("p (a t) -> p a t", a=nck),
                        scalar=cv(l, V_LNG, h),
                        in1=bf.unsqueeze(1).broadcast_to([128, nck, 128]),
                        op0=ALU.mult, op1=ALU.add), reads=[rPS[bk], rC, rBF], writes=[rTMP[tb]])
                    P.op("dve", lambda e, tb=tb, h=h, c0=c0, w=w: e.tensor_tensor(
                        out=U[:, h, c0:c0 + w], in0=TMP[:, tb, 0:w], in1=U[:, h, c0:c0 + w], op=ALU.mult),
                        reads=[rTMP[tb], rU[h]], writes=[rU[h]])
            dump("SG", U, rU)
            dump("MS", MS, rMS)

            pool_groups((2, 3))
            pool_finish()
            w_in_group(GA, rGA, AF.Sigmoid, 16, 0, cbs, False)
            w_in_group(GB, rGB, AF.Sigmoid, 24, 0, cbs, False)
            preload_ln_table()

            b4 = w_next()

            def m4_groups(idx):
                for gi in idx:
                    g, dh = gi // 2, gi % 2
                    bks = banks(len(cbs))
                    fns = []
                    for cc in range(2):
                        for ci, (c0, w) in enumerate(cbs):
                            o0 = (g * 2 + cc) * 256 + dh * 128
                            fns.append(lambda e, bk=bks[ci], o0=o0, g=g, cc=cc, c0=c0, w=w: e.matmul(
                                PS[:, bk, 0:w], WB[:, b4, o0:o0 + 128], PL[:, 2 * g + cc, c0:c0 + w],
                                start=(cc == 0), stop=(cc == 1)))
                    P.group("pe", fns, reads=[rW[b4], rPL[2 * g], rPL[2 * g + 1]], writes=[rPS[bk] for bk in bks])
                    oc = 2 * g + dh
                    for ci, (c0, w) in enumerate(cbs):
                        if gi % 2 == 0:
                            P.op("dve", lambda e, bk=bks[ci], oc=oc, c0=c0, w=w: e.tensor_scalar(
                                out=MS[:, oc, c0:c0 + w], in0=PS[:, bk, 0:w], scalar1=cv(l, V_PSCALE, oc),
                                scalar2=None, op0=ALU.mult), reads=[rPS[bks[ci]], rC], writes=[rMS[oc]])
                        else:
                            P.op("act", lambda e, bk=bks[ci], oc=oc, c0=c0, w=w: e.activation(
                                out=MS[:, oc, c0:c0 + w], in_=PS[:, bk, 0:w], func=AF.Identity,
                                scale=cv(l, V_PSCALE, oc)), reads=[rPS[bks[ci]], rC], writes=[rMS[oc]])

            m4_groups(range(0, 8))
            w_done()
            dump("VTM", VTM, rO)
            dump("NTM", NTM, rN)

            for half in range(2):
                b = w_next()

                def evac(fi, c0, w, bk, half=half):
                    oc = half * 4 + fi
                    P.op("dve", lambda e: e.tensor_tensor(
                        out=GA[:, oc, c0:c0 + w], in0=PS[:, bk, 0:w], in1=GA[:, oc, c0:c0 + w], op=ALU.mult),
                        reads=[rPS[bk], rGA[oc]], writes=[rGA[oc]])
                fm_matmul(b, MS, rMS, 8, evac, perk=(half == 0))
                w_done()
            dump("M1", GA, rGA)

            for half in range(2):
                b = w_next()

                def evac(fi, c0, w, bk, half=half):
                    oc = half * 4 + fi
                    P.op("dve", lambda e: e.tensor_tensor(
                        out=GB[:, oc, c0:c0 + w], in0=PS[:, bk, 0:w], in1=GB[:, oc, c0:c0 + w], op=ALU.mult),
                        reads=[rPS[bk], rGB[oc]], writes=[rGB[oc]])
                    P.op("dve", lambda e: e.tensor_tensor(
                        out=GB[:, oc, c0:c0 + w], in0=GB[:, oc, c0:c0 + w], in1=GA[:, oc, c0:c0 + w], op=ALU.add),
                        reads=[rGB[oc], rGA[oc]], writes=[rGB[oc]])
                fm_matmul(b, U, rU, 8, evac, perk=(half == 0))
                w_done()

            def evac_O(oc, c0, w, bk, vi):
                P.op("act", lambda e: e.activation(out=SQ[:, oc, c0:c0 + w], in_=PS[:, bk, 0:w], func=AF.Square),
                     reads=[rPS[bk]], writes=[rSQ[oc], rBT[bk]])
                P.op("dve", lambda e: e.tensor_scalar(out=O[:, oc, c0:c0 + w], in0=PS[:, bk, 0:w],
                                                      scalar1=cv(l, vi, oc), scalar2=None, op0=ALU.mult),
                     reads=[rPS[bk], rBT[bk], rC], writes=[rO[oc]])

            for half in range(2):
                b = w_next()
                fm_matmul(b, GB, rGB, 8, lambda fi, c0, w, bk, half=half: evac_O(half * 4 + fi, c0, w, bk, V_POSTMIX),
                          perk=(half == 0))
                w_done()
            dump("MG", GB, rGB)
            dump("O1", O[:, :, :], rO)

            norm_R(rSQ)
            residual(V_POSTMIX, hmode="gain", hvi=V_PREFFN, gained=True)
            dump("X1", X[:, :, :], rX)

            for j in range(8):
                b = w_next()

                def evac(fi, c0, w, bk, j=j):
                    fc = j * 4 + fi
                    tb = fi % 2
                    P.op("act", lambda e: e.activation(out=TMP[:, tb, c0:c0 + w], in_=PS[:, bk, 0:w], func=AF.Relu),
                         reads=[rPS[bk]], writes=[rTMP[tb]])
                    P.op("dve", lambda e: e.tensor_tensor(
                        out=ACTB[:, fc, c0:c0 + w], in0=PS[:, bk, 0:w], in1=TMP[:, tb, c0:c0 + w], op=ALU.mult),
                        reads=[rPS[bk], rTMP[tb]], writes=[rACT[fc]])
                fm_matmul(b, H, rH, 8, evac, kouter=(j == 0))
                w_done()
                if j == 0:
                    squares_of_X(ca, nt)
                    norm_R(rSQ, mode="epsp")

            for j in range(8):
                b = w_next()
                fm_matmul(b, ACTB, rACT, 32, lambda fi, c0, w, bk, j=j: evac_O(j, c0, w, bk, V_POSTFFN),
                          fis=[0], kstride=128, perk=(j == 0))
                w_done()

            norm_R(rSQ, mode="rsqrt_epsp")
            residual(V_POSTFFN, hmode="copy", gained=True)

            for half in range(2):
                b = w_next()

                def evac(fi, c0, w, bk, half=half):
                    oc = half * 4 + fi
                    P.op("act", lambda e: e.activation(
                        out=GF[:, oc, c0:c0 + w], in_=PS[:, bk, 0:w], func=AF.Sigmoid),
                        reads=[rPS[bk]], writes=rG[oc])
                fm_matmul(b, H, rH, 8, evac, kouter=(half == 0))
                w_done()

            preload_ln_table()
            bp = w_next()
            for oc in range(8):
                bks = banks(len(cbs))
                fns = []
                for kc in range(2):
                    for ci, (c0, w) in enumerate(cbs):
                        fns.append(lambda e, bk=bks[ci], kc=kc, oc=oc, c0=c0, w=w: e.matmul(
                            PS[:, bk, 0:w], WB[:, bp, kc * 1024 + oc * 128: kc * 1024 + (oc + 1) * 128],
                            PB[:, kc, c0:c0 + w], start=(kc == 0), stop=(kc == 1)))
                P.group("pe", fns, reads=[rW[bp], rPB], writes=[rPS[bk] for bk in bks])
                for ci, (c0, w) in enumerate(cbs):
                    bk = bks[ci]
                    P.op("dve", lambda e, bk=bk, oc=oc, c0=c0, w=w: e.tensor_tensor(
                        out=O[:, oc, c0:c0 + w], in0=PS[:, bk, 0:w], in1=GF[:, oc, c0:c0 + w], op=ALU.mult),
                        reads=[rPS[bk]] + rG[oc], writes=[rO[oc]])
                    P.op("act", lambda e, oc=oc, c0=c0, w=w: e.activation(
                        out=SQ[:, oc, c0:c0 + w], in_=O[:, oc, c0:c0 + w], func=AF.Square),
                        reads=[rO[oc]], writes=[rSQ[oc]])
            w_done()

            norm_R(rSQ)
            residual(V_POSTPLE)

        bufs = [(X, rX), (Z, rZ)]
        last_store = None
        for ti, (t0, nt) in enumerate(TILES):
            (Xc, rXc), (Zc, rZc) = bufs[ti % 2], bufs[(ti + 1) % 2]
            prefetch = TILES[ti + 1] if ti + 1 < len(TILES) else None
            for l in range(L_RUN):
                tile_layer(ti, t0, nt, l, Xc, rXc, Zc, rZc, prefetch)
            s0 = HALO if ti == 0 else 0
            o0 = t0 + s0 - HALO
            n_out = nt - s0
            for dc in range(8):
                last_store = P.dma("sp", "ost", lambda e, s0=s0, o0=o0, n_out=n_out, Xc=Xc, dc=dc: e.dma_start(
                    out=outT[:, dc, o0:o0 + n_out], in_=Xc[:, dc, s0:s0 + n_out]), reads=[rXc[dc]])
            for r_ in rXc:
                r_.rs["ost"] = P.cnt["ost"]
        P.final_wait("sp", [last_store] + [d[1] for d in dump_toks])

        with nc.Block() as block:
            @block.tensor
            def _(e):
                for f in P.q["pe"]:
                    f(e)

            @block.scalar
            def _(e):
                for f in P.q["act"]:
                    f(e)

            @block.vector
            def _(e):
                for f in P.q["dve"]:
                    f(e)

            @block.gpsimd
            def _(e):
                for f in P.q["pool"]:
                    f(e)

            @block.sync
            def _(e):
                for f in P.q["sp"]:
                    f(e)
    return nc


def _fm8(v):
    return np.ascontiguousarray(v.reshape(8, 128).T)


def _blk_k512(W, col0):
    K = W.shape[0]
    return W[:, col0:col0 + 512].reshape(K // 128, 128, 512).transpose(1, 0, 2).reshape(128, -1)


def _build_wstream(inp):
    ws = np.zeros((L, NBLK, 128, BLK), np.float32)
    for l in range(L):
        w_in = inp["w_in"][l]
        order = [0, 512, 2048, 2560, 1024, 1536, 3072, 3584, 4096, 4608]
        for j, c0 in enumerate(order):
            ws[l, j] = _blk_k512(w_in, c0)
        ws[l, 10, :, :2048] = inp["pool_w"][l].reshape(4, 2, 128, 256).transpose(2, 0, 1, 3).reshape(128, 2048)
        for i, name in enumerate(("w_pa", "w_pb", "w_o")):
            for half in range(2):
                ws[l, 11 + 2 * i + half] = _blk_k512(inp[name][l], half * 512)
        for j in range(8):
            ws[l, 17 + j] = _blk_k512(inp["w_ff1"][l], j * 512)
        w2 = inp["w_ff2"][l]
        for j in range(8):
            ws[l, 25 + j] = w2[:, j * 128:(j + 1) * 128].reshape(32, 128, 128).transpose(1, 0, 2).reshape(128, BLK)
        for half in range(2):
            ws[l, 33 + half] = _blk_k512(inp["w_ple_gate"][l], half * 512)
        ws[l, 35, :, :2048] = inp["w_ple_proj"][l].reshape(2, 128, 1024).transpose(1, 0, 2).reshape(128, 2048)
    return ws


_NC_CACHE = {}


def make_in_maps(inp):
    x, p = inp["x"], inp["p"]
    B, S, _ = x.shape
    wst = _build_wstream(inp)
    cvec = np.zeros((128, L * 64), np.float32)
    names = ["pre_mix_g", "post_mix_g", "pre_ffn_g", "post_ffn_g", "post_ple_g", "pool_scale", "sgu_ln_g", "sgu_ln_b"]
    for l in range(L):
        for vi, nm in enumerate(names):
            cvec[:, (l * 8 + vi) * 8:(l * 8 + vi + 1) * 8] = _fm8(inp[nm][l])
    binfm = np.zeros((128, L * 32), np.float32)
    bvrow = np.zeros((L, 1024), np.float32)
    for l in range(L):
        b = inp["b_in"][l]
        for gi, c0 in enumerate((0, 1024, 3072, 4096)):
            binfm[:, l * 32 + gi * 8: l * 32 + (gi + 1) * 8] = _fm8(b[c0:c0 + 1024])
        bvrow[l] = b[2048:3072]
    wsT = np.ascontiguousarray(inp["sgu_w_s"].transpose(3, 0, 1, 2)).reshape(128, L * 1024)
    bsbc = np.ascontiguousarray(np.broadcast_to(inp["sgu_b_s"].reshape(1, L * 1024), (128, L * 1024)))
    si = np.arange(128)
    cmask = (si[:, None] <= si[None, :]).astype(np.float32)

    in_maps = []
    for c in range(NCORES):
        b, half = c // 2, c % 2
        s0 = half * OWN
        xt = np.zeros((NTOK, D), np.float32)
        pt = np.zeros((L, NTOK, 256), np.float32)
        xt[HALO:] = x[b, s0:s0 + OWN]
        pt[:, HALO:] = p[:, b, s0:s0 + OWN]
        pcore = np.zeros((128, 80), np.float32)
        if half == 1:
            xt[:HALO] = x[b, s0 - HALO:s0]
            pt[:, :HALO] = p[:, b, s0 - HALO:s0]
            pcore[:, 0] = 1.0
        for g, w in enumerate(POOL_WINDOWS):
            j = np.arange(16)
            cnt = np.minimum(j + 1, w) if half == 0 else np.full(16, w)
            pcore[:, 1 + g * 16: 1 + (g + 1) * 16] = (1.0 / cnt.astype(np.float32))[None, :]
        xTc = np.ascontiguousarray(xt.T.reshape(8, 128, NTOK).transpose(1, 0, 2))
        pTc = np.ascontiguousarray(pt.transpose(0, 2, 1).reshape(L, 2, 128, NTOK).transpose(0, 2, 1, 3))
        in_maps.append({"xT": xTc, "pT": pTc, "wst": wst, "cvec": cvec, "binfm": binfm, "bvrow": bvrow,
                        "wsT": wsT, "bsbc": bsbc, "cmask": cmask, "pcore": pcore})
    return in_maps


def kernel(**inputs):
    inp = {k: np.asarray(v, dtype=np.float32) for k, v in inputs.items()}
    B, S, _ = inp["x"].shape
    in_maps = make_in_maps(inp)
    if "nc" not in _NC_CACHE:
        _NC_CACHE["nc"] = build_nc()
    nc = _NC_CACHE["nc"]
    res = run_bass_kernel_spmd(nc, in_maps, core_ids=list(range(NCORES)))
    out = np.empty((B, S, D), np.float32)
    for c in range(NCORES):
        b, half = c // 2, c % 2
        o = np.asarray(res.results[c]["outT"], dtype=np.float32)
        out[b, half * OWN:(half + 1) * OWN, :] = o.transpose(1, 0, 2).reshape(D, OWN).T
    return out
```
